# Optimizing a Trainium2 kernel written in Bass

```python
import jax, jax.numpy as jnp
from jax import lax
import numpy as np

D_MODEL = 1024
BATCH = 4
SEQ = 8192
DEPTH = 1

CHUNK = 64
SSM_HEADS = 16
SSM_HEAD_DIM = 64
SSM_INNER = SSM_HEADS * SSM_HEAD_DIM
SSM_GROUPS = 2
SSM_STATE = 128
CONV_WIDTH = 4
CONV_DIM = SSM_INNER + 2 * SSM_GROUPS * SSM_STATE
SSM_CHUNK = CHUNK
ATTN_HEADS = 16
ATTN_HEAD_DIM = 64
ATTN_INNER = ATTN_HEADS * ATTN_HEAD_DIM
LEFT_CHUNKS = 8
BAND = (LEFT_CHUNKS + 1) * CHUNK
MAX_REL_DIST = 128
N_REL = 2 * MAX_REL_DIST + 1
D_FF = 2816
FFN_RES_SCALE = 0.5
N_BRANCHES = 2
RMS_EPS = 1e-6
IN_SPLIT_SIZES = (SSM_INNER, CONV_DIM, SSM_HEADS, ATTN_INNER, ATTN_INNER, ATTN_INNER, N_BRANCHES * D_MODEL)
IN_PROJ_DIM = SSM_INNER + CONV_DIM + SSM_HEADS + 3 * ATTN_INNER + N_BRANCHES * D_MODEL

kernel_name = "hybrid_ssd_chunkattn_macaron"


def rmsnorm(x, w):
    xf = x.astype(jnp.float32)
    y = xf * lax.rsqrt(jnp.mean(xf * xf, axis=-1, keepdims=True) + RMS_EPS)
    return (y * w.astype(jnp.float32)).astype(x.dtype)


def swiglu_ffn(h, w_gu, w_down):
    g, u = jnp.split(h @ w_gu, 2, axis=-1)
    return (jax.nn.silu(g) * u) @ w_down


def split_in_proj(p):
    offs = []
    acc = 0
    for sz in IN_SPLIT_SIZES[:-1]:
        acc += sz
        offs.append(acc)
    return jnp.split(p, offs, axis=-1)


def causal_depthwise_conv(x, w, b):
    y = lax.conv_general_dilated(
        x, w[:, None, :].astype(x.dtype), window_strides=(1,),
        padding=[(CONV_WIDTH - 1, 0)], dimension_numbers=('NWC', 'WIO', 'NWC'),
        feature_group_count=x.shape[-1])
    return y + b.astype(x.dtype)


def ssd_chunked(xh, dt, A, Bm, Cm):
    b, s = xh.shape[:2]
    nc = s // SSM_CHUNK
    r = SSM_HEADS // SSM_GROUPS
    X = (xh * dt[..., None]).reshape(b, nc, SSM_CHUNK, SSM_GROUPS, r, SSM_HEAD_DIM)
    a = (dt * A).reshape(b, nc, SSM_CHUNK, SSM_GROUPS, r)
    Bc = Bm.reshape(b, nc, SSM_CHUNK, SSM_GROUPS, SSM_STATE)
    Cc = Cm.reshape(b, nc, SSM_CHUNK, SSM_GROUPS, SSM_STATE)
    a_cum = jnp.cumsum(a, axis=2)
    seg = a_cum[:, :, :, None] - a_cum[:, :, None, :]
    causal = jnp.tril(jnp.ones((SSM_CHUNK, SSM_CHUNK), dtype=bool))[:, :, None, None]
    Ldec = jnp.exp(jnp.where(causal, seg, -jnp.inf))
    CB = jnp.einsum('bclgn,bcsgn->bclsg', Cc, Bc)
    y_diag = jnp.einsum('bclsg,bclsgr,bcsgrp->bclgrp', CB, Ldec, X)
    decay_to_end = jnp.exp(a_cum[:, :, -1:] - a_cum)
    states = jnp.einsum('bclgn,bclgr,bclgrp->bcgrpn', Bc, decay_to_end, X)
    chunk_decay = jnp.exp(a_cum[:, :, -1])

    def step(h, inp):
        st, dec = inp
        return h * dec[..., None, None] + st, h

    h0 = jnp.zeros((b, SSM_GROUPS, r, SSM_HEAD_DIM, SSM_STATE), dtype=X.dtype)
    _, prev = lax.scan(step, h0, (jnp.moveaxis(states, 1, 0), jnp.moveaxis(chunk_decay, 1, 0)))
    prev = jnp.moveaxis(prev, 0, 1)
    y_off = jnp.einsum('bclgn,bcgrpn,bclgr->bclgrp', Cc, prev, jnp.exp(a_cum))
    return (y_diag + y_off).reshape(b, s, SSM_HEADS, SSM_HEAD_DIM)


def ssd_branch(z, xBC, dt_raw, conv_w, conv_b, dt_bias, A_log, D_skip, norm_w):
    dtype = z.dtype
    b, s = z.shape[:2]
    xBC = jax.nn.silu(causal_depthwise_conv(xBC, conv_w, conv_b))
    xs, Bm, Cm = jnp.split(xBC, [SSM_INNER, SSM_INNER + SSM_GROUPS * SSM_STATE], axis=-1)
    xh = xs.reshape(b, s, SSM_HEADS, SSM_HEAD_DIM).astype(jnp.float32)
    Bm = Bm.reshape(b, s, SSM_GROUPS, SSM_STATE).astype(jnp.float32)
    Cm = Cm.reshape(b, s, SSM_GROUPS, SSM_STATE).astype(jnp.float32)
    dt = jax.nn.softplus(dt_raw.astype(jnp.float32) + dt_bias.astype(jnp.float32))
    A = -jnp.exp(A_log.astype(jnp.float32))
    y = ssd_chunked(xh, dt, A, Bm, Cm)
    y = y + D_skip.astype(jnp.float32)[:, None] * xh
    y = y.reshape(b, s, SSM_INNER) * jax.nn.silu(z.astype(jnp.float32))
    yg = y.reshape(b, s, SSM_GROUPS, SSM_INNER // SSM_GROUPS)
    yg = yg * lax.rsqrt(jnp.mean(yg * yg, axis=-1, keepdims=True) + RMS_EPS)
    y = yg.reshape(b, s, SSM_INNER) * norm_w.astype(jnp.float32)
    return y.astype(dtype)


def chunk_band_attention(q, k, v, rel_bias):
    dtype = q.dtype
    b, s = q.shape[:2]
    nc = s // CHUNK
    pad = LEFT_CHUNKS * CHUNK
    scale = ATTN_HEAD_DIM ** -0.5
    qh = q.reshape(b, s, ATTN_HEADS, ATTN_HEAD_DIM) * jnp.asarray(scale, dtype)
    kp = jnp.pad(k.reshape(b, s, ATTN_HEADS, ATTN_HEAD_DIM), ((0, 0), (pad, 0), (0, 0), (0, 0)))
    vp = jnp.pad(v.reshape(b, s, ATTN_HEADS, ATTN_HEAD_DIM), ((0, 0), (pad, 0), (0, 0), (0, 0)))
    qc = jnp.moveaxis(qh.reshape(b, nc, CHUNK, ATTN_HEADS, ATTN_HEAD_DIM), 1, 0)
    rel = (jnp.arange(CHUNK)[:, None] + pad) - jnp.arange(BAND)[None, :]
    idx = jnp.clip(rel, -MAX_REL_DIST, MAX_REL_DIST) + MAX_REL_DIST
    bias = rel_bias.astype(jnp.float32)[:, idx]
    band_pos = jnp.arange(BAND) - pad

    def one_chunk(args):
        c, q_c = args
        start = c * CHUNK
        k_b = lax.dynamic_slice_in_dim(kp, start, BAND, axis=1)
        v_b = lax.dynamic_slice_in_dim(vp, start, BAND, axis=1)
        sc = jnp.einsum('blhd,bjhd->bhlj', q_c, k_b).astype(jnp.float32) + bias[None]
        valid = (start + band_pos) >= 0
        sc = jnp.where(valid[None, None, None, :], sc, -jnp.inf)
        p = jax.nn.softmax(sc, axis=-1).astype(dtype)
        return jnp.einsum('bhlj,bjhd->blhd', p, v_b)

    out = lax.map(one_chunk, (jnp.arange(nc, dtype=jnp.int32), qc))
    return jnp.moveaxis(out, 0, 1).reshape(b, s, ATTN_INNER)


def setup_inputs(seed: int = 0) -> dict:
    key = jax.random.key(seed)
    ks = jax.random.split(key, 24)
    f32 = jnp.float32
    L = DEPTH

    def nrm(k, shape, fan_in):
        return jax.random.normal(k, shape, f32) * (fan_in ** -0.5)

    def gain(k, shape):
        return 1.0 + 0.02 * jax.random.normal(k, shape, f32)

    dt_init = jnp.exp(jax.random.uniform(ks[10], (L, SSM_HEADS), f32, np.log(1e-3), np.log(1e-1)))
    dt_bias = dt_init + jnp.log(-jnp.expm1(-dt_init))
    return {
        "x": jax.random.normal(ks[0], (BATCH, SEQ, D_MODEL), f32),
        "ffn1_norm_w": gain(ks[1], (L, D_MODEL)),
        "ffn1_w_gu": nrm(ks[2], (L, D_MODEL, 2 * D_FF), D_MODEL),
        "ffn1_w_down": nrm(ks[3], (L, D_FF, D_MODEL), D_FF),
        "mix_norm_w": gain(ks[4], (L, D_MODEL)),
        "w_in": nrm(ks[5], (L, D_MODEL, IN_PROJ_DIM), D_MODEL),
        "conv_w": nrm(ks[6], (L, CONV_WIDTH, CONV_DIM), CONV_WIDTH),
        "conv_b": 0.02 * jax.random.normal(ks[7], (L, CONV_DIM), f32),
        "dt_bias": dt_bias,
        "A_log": jnp.log(jax.random.uniform(ks[8], (L, SSM_HEADS), f32, 1.0, 16.0)),
        "D_skip": 1.0 + 0.1 * jax.random.normal(ks[9], (L, SSM_HEADS), f32),
        "ssm_norm_w": gain(ks[11], (L, SSM_INNER)),
        "rel_bias": 0.1 * jax.random.normal(ks[12], (L, ATTN_HEADS, N_REL), f32),
        "w_branch_ssm": nrm(ks[13], (L, SSM_INNER, D_MODEL), SSM_INNER),
        "w_branch_attn": nrm(ks[14], (L, ATTN_INNER, D_MODEL), ATTN_INNER),
        "w_out": nrm(ks[15], (L, D_MODEL, D_MODEL), D_MODEL),
        "ffn2_norm_w": gain(ks[16], (L, D_MODEL)),
        "ffn2_w_gu": nrm(ks[17], (L, D_MODEL, 2 * D_FF), D_MODEL),
        "ffn2_w_down": nrm(ks[18], (L, D_FF, D_MODEL), D_FF),
        "final_norm_w": gain(ks[19], (D_MODEL,)),
    }


def reference(x, ffn1_norm_w, ffn1_w_gu, ffn1_w_down, mix_norm_w, w_in, conv_w, conv_b,
              dt_bias, A_log, D_skip, ssm_norm_w, rel_bias, w_branch_ssm, w_branch_attn,
              w_out, ffn2_norm_w, ffn2_w_gu, ffn2_w_down, final_norm_w):
    for l in range(DEPTH):
        x = x + FFN_RES_SCALE * swiglu_ffn(rmsnorm(x, ffn1_norm_w[l]), ffn1_w_gu[l], ffn1_w_down[l])
        h = rmsnorm(x, mix_norm_w[l])
        z, xBC, dt_raw, q, k, v, g = split_in_proj(h @ w_in[l])
        y_ssm = ssd_branch(z, xBC, dt_raw, conv_w[l], conv_b[l], dt_bias[l], A_log[l],
                           D_skip[l], ssm_norm_w[l])
        y_attn = chunk_band_attention(q, k, v, rel_bias[l])
        g_ssm, g_attn = jnp.split(jax.nn.sigmoid(g), 2, axis=-1)
        merged = g_ssm * (y_ssm @ w_branch_ssm[l]) + g_attn * (y_attn @ w_branch_attn[l])
        x = x + merged @ w_out[l]
        x = x + FFN_RES_SCALE * swiglu_ffn(rmsnorm(x, ffn2_norm_w[l]), ffn2_w_gu[l], ffn2_w_down[l])
    return rmsnorm(x, final_norm_w)
```

```python
import numpy as np
from contextlib import ExitStack
import concourse.bass as bass
import concourse.mybir as mybir
from concourse.bass_utils import run_bass_kernel_spmd

F32 = mybir.dt.float32
BF16 = mybir.dt.bfloat16
AF = mybir.ActivationFunctionType
ALU = mybir.AluOpType

D = 1024
DFF = 2816
NFC = 22
T = 512
NTT = 4
NTOK = 4096
INP = 7696
EPS = 1e-6
NEG = -30000.0
C_Z, C_X, C_B, C_C, C_DT, C_Q, C_K, C_V, C_GS, C_GA = 0, 1024, 2048, 2304, 2560, 2576, 3600, 4624, 5648, 6672
NS = 5


class Buf:
    __slots__ = ("name", "lw", "rd")

    def __init__(self, name):
        self.name = name
        self.lw = None
        self.rd = {}


class Tn:
    def __init__(self, t, bufs):
        self.t = t
        self.b = bufs


class FW:
    def __init__(self, nc, stack, n_dma_sems=32):
        self.nc = nc
        self.engs = {"pe": nc.tensor, "act": nc.scalar, "dve": nc.vector, "pool": nc.gpsimd, "sp": nc.sync}
        self.sem = {}
        self.cnt = {}
        for k in self.engs:
            self.sem[k] = stack.enter_context(nc.semaphore("tick_" + k))
            self.cnt[k] = 0
        self.dsem = [stack.enter_context(nc.semaphore("dma_%d" % i)) for i in range(n_dma_sems)]
        self.dcnt = [0] * n_dma_sems
        self.dnext2 = {"sp": 0, "pool": 0}
        self.seen = {k: {} for k in self.engs}
        self.ninst = 0

    def _wait(self, eng, dep):
        kind, idx, val = dep
        key = (kind, idx)
        if kind == "e" and idx == eng and eng == "pe":
            return
        if self.seen[eng].get(key, 0) >= val:
            return
        s = self.sem[idx] if kind == "e" else self.dsem[idx]
        self.engs[eng].wait_ge(s, val)
        self.seen[eng][key] = val

    def _deps(self, eng, reads, writes):
        for b in reads:
            if b.lw is not None:
                self._wait(eng, b.lw)
        for b in writes:
            if b.lw is not None:
                self._wait(eng, b.lw)
            for d in b.rd.values():
                if d[0] == "e" and d[1] == eng and eng != "sp" and eng != "pool":
                    continue
                self._wait(eng, d)

    def _mark(self, dep, reads, writes):
        for b in writes:
            b.lw = dep
            b.rd = {}
        for b in reads:
            if b in writes:
                continue
            b.rd[(dep[0], dep[1])] = dep

    def op(self, eng, fn, reads=(), writes=()):
        self._deps(eng, reads, writes)
        ins = fn(self.engs[eng])
        self.cnt[eng] += 1
        ins.then_inc(self.sem[eng], 1)
        dep = ("e", eng, self.cnt[eng])
        self._mark(dep, reads, writes)
        self.ninst += 1
        return dep

    def dma(self, out, in_, reads=(), writes=(), q="sp"):
        half = len(self.dsem) // 2
        base = 0 if q == "sp" else half
        i = base + self.dnext2[q]
        self.dnext2[q] = (self.dnext2[q] + 1) % half
        if self.dcnt[i] > 0:
            self._wait(q, ("d", i, self.dcnt[i]))
        self._deps(q, reads, writes)
        ins = self.engs[q].dma_start(out=out, in_=in_)
        self.dcnt[i] += 16
        ins.then_inc(self.dsem[i], 16)
        dep = ("d", i, self.dcnt[i])
        self._mark(dep, reads, writes)
        self.ninst += 1
        return dep

    def barrier(self, engs=("pe", "act", "dve"), dmas=False):
        for e in engs:
            for f in self.engs:
                if self.cnt[f] > 0 and not (f == e and e == "pe"):
                    self._wait(e, ("e", f, self.cnt[f]))
            if dmas:
                for i in range(len(self.dsem)):
                    if self.dcnt[i] > 0:
                        self._wait(e, ("d", i, self.dcnt[i]))


def build(n_state=8, n_full=8, dbg=()):
    nc = bass.Bass("TRN2", target_bir_lowering=False)
    di = lambda n, s, dt=F32: nc.dram_tensor(n, s, dt, kind="ExternalInput").ap()
    x_prev = di("x_prev", [NTOK, D])
    x_main = di("x_main", [NTOK, D])
    w_gu = [di("w_gu1", [D, 2 * DFF]), di("w_gu2", [D, 2 * DFF])]
    w_dn = [di("w_dn1", [DFF, D]), di("w_dn2", [DFF, D])]
    w_in = di("w_in", [D, INP])
    w_bs = di("w_bs", [D, D])
    w_ba = di("w_ba", [D, D])
    w_out = di("w_out", [D, D])
    nwf_d = di("nwf", [128, 4, 8])
    fnw_d = di("fnw", [128, D])
    cw_d = di("cw", [128, 12, 4])
    cb_d = di("cb", [128, 12])
    hp_d = di("hp", [128, 3, 16])
    btab_d = di("btab", [16, 128, 640])
    hmask_d = di("hmask", [128, 1])
    cst_d = di("cst", [128, 4, 128])
    y_out = nc.dram_tensor("y", [NTOK, D], F32, kind="ExternalOutput").ap()
    dbg_out = {}
    for name, shape, dt in dbg:
        dbg_out[name] = nc.dram_tensor("dbg_" + name, list(shape), dt, kind="ExternalOutput").ap()

    st = ExitStack()
    fw = FW(nc, st)

    uid = [0]

    def sb(stack, name, shape, dt, nb=1):
        uid[0] += 1
        t = stack.enter_context(nc.sbuf_tensor("%s_%d" % (name, uid[0]), list(shape), dt))
        return Tn(t, [Buf("%s_%d" % (name, i)) for i in range(nb)])

    x_res = sb(st, "x_res", [128, NTT, D], F32, NTT)
    hT = sb(st, "hT", [128, 8, T], BF16, 1)
    ring = sb(st, "ring", [128, NS, 4096], BF16, NS)
    Hst = sb(st, "Hst", [128, 1024], F32)
    Hbf = sb(st, "Hbf", [128, 1024], BF16)
    EB = sb(st, "EB", [128, 16, 640], BF16)
    kT = sb(st, "kT", [128, 8, 2 * T], BF16)
    Vtok = sb(st, "Vtok", [128, 8, 1152], BF16)
    halo = sb(st, "halo", [128, 12, 3], BF16)
    cst = sb(st, "cst", [128, 4, 128], F32)
    identb = sb(st, "identb", [128, 128], BF16)
    NM8 = sb(st, "NM8", [128, 4, 128], BF16)
    nwf = sb(st, "nwf", [128, 4, 8], F32)
    fnw = sb(st, "fnw", [128, D], F32)
    cw = sb(st, "cw", [128, 12, 4], F32)
    cb = sb(st, "cb", [128, 12], F32)
    hp = sb(st, "hp", [128, 3, 16], F32)
    Aneg = sb(st, "Aneg", [128, 16], F32)
    hmask = sb(st, "hmask", [128, 1], F32)
    onesb = sb(st, "onesb", [128, 64], BF16)
    onesh = sb(st, "onesh", [128, 64], BF16)
    col1 = sb(st, "col1", [128, 1], F32)
    xnp = sb(st, "xnp", [128, 2, D], BF16, 2)
    ssp = sb(st, "ssp", [128, NTT], F32)
    rstdp = sb(st, "rstdp", [128, NTT], F32)

    psf_t = [st.enter_context(nc.psum_tensor("psf%d" % i, [128, 512], F32)) for i in range(6)]
    psf_b = [Buf("psf%d" % i) for i in range(6)]
    psb_t = [st.enter_context(nc.psum_tensor("psb%d" % i, [128, 1024], BF16)) for i in range(2)]
    psb_b = [Buf("psb%d" % i) for i in range(2)]
    pctr = {"f": 0, "b": 0, "r": 0}

    def psf():
        i = pctr["f"]
        pctr["f"] = (i + 1) % 4
        return psf_t[i], psf_b[i]

    def psfix(i):
        return psf_t[i], psf_b[i]

    def psb():
        i = pctr["b"]
        pctr["b"] = (i + 1) % 2
        return psb_t[i], psb_b[i]

    scratch = {}

    def wload(src2d, kc0, nkc, c0, ncols):
        i = pctr["r"]
        pctr["r"] = (i + 1) % NS
        slot2d = ring.t[:, i, 0:nkc * ncols]
        slot = slot2d.rearrange("p (k c) -> p k c", k=nkc)
        key = (src2d.tensor.name, kc0, nkc, c0, ncols)
        if key not in scratch:
            srcap = src2d.rearrange("(kc p) c -> p kc c", p=128)[:, kc0:kc0 + nkc, c0:c0 + ncols]
            fw.dma(slot, srcap, writes=[ring.b[i]], q="pool")
            d = nc.dram_tensor("ws%d" % len(scratch), [128, nkc * ncols], BF16, kind="Internal").ap()
            b = Buf("ws%d" % len(scratch))
            scratch[key] = (d, b)
            fw.dma(d, slot2d, reads=[ring.b[i]], writes=[b], q="sp")
        else:
            d, b = scratch[key]
            fw.dma(slot2d, d, reads=[b], writes=[ring.b[i]], q="sp")
        return slot, ring.b[i]

    def dump(name, ap, bufs):
        if name in dbg_out:
            fw.dma(dbg_out[name], ap, reads=bufs, q="sp")

    fw.dma(cst.t[:], cst_d, writes=cst.b)
    fw.dma(nwf.t[:], nwf_d, writes=nwf.b)
    fw.dma(fnw.t[:], fnw_d, writes=fnw.b)
    fw.dma(cw.t[:], cw_d, writes=cw.b)
    fw.dma(cb.t[:], cb_d, writes=cb.b)
    fw.dma(hp.t[:], hp_d, writes=hp.b)
    fw.dma(hmask.t[:], hmask_d, writes=hmask.b)
    fw.op("dve", lambda e: e.tensor_copy(identb.t[:], cst.t[:, 0, :]), reads=cst.b, writes=identb.b)
    for g in range(4):
        fw.op("dve", lambda e: e.tensor_copy(NM8.t[:, g, :], cst.t[:, 3, :]), reads=cst.b, writes=NM8.b)
    fw.op("dve", lambda e: e.memset(onesb.t[:], 1.0), writes=onesb.b)
    fw.op("dve", lambda e: e.memset(col1.t[:], 1.0), writes=col1.b)
    fw.op("dve", lambda e: e.memset(Hst.t[:], 0.0), writes=Hst.b)
    fw.op("dve", lambda e: e.memset(halo.t[:], 0.0), writes=halo.b)
    fw.op("dve", lambda e: e.memset(kT.t[:], 0.0), writes=kT.b)
    fw.op("dve", lambda e: e.memset(Vtok.t[:], 0.0), writes=Vtok.b)
    fw.op("dve", lambda e: e.memset(Vtok.t[:, 4:8, 0:64], 1.0), writes=Vtok.b)
    fw.op("dve", lambda e: e.memset(Vtok.t[:, 4:8, 1088:1152], 1.0), writes=Vtok.b)
    fw.op("dve", lambda e: e.tensor_copy(onesh.t[:], hmask.t[:, 0:1].to_broadcast([128, 64])), reads=hmask.b, writes=onesh.b)
    fw.op("act", lambda e: e.activation(Aneg.t[:], hp.t[:, 1, :], AF.Exp), reads=hp.b, writes=Aneg.b)
    fw.op("dve", lambda e: e.tensor_scalar(Aneg.t[:], Aneg.t[:], -1.0, None, ALU.mult), reads=Aneg.b, writes=Aneg.b)
    dgw = nc.dram_tensor("dgw", [128, 48, 128], BF16, kind="Internal").ap()
    with ExitStack() as phd:
        dg = sb(phd, "dg", [128, 48, 128], BF16)
        for m in range(12):
            for tap in range(4):
                fw.op("dve", lambda e: e.tensor_scalar(dg.t[:, m * 4 + tap, :], identb.t[:], cw.t[:, m, tap:tap + 1], None, ALU.mult),
                      reads=identb.b + cw.b, writes=dg.b)
        dgdep = fw.dma(dgw, dg.t[:], reads=dg.b)
        fw.barrier(("pe", "act", "dve", "sp"), dmas=True)

    for h in range(16):
        fw.dma(EB.t[:, h, :], btab_d[h], writes=EB.b, q="pool")

    identb_ap = identb.t[:]
    epsc = sb(st, "epsc", [128, 1], F32)
    fw.op("dve", lambda e: e.memset(epsc.t[:], EPS), writes=epsc.b)

    def rsqrt_eps(out_ap, in_ap, rb, wb):
        fw.op("act", lambda e: e.activation(out_ap, in_ap, AF.Ln, bias=epsc.t[:, 0:1]), reads=rb + epsc.b, writes=wb)
        fw.op("act", lambda e: e.activation(out_ap, out_ap, AF.Exp, scale=-0.5), reads=wb, writes=wb)

    def norm_prep():
        fw.op("dve", lambda e: e.memset(ssp.t[:], 0.0), writes=ssp.b)

    def norm_to_hT(ni, ph):
        xn, ss, rstd = xnp, ssp, rstdp
        junk = sb(ph, "njunk", [128, D], BF16)
        for tt in range(NTT):
            fw.op("act", lambda e: e.activation(junk.t[:], x_res.t[:, tt, :], AF.Square, scale=1.0 / 32.0,
                                                accum_out=ss.t[:, tt:tt + 1]),
                  reads=[x_res.b[tt]], writes=junk.b + ss.b)
        fw.op("act", lambda e: e.activation(rstd.t[:], ss.t[:], AF.Ln, bias=epsc.t[:, 0:1]), reads=ss.b + epsc.b, writes=rstd.b)
        fw.op("act", lambda e: e.activation(rstd.t[:], rstd.t[:], AF.Exp, scale=-0.5), reads=rstd.b, writes=rstd.b)
        for tt in range(NTT):
            k = tt % 2
            fw.op("act", lambda e: e.activation(xn.t[:, k, :], x_res.t[:, tt, :], AF.Copy, scale=rstd.t[:, tt:tt + 1]),
                  reads=[x_res.b[tt]] + rstd.b, writes=[xn.b[k]])
            pt, pb = psb()
            for j in range(8):
                fw.op("pe", lambda e: e.transpose(pt[:, j * 128:(j + 1) * 128], xn.t[:, k, j * 128:(j + 1) * 128], identb_ap),
                      reads=[xn.b[k]] + identb.b, writes=[pb])
            fw.op("dve", lambda e: e.tensor_tensor(hT.t[:, :, tt * 128:(tt + 1) * 128],
                                                   pt[:].rearrange("p (j c) -> p j c", j=8),
                                                   nwf.t[:, ni, :].unsqueeze(2).to_broadcast([128, 8, 128]), ALU.mult),
                  reads=[pb] + nwf.b, writes=hT.b)

    def proj_fm(src2d, c0, nchunks, nkc, rhs, rhs_bufs, evac, ncol=T, evac2=None, evac3=None):
        pend = []
        for u0 in range(0, nchunks, 4):
            nu = min(4, nchunks - u0)
            slot, sbuf_ = wload(src2d, 0, nkc, c0 + u0 * 128, nu * 128)
            for mo in range(nu):
                ps, pb = psf()
                for kc in range(nkc):
                    fw.op("pe", lambda e: e.matmul(ps[:, 0:ncol], slot[:, kc, mo * 128:(mo + 1) * 128], rhs(kc),
                                                   start=(kc == 0), stop=(kc == nkc - 1)),
                          reads=[sbuf_] + rhs_bufs, writes=[pb])
                evac(u0 + mo, ps, pb)
                pend.append(u0 + mo)
                if evac2 is not None and len(pend) >= 2:
                    evac2(pend[-2])
                if evac3 is not None and len(pend) >= 3:
                    evac3(pend[-3])
        if evac2 is not None and pend:
            evac2(pend[-1])
        if evac3 is not None:
            if len(pend) >= 2:
                evac3(pend[-2])
            if pend:
                evac3(pend[-1])

    def proj_tm(src2d, c0, ncols, nkc, lhs, lhs_bufs, evac):
        for u0 in range(0, ncols, 512):
            nc_ = min(512, ncols - u0)
            slot, sbuf_ = wload(src2d, 0, nkc, c0 + u0, nc_)
            for tt in range(NTT):
                ps, pb = psf()
                for kc in range(nkc):
                    fw.op("pe", lambda e: e.matmul(ps[:, 0:nc_], lhs(kc, tt), slot[:, kc, :],
                                                   start=(kc == 0), stop=(kc == nkc - 1)),
                          reads=[sbuf_] + lhs_bufs, writes=[pb])
                evac(u0 // 512, tt, ps, pb)

    def ffn(which, after=None):
        with ExitStack() as ph:
            actT = sb(ph, "actT", [128, NFC, T], BF16, 1)
            sg = sb(ph, "sg", [128, 2, T], F32, 2)
            norm_prep()
            wgu = w_gu[which]
            wd = w_dn[which]
            for f0 in range(0, NFC, 4):
                nf = min(4, NFC - f0)
                gs, gb = wload(wgu, 0, 8, f0 * 128, nf * 128)
                us, ub = wload(wgu, 0, 8, DFF + f0 * 128, nf * 128)
                for fo in range(nf):
                    f = f0 + fo
                    pg, pgb = psf()
                    pu, pub = psf()
                    for kc in range(8):
                        fw.op("pe", lambda e: e.matmul(pg[:, 0:T], gs[:, kc, fo * 128:(fo + 1) * 128], hT.t[:, kc, :],
                                                       start=(kc == 0), stop=(kc == 7)), reads=[gb] + hT.b, writes=[pgb])
                    for kc in range(8):
                        fw.op("pe", lambda e: e.matmul(pu[:, 0:T], us[:, kc, fo * 128:(fo + 1) * 128], hT.t[:, kc, :],
                                                       start=(kc == 0), stop=(kc == 7)), reads=[ub] + hT.b, writes=[pub])
                    k = f % 2
                    fw.op("act", lambda e: e.activation(sg.t[:, k, :], pg[:, 0:T], AF.Silu), reads=[pgb], writes=[sg.b[k]])
                    fw.op("dve", lambda e: e.tensor_tensor(actT.t[:, f, :], sg.t[:, k, :], pu[:, 0:T], ALU.mult),
                          reads=[sg.b[k], pub], writes=actT.b)
            groups = [(0, 8), (8, 8), (16, 6)]
            for dh in range(2):
                slots = [wload(wd, g0, gn, dh * 512, 512) for (g0, gn) in groups]
                for tt in range(NTT):
                    ps, pb = psf()
                    for gi, (g0, gn) in enumerate(groups):
                        sl, slb = slots[gi]
                        for fo in range(gn):
                            f = g0 + fo
                            fw.op("pe", lambda e: e.matmul(ps[:, 0:512], actT.t[:, f, tt * 128:(tt + 1) * 128], sl[:, fo, :],
                                                           start=(f == 0), stop=(f == NFC - 1)),
                                  reads=[slb] + actT.b, writes=[pb])
                    xs = x_res.t[:, tt, dh * 512:(dh + 1) * 512]
                    fw.op("dve", lambda e: e.scalar_tensor_tensor(xs, ps[:, 0:512], 0.5, xs, ALU.mult, ALU.add),
                          reads=[pb, x_res.b[tt]], writes=[x_res.b[tt]])
            dm = False
            if after is not None:
                dm = after(ph)
            fw.barrier(("pe", "act", "dve"), dmas=bool(dm))

    def mixer(full, need_kv, first_full, after=None):
        with ExitStack() as ph:
            xtok = sb(ph, "xtok", [128, NTT, D], BF16, 1)
            BT = sb(ph, "BT", [128, 2, T], BF16)
            CT = sb(ph, "CT", [128, 2, T], BF16)
            Btok = sb(ph, "Btok", [128, NTT, 256], BF16)
            dt = sb(ph, "dt", [128, NTT, 16], F32)
            if full:
                yssmT = sb(ph, "yssmT", [128, 8, T], BF16)
                yattT = sb(ph, "yattT", [128, 8, T], BF16)
            with ExitStack() as p1:
                pre = sb(p1, "pre", [128, 2, T + 4], BF16, 2)
                xfm = sb(p1, "xfm", [128, 2, T], BF16, 2)
                dgs = sb(p1, "dgs", [128, 48, 128], BF16)
                fw.barrier(("pool",))
                fw.dma(dgs.t[:], dgw, writes=dgs.b, q="pool")
                cps = {}

                def ev_xbc(m, ps, pb):
                    k = m % 2
                    fw.op("act", lambda e: e.copy(pre.t[:, k, 3:T + 3], ps[:, 0:T]), reads=[pb], writes=[pre.b[k]])
                    fw.op("dve", lambda e: e.tensor_copy(pre.t[:, k, 0:3], halo.t[:, m, :]), reads=halo.b, writes=[pre.b[k]])
                    fw.op("dve", lambda e: e.tensor_copy(halo.t[:, m, :], pre.t[:, k, T:T + 3]), reads=[pre.b[k]], writes=halo.b)

                def ev_xbc2(m):
                    k = m % 2
                    pc, pcb = psf()
                    for tap in range(4):
                        fw.op("pe", lambda e: e.matmul(pc[:, 0:T], dgs.t[:, m * 4 + tap, :], pre.t[:, k, tap:tap + T],
                                                       start=(tap == 0), stop=(tap == 3)), reads=dgs.b + [pre.b[k]], writes=[pcb])
                    if m < 8:
                        fw.op("act", lambda e: e.activation(xfm.t[:, k, :], pc[:, 0:T], AF.Silu, bias=cb.t[:, m:m + 1]), reads=[pcb] + cb.b, writes=[xfm.b[k]])
                    elif m < 10:
                        g = m - 8
                        fw.op("act", lambda e: e.activation(BT.t[:, g, :], pc[:, 0:T], AF.Silu, bias=cb.t[:, m:m + 1]), reads=[pcb] + cb.b, writes=BT.b)
                    else:
                        g = m - 10
                        fw.op("act", lambda e: e.activation(CT.t[:, g, :], pc[:, 0:T], AF.Silu, bias=cb.t[:, m:m + 1]), reads=[pcb] + cb.b, writes=CT.b)

                def ev_xbc3(m):
                    k = m % 2
                    if m < 8:
                        pt, ptb = psb()
                        for tt in range(NTT):
                            fw.op("pe", lambda e: e.transpose(pt[:, tt * 128:(tt + 1) * 128], xfm.t[:, k, tt * 128:(tt + 1) * 128], identb_ap),
                                  reads=[xfm.b[k]] + identb.b, writes=[ptb])
                        fw.op("dve", lambda e: e.tensor_copy(xtok.t[:, :, m * 128:(m + 1) * 128], pt[:, 0:512].rearrange("p (t c) -> p t c", t=NTT)),
                              reads=[ptb], writes=xtok.b)
                    elif m < 10:
                        g = m - 8
                        pt, ptb = psb()
                        for tt in range(NTT):
                            fw.op("pe", lambda e: e.transpose(pt[:, tt * 128:(tt + 1) * 128], BT.t[:, g, tt * 128:(tt + 1) * 128], identb_ap),
                                  reads=BT.b + identb.b, writes=[ptb])
                        fw.op("dve", lambda e: e.tensor_copy(Btok.t[:, :, g * 128:(g + 1) * 128], pt[:, 0:512].rearrange("p (t c) -> p t c", t=NTT)),
                              reads=[ptb], writes=Btok.b)

                def ev_dt(u, tt, ps, pb):
                    fw.op("dve", lambda e: e.tensor_tensor(dt.t[:, tt, :], ps[:, 0:16], hp.t[:, 0, :], ALU.add), reads=[pb] + hp.b, writes=dt.b)
                    fw.op("act", lambda e: e.activation(dt.t[:, tt, :], dt.t[:, tt, :], AF.Exp), reads=dt.b, writes=dt.b)
                    fw.op("act", lambda e: e.activation(dt.t[:, tt, :], dt.t[:, tt, :], AF.Ln, bias=col1.t[:, 0:1]), reads=dt.b + col1.b, writes=dt.b)

                proj_tm(w_in, C_DT, 16, 8, lambda kc, tt: hT.t[:, kc, tt * 128:(tt + 1) * 128], hT.b, ev_dt)
                proj_fm(w_in, C_X, 12, 8, lambda kc: hT.t[:, kc, :], hT.b, ev_xbc, evac2=ev_xbc2, evac3=ev_xbc3)

                fw.barrier()
            dump("xtok", xtok.t[:], xtok.b)
            dump("dt", dt.t[:], dt.b)

            def kv_unit(i):
                if i < 2:
                    slot, sbuf_ = wload(w_in, 0, 8, C_K + i * 512, 512)
                    for mo in range(4):
                        m = i * 4 + mo
                        ps, pb = psf()
                        for kc in range(8):
                            fw.op("pe", lambda e: e.matmul(ps[:, 0:T], slot[:, kc, mo * 128:(mo + 1) * 128], hT.t[:, kc, :],
                                                           start=(kc == 0), stop=(kc == 7)), reads=[sbuf_] + hT.b, writes=[pb])
                        fw.op("act", lambda e: e.copy(kT.t[:, m, T:2 * T], ps[:, 0:T]), reads=[pb], writes=kT.b)
                else:
                    u = i - 2
                    slot, sbuf_ = wload(w_in, 0, 8, C_V + u * 512, 512)
                    for tt in range(NTT):
                        ps, pb = psf()
                        for kc in range(8):
                            fw.op("pe", lambda e: e.matmul(ps[:, 0:512], hT.t[:, kc, tt * 128:(tt + 1) * 128], slot[:, kc, :],
                                                           start=(kc == 0), stop=(kc == 7)), reads=[sbuf_] + hT.b, writes=[pb])
                        fw.op("act", lambda e: e.copy(Vtok.t[:, 4 + tt, 64 + u * 512:64 + (u + 1) * 512], ps[:, 0:512]), reads=[pb], writes=Vtok.b)

            with ExitStack() as p2:
                NB = 2 if full else NTT
                a_t = sb(p2, "a_t", [128, NB, 16], F32, NB)
                wdt = sb(p2, "wdt", [128, NB, 16], F32, NB)
                decr = sb(p2, "decr", [128, NB, 16], F32, NB)
                xw = sb(p2, "xw", [128, NB, D], BF16, NB)
                if full:
                    sz = sb(p2, "sz", [128, 2, D], BF16, 2)
                    zslots = [wload(w_in, 0, 8, C_Z + u * 512, 512) for u in range(2)]
                    Y = sb(p2, "Y", [128, 2, 8, 128], F32, 2)
                    E = sb(p2, "E", [128, 2, 8, 128], BF16, 2)
                    Mm = sb(p2, "Mm", [128, 2, 8, 128], BF16, 2)
                    CBs = sb(p2, "CBs", [128, 2, 128], F32, 2)
                    t1 = sb(p2, "t1", [128, D], F32)
                    t2 = sb(p2, "t2", [128, 512], F32)
                    ynb = sb(p2, "ynb", [128, D], BF16)
                    ebias = sb(p2, "ebias", [128, 2, 16], F32, 2)
                    ecum = sb(p2, "ecum", [128, 2, 16], F32, 2)
                    gss = sb(p2, "gss", [128, 2], F32)
                    jk2 = sb(p2, "jk2", [128, 512], BF16)

                def ssd_pre(tt):
                    k = tt % NB
                    fw.op("dve", lambda e: e.tensor_tensor(a_t.t[:, k, :], dt.t[:, tt, :], Aneg.t[:], ALU.mult), reads=dt.b + Aneg.b, writes=[a_t.b[k]])
                    if full:
                        for u in range(2):
                            zs, zb = zslots[u]
                            pz, pzb = psf()
                            for kc in range(8):
                                fw.op("pe", lambda e: e.matmul(pz[:, 0:512], hT.t[:, kc, tt * 128:(tt + 1) * 128], zs[:, kc, :],
                                                               start=(kc == 0), stop=(kc == 7)), reads=[zb] + hT.b, writes=[pzb])
                            fw.op("act", lambda e: e.activation(sz.t[:, k, u * 512:(u + 1) * 512], pz[:, 0:512], AF.Silu), reads=[pzb], writes=[sz.b[k]])
                    pss, pssb = psf()
                    fw.op("pe", lambda e: e.matmul(pss[:, 0:16], cst.t[:, 2, :], a_t.t[:, k, :], start=True, stop=True), reads=cst.b + [a_t.b[k]], writes=[pssb])
                    fw.op("pe", lambda e: e.matmul(pss[:, 16:32], onesf.t[:], a_t.t[:, k, :], start=True, stop=True), reads=onesf.b + [a_t.b[k]], writes=[pssb])
                    if full:
                        fw.op("pe", lambda e: e.matmul(pss[:, 32:48], cst.t[:, 1, :], a_t.t[:, k, :], start=True, stop=True), reads=cst.b + [a_t.b[k]], writes=[pssb])
                    fw.op("act", lambda e: e.activation(wdt.t[:, k, :], pss[:, 0:16], AF.Exp), reads=[pssb], writes=[wdt.b[k]])
                    fw.op("dve", lambda e: e.tensor_tensor(wdt.t[:, k, :], wdt.t[:, k, :], dt.t[:, tt, :], ALU.mult), reads=[wdt.b[k]] + dt.b, writes=[wdt.b[k]])
                    fw.op("act", lambda e: e.activation(decr.t[:, k, :], pss[:, 16:32], AF.Exp), reads=[pssb], writes=[decr.b[k]])
                    if full:
                        fw.op("act", lambda e: e.activation(ebias.t[:, k, :], dt.t[:, tt, :], AF.Ln), reads=dt.b, writes=[ebias.b[k]])
                        fw.op("dve", lambda e: e.tensor_tensor(ebias.t[:, k, :], ebias.t[:, k, :], pss[:, 32:48], ALU.subtract), reads=[ebias.b[k], pssb], writes=[ebias.b[k]])
                        fw.op("act", lambda e: e.activation(ecum.t[:, k, :], pss[:, 32:48], AF.Exp), reads=[pssb], writes=[ecum.b[k]])
                    fw.op("dve", lambda e: e.tensor_tensor(xw.t[:, k, :].rearrange("p (h c) -> p h c", h=16),
                                                           xtok.t[:, tt, :].rearrange("p (h c) -> p h c", h=16),
                                                           wdt.t[:, k, :].unsqueeze(2).to_broadcast([128, 16, 64]), ALU.mult),
                          reads=xtok.b + [wdt.b[k]], writes=[xw.b[k]])

                def ssd_front(tt, g):
                    k = tt % 2
                    tsl = slice(tt * 128, (tt + 1) * 128)
                    fw.op("dve", lambda e: e.tensor_tensor(Y.t[:, g], cst.t[:, 1, :].unsqueeze(1).to_broadcast([128, 8, 128]),
                                                           a_t.t[:, k, g * 8:(g + 1) * 8].unsqueeze(2).to_broadcast([128, 8, 128]), ALU.mult),
                          reads=cst.b + [a_t.b[k]], writes=[Y.b[g]])
                    pc, pcb = psf()
                    fw.op("pe", lambda e: e.matmul(pc[:, 0:128], BT.t[:, g, tsl], CT.t[:, g, tsl], start=True, stop=True),
                          reads=BT.b + CT.b, writes=[pcb])
                    fw.op("act", lambda e: e.copy(CBs.t[:, g, :], pc[:, 0:128]), reads=[pcb], writes=[CBs.b[g]])

                def ssd_front_r(tt, g):
                    k = tt % 2
                    for q in range(2):
                        prt, prb = psf()
                        fw.op("pe", lambda e: e.matmul(prt[:, 0:512], onesf.t[:], Y.t[:, g, q * 4:(q + 1) * 4, :].rearrange("p h l -> p (h l)"),
                                                       start=True, stop=False), reads=onesf.b + [Y.b[g]], writes=[prb])
                        fw.op("pe", lambda e: e.matmul(prt[:, 0:512], identb_ap, NM8.t[:, 0:4, :].rearrange("p h l -> p (h l)"),
                                                       start=False, stop=True), reads=identb.b + NM8.b, writes=[prb])
                        for hh in range(4):
                            h = g * 8 + q * 4 + hh
                            fw.op("act", lambda e: e.activation(E.t[:, g, q * 4 + hh, :], prt[:, hh * 128:(hh + 1) * 128], AF.Exp,
                                                                bias=ebias.t[:, k, h:h + 1]), reads=[prb, ebias.b[k]], writes=[E.b[g]])

                def ssd_front_b(tt, g):
                    fw.op("dve", lambda e: e.tensor_tensor(Mm.t[:, g], E.t[:, g],
                                                           CBs.t[:, g, :].unsqueeze(1).to_broadcast([128, 8, 128]), ALU.mult),
                          reads=[E.b[g], CBs.b[g]], writes=[Mm.b[g]])

                def ssd_back(tt, g):
                    k = tt % 2
                    tsl = slice(tt * 128, (tt + 1) * 128)
                    gsl = slice(g * 512, (g + 1) * 512)
                    if g == 0:
                        fw.op("act", lambda e: e.copy(Hbf.t[:], Hst.t[:]), reads=Hst.b, writes=Hbf.b)
                    pyt, pyb = psfix(4)
                    for hh in range(8):
                        h = g * 8 + hh
                        fw.op("pe", lambda e: e.matmul(pyt[:, hh * 64:(hh + 1) * 64], Mm.t[:, g, hh, :], xtok.t[:, tt, h * 64:(h + 1) * 64],
                                                       start=True, stop=True), reads=[Mm.b[g]] + xtok.b, writes=[pyb])
                    po, pob = psfix(5)
                    fw.op("pe", lambda e: e.matmul(po[:, 0:512], CT.t[:, g, tsl], Hbf.t[:, gsl], start=True, stop=True),
                          reads=CT.b + Hbf.b, writes=[pob])

                def ssd_back_rest(tt, g):
                    k = tt % 2
                    tsl = slice(tt * 128, (tt + 1) * 128)
                    gsl = slice(g * 512, (g + 1) * 512)
                    pyt, pyb = psfix(4)
                    po, pob = psfix(5)
                    fw.op("dve", lambda e: e.tensor_tensor(t1.t[:, gsl].rearrange("p (h c) -> p h c", h=8),
                                                           po[:, 0:512].rearrange("p (h c) -> p h c", h=8),
                                                           ecum.t[:, k, g * 8:(g + 1) * 8].unsqueeze(2).to_broadcast([128, 8, 64]), ALU.mult),
                          reads=[pob, ecum.b[k]], writes=t1.b)
                    fw.op("dve", lambda e: e.tensor_tensor(t2.t[:].rearrange("p (h c) -> p h c", h=8),
                                                           xtok.t[:, tt, gsl].rearrange("p (h c) -> p h c", h=8),
                                                           hp.t[:, 2, g * 8:(g + 1) * 8].unsqueeze(2).to_broadcast([128, 8, 64]), ALU.mult),
                          reads=xtok.b + hp.b, writes=t2.b)
                    fw.op("dve", lambda e: e.tensor_tensor(t1.t[:, gsl], t1.t[:, gsl], t2.t[:], ALU.add), reads=t1.b + t2.b, writes=t1.b)
                    fw.op("dve", lambda e: e.tensor_tensor(t1.t[:, gsl], t1.t[:, gsl], pyt[:, 0:512], ALU.add), reads=t1.b + [pyb], writes=t1.b)
                    fw.op("dve", lambda e: e.tensor_tensor(t1.t[:, gsl], t1.t[:, gsl], sz.t[:, k, gsl], ALU.mult), reads=t1.b + [sz.b[k]], writes=t1.b)
                    fw.op("dve", lambda e: e.memset(gss.t[:, g:g + 1], 0.0), writes=gss.b)
                    fw.op("act", lambda e: e.activation(jk2.t[:], t1.t[:, gsl], AF.Square, scale=float(512 ** -0.5),
                                                        accum_out=gss.t[:, g:g + 1]), reads=t1.b, writes=jk2.b + gss.b)
                    rsqrt_eps(gss.t[:, g:g + 1], gss.t[:, g:g + 1], gss.b, gss.b)
                    fw.op("dve", lambda e: e.tensor_scalar(ynb.t[:, gsl], t1.t[:, gsl], gss.t[:, g:g + 1], None, ALU.mult),
                          reads=t1.b + gss.b, writes=ynb.b)
                    if g == 1:
                        if tt == 0:
                            dump("yssd", t1.t[:], t1.b)
                        pt, ptb = psb()
                        for j in range(8):
                            fw.op("pe", lambda e: e.transpose(pt[:, j * 128:(j + 1) * 128], ynb.t[:, j * 128:(j + 1) * 128], identb_ap),
                                  reads=ynb.b + identb.b, writes=[ptb])
                        fw.op("dve", lambda e: e.tensor_tensor(yssmT.t[:, :, tsl], pt[:].rearrange("p (j c) -> p j c", j=8),
                                                               nwf.t[:, 2, :].unsqueeze(2).to_broadcast([128, 8, 128]), ALU.mult),
                              reads=[ptb] + nwf.b, writes=yssmT.b)

                def ssd_post(tt):
                    k = tt % NB
                    for g in range(2):
                        pst_, pstb = psf()
                        fw.op("pe", lambda e: e.matmul(pst_[:, 0:512], Btok.t[:, tt, g * 128:(g + 1) * 128], xw.t[:, k, g * 512:(g + 1) * 512],
                                                       start=True, stop=True), reads=Btok.b + [xw.b[k]], writes=[pstb])
                        hs = Hst.t[:, g * 512:(g + 1) * 512]
                        fw.op("dve", lambda e: e.tensor_tensor(hs.rearrange("p (h c) -> p h c", h=8), hs.rearrange("p (h c) -> p h c", h=8),
                                                               decr.t[:, k, g * 8:(g + 1) * 8].unsqueeze(2).to_broadcast([128, 8, 64]), ALU.mult),
                              reads=Hst.b + [decr.b[k]], writes=Hst.b)
                        fw.op("dve", lambda e: e.tensor_tensor(hs, hs, pst_[:, 0:512], ALU.add), reads=Hst.b + [pstb], writes=Hst.b)

                if not full:
                    for tt in range(NTT):
                        ssd_pre(tt)
                    for tt in range(NTT):
                        ssd_post(tt)
                else:
                    steps = [(tt, g) for tt in range(NTT) for g in range(2)]
                    prev = None
                    for (tt, g) in steps:
                        if g == 0:
                            ssd_pre(tt)
                        ssd_front(tt, g)
                        if prev is not None:
                            ssd_back(*prev)
                        ssd_front_r(tt, g)
                        if prev is not None:
                            ssd_back_rest(*prev)
                            if prev[1] == 1:
                                ssd_post(prev[0])
                        ssd_front_b(tt, g)
                        prev = (tt, g)
                        if g == 1:
                            kv_unit(tt)
                    ssd_back(*prev)
                    ssd_back_rest(*prev)
                    ssd_post(prev[0])
                fw.barrier()
            dump("Hst", Hst.t[:], Hst.b)

            if need_kv:
                with ExitStack() as p3:
                    if not full:
                        for i in range(4):
                            kv_unit(i)
                    if full:
                        qT0 = sb(p3, "qT0", [128, 8, T], BF16)
                        qT1 = sb(p3, "qT1", [128, 8, T], BF16)
                        qTs = [qT0, qT1]
                        PT = sb(p3, "PT", [128, 4, 512], BF16, 4)
                        rden = sb(p3, "rden", [128, 512], F32)
                        dtmp = sb(p3, "dtmp", [128, 512], F32)
                        fw.op("dve", lambda e: e.memset(qT0.t[64:128, :, :], 0.0), writes=qT0.b)
                        fw.op("dve", lambda e: e.memset(qT1.t[0:64, :, :], 0.0), writes=qT1.b)

                        def ev_q(m, ps, pb):
                            fw.op("act", lambda e: e.activation(qT0.t[0:64, m, :], ps[0:64, 0:T], AF.Copy, scale=0.125), reads=[pb], writes=qT0.b)
                            fw.op("act", lambda e: e.activation(qT1.t[64:128, m, :], ps[64:128, 0:T], AF.Copy, scale=0.125), reads=[pb], writes=qT1.b)

                        proj_fm(w_in, C_Q, 8, 8, lambda kc: hT.t[:, kc, :], hT.b, ev_q)
                        vaug = sb(p3, "vaug", [128, 2, 8, 128], BF16, 2)

                        def prep_v(h):
                            hb = h % 2
                            if hb == 0:
                                vc, oc, osrc = 0, 64, 1088
                            else:
                                vc, oc, osrc = 64, 0, 0
                            fw.op("dve", lambda e: e.tensor_copy(vaug.t[:, hb, :, vc:vc + 64], Vtok.t[:, :, 64 + 64 * h:128 + 64 * h]),
                                  reads=Vtok.b, writes=[vaug.b[hb]])
                            fw.op("dve", lambda e: e.tensor_copy(vaug.t[:, hb, :, oc:oc + 64], Vtok.t[:, :, osrc:osrc + 64]),
                                  reads=Vtok.b, writes=[vaug.b[hb]])

                        def stageA(hpair, h2, j, k):
                            h = hpair * 2 + h2
                            qt_lo = max(0, j - 4)
                            qt_hi = min(3, j)
                            q0 = qt_lo * 128
                            nq = (qt_hi - qt_lo + 1) * 128
                            rel0 = (qt_lo * 2 + 8 - 2 * j) * 64
                            if j == 0:
                                prep_v(h)
                            sp_, spb = psf()
                            qh = qTs[h2]
                            fw.op("pe", lambda e: e.matmul(sp_[:, 0:nq], kT.t[:, hpair, j * 128:(j + 1) * 128], qh.t[:, hpair, q0:q0 + nq],
                                                           start=True, stop=False), reads=kT.b + qh.b, writes=[spb])
                            fw.op("pe", lambda e: e.matmul(sp_[:, 0:nq], identb_ap, EB.t[:, h, rel0:rel0 + nq],
                                                           start=False, stop=True), reads=identb.b + EB.b, writes=[spb])
                            fw.op("act", lambda e: e.activation(PT.t[:, k, 0:nq], sp_[:, 0:nq], AF.Exp), reads=[spb], writes=[PT.b[k]])

                        def stageB(hpair, h2, j, k):
                            h = hpair * 2 + h2
                            qt_lo = max(0, j - 4)
                            qt_hi = min(3, j)
                            q0 = qt_lo * 128
                            nq = (qt_hi - qt_lo + 1) * 128
                            pX, pXb = psfix(4 + h2)
                            segs = [(q0, nq)]
                            if 1 <= j <= 3:
                                segs = [(q0, nq - 128), (q0 + nq - 128, 128)]
                            for (sq0, snq) in segs:
                                fw.op("pe", lambda e: e.matmul(pX[:, sq0:sq0 + snq], vaug.t[:, h2, j, :], PT.t[:, k, sq0 - q0:sq0 - q0 + snq],
                                                               start=(j == 0), stop=(j == 7)), reads=[vaug.b[h2], PT.b[k]], writes=[pXb])
                            if h2 == 1 and j == 7:
                                pA, pAb = psfix(4)
                                pB, pBb = psfix(5)
                                fw.op("act", lambda e: e.copy(dtmp.t[0:64, :], pA[64:128, 0:512]), reads=[pAb], writes=dtmp.b)
                                fw.op("act", lambda e: e.copy(dtmp.t[64:128, :], pB[0:64, 0:512]), reads=[pBb], writes=dtmp.b)
                                fw.op("dve", lambda e: e.reciprocal(rden.t[:], dtmp.t[:]), reads=dtmp.b, writes=rden.b)
                                fw.op("dve", lambda e: e.tensor_tensor(yattT.t[0:64, hpair, :], pA[0:64, 0:512], rden.t[0:64, :], ALU.mult),
                                      reads=[pAb] + rden.b, writes=yattT.b)
                                fw.op("dve", lambda e: e.tensor_tensor(yattT.t[64:128, hpair, :], pB[64:128, 0:512], rden.t[64:128, :], ALU.mult),
                                      reads=[pBb] + rden.b, writes=yattT.b)

                        its = [(hpair, h2, j) for hpair in range(8) for h2 in range(2) for j in range(8)]
                        pend = []
                        for n_, (hpair, h2, j) in enumerate(its):
                            k = n_ % 4
                            stageA(hpair, h2, j, k)
                            pend.append((hpair, h2, j, k))
                            if len(pend) > 3:
                                stageB(*pend.pop(0))
                        while pend:
                            stageB(*pend.pop(0))
                    fw.barrier()
                fw.op("act", lambda e: e.copy(kT.t[:, :, 0:T], kT.t[:, :, T:2 * T]), reads=kT.b, writes=kT.b)
                fw.op("dve", lambda e: e.tensor_copy(Vtok.t[:, 0:4, :], Vtok.t[:, 4:8, :]), reads=Vtok.b, writes=Vtok.b)

            if full:
                dump("yssmT", yssmT.t[:], yssmT.b)
                dump("yattT", yattT.t[:], yattT.b)
                with ExitStack() as p4:
                    norm_prep()
                    mT = sb(p4, "mT", [128, 8, T], BF16)
                    s3 = sb(p4, "s3", [128, 2, T], F32, 2)
                    s4 = sb(p4, "s4", [128, 2, T], F32, 2)
                    for u0 in (0, 4):
                        sl1, b1 = wload(w_bs, 0, 8, u0 * 128, 512)
                        sl2, b2 = wload(w_ba, 0, 8, u0 * 128, 512)
                        sl3, b3 = wload(w_in, 0, 8, C_GS + u0 * 128, 512)
                        sl4, b4 = wload(w_in, 0, 8, C_GA + u0 * 128, 512)
                        for mo in range(4):
                            m = u0 + mo
                            k = m % 2
                            msl = slice(mo * 128, (mo + 1) * 128)
                            p1_, p1b = psf()
                            p2_, p2b = psf()
                            p3_, p3b = psf()
                            p4_, p4b = psf()
                            for kc in range(8):
                                fw.op("pe", lambda e: e.matmul(p3_[:, 0:T], sl3[:, kc, msl], hT.t[:, kc, :], start=(kc == 0), stop=(kc == 7)),
                                      reads=[b3] + hT.b, writes=[p3b])
                            for kc in range(8):
                                fw.op("pe", lambda e: e.matmul(p4_[:, 0:T], sl4[:, kc, msl], hT.t[:, kc, :], start=(kc == 0), stop=(kc == 7)),
                                      reads=[b4] + hT.b, writes=[p4b])
                            for kc in range(8):
                                fw.op("pe", lambda e: e.matmul(p1_[:, 0:T], sl1[:, kc, msl], yssmT.t[:, kc, :], start=(kc == 0), stop=(kc == 7)),
                                      reads=[b1] + yssmT.b, writes=[p1b])
                            for kc in range(8):
                                fw.op("pe", lambda e: e.matmul(p2_[:, 0:T], sl2[:, kc, msl], yattT.t[:, kc, :], start=(kc == 0), stop=(kc == 7)),
                                      reads=[b2] + yattT.b, writes=[p2b])
                            fw.op("act", lambda e: e.activation(s3.t[:, k, :], p3_[:, 0:T], AF.Sigmoid), reads=[p3b], writes=[s3.b[k]])
                            fw.op("act", lambda e: e.activation(s4.t[:, k, :], p4_[:, 0:T], AF.Sigmoid), reads=[p4b], writes=[s4.b[k]])
                            fw.op("dve", lambda e: e.tensor_tensor(s3.t[:, k, :], s3.t[:, k, :], p1_[:, 0:T], ALU.mult), reads=[s3.b[k], p1b], writes=[s3.b[k]])
                            fw.op("dve", lambda e: e.tensor_tensor(s4.t[:, k, :], s4.t[:, k, :], p2_[:, 0:T], ALU.mult), reads=[s4.b[k], p2b], writes=[s4.b[k]])
                            fw.op("dve", lambda e: e.tensor_tensor(mT.t[:, m, :], s3.t[:, k, :], s4.t[:, k, :], ALU.add), reads=[s3.b[k], s4.b[k]], writes=mT.b)

                    def ev_out(u, tt, ps, pb):
                        xs = x_res.t[:, tt, u * 512:(u + 1) * 512]
                        fw.op("dve", lambda e: e.tensor_tensor(xs, xs, ps[:, 0:512], ALU.add), reads=[pb, x_res.b[tt]], writes=[x_res.b[tt]])

                    proj_tm(w_out, 0, 1024, 8, lambda kc, tt: mT.t[:, kc, tt * 128:(tt + 1) * 128], mT.b, ev_out)
                    if after is not None:
                        after(p4)
                    fw.barrier()
            if not full:
                norm_prep()
                if after is not None:
                    after(ph)
                fw.barrier()

    onesf = sb(st, "onesf", [128, 128], F32)
    fw.op("dve", lambda e: e.memset(onesf.t[:], 1.0), writes=onesf.b)

    def tile_src(ti):
        full = ti >= n_state
        src = x_main if full else x_prev
        tok0 = ((ti - n_state) if full else (NTOK // T - n_state + ti)) * T
        return src, tok0

    def load_x(ti):
        if ti >= n_state + n_full:
            return
        src, tok0 = tile_src(ti)
        for tt in range(NTT):
            fw.dma(x_res.t[:, tt, :], src[tok0 + tt * 128: tok0 + (tt + 1) * 128, :], writes=[x_res.b[tt]], q="pool")

    def final_out(ph, tok0):
        yo = sb(ph, "yo", [128, 2, D], F32, 2)
        ss = ssp
        for tt in range(NTT):
            k = tt % 2
            fw.op("dve", lambda e: e.memset(ss.t[:, tt:tt + 1], 0.0), writes=ss.b)
            fw.op("act", lambda e: e.activation(xnp.t[:, k, :], x_res.t[:, tt, :], AF.Square, scale=1.0 / 32.0, accum_out=ss.t[:, tt:tt + 1]),
                  reads=[x_res.b[tt]], writes=[xnp.b[k]] + ss.b)
            rsqrt_eps(ss.t[:, tt:tt + 1], ss.t[:, tt:tt + 1], ss.b, ss.b)
            fw.op("dve", lambda e: e.scalar_tensor_tensor(yo.t[:, k, :], x_res.t[:, tt, :], ss.t[:, tt:tt + 1], fnw.t[:], ALU.mult, ALU.mult),
                  reads=[x_res.b[tt]] + ss.b + fnw.b, writes=[yo.b[k]])
            fw.dma(y_out[tok0 + tt * 128: tok0 + (tt + 1) * 128, :], yo.t[:, k, :], reads=[yo.b[k]], q="pool")

    load_x(0)
    norm_prep()
    with ExitStack() as ph0:
        norm_to_hT(0, ph0)
        fw.barrier()
    for ti in range(n_state + n_full):
        full = ti >= n_state
        src, tok0 = tile_src(ti)
        last_state = (ti == n_state - 1)

        def after_ffn1(ph, full=full, ti=ti):
            norm_to_hT(1, ph)
            if not full:
                load_x(ti + 1)
            return False

        ffn(0, after=after_ffn1)
        if full and ti == n_state:
            dump("x1", x_res.t[:], x_res.b)

        def after_mixer(ph, full=full, ti=ti, last_state=last_state):
            if last_state:
                hm = hmask.t[:, 0:1]
                fw.op("dve", lambda e: e.tensor_scalar(Hst.t[:], Hst.t[:], hm, None, ALU.mult), reads=Hst.b + hmask.b, writes=Hst.b)
                fw.op("dve", lambda e: e.tensor_scalar(halo.t[:], halo.t[:], hm, None, ALU.mult), reads=halo.b + hmask.b, writes=halo.b)
                fw.op("dve", lambda e: e.tensor_scalar(Vtok.t[:, 0:4, :], Vtok.t[:, 0:4, :], hm, None, ALU.mult), reads=Vtok.b + hmask.b, writes=Vtok.b)
            if full:
                if ti == n_state:
                    dump("x2", x_res.t[:], x_res.b)
                norm_to_hT(3, ph)
            else:
                norm_to_hT(0, ph)

        mixer(full, full or last_state, ti == n_state, after=after_mixer)
        if full:
            def after_ffn2(ph, tok0=tok0, ti=ti):
                final_out(ph, tok0)
                load_x(ti + 1)
                if ti + 1 < n_state + n_full:
                    norm_prep()
                    norm_to_hT(0, ph)
                return True

            ffn(1, after=after_ffn2)
    fw.barrier(("sp", "act", "pool"), dmas=True)
    st.close()
    return nc


def _consts():
    t = np.arange(128)
    ident = np.eye(128, dtype=np.float32)
    U = (t[:, None] <= t[None, :]).astype(np.float32)
    Ls = (t[:, None] > t[None, :]).astype(np.float32)
    nm = np.where(t[:, None] > t[None, :], NEG, 0.0).astype(np.float32)
    return np.ascontiguousarray(np.stack([ident, U, Ls, nm], axis=1))


def _prep_shared(inp):
    f = lambda a: np.ascontiguousarray(np.asarray(a, dtype=np.float32))
    fm = lambda w: np.asarray(w, dtype=np.float32).reshape(8, 128).T
    sh = {}
    sh["w_gu1"] = f(inp["ffn1_w_gu"][0])
    sh["w_gu2"] = f(inp["ffn2_w_gu"][0])
    sh["w_dn1"] = f(inp["ffn1_w_down"][0])
    sh["w_dn2"] = f(inp["ffn2_w_down"][0])
    sh["w_in"] = f(inp["w_in"][0])
    sh["w_bs"] = f(inp["w_branch_ssm"][0])
    sh["w_ba"] = f(inp["w_branch_attn"][0])
    sh["w_out"] = f(inp["w_out"][0])
    sh["nwf"] = f(np.stack([fm(inp["ffn1_norm_w"][0]), fm(inp["mix_norm_w"][0]), fm(inp["ssm_norm_w"][0]),
                            fm(inp["ffn2_norm_w"][0])], axis=1))
    sh["fnw"] = f(np.broadcast_to(np.asarray(inp["final_norm_w"], np.float32)[None, :], (128, D)))
    cwv = np.asarray(inp["conv_w"][0], np.float32)
    sh["cw"] = f(cwv.reshape(4, 12, 128).transpose(2, 1, 0))
    sh["cb"] = f(np.asarray(inp["conv_b"][0], np.float32).reshape(12, 128).T)
    hpv = np.stack([np.asarray(inp["dt_bias"][0], np.float32), np.asarray(inp["A_log"][0], np.float32),
                    np.asarray(inp["D_skip"][0], np.float32)], axis=0)
    sh["hp"] = f(np.broadcast_to(hpv[None], (128, 3, 16)))
    rb = np.asarray(inp["rel_bias"][0], np.float32)
    r = np.arange(128)[:, None]
    l = np.arange(640)[None, :]
    idx = np.clip(l - r, -128, 128) + 128
    kc = r // 64
    qc = l // 64
    valid = (qc >= kc) & (qc <= kc + 8)
    tab = rb[:, idx]
    tab = np.where(valid[None], tab, np.float32(NEG))
    sh["btab"] = f(tab)
    sh["cst"] = _consts()
    return sh


_NC_CACHE = {}


def kernel(**inputs):
    x = np.asarray(inputs["x"], dtype=np.float32)
    sh = _prep_shared(inputs)
    in_maps = []
    for c in range(8):
        b, h = c // 2, c % 2
        m = dict(sh)
        m["x_main"] = np.ascontiguousarray(x[b, h * NTOK:(h + 1) * NTOK])
        m["x_prev"] = np.ascontiguousarray(x[b, 0:NTOK])
        m["hmask"] = np.full((128, 1), float(h), dtype=np.float32)
        in_maps.append(m)
    if "nc" not in _NC_CACHE:
        _NC_CACHE["nc"] = build()
    res = run_bass_kernel_spmd(_NC_CACHE["nc"], in_maps, core_ids=list(range(8)))
    out = np.empty((4, 2 * NTOK, D), dtype=np.float32)
    for c in range(8):
        b, h = c // 2, c % 2
        out[b, h * NTOK:(h + 1) * NTOK] = np.asarray(res.results[c]["y"], dtype=np.float32)
    return out
```

```python
import numpy as np
from contextlib import ExitStack
import concourse.bass as bass
import concourse.mybir as mybir
from concourse.bass_utils import run_bass_kernel_spmd

F32 = mybir.dt.float32
BF16 = mybir.dt.bfloat16
AF = mybir.ActivationFunctionType
ALU = mybir.AluOpType

D = 1024
DFF = 2816
NFC = 22
T = 512
NTT = 4
NTOK = 4096
INP = 7696
EPS = 1e-6
NEG = -30000.0
C_Z, C_X, C_B, C_C, C_DT, C_Q, C_K, C_V, C_GS, C_GA = 0, 1024, 2048, 2304, 2560, 2576, 3600, 4624, 5648, 6672
NS = 5


class Buf:
    __slots__ = ("name", "lw", "rd")

    def __init__(self, name):
        self.name = name
        self.lw = None
        self.rd = {}


class Tn:
    def __init__(self, t, bufs):
        self.t = t
        self.b = bufs


class FW:
    def __init__(self, nc, stack, n_dma_sems=32):
        self.nc = nc
        self.engs = {"pe": nc.tensor, "act": nc.scalar, "dve": nc.vector, "pool": nc.gpsimd, "sp": nc.sync}
        self.sem = {}
        self.cnt = {}
        for k in self.engs:
            self.sem[k] = stack.enter_context(nc.semaphore("tick_" + k))
            self.cnt[k] = 0
        self.dsem = [stack.enter_context(nc.semaphore("dma_%d" % i)) for i in range(n_dma_sems)]
        self.dcnt = [0] * n_dma_sems
        self.dnext2 = {"sp": 0, "pool": 0}
        self.seen = {k: {} for k in self.engs}
        self.ninst = 0

    def _wait(self, eng, dep):
        kind, idx, val = dep
        key = (kind, idx)
        if kind == "e" and idx == eng and eng == "pe":
            return
        if self.seen[eng].get(key, 0) >= val:
            return
        s = self.sem[idx] if kind == "e" else self.dsem[idx]
        self.engs[eng].wait_ge(s, val)
        self.seen[eng][key] = val

    def _deps(self, eng, reads, writes):
        for b in reads:
            if b.lw is not None:
                self._wait(eng, b.lw)
        for b in writes:
            if b.lw is not None:
                self._wait(eng, b.lw)
            for d in b.rd.values():
                if d[0] == "e" and d[1] == eng and eng != "sp" and eng != "pool":
                    continue
                self._wait(eng, d)

    def _mark(self, dep, reads, writes):
        for b in writes:
            b.lw = dep
            b.rd = {}
        for b in reads:
            if b in writes:
                continue
            b.rd[(dep[0], dep[1])] = dep

    def op(self, eng, fn, reads=(), writes=()):
        self._deps(eng, reads, writes)
        ins = fn(self.engs[eng])
        self.cnt[eng] += 1
        ins.then_inc(self.sem[eng], 1)
        dep = ("e", eng, self.cnt[eng])
        self._mark(dep, reads, writes)
        self.ninst += 1
        return dep

    def dma(self, out, in_, reads=(), writes=(), q="sp"):
        half = len(self.dsem) // 2
        base = 0 if q == "sp" else half
        i = base + self.dnext2[q]
        self.dnext2[q] = (self.dnext2[q] + 1) % half
        if self.dcnt[i] > 0:
            self._wait(q, ("d", i, self.dcnt[i]))
        self._deps(q, reads, writes)
        ins = self.engs[q].dma_start(out=out, in_=in_)
        self.dcnt[i] += 16
        ins.then_inc(self.dsem[i], 16)
        dep = ("d", i, self.dcnt[i])
        self._mark(dep, reads, writes)
        self.ninst += 1
        return dep

    def barrier(self, engs=("pe", "act", "dve"), dmas=False):
        for e in engs:
            for f in self.engs:
                if self.cnt[f] > 0 and not (f == e and e == "pe"):
                    self._wait(e, ("e", f, self.cnt[f]))
            if dmas:
                for i in range(len(self.dsem)):
                    if self.dcnt[i] > 0:
                        self._wait(e, ("d", i, self.dcnt[i]))


def build(n_state=8, n_full=8, dbg=()):
    nc = bass.Bass("TRN2", target_bir_lowering=False)
    di = lambda n, s, dt=F32: nc.dram_tensor(n, s, dt, kind="ExternalInput").ap()
    x_prev = di("x_prev", [NTOK, D])
    x_main = di("x_main", [NTOK, D])
    w_gu = [di("w_gu1", [D, 2 * DFF]), di("w_gu2", [D, 2 * DFF])]
    w_dn = [di("w_dn1", [DFF, D]), di("w_dn2", [DFF, D])]
    w_in = di("w_in", [D, INP])
    w_bs = di("w_bs", [D, D])
    w_ba = di("w_ba", [D, D])
    w_out = di("w_out", [D, D])
    nwf_d = di("nwf", [128, 4, 8])
    fnw_d = di("fnw", [128, D])
    cw_d = di("cw", [128, 12, 4])
    cb_d = di("cb", [128, 12])
    hp_d = di("hp", [128, 3, 16])
    btab_d = di("btab", [16, 128, 640])
    hmask_d = di("hmask", [128, 1])
    cst_d = di("cst", [128, 4, 128])
    y_out = nc.dram_tensor("y", [NTOK, D], F32, kind="ExternalOutput").ap()
    dbg_out = {}
    for name, shape, dt in dbg:
        dbg_out[name] = nc.dram_tensor("dbg_" + name, list(shape), dt, kind="ExternalOutput").ap()

    st = ExitStack()
    fw = FW(nc, st)

    uid = [0]

    def sb(stack, name, shape, dt, nb=1):
        uid[0] += 1
        t = stack.enter_context(nc.sbuf_tensor("%s_%d" % (name, uid[0]), list(shape), dt))
        return Tn(t, [Buf("%s_%d" % (name, i)) for i in range(nb)])

    x_res = sb(st, "x_res", [128, NTT, D], F32, NTT)
    hT = sb(st, "hT", [128, 8, T], BF16, 1)
    ring = sb(st, "ring", [128, NS, 4096], BF16, NS)
    Hst = sb(st, "Hst", [128, 1024], F32)
    Hbf = sb(st, "Hbf", [128, 1024], BF16)
    EB = sb(st, "EB", [128, 16, 640], BF16)
    kT = sb(st, "kT", [128, 8, 2 * T], BF16)
    Vtok = sb(st, "Vtok", [128, 8, 1152], BF16)
    halo = sb(st, "halo", [128, 12, 3], BF16)
    cst = sb(st, "cst", [128, 4, 128], F32)
    identb = sb(st, "identb", [128, 128], BF16)
    NM8 = sb(st, "NM8", [128, 4, 128], BF16)
    nwf = sb(st, "nwf", [128, 4, 8], F32)
    cw = sb(st, "cw", [128, 12, 4], F32)
    cb = sb(st, "cb", [128, 12], F32)
    hp = sb(st, "hp", [128, 3, 16], F32)
    Aneg = sb(st, "Aneg", [128, 16], F32)
    hmask = sb(st, "hmask", [128, 1], F32)
    col1 = sb(st, "col1", [128, 1], F32)
    ssp = sb(st, "ssp", [128, NTT], F32)
    rstdp = sb(st, "rstdp", [128, NTT], F32)

    psf_t = [st.enter_context(nc.psum_tensor("psf%d" % i, [128, 512], F32)) for i in range(6)]
    psf_b = [Buf("psf%d" % i) for i in range(6)]
    psb_t = [st.enter_context(nc.psum_tensor("psb%d" % i, [128, 1024], BF16)) for i in range(2)]
    psb_b = [Buf("psb%d" % i) for i in range(2)]
    pctr = {"f": 0, "b": 0, "r": 0}

    def psf():
        i = pctr["f"]
        pctr["f"] = (i + 1) % 3
        return psf_t[i], psf_b[i]

    def psfix(i):
        return psf_t[i], psf_b[i]

    def psb():
        i = pctr["b"]
        pctr["b"] = (i + 1) % 2
        return psb_t[i], psb_b[i]

    scratch = {}

    def wload(src2d, kc0, nkc, c0, ncols):
        i = pctr["r"]
        pctr["r"] = (i + 1) % NS
        slot2d = ring.t[:, i, 0:nkc * ncols]
        slot = slot2d.rearrange("p (k c) -> p k c", k=nkc)
        key = (src2d.tensor.name, kc0, nkc, c0, ncols)
        if key not in scratch:
            srcap = src2d.rearrange("(kc p) c -> p kc c", p=128)[:, kc0:kc0 + nkc, c0:c0 + ncols]
            fw.dma(slot, srcap, writes=[ring.b[i]], q="pool")
            d = nc.dram_tensor("ws%d" % len(scratch), [128, nkc * ncols], BF16, kind="Internal").ap()
            b = Buf("ws%d" % len(scratch))
            scratch[key] = (d, b)
            fw.dma(d, slot2d, reads=[ring.b[i]], writes=[b], q="sp")
        else:
            d, b = scratch[key]
            fw.dma(slot2d, d, reads=[b], writes=[ring.b[i]], q="sp")
        return slot, ring.b[i]

    def dump(name, ap, bufs):
        if name in dbg_out:
            fw.dma(dbg_out[name], ap, reads=bufs, q="sp")

    fw.dma(cst.t[:], cst_d, writes=cst.b)
    fw.dma(nwf.t[:], nwf_d, writes=nwf.b)
    fw.dma(cw.t[:], cw_d, writes=cw.b)
    fw.dma(cb.t[:], cb_d, writes=cb.b)
    fw.dma(hp.t[:], hp_d, writes=hp.b)
    fw.dma(hmask.t[:], hmask_d, writes=hmask.b)
    fw.op("dve", lambda e: e.tensor_copy(identb.t[:], cst.t[:, 0, :]), reads=cst.b, writes=identb.b)
    for g in range(4):
        fw.op("dve", lambda e: e.tensor_copy(NM8.t[:, g, :], cst.t[:, 3, :]), reads=cst.b, writes=NM8.b)
    fw.op("dve", lambda e: e.memset(col1.t[:], 1.0), writes=col1.b)
    fw.op("dve", lambda e: e.memset(Hst.t[:], 0.0), writes=Hst.b)
    fw.op("dve", lambda e: e.memset(halo.t[:], 0.0), writes=halo.b)
    fw.op("dve", lambda e: e.memset(kT.t[:], 0.0), writes=kT.b)
    fw.op("dve", lambda e: e.memset(Vtok.t[:], 0.0), writes=Vtok.b)
    fw.op("dve", lambda e: e.memset(Vtok.t[:, 4:8, 0:64], 1.0), writes=Vtok.b)
    fw.op("dve", lambda e: e.memset(Vtok.t[:, 4:8, 1088:1152], 1.0), writes=Vtok.b)
    fw.op("act", lambda e: e.activation(Aneg.t[:], hp.t[:, 1, :], AF.Exp), reads=hp.b, writes=Aneg.b)
    fw.op("dve", lambda e: e.tensor_scalar(Aneg.t[:], Aneg.t[:], -1.0, None, ALU.mult), reads=Aneg.b, writes=Aneg.b)
    dgw = nc.dram_tensor("dgw", [128, 48, 128], BF16, kind="Internal").ap()
    with ExitStack() as phd:
        dg = sb(phd, "dg", [128, 48, 128], BF16)
        for m in range(12):
            for tap in range(4):
                fw.op("dve", lambda e: e.tensor_scalar(dg.t[:, m * 4 + tap, :], identb.t[:], cw.t[:, m, tap:tap + 1], None, ALU.mult),
                      reads=identb.b + cw.b, writes=dg.b)
        dgdep = fw.dma(dgw, dg.t[:], reads=dg.b)
        fw.barrier(("pe", "act", "dve", "sp"), dmas=True)

    for h in range(16):
        fw.dma(EB.t[:, h, :], btab_d[h], writes=EB.b, q="pool")

    identb_ap = identb.t[:]
    epsc = sb(st, "epsc", [128, 1], F32)
    fw.op("dve", lambda e: e.memset(epsc.t[:], EPS), writes=epsc.b)

    def rsqrt_eps(out_ap, in_ap, rb, wb):
        fw.op("act", lambda e: e.activation(out_ap, in_ap, AF.Ln, bias=epsc.t[:, 0:1]), reads=rb + epsc.b, writes=wb)
        fw.op("act", lambda e: e.activation(out_ap, out_ap, AF.Exp, scale=-0.5), reads=wb, writes=wb)

    def norm_prep():
        fw.op("dve", lambda e: e.memset(ssp.t[:], 0.0), writes=ssp.b)

    def norm_to_hT(ni, ph):
        ss, rstd = ssp, rstdp
        xn = sb(ph, "nxn", [128, 2, D], BF16, 2)
        junk = sb(ph, "njunk", [128, D], BF16)
        for tt in range(NTT):
            fw.op("act", lambda e: e.activation(junk.t[:], x_res.t[:, tt, :], AF.Square, scale=1.0 / 32.0,
                                                accum_out=ss.t[:, tt:tt + 1]),
                  reads=[x_res.b[tt]], writes=junk.b + ss.b)
        fw.op("act", lambda e: e.activation(rstd.t[:], ss.t[:], AF.Ln, bias=epsc.t[:, 0:1]), reads=ss.b + epsc.b, writes=rstd.b)
        fw.op("act", lambda e: e.activation(rstd.t[:], rstd.t[:], AF.Exp, scale=-0.5), reads=rstd.b, writes=rstd.b)
        for tt in range(NTT):
            k = tt % 2
            fw.op("act", lambda e: e.activation(xn.t[:, k, :], x_res.t[:, tt, :], AF.Copy, scale=rstd.t[:, tt:tt + 1]),
                  reads=[x_res.b[tt]] + rstd.b, writes=[xn.b[k]])
            pt, pb = psb()
            for j in range(8):
                fw.op("pe", lambda e: e.transpose(pt[:, j * 128:(j + 1) * 128], xn.t[:, k, j * 128:(j + 1) * 128], identb_ap),
                      reads=[xn.b[k]] + identb.b, writes=[pb])
            fw.op("dve", lambda e: e.tensor_tensor(hT.t[:, :, tt * 128:(tt + 1) * 128],
                                                   pt[:].rearrange("p (j c) -> p j c", j=8),
                                                   nwf.t[:, ni, :].unsqueeze(2).to_broadcast([128, 8, 128]), ALU.mult),
                  reads=[pb] + nwf.b, writes=hT.b)

    def proj_fm(src2d, c0, nchunks, nkc, rhs, rhs_bufs, evac, ncol=T, evac2=None, evac3=None):
        pend = []
        for u0 in range(0, nchunks, 4):
            nu = min(4, nchunks - u0)
            slot, sbuf_ = wload(src2d, 0, nkc, c0 + u0 * 128, nu * 128)
            for mo in range(nu):
                ps, pb = psf()
                for kc in range(nkc):
                    fw.op("pe", lambda e: e.matmul(ps[:, 0:ncol], slot[:, kc, mo * 128:(mo + 1) * 128], rhs(kc),
                                                   start=(kc == 0), stop=(kc == nkc - 1)),
                          reads=[sbuf_] + rhs_bufs, writes=[pb])
                evac(u0 + mo, ps, pb)
                pend.append(u0 + mo)
                if evac2 is not None and len(pend) >= 2:
                    evac2(pend[-2])
                if evac3 is not None and len(pend) >= 3:
                    evac3(pend[-3])
        if evac2 is not None and pend:
            evac2(pend[-1])
        if evac3 is not None:
            if len(pend) >= 2:
                evac3(pend[-2])
            if pend:
                evac3(pend[-1])

    def proj_tm(src2d, c0, ncols, nkc, lhs, lhs_bufs, evac):
        for u0 in range(0, ncols, 512):
            nc_ = min(512, ncols - u0)
            slot, sbuf_ = wload(src2d, 0, nkc, c0 + u0, nc_)
            for tt in range(NTT):
                ps, pb = psf()
                for kc in range(nkc):
                    fw.op("pe", lambda e: e.matmul(ps[:, 0:nc_], lhs(kc, tt), slot[:, kc, :],
                                                   start=(kc == 0), stop=(kc == nkc - 1)),
                          reads=[sbuf_] + lhs_bufs, writes=[pb])
                evac(u0 // 512, tt, ps, pb)

    def ffn(which, after=None):
        with ExitStack() as ph:
            actT = sb(ph, "actT", [128, NFC, T], BF16, 1)
            sg = sb(ph, "sg", [128, 2, T], F32, 2)
            norm_prep()
            wgu = w_gu[which]
            wd = w_dn[which]
            for f0 in range(0, NFC, 4):
                nf = min(4, NFC - f0)
                gs, gb = wload(wgu, 0, 8, f0 * 128, nf * 128)
                us, ub = wload(wgu, 0, 8, DFF + f0 * 128, nf * 128)
                for fo in range(nf):
                    f = f0 + fo
                    pg, pgb = psf()
                    pu, pub = psf()
                    for kc in range(8):
                        fw.op("pe", lambda e: e.matmul(pg[:, 0:T], gs[:, kc, fo * 128:(fo + 1) * 128], hT.t[:, kc, :],
                                                       start=(kc == 0), stop=(kc == 7)), reads=[gb] + hT.b, writes=[pgb])
                    for kc in range(8):
                        fw.op("pe", lambda e: e.matmul(pu[:, 0:T], us[:, kc, fo * 128:(fo + 1) * 128], hT.t[:, kc, :],
                                                       start=(kc == 0), stop=(kc == 7)), reads=[ub] + hT.b, writes=[pub])
                    k = f % 2
                    fw.op("act", lambda e: e.activation(sg.t[:, k, :], pg[:, 0:T], AF.Silu), reads=[pgb], writes=[sg.b[k]])
                    fw.op("dve", lambda e: e.tensor_tensor(actT.t[:, f, :], sg.t[:, k, :], pu[:, 0:T], ALU.mult),
                          reads=[sg.b[k], pub], writes=actT.b)
            groups = [(0, 8), (8, 8), (16, 6)]
            for dh in range(2):
                slots = [wload(wd, g0, gn, dh * 512, 512) for (g0, gn) in groups]
                for tt in range(NTT):
                    ps, pb = psf()
                    for gi, (g0, gn) in enumerate(groups):
                        sl, slb = slots[gi]
                        for fo in range(gn):
                            f = g0 + fo
                            fw.op("pe", lambda e: e.matmul(ps[:, 0:512], actT.t[:, f, tt * 128:(tt + 1) * 128], sl[:, fo, :],
                                                           start=(f == 0), stop=(f == NFC - 1)),
                                  reads=[slb] + actT.b, writes=[pb])
                    xs = x_res.t[:, tt, dh * 512:(dh + 1) * 512]
                    fw.op("dve", lambda e: e.scalar_tensor_tensor(xs, ps[:, 0:512], 0.5, xs, ALU.mult, ALU.add),
                          reads=[pb, x_res.b[tt]], writes=[x_res.b[tt]])
            dm = False
            if after is not None:
                dm = after(ph)
            fw.barrier(("pe", "act", "dve"), dmas=bool(dm))

    def mixer(full, need_kv, first_full, after=None):
        with ExitStack() as ph:
            xtok = sb(ph, "xtok", [128, NTT, D], BF16, 1)
            BT = sb(ph, "BT", [128, 2, T], BF16)
            CT = sb(ph, "CT", [128, 2, T], BF16)
            Btok = sb(ph, "Btok", [128, NTT, 256], BF16)
            dt = sb(ph, "dt", [128, NTT, 16], F32)
            if full:
                yssmT = sb(ph, "yssmT", [128, 8, T], BF16)
                yattT = sb(ph, "yattT", [128, 8, T], BF16)
            with ExitStack() as p1:
                pre = sb(p1, "pre", [128, 2, T + 4], BF16, 2)
                xfm = sb(p1, "xfm", [128, 2, T], BF16, 2)
                dgs = sb(p1, "dgs", [128, 48, 128], BF16)
                fw.barrier(("pool",))
                fw.dma(dgs.t[:], dgw, writes=dgs.b, q="pool")
                cps = {}

                def ev_xbc(m, ps, pb):
                    k = m % 2
                    fw.op("act", lambda e: e.copy(pre.t[:, k, 3:T + 3], ps[:, 0:T]), reads=[pb], writes=[pre.b[k]])
                    fw.op("dve", lambda e: e.tensor_copy(pre.t[:, k, 0:3], halo.t[:, m, :]), reads=halo.b, writes=[pre.b[k]])
                    fw.op("dve", lambda e: e.tensor_copy(halo.t[:, m, :], pre.t[:, k, T:T + 3]), reads=[pre.b[k]], writes=halo.b)

                def ev_xbc2(m):
                    k = m % 2
                    pc, pcb = psf()
                    for tap in range(4):
                        fw.op("pe", lambda e: e.matmul(pc[:, 0:T], dgs.t[:, m * 4 + tap, :], pre.t[:, k, tap:tap + T],
                                                       start=(tap == 0), stop=(tap == 3)), reads=dgs.b + [pre.b[k]], writes=[pcb])
                    if m < 8:
                        fw.op("act", lambda e: e.activation(xfm.t[:, k, :], pc[:, 0:T], AF.Silu, bias=cb.t[:, m:m + 1]), reads=[pcb] + cb.b, writes=[xfm.b[k]])
                    elif m < 10:
                        g = m - 8
                        fw.op("act", lambda e: e.activation(BT.t[:, g, :], pc[:, 0:T], AF.Silu, bias=cb.t[:, m:m + 1]), reads=[pcb] + cb.b, writes=BT.b)
                    else:
                        g = m - 10
                        fw.op("act", lambda e: e.activation(CT.t[:, g, :], pc[:, 0:T], AF.Silu, bias=cb.t[:, m:m + 1]), reads=[pcb] + cb.b, writes=CT.b)

                def ev_xbc3(m):
                    k = m % 2
                    if m < 8:
                        pt, ptb = psb()
                        for tt in range(NTT):
                            fw.op("pe", lambda e: e.transpose(pt[:, tt * 128:(tt + 1) * 128], xfm.t[:, k, tt * 128:(tt + 1) * 128], identb_ap),
                                  reads=[xfm.b[k]] + identb.b, writes=[ptb])
                        fw.op("dve", lambda e: e.tensor_copy(xtok.t[:, :, m * 128:(m + 1) * 128], pt[:, 0:512].rearrange("p (t c) -> p t c", t=NTT)),
                              reads=[ptb], writes=xtok.b)
                    elif m < 10:
                        g = m - 8
                        pt, ptb = psb()
                        for tt in range(NTT):
                            fw.op("pe", lambda e: e.transpose(pt[:, tt * 128:(tt + 1) * 128], BT.t[:, g, tt * 128:(tt + 1) * 128], identb_ap),
                                  reads=BT.b + identb.b, writes=[ptb])
                        fw.op("dve", lambda e: e.tensor_copy(Btok.t[:, :, g * 128:(g + 1) * 128], pt[:, 0:512].rearrange("p (t c) -> p t c", t=NTT)),
                              reads=[ptb], writes=Btok.b)

                def ev_dt(u, tt, ps, pb):
                    fw.op("dve", lambda e: e.tensor_tensor(dt.t[:, tt, :], ps[:, 0:16], hp.t[:, 0, :], ALU.add), reads=[pb] + hp.b, writes=dt.b)
                    fw.op("act", lambda e: e.activation(dt.t[:, tt, :], dt.t[:, tt, :], AF.Exp), reads=dt.b, writes=dt.b)
                    fw.op("act", lambda e: e.activation(dt.t[:, tt, :], dt.t[:, tt, :], AF.Ln, bias=col1.t[:, 0:1]), reads=dt.b + col1.b, writes=dt.b)

                proj_tm(w_in, C_DT, 16, 8, lambda kc, tt: hT.t[:, kc, tt * 128:(tt + 1) * 128], hT.b, ev_dt)
                proj_fm(w_in, C_X, 12, 8, lambda kc: hT.t[:, kc, :], hT.b, ev_xbc, evac2=ev_xbc2, evac3=ev_xbc3)

                fw.barrier()
            dump("xtok", xtok.t[:], xtok.b)
            dump("dt", dt.t[:], dt.b)

            def kv_unit(i):
                if i < 2:
                    slot, sbuf_ = wload(w_in, 0, 8, C_K + i * 512, 512)
                    for mo in range(4):
                        m = i * 4 + mo
                        ps, pb = psf()
                        for kc in range(8):
                            fw.op("pe", lambda e: e.matmul(ps[:, 0:T], slot[:, kc, mo * 128:(mo + 1) * 128], hT.t[:, kc, :],
                                                           start=(kc == 0), stop=(kc == 7)), reads=[sbuf_] + hT.b, writes=[pb])
                        fw.op("act", lambda e: e.copy(kT.t[:, m, T:2 * T], ps[:, 0:T]), reads=[pb], writes=kT.b)
                else:
                    u = i - 2
                    slot, sbuf_ = wload(w_in, 0, 8, C_V + u * 512, 512)
                    for tt in range(NTT):
                        ps, pb = psf()
                        for kc in range(8):
                            fw.op("pe", lambda e: e.matmul(ps[:, 0:512], hT.t[:, kc, tt * 128:(tt + 1) * 128], slot[:, kc, :],
                                                           start=(kc == 0), stop=(kc == 7)), reads=[sbuf_] + hT.b, writes=[pb])
                        fw.op("act", lambda e: e.copy(Vtok.t[:, 4 + tt, 64 + u * 512:64 + (u + 1) * 512], ps[:, 0:512]), reads=[pb], writes=Vtok.b)

            def make_attention(p3):
                qT = sb(p3, "qT", [128, 8, T], BF16)
                qpad = sb(p3, "qpad", [128, 2, 2, T], BF16, 2)
                PT = sb(p3, "PT", [128, 3, 512], BF16, 3)
                rden = sb(p3, "rden", [128, 512], F32)
                vaug = sb(p3, "vaug", [128, 2, 8, 128], BF16, 2)
                fw.op("dve", lambda e: e.memset(qpad.t[:], 0.0), writes=qpad.b)

                def ev_q(m, ps, pb):
                    fw.op("act", lambda e: e.activation(qT.t[:, m, :], ps[:, 0:T], AF.Copy, scale=0.125), reads=[pb], writes=qT.b)

                proj_fm(w_in, C_Q, 8, 8, lambda kc: hT.t[:, kc, :], hT.b, ev_q)

                def prep_q(hpair):
                    qb = hpair % 2
                    fw.op("act", lambda e: e.copy(qpad.t[0:64, qb, 0, :], qT.t[0:64, hpair, :]), reads=qT.b, writes=[qpad.b[qb]])
                    fw.op("act", lambda e: e.copy(qpad.t[64:128, qb, 1, :], qT.t[64:128, hpair, :]), reads=qT.b, writes=[qpad.b[qb]])

                def prep_v(h):
                    hb = h % 2
                    if hb == 0:
                        vc, oc, osrc = 0, 64, 1088
                    else:
                        vc, oc, osrc = 64, 0, 0
                    fw.op("dve", lambda e: e.tensor_copy(vaug.t[:, hb, :, vc:vc + 64], Vtok.t[:, :, 64 + 64 * h:128 + 64 * h]),
                          reads=Vtok.b, writes=[vaug.b[hb]])
                    fw.op("dve", lambda e: e.tensor_copy(vaug.t[:, hb, :, oc:oc + 64], Vtok.t[:, :, osrc:osrc + 64]),
                          reads=Vtok.b, writes=[vaug.b[hb]])

                def stageA(hpair, h2, j, k):
                    h = hpair * 2 + h2
                    qb = hpair % 2
                    qt_lo = max(0, j - 4)
                    qt_hi = min(3, j)
                    q0 = qt_lo * 128
                    nq = (qt_hi - qt_lo + 1) * 128
                    rel0 = (qt_lo * 2 + 8 - 2 * j) * 64
                    if j == 0:
                        if h2 == 0:
                            prep_q(hpair)
                        prep_v(h)
                    sp_, spb = psf()
                    fw.op("pe", lambda e: e.matmul(sp_[:, 0:nq], kT.t[:, hpair, j * 128:(j + 1) * 128], qpad.t[:, qb, h2, q0:q0 + nq],
                                                   start=True, stop=False), reads=kT.b + [qpad.b[qb]], writes=[spb])
                    fw.op("pe", lambda e: e.matmul(sp_[:, 0:nq], identb_ap, EB.t[:, h, rel0:rel0 + nq],
                                                   start=False, stop=True), reads=identb.b + EB.b, writes=[spb])
                    fw.op("act", lambda e: e.activation(PT.t[:, k, 0:nq], sp_[:, 0:nq], AF.Exp), reads=[spb], writes=[PT.b[k]])

                def stageB(hpair, h2, j, k):
                    qt_lo = max(0, j - 4)
                    qt_hi = min(3, j)
                    q0 = qt_lo * 128
                    nq = (qt_hi - qt_lo + 1) * 128
                    pX, pXb = psfix(4 + h2)
                    segs = [(q0, nq)]
                    if 1 <= j <= 3:
                        segs = [(q0, nq - 128), (q0 + nq - 128, 128)]
                    for (sq0, snq) in segs:
                        fw.op("pe", lambda e: e.matmul(pX[:, sq0:sq0 + snq], vaug.t[:, h2, j, :], PT.t[:, k, sq0 - q0:sq0 - q0 + snq],
                                                       start=(j == 0), stop=(j == 7)), reads=[vaug.b[h2], PT.b[k]], writes=[pXb])
                    if h2 == 1 and j == 7:
                        pA, pAb = psfix(4)
                        pB, pBb = psfix(5)
                        fw.op("act", lambda e: e.copy(rden.t[0:64, :], pA[64:128, 0:512]), reads=[pAb], writes=rden.b)
                        fw.op("act", lambda e: e.copy(rden.t[64:128, :], pB[0:64, 0:512]), reads=[pBb], writes=rden.b)
                        fw.op("dve", lambda e: e.reciprocal(rden.t[:], rden.t[:]), reads=rden.b, writes=rden.b)
                        fw.op("dve", lambda e: e.tensor_tensor(yattT.t[0:64, hpair, :], pA[0:64, 0:512], rden.t[0:64, :], ALU.mult),
                              reads=[pAb] + rden.b, writes=yattT.b)
                        fw.op("dve", lambda e: e.tensor_tensor(yattT.t[64:128, hpair, :], pB[64:128, 0:512], rden.t[64:128, :], ALU.mult),
                              reads=[pBb] + rden.b, writes=yattT.b)

                its = [(hpair, h2, j) for hpair in range(8) for h2 in range(2) for j in range(8)]
                state = {"pos": 0, "pend": []}

                def advance(n):
                    for _ in range(n):
                        if state["pos"] >= len(its):
                            return
                        hpair, h2, j = its[state["pos"]]
                        k = state["pos"] % 3
                        state["pos"] += 1
                        stageA(hpair, h2, j, k)
                        state["pend"].append((hpair, h2, j, k))
                        if len(state["pend"]) > 2:
                            stageB(*state["pend"].pop(0))

                def flush():
                    advance(len(its))
                    while state["pend"]:
                        stageB(*state["pend"].pop(0))

                return advance, flush

            with ExitStack() as p2:
                NB = 2 if full else NTT
                a_t = sb(p2, "a_t", [128, NB, 16], F32, NB)
                wdt = sb(p2, "wdt", [128, NB, 16], F32, NB)
                decr = sb(p2, "decr", [128, NB, 16], F32, NB)
                xw = sb(p2, "xw", [128, NB, D], BF16, NB)
                if full:
                    sz = sb(p2, "sz", [128, 2, D], BF16, 2)
                    zslots = []
                    Y = sb(p2, "Y", [128, 1, 8, 128], F32, 1)
                    Mm = sb(p2, "Mm", [128, 2, 8, 128], BF16, 2)
                    CBs = sb(p2, "CBs", [128, 2, 128], F32, 2)
                    t1 = sb(p2, "t1", [128, D], F32)
                    t2 = sb(p2, "t2", [128, 512], F32)
                    ynb = sb(p2, "ynb", [128, D], BF16)
                    ebias = sb(p2, "ebias", [128, 2, 16], F32, 2)
                    ecum = sb(p2, "ecum", [128, 2, 16], F32, 2)
                    gss = sb(p2, "gss", [128, 2], F32)

                pos = {}

                def ssd_pre(tt):
                    k = tt % NB
                    fw.op("dve", lambda e: e.tensor_tensor(a_t.t[:, k, :], dt.t[:, tt, :], Aneg.t[:], ALU.mult), reads=dt.b + Aneg.b, writes=[a_t.b[k]])
                    if full:
                        for u in range(2):
                            zs, zb = zslots[u]
                            pz, pzb = psf()
                            for kc in range(8):
                                fw.op("pe", lambda e: e.matmul(pz[:, 0:512], hT.t[:, kc, tt * 128:(tt + 1) * 128], zs[:, kc, :],
                                                               start=(kc == 0), stop=(kc == 7)), reads=[zb] + hT.b, writes=[pzb])
                            fw.op("act", lambda e: e.activation(sz.t[:, k, u * 512:(u + 1) * 512], pz[:, 0:512], AF.Silu), reads=[pzb], writes=[sz.b[k]])
                    pss, pssb = psf()
                    fw.op("pe", lambda e: e.matmul(pss[:, 0:16], cst.t[:, 2, :], a_t.t[:, k, :], start=True, stop=True), reads=cst.b + [a_t.b[k]], writes=[pssb])
                    fw.op("pe", lambda e: e.matmul(pss[:, 16:32], onesf.t[:], a_t.t[:, k, :], start=True, stop=True), reads=onesf.b + [a_t.b[k]], writes=[pssb])
                    if full:
                        fw.op("pe", lambda e: e.matmul(pss[:, 32:48], cst.t[:, 1, :], a_t.t[:, k, :], start=True, stop=True), reads=cst.b + [a_t.b[k]], writes=[pssb])
                    fw.op("act", lambda e: e.activation(wdt.t[:, k, :], pss[:, 0:16], AF.Exp), reads=[pssb], writes=[wdt.b[k]])
                    fw.op("dve", lambda e: e.tensor_tensor(wdt.t[:, k, :], wdt.t[:, k, :], dt.t[:, tt, :], ALU.mult), reads=[wdt.b[k]] + dt.b, writes=[wdt.b[k]])
                    fw.op("act", lambda e: e.activation(decr.t[:, k, :], pss[:, 16:32], AF.Exp), reads=[pssb], writes=[decr.b[k]])
                    if full:
                        fw.op("act", lambda e: e.activation(ebias.t[:, k, :], dt.t[:, tt, :], AF.Ln), reads=dt.b, writes=[ebias.b[k]])
                        fw.op("dve", lambda e: e.tensor_tensor(ebias.t[:, k, :], ebias.t[:, k, :], pss[:, 32:48], ALU.subtract), reads=[ebias.b[k], pssb], writes=[ebias.b[k]])
                        fw.op("act", lambda e: e.activation(ecum.t[:, k, :], pss[:, 32:48], AF.Exp), reads=[pssb], writes=[ecum.b[k]])
                    fw.op("dve", lambda e: e.tensor_tensor(xw.t[:, k, :].rearrange("p (h c) -> p h c", h=16),
                                                           xtok.t[:, tt, :].rearrange("p (h c) -> p h c", h=16),
                                                           wdt.t[:, k, :].unsqueeze(2).to_broadcast([128, 16, 64]), ALU.mult),
                          reads=xtok.b + [wdt.b[k]], writes=[xw.b[k]])

                def ssd_front(tt, g):
                    k = tt % 2
                    tsl = slice(tt * 128, (tt + 1) * 128)
                    fw.op("dve", lambda e: e.tensor_tensor(Y.t[:, 0], cst.t[:, 1, :].unsqueeze(1).to_broadcast([128, 8, 128]),
                                                           a_t.t[:, k, g * 8:(g + 1) * 8].unsqueeze(2).to_broadcast([128, 8, 128]), ALU.mult),
                          reads=cst.b + [a_t.b[k]], writes=[Y.b[0]])
                    pc, pcb = psf()
                    fw.op("pe", lambda e: e.matmul(pc[:, 0:128], BT.t[:, g, tsl], CT.t[:, g, tsl], start=True, stop=True),
                          reads=BT.b + CT.b, writes=[pcb])
                    fw.op("act", lambda e: e.copy(CBs.t[:, g, :], pc[:, 0:128]), reads=[pcb], writes=[CBs.b[g]])

                def ssd_front_r(tt, g):
                    k = tt % 2
                    for q in range(2):
                        prt, prb = psf()
                        fw.op("pe", lambda e: e.matmul(prt[:, 0:512], onesf.t[:], Y.t[:, 0, q * 4:(q + 1) * 4, :].rearrange("p h l -> p (h l)"),
                                                       start=True, stop=False), reads=onesf.b + [Y.b[0]], writes=[prb])
                        fw.op("pe", lambda e: e.matmul(prt[:, 0:512], identb_ap, NM8.t[:, 0:4, :].rearrange("p h l -> p (h l)"),
                                                       start=False, stop=True), reads=identb.b + NM8.b, writes=[prb])
                        for hh in range(4):
                            h = g * 8 + q * 4 + hh
                            fw.op("act", lambda e: e.activation(Mm.t[:, g, q * 4 + hh, :], prt[:, hh * 128:(hh + 1) * 128], AF.Exp,
                                                                bias=ebias.t[:, k, h:h + 1]), reads=[prb, ebias.b[k]], writes=[Mm.b[g]])

                def ssd_front_b(tt, g):
                    fw.op("dve", lambda e: e.tensor_tensor(Mm.t[:, g], Mm.t[:, g],
                                                           CBs.t[:, g, :].unsqueeze(1).to_broadcast([128, 8, 128]), ALU.mult),
                          reads=[Mm.b[g], CBs.b[g]], writes=[Mm.b[g]])

                def ssd_back(tt, g):
                    k = tt % 2
                    tsl = slice(tt * 128, (tt + 1) * 128)
                    gsl = slice(g * 512, (g + 1) * 512)
                    if g == 0:
                        fw.op("act", lambda e: e.copy(Hbf.t[:], Hst.t[:]), reads=Hst.b, writes=Hbf.b)
                    pyt, pyb = psfix(3)
                    for hh in range(8):
                        h = g * 8 + hh
                        fw.op("pe", lambda e: e.matmul(pyt[:, hh * 64:(hh + 1) * 64], Mm.t[:, g, hh, :], xtok.t[:, tt, h * 64:(h + 1) * 64],
                                                       start=True, stop=True), reads=[Mm.b[g]] + xtok.b, writes=[pyb])

                def ssd_back_rest(tt, g):
                    k = tt % 2
                    tsl = slice(tt * 128, (tt + 1) * 128)
                    gsl = slice(g * 512, (g + 1) * 512)
                    pyt, pyb = psfix(3)
                    po, pob = psf()
                    fw.op("pe", lambda e: e.matmul(po[:, 0:512], CT.t[:, g, tsl], Hbf.t[:, gsl], start=True, stop=True),
                          reads=CT.b + Hbf.b, writes=[pob])
                    fw.op("dve", lambda e: e.tensor_tensor(t1.t[:, gsl].rearrange("p (h c) -> p h c", h=8),
                                                           po[:, 0:512].rearrange("p (h c) -> p h c", h=8),
                                                           ecum.t[:, k, g * 8:(g + 1) * 8].unsqueeze(2).to_broadcast([128, 8, 64]), ALU.mult),
                          reads=[pob, ecum.b[k]], writes=t1.b)
                    fw.op("dve", lambda e: e.tensor_tensor(t2.t[:].rearrange("p (h c) -> p h c", h=8),
                                                           xtok.t[:, tt, gsl].rearrange("p (h c) -> p h c", h=8),
                                                           hp.t[:, 2, g * 8:(g + 1) * 8].unsqueeze(2).to_broadcast([128, 8, 64]), ALU.mult),
                          reads=xtok.b + hp.b, writes=t2.b)
                    fw.op("dve", lambda e: e.tensor_tensor(t1.t[:, gsl], t1.t[:, gsl], t2.t[:], ALU.add), reads=t1.b + t2.b, writes=t1.b)
                    fw.op("dve", lambda e: e.tensor_tensor(t1.t[:, gsl], t1.t[:, gsl], pyt[:, 0:512], ALU.add), reads=t1.b + [pyb], writes=t1.b)
                    fw.op("dve", lambda e: e.tensor_tensor(t1.t[:, gsl], t1.t[:, gsl], sz.t[:, k, gsl], ALU.mult), reads=t1.b + [sz.b[k]], writes=t1.b)
                    fw.op("dve", lambda e: e.memset(gss.t[:, g:g + 1], 0.0), writes=gss.b)
                    fw.op("act", lambda e: e.activation(ynb.t[:, gsl], t1.t[:, gsl], AF.Square, scale=float(512 ** -0.5),
                                                        accum_out=gss.t[:, g:g + 1]), reads=t1.b, writes=ynb.b + gss.b)
                    rsqrt_eps(gss.t[:, g:g + 1], gss.t[:, g:g + 1], gss.b, gss.b)
                    fw.op("dve", lambda e: e.tensor_scalar(ynb.t[:, gsl], t1.t[:, gsl], gss.t[:, g:g + 1], None, ALU.mult),
                          reads=t1.b + gss.b, writes=ynb.b)
                    if g == 1:
                        if tt == 0:
                            dump("yssd", t1.t[:], t1.b)
                        pt, ptb = psb()
                        for j in range(8):
                            fw.op("pe", lambda e: e.transpose(pt[:, j * 128:(j + 1) * 128], ynb.t[:, j * 128:(j + 1) * 128], identb_ap),
                                  reads=ynb.b + identb.b, writes=[ptb])
                        fw.op("dve", lambda e: e.tensor_tensor(yssmT.t[:, :, tsl], pt[:].rearrange("p (j c) -> p j c", j=8),
                                                               nwf.t[:, 2, :].unsqueeze(2).to_broadcast([128, 8, 128]), ALU.mult),
                              reads=[ptb] + nwf.b, writes=yssmT.b)

                def ssd_post(tt):
                    k = tt % NB
                    for g in range(2):
                        pst_, pstb = psf()
                        fw.op("pe", lambda e: e.matmul(pst_[:, 0:512], Btok.t[:, tt, g * 128:(g + 1) * 128], xw.t[:, k, g * 512:(g + 1) * 512],
                                                       start=True, stop=True), reads=Btok.b + [xw.b[k]], writes=[pstb])
                        hs = Hst.t[:, g * 512:(g + 1) * 512]
                        fw.op("dve", lambda e: e.tensor_tensor(hs.rearrange("p (h c) -> p h c", h=8), hs.rearrange("p (h c) -> p h c", h=8),
                                                               decr.t[:, k, g * 8:(g + 1) * 8].unsqueeze(2).to_broadcast([128, 8, 64]), ALU.mult),
                              reads=Hst.b + [decr.b[k]], writes=Hst.b)
                        fw.op("dve", lambda e: e.tensor_tensor(hs, hs, pst_[:, 0:512], ALU.add), reads=Hst.b + [pstb], writes=Hst.b)

                if not full:
                    for tt in range(NTT):
                        ssd_pre(tt)
                    for tt in range(NTT):
                        ssd_post(tt)
                else:
                    for i in range(4):
                        kv_unit(i)
                    att_adv, att_flush = make_attention(p2)
                    zslots.extend([wload(w_in, 0, 8, C_Z + u * 512, 512) for u in range(2)])
                    steps = [(tt, g) for tt in range(NTT) for g in range(2)]
                    prev = None
                    for (tt, g) in steps:
                        if g == 0:
                            ssd_pre(tt)
                        ssd_front(tt, g)
                        att_adv(3)
                        if prev is not None:
                            ssd_back(*prev)
                        att_adv(3)
                        ssd_front_r(tt, g)
                        att_adv(4)
                        if prev is not None:
                            ssd_back_rest(*prev)
                            if prev[1] == 1:
                                ssd_post(prev[0])
                        att_adv(3)
                        ssd_front_b(tt, g)
                        att_adv(3)
                        prev = (tt, g)
                    ssd_back(*prev)
                    ssd_back_rest(*prev)
                    ssd_post(prev[0])
                    att_flush()
                fw.barrier()
            dump("Hst", Hst.t[:], Hst.b)

            if need_kv:
                if not full:
                    for i in range(4):
                        kv_unit(i)
                    fw.barrier()
                fw.op("act", lambda e: e.copy(kT.t[:, :, 0:T], kT.t[:, :, T:2 * T]), reads=kT.b, writes=kT.b)
                fw.op("dve", lambda e: e.tensor_copy(Vtok.t[:, 0:4, :], Vtok.t[:, 4:8, :]), reads=Vtok.b, writes=Vtok.b)

            if full:
                dump("yssmT", yssmT.t[:], yssmT.b)
                dump("yattT", yattT.t[:], yattT.b)
                with ExitStack() as p4:
                    norm_prep()
                    mT = sb(p4, "mT", [128, 8, T], BF16)
                    s3 = sb(p4, "s3", [128, 2, T], F32, 2)
                    s4 = sb(p4, "s4", [128, 2, T], F32, 2)
                    for u0 in (0, 4):
                        sl1, b1 = wload(w_bs, 0, 8, u0 * 128, 512)
                        sl2, b2 = wload(w_ba, 0, 8, u0 * 128, 512)
                        sl3, b3 = wload(w_in, 0, 8, C_GS + u0 * 128, 512)
                        sl4, b4 = wload(w_in, 0, 8, C_GA + u0 * 128, 512)
                        for mo in range(4):
                            m = u0 + mo
                            k = m % 2
                            msl = slice(mo * 128, (mo + 1) * 128)
                            p1_, p1b = psf()
                            p2_, p2b = psf()
                            p3_, p3b = psf()
                            p4_, p4b = psfix(3)
                            for kc in range(8):
                                fw.op("pe", lambda e: e.matmul(p3_[:, 0:T], sl3[:, kc, msl], hT.t[:, kc, :], start=(kc == 0), stop=(kc == 7)),
                                      reads=[b3] + hT.b, writes=[p3b])
                            for kc in range(8):
                                fw.op("pe", lambda e: e.matmul(p4_[:, 0:T], sl4[:, kc, msl], hT.t[:, kc, :], start=(kc == 0), stop=(kc == 7)),
                                      reads=[b4] + hT.b, writes=[p4b])
                            for kc in range(8):
                                fw.op("pe", lambda e: e.matmul(p1_[:, 0:T], sl1[:, kc, msl], yssmT.t[:, kc, :], start=(kc == 0), stop=(kc == 7)),
                                      reads=[b1] + yssmT.b, writes=[p1b])
                            for kc in range(8):
                                fw.op("pe", lambda e: e.matmul(p2_[:, 0:T], sl2[:, kc, msl], yattT.t[:, kc, :], start=(kc == 0), stop=(kc == 7)),
                                      reads=[b2] + yattT.b, writes=[p2b])
                            fw.op("act", lambda e: e.activation(s3.t[:, k, :], p3_[:, 0:T], AF.Sigmoid), reads=[p3b], writes=[s3.b[k]])
                            fw.op("act", lambda e: e.activation(s4.t[:, k, :], p4_[:, 0:T], AF.Sigmoid), reads=[p4b], writes=[s4.b[k]])
                            fw.op("dve", lambda e: e.tensor_tensor(s3.t[:, k, :], s3.t[:, k, :], p1_[:, 0:T], ALU.mult), reads=[s3.b[k], p1b], writes=[s3.b[k]])
                            fw.op("dve", lambda e: e.tensor_tensor(s4.t[:, k, :], s4.t[:, k, :], p2_[:, 0:T], ALU.mult), reads=[s4.b[k], p2b], writes=[s4.b[k]])
                            fw.op("dve", lambda e: e.tensor_tensor(mT.t[:, m, :], s3.t[:, k, :], s4.t[:, k, :], ALU.add), reads=[s3.b[k], s4.b[k]], writes=mT.b)

                    def ev_out(u, tt, ps, pb):
                        xs = x_res.t[:, tt, u * 512:(u + 1) * 512]
                        fw.op("dve", lambda e: e.tensor_tensor(xs, xs, ps[:, 0:512], ALU.add), reads=[pb, x_res.b[tt]], writes=[x_res.b[tt]])

                    proj_tm(w_out, 0, 1024, 8, lambda kc, tt: mT.t[:, kc, tt * 128:(tt + 1) * 128], mT.b, ev_out)
                    if after is not None:
                        after(p4)
                    fw.barrier()
            if not full:
                norm_prep()
                if after is not None:
                    after(ph)
                fw.barrier()

    onesf = sb(st, "onesf", [128, 128], F32)
    fw.op("dve", lambda e: e.memset(onesf.t[:], 1.0), writes=onesf.b)

    def tile_src(ti):
        full = ti >= n_state
        src = x_main if full else x_prev
        tok0 = ((ti - n_state) if full else (NTOK // T - n_state + ti)) * T
        return src, tok0

    def load_x(ti):
        if ti >= n_state + n_full:
            return
        src, tok0 = tile_src(ti)
        for tt in range(NTT):
            fw.dma(x_res.t[:, tt, :], src[tok0 + tt * 128: tok0 + (tt + 1) * 128, :], writes=[x_res.b[tt]], q="pool")

    def final_out(ph, tok0):
        yo = sb(ph, "yo", [128, 2, D], F32, 2)
        fjk = sb(ph, "fjk", [128, D], BF16)
        fnw = sb(ph, "fnw", [128, D], F32)
        fw.barrier(("pool",))
        fw.dma(fnw.t[:], fnw_d, writes=fnw.b, q="pool")
        ss = ssp
        for tt in range(NTT):
            k = tt % 2
            fw.op("dve", lambda e: e.memset(ss.t[:, tt:tt + 1], 0.0), writes=ss.b)
            fw.op("act", lambda e: e.activation(fjk.t[:], x_res.t[:, tt, :], AF.Square, scale=1.0 / 32.0, accum_out=ss.t[:, tt:tt + 1]),
                  reads=[x_res.b[tt]], writes=fjk.b + ss.b)
            rsqrt_eps(ss.t[:, tt:tt + 1], ss.t[:, tt:tt + 1], ss.b, ss.b)
            fw.op("dve", lambda e: e.scalar_tensor_tensor(yo.t[:, k, :], x_res.t[:, tt, :], ss.t[:, tt:tt + 1], fnw.t[:], ALU.mult, ALU.mult),
                  reads=[x_res.b[tt]] + ss.b + fnw.b, writes=[yo.b[k]])
            fw.dma(y_out[tok0 + tt * 128: tok0 + (tt + 1) * 128, :], yo.t[:, k, :], reads=[yo.b[k]], q="pool")

    load_x(0)
    norm_prep()
    with ExitStack() as ph0:
        norm_to_hT(0, ph0)
        fw.barrier()
    for ti in range(n_state + n_full):
        full = ti >= n_state
        src, tok0 = tile_src(ti)
        last_state = (ti == n_state - 1)

        def after_ffn1(ph, full=full, ti=ti):
            norm_to_hT(1, ph)
            if not full:
                load_x(ti + 1)
            return False

        ffn(0, after=after_ffn1)
        if full and ti == n_state:
            dump("x1", x_res.t[:], x_res.b)

        def after_mixer(ph, full=full, ti=ti, last_state=last_state):
            if last_state:
                hm = hmask.t[:, 0:1]
                fw.op("dve", lambda e: e.tensor_scalar(Hst.t[:], Hst.t[:], hm, None, ALU.mult), reads=Hst.b + hmask.b, writes=Hst.b)
                fw.op("dve", lambda e: e.tensor_scalar(halo.t[:], halo.t[:], hm, None, ALU.mult), reads=halo.b + hmask.b, writes=halo.b)
                fw.op("dve", lambda e: e.tensor_scalar(Vtok.t[:, 0:4, :], Vtok.t[:, 0:4, :], hm, None, ALU.mult), reads=Vtok.b + hmask.b, writes=Vtok.b)
            if full:
                if ti == n_state:
                    dump("x2", x_res.t[:], x_res.b)
                norm_to_hT(3, ph)
            else:
                norm_to_hT(0, ph)

        mixer(full, full or last_state, ti == n_state, after=after_mixer)
        if full:
            def after_ffn2(ph, tok0=tok0, ti=ti):
                final_out(ph, tok0)
                load_x(ti + 1)
                if ti + 1 < n_state + n_full:
                    norm_prep()
                    norm_to_hT(0, ph)
                return True

            ffn(1, after=after_ffn2)
    fw.barrier(("sp", "act", "pool"), dmas=True)
    st.close()
    return nc


def _consts():
    t = np.arange(128)
    ident = np.eye(128, dtype=np.float32)
    U = (t[:, None] <= t[None, :]).astype(np.float32)
    Ls = (t[:, None] > t[None, :]).astype(np.float32)
    nm = np.where(t[:, None] > t[None, :], NEG, 0.0).astype(np.float32)
    return np.ascontiguousarray(np.stack([ident, U, Ls, nm], axis=1))


def _prep_shared(inp):
    f = lambda a: np.ascontiguousarray(np.asarray(a, dtype=np.float32))
    fm = lambda w: np.asarray(w, dtype=np.float32).reshape(8, 128).T
    sh = {}
    sh["w_gu1"] = f(inp["ffn1_w_gu"][0])
    sh["w_gu2"] = f(inp["ffn2_w_gu"][0])
    sh["w_dn1"] = f(inp["ffn1_w_down"][0])
    sh["w_dn2"] = f(inp["ffn2_w_down"][0])
    sh["w_in"] = f(inp["w_in"][0])
    sh["w_bs"] = f(inp["w_branch_ssm"][0])
    sh["w_ba"] = f(inp["w_branch_attn"][0])
    sh["w_out"] = f(inp["w_out"][0])
    sh["nwf"] = f(np.stack([fm(inp["ffn1_norm_w"][0]), fm(inp["mix_norm_w"][0]), fm(inp["ssm_norm_w"][0]),
                            fm(inp["ffn2_norm_w"][0])], axis=1))
    sh["fnw"] = f(np.broadcast_to(np.asarray(inp["final_norm_w"], np.float32)[None, :], (128, D)))
    cwv = np.asarray(inp["conv_w"][0], np.float32)
    sh["cw"] = f(cwv.reshape(4, 12, 128).transpose(2, 1, 0))
    sh["cb"] = f(np.asarray(inp["conv_b"][0], np.float32).reshape(12, 128).T)
    hpv = np.stack([np.asarray(inp["dt_bias"][0], np.float32), np.asarray(inp["A_log"][0], np.float32),
                    np.asarray(inp["D_skip"][0], np.float32)], axis=0)
    sh["hp"] = f(np.broadcast_to(hpv[None], (128, 3, 16)))
    rb = np.asarray(inp["rel_bias"][0], np.float32)
    r = np.arange(128)[:, None]
    l = np.arange(640)[None, :]
    idx = np.clip(l - r, -128, 128) + 128
    kc = r // 64
    qc = l // 64
    valid = (qc >= kc) & (qc <= kc + 8)
    tab = rb[:, idx]
    tab = np.where(valid[None], tab, np.float32(NEG))
    sh["btab"] = f(tab)
    sh["cst"] = _consts()
    return sh


_NC_CACHE = {}


def kernel(**inputs):
    x = np.asarray(inputs["x"], dtype=np.float32)
    sh = _prep_shared(inputs)
    in_maps = []
    for c in range(8):
        b, h = c // 2, c % 2
        m = dict(sh)
        m["x_main"] = np.ascontiguousarray(x[b, h * NTOK:(h + 1) * NTOK])
        m["x_prev"] = np.ascontiguousarray(x[b, 0:NTOK])
        m["hmask"] = np.full((128, 1), float(h), dtype=np.float32)
        in_maps.append(m)
    if "nc" not in _NC_CACHE:
        _NC_CACHE["nc"] = build()
    res = run_bass_kernel_spmd(_NC_CACHE["nc"], in_maps, core_ids=list(range(8)))
    out = np.empty((4, 2 * NTOK, D), dtype=np.float32)
    for c in range(8):
        b, h = c // 2, c % 2
        out[b, h * NTOK:(h + 1) * NTOK] = np.asarray(res.results[c]["y"], dtype=np.float32)
    return out
```

```python
import numpy as np
from contextlib import ExitStack
import concourse.bass as bass
import concourse.mybir as mybir
from concourse.bass_utils import run_bass_kernel_spmd

F32 = mybir.dt.float32
BF16 = mybir.dt.bfloat16
AF = mybir.ActivationFunctionType
ALU = mybir.AluOpType

D = 1024
DFF = 2816
NFC = 22
T = 512
NTT = 4
NTOK = 4096
INP = 7696
EPS = 1e-6
NEG = -30000.0
C_Z, C_X, C_B, C_C, C_DT, C_Q, C_K, C_V, C_GS, C_GA = 0, 1024, 2048, 2304, 2560, 2576, 3600, 4624, 5648, 6672
NS = 5


class Buf:
    __slots__ = ("name", "lw", "rd")

    def __init__(self, name):
        self.name = name
        self.lw = None
        self.rd = {}


class Tn:
    def __init__(self, t, bufs):
        self.t = t
        self.b = bufs


class FW:
    def __init__(self, nc, stack, n_dma_sems=32):
        self.nc = nc
        self.engs = {"pe": nc.tensor, "act": nc.scalar, "dve": nc.vector, "pool": nc.gpsimd, "sp": nc.sync}
        self.sem = {}
        self.cnt = {}
        for k in self.engs:
            self.sem[k] = stack.enter_context(nc.semaphore("tick_" + k))
            self.cnt[k] = 0
        self.dsem = [stack.enter_context(nc.semaphore("dma_%d" % i)) for i in range(n_dma_sems)]
        self.dcnt = [0] * n_dma_sems
        self.dnext2 = {"sp": 0, "pool": 0}
        self.seen = {k: {} for k in self.engs}
        self.ninst = 0

    def _wait(self, eng, dep):
        kind, idx, val = dep
        key = (kind, idx)
        if kind == "e" and idx == eng and eng == "pe":
            return
        if self.seen[eng].get(key, 0) >= val:
            return
        s = self.sem[idx] if kind == "e" else self.dsem[idx]
        self.engs[eng].wait_ge(s, val)
        self.seen[eng][key] = val

    def _deps(self, eng, reads, writes):
        for b in reads:
            if b.lw is not None:
                self._wait(eng, b.lw)
        for b in writes:
            if b.lw is not None:
                self._wait(eng, b.lw)
            for d in b.rd.values():
                if d[0] == "e" and d[1] == eng and eng != "sp" and eng != "pool":
                    continue
                self._wait(eng, d)

    def _mark(self, dep, reads, writes):
        for b in writes:
            b.lw = dep
            b.rd = {}
        for b in reads:
            if b in writes:
                continue
            b.rd[(dep[0], dep[1])] = dep

    def op(self, eng, fn, reads=(), writes=()):
        self._deps(eng, reads, writes)
        ins = fn(self.engs[eng])
        self.cnt[eng] += 1
        ins.then_inc(self.sem[eng], 1)
        dep = ("e", eng, self.cnt[eng])
        self._mark(dep, reads, writes)
        self.ninst += 1
        return dep

    def dma(self, out, in_, reads=(), writes=(), q="sp"):
        half = len(self.dsem) // 2
        base = 0 if q == "sp" else half
        i = base + self.dnext2[q]
        self.dnext2[q] = (self.dnext2[q] + 1) % half
        if self.dcnt[i] > 0:
            self._wait(q, ("d", i, self.dcnt[i]))
        self._deps(q, reads, writes)
        ins = self.engs[q].dma_start(out=out, in_=in_)
        self.dcnt[i] += 16
        ins.then_inc(self.dsem[i], 16)
        dep = ("d", i, self.dcnt[i])
        self._mark(dep, reads, writes)
        self.ninst += 1
        return dep

    def barrier(self, engs=("pe", "act", "dve"), dmas=False):
        for e in engs:
            for f in self.engs:
                if self.cnt[f] > 0 and not (f == e and e == "pe"):
                    self._wait(e, ("e", f, self.cnt[f]))
            if dmas:
                for i in range(len(self.dsem)):
                    if self.dcnt[i] > 0:
                        self._wait(e, ("d", i, self.dcnt[i]))


def build(n_state=8, n_full=8, dbg=()):
    nc = bass.Bass("TRN2", target_bir_lowering=False)
    di = lambda n, s, dt=F32: nc.dram_tensor(n, s, dt, kind="ExternalInput").ap()
    x_prev = di("x_prev", [NTOK, D])
    x_main = di("x_main", [NTOK, D])
    w_gu = [di("w_gu1", [D, 2 * DFF]), di("w_gu2", [D, 2 * DFF])]
    w_dn = [di("w_dn1", [DFF, D]), di("w_dn2", [DFF, D])]
    w_in = di("w_in", [D, INP])
    w_bs = di("w_bs", [D, D])
    w_ba = di("w_ba", [D, D])
    w_out = di("w_out", [D, D])
    nwf_d = di("nwf", [128, 4, 8])
    fnw_d = di("fnw", [128, D])
    cw_d = di("cw", [128, 12, 4])
    cb_d = di("cb", [128, 12])
    hp_d = di("hp", [128, 3, 16])
    btab_d = di("btab", [16, 128, 640])
    hmask_d = di("hmask", [128, 1])
    cst_d = di("cst", [128, 4, 128])
    y_out = nc.dram_tensor("y", [NTOK, D], F32, kind="ExternalOutput").ap()
    dbg_out = {}
    for name, shape, dt in dbg:
        dbg_out[name] = nc.dram_tensor("dbg_" + name, list(shape), dt, kind="ExternalOutput").ap()

    st = ExitStack()
    fw = FW(nc, st)

    uid = [0]

    def sb(stack, name, shape, dt, nb=1):
        uid[0] += 1
        t = stack.enter_context(nc.sbuf_tensor("%s_%d" % (name, uid[0]), list(shape), dt))
        return Tn(t, [Buf("%s_%d" % (name, i)) for i in range(nb)])

    x_res = sb(st, "x_res", [128, NTT, D], F32, NTT)
    hT = sb(st, "hT", [128, 8, T], BF16, 1)
    ring = sb(st, "ring", [128, NS, 4096], BF16, NS)
    Hst = sb(st, "Hst", [128, 1024], F32)
    Hbf = sb(st, "Hbf", [128, 1024], BF16)
    EB = sb(st, "EB", [128, 16, 640], BF16)
    kT = sb(st, "kT", [128, 8, 2 * T], BF16)
    Vtok = sb(st, "Vtok", [128, 8, 1152], BF16)
    halo = sb(st, "halo", [128, 12, 3], BF16)
    cst = sb(st, "cst", [128, 4, 128], F32)
    identb = sb(st, "identb", [128, 128], BF16)
    NM8 = sb(st, "NM8", [128, 4, 128], BF16)
    nwf = sb(st, "nwf", [128, 4, 8], F32)
    fnw = sb(st, "fnw", [128, D], F32)
    cw = sb(st, "cw", [128, 12, 4], F32)
    cb = sb(st, "cb", [128, 12], F32)
    hp = sb(st, "hp", [128, 3, 16], F32)
    Aneg = sb(st, "Aneg", [128, 16], F32)
    hmask = sb(st, "hmask", [128, 1], F32)
    onesb = sb(st, "onesb", [128, 64], BF16)
    onesh = sb(st, "onesh", [128, 64], BF16)
    col1 = sb(st, "col1", [128, 1], F32)
    xnp = sb(st, "xnp", [128, 2, D], BF16, 2)
    ssp = sb(st, "ssp", [128, NTT], F32)
    rstdp = sb(st, "rstdp", [128, NTT], F32)

    psf_t = [st.enter_context(nc.psum_tensor("psf%d" % i, [128, 512], F32)) for i in range(6)]
    psf_b = [Buf("psf%d" % i) for i in range(6)]
    psb_t = [st.enter_context(nc.psum_tensor("psb%d" % i, [128, 1024], BF16)) for i in range(2)]
    psb_b = [Buf("psb%d" % i) for i in range(2)]
    pctr = {"f": 0, "b": 0, "r": 0}

    def psf():
        i = pctr["f"]
        pctr["f"] = (i + 1) % 4
        return psf_t[i], psf_b[i]

    def psfix(i):
        return psf_t[i], psf_b[i]

    def psb():
        i = pctr["b"]
        pctr["b"] = (i + 1) % 2
        return psb_t[i], psb_b[i]

    scratch = {}

    def wload(src2d, kc0, nkc, c0, ncols):
        i = pctr["r"]
        pctr["r"] = (i + 1) % NS
        slot2d = ring.t[:, i, 0:nkc * ncols]
        slot = slot2d.rearrange("p (k c) -> p k c", k=nkc)
        key = (src2d.tensor.name, kc0, nkc, c0, ncols)
        if key not in scratch:
            srcap = src2d.rearrange("(kc p) c -> p kc c", p=128)[:, kc0:kc0 + nkc, c0:c0 + ncols]
            fw.dma(slot, srcap, writes=[ring.b[i]], q="pool")
            d = nc.dram_tensor("ws%d" % len(scratch), [128, nkc * ncols], BF16, kind="Internal").ap()
            b = Buf("ws%d" % len(scratch))
            scratch[key] = (d, b)
            fw.dma(d, slot2d, reads=[ring.b[i]], writes=[b], q="sp")
        else:
            d, b = scratch[key]
            fw.dma(slot2d, d, reads=[b], writes=[ring.b[i]], q="sp")
        return slot, ring.b[i]

    def dump(name, ap, bufs):
        if name in dbg_out:
            fw.dma(dbg_out[name], ap, reads=bufs, q="sp")

    fw.dma(cst.t[:], cst_d, writes=cst.b)
    fw.dma(nwf.t[:], nwf_d, writes=nwf.b)
    fw.dma(fnw.t[:], fnw_d, writes=fnw.b)
    fw.dma(cw.t[:], cw_d, writes=cw.b)
    fw.dma(cb.t[:], cb_d, writes=cb.b)
    fw.dma(hp.t[:], hp_d, writes=hp.b)
    fw.dma(hmask.t[:], hmask_d, writes=hmask.b)
    fw.op("dve", lambda e: e.tensor_copy(identb.t[:], cst.t[:, 0, :]), reads=cst.b, writes=identb.b)
    for g in range(4):
        fw.op("dve", lambda e: e.tensor_copy(NM8.t[:, g, :], cst.t[:, 3, :]), reads=cst.b, writes=NM8.b)
    fw.op("dve", lambda e: e.memset(onesb.t[:], 1.0), writes=onesb.b)
    fw.op("dve", lambda e: e.memset(col1.t[:], 1.0), writes=col1.b)
    fw.op("dve", lambda e: e.memset(Hst.t[:], 0.0), writes=Hst.b)
    fw.op("dve", lambda e: e.memset(halo.t[:], 0.0), writes=halo.b)
    fw.op("dve", lambda e: e.memset(kT.t[:], 0.0), writes=kT.b)
    fw.op("dve", lambda e: e.memset(Vtok.t[:], 0.0), writes=Vtok.b)
    fw.op("dve", lambda e: e.memset(Vtok.t[:, 4:8, 0:64], 1.0), writes=Vtok.b)
    fw.op("dve", lambda e: e.memset(Vtok.t[:, 4:8, 1088:1152], 1.0), writes=Vtok.b)
    fw.op("dve", lambda e: e.tensor_copy(onesh.t[:], hmask.t[:, 0:1].to_broadcast([128, 64])), reads=hmask.b, writes=onesh.b)
    fw.op("act", lambda e: e.activation(Aneg.t[:], hp.t[:, 1, :], AF.Exp), reads=hp.b, writes=Aneg.b)
    fw.op("dve", lambda e: e.tensor_scalar(Aneg.t[:], Aneg.t[:], -1.0, None, ALU.mult), reads=Aneg.b, writes=Aneg.b)
    dgw = nc.dram_tensor("dgw", [128, 48, 128], BF16, kind="Internal").ap()
    with ExitStack() as phd:
        dg = sb(phd, "dg", [128, 48, 128], BF16)
        for m in range(12):
            for tap in range(4):
                fw.op("dve", lambda e: e.tensor_scalar(dg.t[:, m * 4 + tap, :], identb.t[:], cw.t[:, m, tap:tap + 1], None, ALU.mult),
                      reads=identb.b + cw.b, writes=dg.b)
        dgdep = fw.dma(dgw, dg.t[:], reads=dg.b)
        fw.barrier(("pe", "act", "dve", "sp"), dmas=True)

    for h in range(16):
        fw.dma(EB.t[:, h, :], btab_d[h], writes=EB.b, q="pool")

    identb_ap = identb.t[:]
    epsc = sb(st, "epsc", [128, 1], F32)
    fw.op("dve", lambda e: e.memset(epsc.t[:], EPS), writes=epsc.b)

    def rsqrt_eps(out_ap, in_ap, rb, wb):
        fw.op("act", lambda e: e.activation(out_ap, in_ap, AF.Ln, bias=epsc.t[:, 0:1]), reads=rb + epsc.b, writes=wb)
        fw.op("act", lambda e: e.activation(out_ap, out_ap, AF.Exp, scale=-0.5), reads=wb, writes=wb)

    def norm_prep():
        fw.op("dve", lambda e: e.memset(ssp.t[:], 0.0), writes=ssp.b)

    def norm_to_hT(ni, ph):
        xn, ss, rstd = xnp, ssp, rstdp
        junk = sb(ph, "njunk", [128, D], BF16)
        for tt in range(NTT):
            fw.op("act", lambda e: e.activation(junk.t[:], x_res.t[:, tt, :], AF.Square, scale=1.0 / 32.0,
                                                accum_out=ss.t[:, tt:tt + 1]),
                  reads=[x_res.b[tt]], writes=junk.b + ss.b)
        fw.op("act", lambda e: e.activation(rstd.t[:], ss.t[:], AF.Ln, bias=epsc.t[:, 0:1]), reads=ss.b + epsc.b, writes=rstd.b)
        fw.op("act", lambda e: e.activation(rstd.t[:], rstd.t[:], AF.Exp, scale=-0.5), reads=rstd.b, writes=rstd.b)
        for tt in range(NTT):
            k = tt % 2
            fw.op("act", lambda e: e.activation(xn.t[:, k, :], x_res.t[:, tt, :], AF.Copy, scale=rstd.t[:, tt:tt + 1]),
                  reads=[x_res.b[tt]] + rstd.b, writes=[xn.b[k]])
            pt, pb = psb()
            for j in range(8):
                fw.op("pe", lambda e: e.transpose(pt[:, j * 128:(j + 1) * 128], xn.t[:, k, j * 128:(j + 1) * 128], identb_ap),
                      reads=[xn.b[k]] + identb.b, writes=[pb])
            fw.op("dve", lambda e: e.tensor_tensor(hT.t[:, :, tt * 128:(tt + 1) * 128],
                                                   pt[:].rearrange("p (j c) -> p j c", j=8),
                                                   nwf.t[:, ni, :].unsqueeze(2).to_broadcast([128, 8, 128]), ALU.mult),
                  reads=[pb] + nwf.b, writes=hT.b)

    def proj_fm(src2d, c0, nchunks, nkc, rhs, rhs_bufs, evac, ncol=T, evac2=None, evac3=None):
        pend = []
        for u0 in range(0, nchunks, 4):
            nu = min(4, nchunks - u0)
            slot, sbuf_ = wload(src2d, 0, nkc, c0 + u0 * 128, nu * 128)
            for mo in range(nu):
                ps, pb = psf()
                for kc in range(nkc):
                    fw.op("pe", lambda e: e.matmul(ps[:, 0:ncol], slot[:, kc, mo * 128:(mo + 1) * 128], rhs(kc),
                                                   start=(kc == 0), stop=(kc == nkc - 1)),
                          reads=[sbuf_] + rhs_bufs, writes=[pb])
                evac(u0 + mo, ps, pb)
                pend.append(u0 + mo)
                if evac2 is not None and len(pend) >= 2:
                    evac2(pend[-2])
                if evac3 is not None and len(pend) >= 3:
                    evac3(pend[-3])
        if evac2 is not None and pend:
            evac2(pend[-1])
        if evac3 is not None:
            if len(pend) >= 2:
                evac3(pend[-2])
            if pend:
                evac3(pend[-1])

    def proj_tm(src2d, c0, ncols, nkc, lhs, lhs_bufs, evac):
        for u0 in range(0, ncols, 512):
            nc_ = min(512, ncols - u0)
            slot, sbuf_ = wload(src2d, 0, nkc, c0 + u0, nc_)
            for tt in range(NTT):
                ps, pb = psf()
                for kc in range(nkc):
                    fw.op("pe", lambda e: e.matmul(ps[:, 0:nc_], lhs(kc, tt), slot[:, kc, :],
                                                   start=(kc == 0), stop=(kc == nkc - 1)),
                          reads=[sbuf_] + lhs_bufs, writes=[pb])
                evac(u0 // 512, tt, ps, pb)

    def ffn(which, after=None):
        with ExitStack() as ph:
            actT = sb(ph, "actT", [128, NFC, T], BF16, 1)
            sg = sb(ph, "sg", [128, 2, T], F32, 2)
            norm_prep()
            wgu = w_gu[which]
            wd = w_dn[which]
            for f0 in range(0, NFC, 4):
                nf = min(4, NFC - f0)
                gs, gb = wload(wgu, 0, 8, f0 * 128, nf * 128)
                us, ub = wload(wgu, 0, 8, DFF + f0 * 128, nf * 128)
                for fo in range(nf):
                    f = f0 + fo
                    pg, pgb = psf()
                    pu, pub = psf()
                    for kc in range(8):
                        fw.op("pe", lambda e: e.matmul(pg[:, 0:T], gs[:, kc, fo * 128:(fo + 1) * 128], hT.t[:, kc, :],
                                                       start=(kc == 0), stop=(kc == 7)), reads=[gb] + hT.b, writes=[pgb])
                    for kc in range(8):
                        fw.op("pe", lambda e: e.matmul(pu[:, 0:T], us[:, kc, fo * 128:(fo + 1) * 128], hT.t[:, kc, :],
                                                       start=(kc == 0), stop=(kc == 7)), reads=[ub] + hT.b, writes=[pub])
                    k = f % 2
                    fw.op("act", lambda e: e.activation(sg.t[:, k, :], pg[:, 0:T], AF.Silu), reads=[pgb], writes=[sg.b[k]])
                    fw.op("dve", lambda e: e.tensor_tensor(actT.t[:, f, :], sg.t[:, k, :], pu[:, 0:T], ALU.mult),
                          reads=[sg.b[k], pub], writes=actT.b)
            groups = [(0, 8), (8, 8), (16, 6)]
            for dh in range(2):
                slots = [wload(wd, g0, gn, dh * 512, 512) for (g0, gn) in groups]
                for tt in range(NTT):
                    ps, pb = psf()
                    for gi, (g0, gn) in enumerate(groups):
                        sl, slb = slots[gi]
                        for fo in range(gn):
                            f = g0 + fo
                            fw.op("pe", lambda e: e.matmul(ps[:, 0:512], actT.t[:, f, tt * 128:(tt + 1) * 128], sl[:, fo, :],
                                                           start=(f == 0), stop=(f == NFC - 1)),
                                  reads=[slb] + actT.b, writes=[pb])
                    xs = x_res.t[:, tt, dh * 512:(dh + 1) * 512]
                    fw.op("dve", lambda e: e.scalar_tensor_tensor(xs, ps[:, 0:512], 0.5, xs, ALU.mult, ALU.add),
                          reads=[pb, x_res.b[tt]], writes=[x_res.b[tt]])
            dm = False
            if after is not None:
                dm = after(ph)
            fw.barrier(("pe", "act", "dve"), dmas=bool(dm))

    def mixer(full, need_kv, first_full, after=None):
        with ExitStack() as ph:
            xtok = sb(ph, "xtok", [128, NTT, D], BF16, 1)
            BT = sb(ph, "BT", [128, 2, T], BF16)
            CT = sb(ph, "CT", [128, 2, T], BF16)
            Btok = sb(ph, "Btok", [128, NTT, 256], BF16)
            dt = sb(ph, "dt", [128, NTT, 16], F32)
            if full:
                yssmT = sb(ph, "yssmT", [128, 8, T], BF16)
                yattT = sb(ph, "yattT", [128, 8, T], BF16)
            with ExitStack() as p1:
                pre = sb(p1, "pre", [128, 2, T + 4], BF16, 2)
                xfm = sb(p1, "xfm", [128, 2, T], BF16, 2)
                dgs = sb(p1, "dgs", [128, 48, 128], BF16)
                fw.barrier(("pool",))
                fw.dma(dgs.t[:], dgw, writes=dgs.b, q="pool")
                cps = {}

                def ev_xbc(m, ps, pb):
                    k = m % 2
                    fw.op("act", lambda e: e.copy(pre.t[:, k, 3:T + 3], ps[:, 0:T]), reads=[pb], writes=[pre.b[k]])
                    fw.op("dve", lambda e: e.tensor_copy(pre.t[:, k, 0:3], halo.t[:, m, :]), reads=halo.b, writes=[pre.b[k]])
                    fw.op("dve", lambda e: e.tensor_copy(halo.t[:, m, :], pre.t[:, k, T:T + 3]), reads=[pre.b[k]], writes=halo.b)

                def ev_xbc2(m):
                    k = m % 2
                    pc, pcb = psf()
                    for tap in range(4):
                        fw.op("pe", lambda e: e.matmul(pc[:, 0:T], dgs.t[:, m * 4 + tap, :], pre.t[:, k, tap:tap + T],
                                                       start=(tap == 0), stop=(tap == 3)), reads=dgs.b + [pre.b[k]], writes=[pcb])
                    if m < 8:
                        fw.op("act", lambda e: e.activation(xfm.t[:, k, :], pc[:, 0:T], AF.Silu, bias=cb.t[:, m:m + 1]), reads=[pcb] + cb.b, writes=[xfm.b[k]])
                    elif m < 10:
                        g = m - 8
                        fw.op("act", lambda e: e.activation(BT.t[:, g, :], pc[:, 0:T], AF.Silu, bias=cb.t[:, m:m + 1]), reads=[pcb] + cb.b, writes=BT.b)
                    else:
                        g = m - 10
                        fw.op("act", lambda e: e.activation(CT.t[:, g, :], pc[:, 0:T], AF.Silu, bias=cb.t[:, m:m + 1]), reads=[pcb] + cb.b, writes=CT.b)

                def ev_xbc3(m):
                    k = m % 2
                    if m < 8:
                        pt, ptb = psb()
                        for tt in range(NTT):
                            fw.op("pe", lambda e: e.transpose(pt[:, tt * 128:(tt + 1) * 128], xfm.t[:, k, tt * 128:(tt + 1) * 128], identb_ap),
                                  reads=[xfm.b[k]] + identb.b, writes=[ptb])
                        fw.op("dve", lambda e: e.tensor_copy(xtok.t[:, :, m * 128:(m + 1) * 128], pt[:, 0:512].rearrange("p (t c) -> p t c", t=NTT)),
                              reads=[ptb], writes=xtok.b)
                    elif m < 10:
                        g = m - 8
                        pt, ptb = psb()
                        for tt in range(NTT):
                            fw.op("pe", lambda e: e.transpose(pt[:, tt * 128:(tt + 1) * 128], BT.t[:, g, tt * 128:(tt + 1) * 128], identb_ap),
                                  reads=BT.b + identb.b, writes=[ptb])
                        fw.op("dve", lambda e: e.tensor_copy(Btok.t[:, :, g * 128:(g + 1) * 128], pt[:, 0:512].rearrange("p (t c) -> p t c", t=NTT)),
                              reads=[ptb], writes=Btok.b)

                def ev_dt(u, tt, ps, pb):
                    fw.op("dve", lambda e: e.tensor_tensor(dt.t[:, tt, :], ps[:, 0:16], hp.t[:, 0, :], ALU.add), reads=[pb] + hp.b, writes=dt.b)
                    fw.op("act", lambda e: e.activation(dt.t[:, tt, :], dt.t[:, tt, :], AF.Exp), reads=dt.b, writes=dt.b)
                    fw.op("act", lambda e: e.activation(dt.t[:, tt, :], dt.t[:, tt, :], AF.Ln, bias=col1.t[:, 0:1]), reads=dt.b + col1.b, writes=dt.b)

                proj_tm(w_in, C_DT, 16, 8, lambda kc, tt: hT.t[:, kc, tt * 128:(tt + 1) * 128], hT.b, ev_dt)
                proj_fm(w_in, C_X, 12, 8, lambda kc: hT.t[:, kc, :], hT.b, ev_xbc, evac2=ev_xbc2, evac3=ev_xbc3)

                fw.barrier()
            dump("xtok", xtok.t[:], xtok.b)
            dump("dt", dt.t[:], dt.b)

            def kv_unit(i):
                if i < 2:
                    slot, sbuf_ = wload(w_in, 0, 8, C_K + i * 512, 512)
                    for mo in range(4):
                        m = i * 4 + mo
                        ps, pb = psf()
                        for kc in range(8):
                            fw.op("pe", lambda e: e.matmul(ps[:, 0:T], slot[:, kc, mo * 128:(mo + 1) * 128], hT.t[:, kc, :],
                                                           start=(kc == 0), stop=(kc == 7)), reads=[sbuf_] + hT.b, writes=[pb])
                        fw.op("act", lambda e: e.copy(kT.t[:, m, T:2 * T], ps[:, 0:T]), reads=[pb], writes=kT.b)
                else:
                    u = i - 2
                    slot, sbuf_ = wload(w_in, 0, 8, C_V + u * 512, 512)
                    for tt in range(NTT):
                        ps, pb = psf()
                        for kc in range(8):
                            fw.op("pe", lambda e: e.matmul(ps[:, 0:512], hT.t[:, kc, tt * 128:(tt + 1) * 128], slot[:, kc, :],
                                                           start=(kc == 0), stop=(kc == 7)), reads=[sbuf_] + hT.b, writes=[pb])
                        fw.op("act", lambda e: e.copy(Vtok.t[:, 4 + tt, 64 + u * 512:64 + (u + 1) * 512], ps[:, 0:512]), reads=[pb], writes=Vtok.b)

            with ExitStack() as p2:
                NB = 2 if full else NTT
                a_t = sb(p2, "a_t", [128, NB, 16], F32, NB)
                wdt = sb(p2, "wdt", [128, NB, 16], F32, NB)
                decr = sb(p2, "decr", [128, NB, 16], F32, NB)
                xw = sb(p2, "xw", [128, NB, D], BF16, NB)
                if full:
                    sz = sb(p2, "sz", [128, 2, D], BF16, 2)
                    zslots = [wload(w_in, 0, 8, C_Z + u * 512, 512) for u in range(2)]
                    Y = sb(p2, "Y", [128, 2, 8, 128], F32, 2)
                    E = sb(p2, "E", [128, 2, 8, 128], BF16, 2)
                    Mm = sb(p2, "Mm", [128, 2, 8, 128], BF16, 2)
                    CBs = sb(p2, "CBs", [128, 2, 128], F32, 2)
                    t1 = sb(p2, "t1", [128, D], F32)
                    t2 = sb(p2, "t2", [128, 512], F32)
                    ynb = sb(p2, "ynb", [128, D], BF16)
                    ebias = sb(p2, "ebias", [128, 2, 16], F32, 2)
                    ecum = sb(p2, "ecum", [128, 2, 16], F32, 2)
                    gss = sb(p2, "gss", [128, 2], F32)
                    jk2 = sb(p2, "jk2", [128, 512], BF16)

                def ssd_pre(tt):
                    k = tt % NB
                    fw.op("dve", lambda e: e.tensor_tensor(a_t.t[:, k, :], dt.t[:, tt, :], Aneg.t[:], ALU.mult), reads=dt.b + Aneg.b, writes=[a_t.b[k]])
                    if full:
                        for u in range(2):
                            zs, zb = zslots[u]
                            pz, pzb = psf()
                            for kc in range(8):
                                fw.op("pe", lambda e: e.matmul(pz[:, 0:512], hT.t[:, kc, tt * 128:(tt + 1) * 128], zs[:, kc, :],
                                                               start=(kc == 0), stop=(kc == 7)), reads=[zb] + hT.b, writes=[pzb])
                            fw.op("act", lambda e: e.activation(sz.t[:, k, u * 512:(u + 1) * 512], pz[:, 0:512], AF.Silu), reads=[pzb], writes=[sz.b[k]])
                    pss, pssb = psf()
                    fw.op("pe", lambda e: e.matmul(pss[:, 0:16], cst.t[:, 2, :], a_t.t[:, k, :], start=True, stop=True), reads=cst.b + [a_t.b[k]], writes=[pssb])
                    fw.op("pe", lambda e: e.matmul(pss[:, 16:32], onesf.t[:], a_t.t[:, k, :], start=True, stop=True), reads=onesf.b + [a_t.b[k]], writes=[pssb])
                    if full:
                        fw.op("pe", lambda e: e.matmul(pss[:, 32:48], cst.t[:, 1, :], a_t.t[:, k, :], start=True, stop=True), reads=cst.b + [a_t.b[k]], writes=[pssb])
                    fw.op("act", lambda e: e.activation(wdt.t[:, k, :], pss[:, 0:16], AF.Exp), reads=[pssb], writes=[wdt.b[k]])
                    fw.op("dve", lambda e: e.tensor_tensor(wdt.t[:, k, :], wdt.t[:, k, :], dt.t[:, tt, :], ALU.mult), reads=[wdt.b[k]] + dt.b, writes=[wdt.b[k]])
                    fw.op("act", lambda e: e.activation(decr.t[:, k, :], pss[:, 16:32], AF.Exp), reads=[pssb], writes=[decr.b[k]])
                    if full:
                        fw.op("act", lambda e: e.activation(ebias.t[:, k, :], dt.t[:, tt, :], AF.Ln), reads=dt.b, writes=[ebias.b[k]])
                        fw.op("dve", lambda e: e.tensor_tensor(ebias.t[:, k, :], ebias.t[:, k, :], pss[:, 32:48], ALU.subtract), reads=[ebias.b[k], pssb], writes=[ebias.b[k]])
                        fw.op("act", lambda e: e.activation(ecum.t[:, k, :], pss[:, 32:48], AF.Exp), reads=[pssb], writes=[ecum.b[k]])
                    fw.op("dve", lambda e: e.tensor_tensor(xw.t[:, k, :].rearrange("p (h c) -> p h c", h=16),
                                                           xtok.t[:, tt, :].rearrange("p (h c) -> p h c", h=16),
                                                           wdt.t[:, k, :].unsqueeze(2).to_broadcast([128, 16, 64]), ALU.mult),
                          reads=xtok.b + [wdt.b[k]], writes=[xw.b[k]])

                def ssd_front(tt, g):
                    k = tt % 2
                    tsl = slice(tt * 128, (tt + 1) * 128)
                    fw.op("dve", lambda e: e.tensor_tensor(Y.t[:, g], cst.t[:, 1, :].unsqueeze(1).to_broadcast([128, 8, 128]),
                                                           a_t.t[:, k, g * 8:(g + 1) * 8].unsqueeze(2).to_broadcast([128, 8, 128]), ALU.mult),
                          reads=cst.b + [a_t.b[k]], writes=[Y.b[g]])
                    pc, pcb = psf()
                    fw.op("pe", lambda e: e.matmul(pc[:, 0:128], BT.t[:, g, tsl], CT.t[:, g, tsl], start=True, stop=True),
                          reads=BT.b + CT.b, writes=[pcb])
                    fw.op("act", lambda e: e.copy(CBs.t[:, g, :], pc[:, 0:128]), reads=[pcb], writes=[CBs.b[g]])

                def ssd_front_r(tt, g):
                    k = tt % 2
                    for q in range(2):
                        prt, prb = psf()
                        fw.op("pe", lambda e: e.matmul(prt[:, 0:512], onesf.t[:], Y.t[:, g, q * 4:(q + 1) * 4, :].rearrange("p h l -> p (h l)"),
                                                       start=True, stop=False), reads=onesf.b + [Y.b[g]], writes=[prb])
                        fw.op("pe", lambda e: e.matmul(prt[:, 0:512], identb_ap, NM8.t[:, 0:4, :].rearrange("p h l -> p (h l)"),
                                                       start=False, stop=True), reads=identb.b + NM8.b, writes=[prb])
                        for hh in range(4):
                            h = g * 8 + q * 4 + hh
                            fw.op("act", lambda e: e.activation(E.t[:, g, q * 4 + hh, :], prt[:, hh * 128:(hh + 1) * 128], AF.Exp,
                                                                bias=ebias.t[:, k, h:h + 1]), reads=[prb, ebias.b[k]], writes=[E.b[g]])

                def ssd_front_b(tt, g):
                    fw.op("dve", lambda e: e.tensor_tensor(Mm.t[:, g], E.t[:, g],
                                                           CBs.t[:, g, :].unsqueeze(1).to_broadcast([128, 8, 128]), ALU.mult),
                          reads=[E.b[g], CBs.b[g]], writes=[Mm.b[g]])

                def ssd_back(tt, g):
                    k = tt % 2
                    tsl = slice(tt * 128, (tt + 1) * 128)
                    gsl = slice(g * 512, (g + 1) * 512)
                    if g == 0:
                        fw.op("act", lambda e: e.copy(Hbf.t[:], Hst.t[:]), reads=Hst.b, writes=Hbf.b)
                    pyt, pyb = psfix(4)
                    for hh in range(8):
                        h = g * 8 + hh
                        fw.op("pe", lambda e: e.matmul(pyt[:, hh * 64:(hh + 1) * 64], Mm.t[:, g, hh, :], xtok.t[:, tt, h * 64:(h + 1) * 64],
                                                       start=True, stop=True), reads=[Mm.b[g]] + xtok.b, writes=[pyb])
                    po, pob = psfix(5)
                    fw.op("pe", lambda e: e.matmul(po[:, 0:512], CT.t[:, g, tsl], Hbf.t[:, gsl], start=True, stop=True),
                          reads=CT.b + Hbf.b, writes=[pob])

                def ssd_back_rest(tt, g):
                    k = tt % 2
                    tsl = slice(tt * 128, (tt + 1) * 128)
                    gsl = slice(g * 512, (g + 1) * 512)
                    pyt, pyb = psfix(4)
                    po, pob = psfix(5)
                    fw.op("dve", lambda e: e.tensor_tensor(t1.t[:, gsl].rearrange("p (h c) -> p h c", h=8),
                                                           po[:, 0:512].rearrange("p (h c) -> p h c", h=8),
                                                           ecum.t[:, k, g * 8:(g + 1) * 8].unsqueeze(2).to_broadcast([128, 8, 64]), ALU.mult),
                          reads=[pob, ecum.b[k]], writes=t1.b)
                    fw.op("dve", lambda e: e.tensor_tensor(t2.t[:].rearrange("p (h c) -> p h c", h=8),
                                                           xtok.t[:, tt, gsl].rearrange("p (h c) -> p h c", h=8),
                                                           hp.t[:, 2, g * 8:(g + 1) * 8].unsqueeze(2).to_broadcast([128, 8, 64]), ALU.mult),
                          reads=xtok.b + hp.b, writes=t2.b)
                    fw.op("dve", lambda e: e.tensor_tensor(t1.t[:, gsl], t1.t[:, gsl], t2.t[:], ALU.add), reads=t1.b + t2.b, writes=t1.b)
                    fw.op("dve", lambda e: e.tensor_tensor(t1.t[:, gsl], t1.t[:, gsl], pyt[:, 0:512], ALU.add), reads=t1.b + [pyb], writes=t1.b)
                    fw.op("dve", lambda e: e.tensor_tensor(t1.t[:, gsl], t1.t[:, gsl], sz.t[:, k, gsl], ALU.mult), reads=t1.b + [sz.b[k]], writes=t1.b)
                    fw.op("dve", lambda e: e.memset(gss.t[:, g:g + 1], 0.0), writes=gss.b)
                    fw.op("act", lambda e: e.activation(jk2.t[:], t1.t[:, gsl], AF.Square, scale=float(512 ** -0.5),
                                                        accum_out=gss.t[:, g:g + 1]), reads=t1.b, writes=jk2.b + gss.b)
                    rsqrt_eps(gss.t[:, g:g + 1], gss.t[:, g:g + 1], gss.b, gss.b)
                    fw.op("dve", lambda e: e.tensor_scalar(ynb.t[:, gsl], t1.t[:, gsl], gss.t[:, g:g + 1], None, ALU.mult),
                          reads=t1.b + gss.b, writes=ynb.b)
                    if g == 1:
                        if tt == 0:
                            dump("yssd", t1.t[:], t1.b)
                        pt, ptb = psb()
                        for j in range(8):
                            fw.op("pe", lambda e: e.transpose(pt[:, j * 128:(j + 1) * 128], ynb.t[:, j * 128:(j + 1) * 128], identb_ap),
                                  reads=ynb.b + identb.b, writes=[ptb])
                        fw.op("dve", lambda e: e.tensor_tensor(yssmT.t[:, :, tsl], pt[:].rearrange("p (j c) -> p j c", j=8),
                                                               nwf.t[:, 2, :].unsqueeze(2).to_broadcast([128, 8, 128]), ALU.mult),
                              reads=[ptb] + nwf.b, writes=yssmT.b)

                def ssd_post(tt):
                    k = tt % NB
                    for g in range(2):
                        pst_, pstb = psf()
                        fw.op("pe", lambda e: e.matmul(pst_[:, 0:512], Btok.t[:, tt, g * 128:(g + 1) * 128], xw.t[:, k, g * 512:(g + 1) * 512],
                                                       start=True, stop=True), reads=Btok.b + [xw.b[k]], writes=[pstb])
                        hs = Hst.t[:, g * 512:(g + 1) * 512]
                        fw.op("dve", lambda e: e.tensor_tensor(hs.rearrange("p (h c) -> p h c", h=8), hs.rearrange("p (h c) -> p h c", h=8),
                                                               decr.t[:, k, g * 8:(g + 1) * 8].unsqueeze(2).to_broadcast([128, 8, 64]), ALU.mult),
                              reads=Hst.b + [decr.b[k]], writes=Hst.b)
                        fw.op("dve", lambda e: e.tensor_tensor(hs, hs, pst_[:, 0:512], ALU.add), reads=Hst.b + [pstb], writes=Hst.b)

                if not full:
                    for tt in range(NTT):
                        ssd_pre(tt)
                    for tt in range(NTT):
                        ssd_post(tt)
                else:
                    steps = [(tt, g) for tt in range(NTT) for g in range(2)]
                    prev = None
                    for (tt, g) in steps:
                        if g == 0:
                            ssd_pre(tt)
                        ssd_front(tt, g)
                        if prev is not None:
                            ssd_back(*prev)
                        ssd_front_r(tt, g)
                        if prev is not None:
                            ssd_back_rest(*prev)
                            if prev[1] == 1:
                                ssd_post(prev[0])
                        ssd_front_b(tt, g)
                        prev = (tt, g)
                        if g == 1:
                            kv_unit(tt)
                    ssd_back(*prev)
                    ssd_back_rest(*prev)
                    ssd_post(prev[0])
                fw.barrier()
            dump("Hst", Hst.t[:], Hst.b)

            if need_kv:
                with ExitStack() as p3:
                    if not full:
                        for i in range(4):
                            kv_unit(i)
                    if full:
                        qT0 = sb(p3, "qT0", [128, 8, T], BF16)
                        qT1 = sb(p3, "qT1", [128, 8, T], BF16)
                        qTs = [qT0, qT1]
                        PT = sb(p3, "PT", [128, 4, 512], BF16, 4)
                        rden = sb(p3, "rden", [128, 512], F32)
                        dtmp = sb(p3, "dtmp", [128, 512], F32)
                        fw.op("dve", lambda e: e.memset(qT0.t[64:128, :, :], 0.0), writes=qT0.b)
                        fw.op("dve", lambda e: e.memset(qT1.t[0:64, :, :], 0.0), writes=qT1.b)

                        def ev_q(m, ps, pb):
                            fw.op("act", lambda e: e.activation(qT0.t[0:64, m, :], ps[0:64, 0:T], AF.Copy, scale=0.125), reads=[pb], writes=qT0.b)
                            fw.op("act", lambda e: e.activation(qT1.t[64:128, m, :], ps[64:128, 0:T], AF.Copy, scale=0.125), reads=[pb], writes=qT1.b)

                        proj_fm(w_in, C_Q, 8, 8, lambda kc: hT.t[:, kc, :], hT.b, ev_q)
                        vaug = sb(p3, "vaug", [128, 2, 8, 128], BF16, 2)

                        def prep_v(h):
                            hb = h % 2
                            if hb == 0:
                                vc, oc, osrc = 0, 64, 1088
                            else:
                                vc, oc, osrc = 64, 0, 0
                            fw.op("dve", lambda e: e.tensor_copy(vaug.t[:, hb, :, vc:vc + 64], Vtok.t[:, :, 64 + 64 * h:128 + 64 * h]),
                                  reads=Vtok.b, writes=[vaug.b[hb]])
                            fw.op("dve", lambda e: e.tensor_copy(vaug.t[:, hb, :, oc:oc + 64], Vtok.t[:, :, osrc:osrc + 64]),
                                  reads=Vtok.b, writes=[vaug.b[hb]])

                        def stageA(hpair, h2, j, k):
                            h = hpair * 2 + h2
                            qt_lo = max(0, j - 4)
                            qt_hi = min(3, j)
                            q0 = qt_lo * 128
                            nq = (qt_hi - qt_lo + 1) * 128
                            rel0 = (qt_lo * 2 + 8 - 2 * j) * 64
                            if j == 0 and h == 0:
                                prep_v(0)
                            if j == 3 and h < 15:
                                prep_v(h + 1)
                            sp_, spb = psf()
                            qh = qTs[h2]
                            fw.op("pe", lambda e: e.matmul(sp_[:, 0:nq], kT.t[:, hpair, j * 128:(j + 1) * 128], qh.t[:, hpair, q0:q0 + nq],
                                                           start=True, stop=False), reads=kT.b + qh.b, writes=[spb])
                            fw.op("pe", lambda e: e.matmul(sp_[:, 0:nq], identb_ap, EB.t[:, h, rel0:rel0 + nq],
                                                           start=False, stop=True), reads=identb.b + EB.b, writes=[spb])
                            fw.op("act", lambda e: e.activation(PT.t[:, k, 0:nq], sp_[:, 0:nq], AF.Exp), reads=[spb], writes=[PT.b[k]])

                        def stageB(hpair, h2, j, k):
                            h = hpair * 2 + h2
                            qt_lo = max(0, j - 4)
                            qt_hi = min(3, j)
                            q0 = qt_lo * 128
                            nq = (qt_hi - qt_lo + 1) * 128
                            pX, pXb = psfix(4 + h2)
                            segs = [(q0, nq)]
                            if 1 <= j <= 3:
                                segs = [(q0, nq - 128), (q0 + nq - 128, 128)]
                            for (sq0, snq) in segs:
                                fw.op("pe", lambda e: e.matmul(pX[:, sq0:sq0 + snq], vaug.t[:, h2, j, :], PT.t[:, k, sq0 - q0:sq0 - q0 + snq],
                                                               start=(j == 0), stop=(j == 7)), reads=[vaug.b[h2], PT.b[k]], writes=[pXb])
                            if h2 == 1 and j == 7:
                                pA, pAb = psfix(4)
                                pB, pBb = psfix(5)
                                fw.op("act", lambda e: e.copy(dtmp.t[0:64, :], pA[64:128, 0:512]), reads=[pAb], writes=dtmp.b)
                                fw.op("act", lambda e: e.copy(dtmp.t[64:128, :], pB[0:64, 0:512]), reads=[pBb], writes=dtmp.b)
                                fw.op("dve", lambda e: e.reciprocal(rden.t[:], dtmp.t[:]), reads=dtmp.b, writes=rden.b)
                                fw.op("dve", lambda e: e.tensor_tensor(yattT.t[0:64, hpair, :], pA[0:64, 0:512], rden.t[0:64, :], ALU.mult),
                                      reads=[pAb] + rden.b, writes=yattT.b)
                                fw.op("dve", lambda e: e.tensor_tensor(yattT.t[64:128, hpair, :], pB[64:128, 0:512], rden.t[64:128, :], ALU.mult),
                                      reads=[pBb] + rden.b, writes=yattT.b)

                        its = [(hpair, h2, j) for hpair in range(8) for h2 in range(2) for j in range(8)]
                        pend = []
                        for n_, (hpair, h2, j) in enumerate(its):
                            k = n_ % 4
                            stageA(hpair, h2, j, k)
                            pend.append((hpair, h2, j, k))
                            if len(pend) > 3:
                                stageB(*pend.pop(0))
                        while pend:
                            stageB(*pend.pop(0))
                    fw.barrier()
                fw.op("act", lambda e: e.copy(kT.t[:, :, 0:T], kT.t[:, :, T:2 * T]), reads=kT.b, writes=kT.b)
                fw.op("dve", lambda e: e.tensor_copy(Vtok.t[:, 0:4, :], Vtok.t[:, 4:8, :]), reads=Vtok.b, writes=Vtok.b)

            if full:
                dump("yssmT", yssmT.t[:], yssmT.b)
                dump("yattT", yattT.t[:], yattT.b)
                with ExitStack() as p4:
                    norm_prep()
                    mT = sb(p4, "mT", [128, 8, T], BF16)
                    s3 = sb(p4, "s3", [128, 2, T], F32, 2)
                    s4 = sb(p4, "s4", [128, 2, T], F32, 2)
                    for u0 in (0, 4):
                        sl1, b1 = wload(w_bs, 0, 8, u0 * 128, 512)
                        sl2, b2 = wload(w_ba, 0, 8, u0 * 128, 512)
                        sl3, b3 = wload(w_in, 0, 8, C_GS + u0 * 128, 512)
                        sl4, b4 = wload(w_in, 0, 8, C_GA + u0 * 128, 512)
                        for mo in range(4):
                            m = u0 + mo
                            k = m % 2
                            msl = slice(mo * 128, (mo + 1) * 128)
                            p1_, p1b = psf()
                            p2_, p2b = psf()
                            p3_, p3b = psf()
                            p4_, p4b = psf()
                            for kc in range(8):
                                fw.op("pe", lambda e: e.matmul(p3_[:, 0:T], sl3[:, kc, msl], hT.t[:, kc, :], start=(kc == 0), stop=(kc == 7)),
                                      reads=[b3] + hT.b, writes=[p3b])
                            for kc in range(8):
                                fw.op("pe", lambda e: e.matmul(p4_[:, 0:T], sl4[:, kc, msl], hT.t[:, kc, :], start=(kc == 0), stop=(kc == 7)),
                                      reads=[b4] + hT.b, writes=[p4b])
                            for kc in range(8):
                                fw.op("pe", lambda e: e.matmul(p1_[:, 0:T], sl1[:, kc, msl], yssmT.t[:, kc, :], start=(kc == 0), stop=(kc == 7)),
                                      reads=[b1] + yssmT.b, writes=[p1b])
                            for kc in range(8):
                                fw.op("pe", lambda e: e.matmul(p2_[:, 0:T], sl2[:, kc, msl], yattT.t[:, kc, :], start=(kc == 0), stop=(kc == 7)),
                                      reads=[b2] + yattT.b, writes=[p2b])
                            fw.op("act", lambda e: e.activation(s3.t[:, k, :], p3_[:, 0:T], AF.Sigmoid), reads=[p3b], writes=[s3.b[k]])
                            fw.op("act", lambda e: e.activation(s4.t[:, k, :], p4_[:, 0:T], AF.Sigmoid), reads=[p4b], writes=[s4.b[k]])
                            fw.op("dve", lambda e: e.tensor_tensor(s3.t[:, k, :], s3.t[:, k, :], p1_[:, 0:T], ALU.mult), reads=[s3.b[k], p1b], writes=[s3.b[k]])
                            fw.op("dve", lambda e: e.tensor_tensor(s4.t[:, k, :], s4.t[:, k, :], p2_[:, 0:T], ALU.mult), reads=[s4.b[k], p2b], writes=[s4.b[k]])
                            fw.op("dve", lambda e: e.tensor_tensor(mT.t[:, m, :], s3.t[:, k, :], s4.t[:, k, :], ALU.add), reads=[s3.b[k], s4.b[k]], writes=mT.b)

                    def ev_out(u, tt, ps, pb):
                        xs = x_res.t[:, tt, u * 512:(u + 1) * 512]
                        fw.op("dve", lambda e: e.tensor_tensor(xs, xs, ps[:, 0:512], ALU.add), reads=[pb, x_res.b[tt]], writes=[x_res.b[tt]])

                    proj_tm(w_out, 0, 1024, 8, lambda kc, tt: mT.t[:, kc, tt * 128:(tt + 1) * 128], mT.b, ev_out)
                    if after is not None:
                        after(p4)
                    fw.barrier()
            if not full:
                norm_prep()
                if after is not None:
                    after(ph)
                fw.barrier()

    onesf = sb(st, "onesf", [128, 128], F32)
    fw.op("dve", lambda e: e.memset(onesf.t[:], 1.0), writes=onesf.b)

    def tile_src(ti):
        full = ti >= n_state
        src = x_main if full else x_prev
        tok0 = ((ti - n_state) if full else (NTOK // T - n_state + ti)) * T
        return src, tok0

    def load_x(ti):
        if ti >= n_state + n_full:
            return
        src, tok0 = tile_src(ti)
        for tt in range(NTT):
            fw.dma(x_res.t[:, tt, :], src[tok0 + tt * 128: tok0 + (tt + 1) * 128, :], writes=[x_res.b[tt]], q="pool")

    def final_out(ph, tok0):
        yo = sb(ph, "yo", [128, 2, D], F32, 2)
        ss = ssp
        for tt in range(NTT):
            k = tt % 2
            fw.op("dve", lambda e: e.memset(ss.t[:, tt:tt + 1], 0.0), writes=ss.b)
            fw.op("act", lambda e: e.activation(xnp.t[:, k, :], x_res.t[:, tt, :], AF.Square, scale=1.0 / 32.0, accum_out=ss.t[:, tt:tt + 1]),
                  reads=[x_res.b[tt]], writes=[xnp.b[k]] + ss.b)
            rsqrt_eps(ss.t[:, tt:tt + 1], ss.t[:, tt:tt + 1], ss.b, ss.b)
            fw.op("dve", lambda e: e.scalar_tensor_tensor(yo.t[:, k, :], x_res.t[:, tt, :], ss.t[:, tt:tt + 1], fnw.t[:], ALU.mult, ALU.mult),
                  reads=[x_res.b[tt]] + ss.b + fnw.b, writes=[yo.b[k]])
            fw.dma(y_out[tok0 + tt * 128: tok0 + (tt + 1) * 128, :], yo.t[:, k, :], reads=[yo.b[k]], q="pool")

    load_x(0)
    norm_prep()
    with ExitStack() as ph0:
        norm_to_hT(0, ph0)
        fw.barrier()
    for ti in range(n_state + n_full):
        full = ti >= n_state
        src, tok0 = tile_src(ti)
        last_state = (ti == n_state - 1)

        def after_ffn1(ph, full=full, ti=ti):
            norm_to_hT(1, ph)
            if not full:
                load_x(ti + 1)
            return False

        ffn(0, after=after_ffn1)
        if full and ti == n_state:
            dump("x1", x_res.t[:], x_res.b)

        def after_mixer(ph, full=full, ti=ti, last_state=last_state):
            if last_state:
                hm = hmask.t[:, 0:1]
                fw.op("dve", lambda e: e.tensor_scalar(Hst.t[:], Hst.t[:], hm, None, ALU.mult), reads=Hst.b + hmask.b, writes=Hst.b)
                fw.op("dve", lambda e: e.tensor_scalar(halo.t[:], halo.t[:], hm, None, ALU.mult), reads=halo.b + hmask.b, writes=halo.b)
                fw.op("dve", lambda e: e.tensor_scalar(Vtok.t[:, 0:4, :], Vtok.t[:, 0:4, :], hm, None, ALU.mult), reads=Vtok.b + hmask.b, writes=Vtok.b)
            if full:
                if ti == n_state:
                    dump("x2", x_res.t[:], x_res.b)
                norm_to_hT(3, ph)
            else:
                norm_to_hT(0, ph)

        mixer(full, full or last_state, ti == n_state, after=after_mixer)
        if full:
            def after_ffn2(ph, tok0=tok0, ti=ti):
                final_out(ph, tok0)
                load_x(ti + 1)
                if ti + 1 < n_state + n_full:
                    norm_prep()
                    norm_to_hT(0, ph)
                return True

            ffn(1, after=after_ffn2)
    fw.barrier(("sp", "act", "pool"), dmas=True)
    st.close()
    return nc


def _consts():
    t = np.arange(128)
    ident = np.eye(128, dtype=np.float32)
    U = (t[:, None] <= t[None, :]).astype(np.float32)
    Ls = (t[:, None] > t[None, :]).astype(np.float32)
    nm = np.where(t[:, None] > t[None, :], NEG, 0.0).astype(np.float32)
    return np.ascontiguousarray(np.stack([ident, U, Ls, nm], axis=1))


def _prep_shared(inp):
    f = lambda a: np.ascontiguousarray(np.asarray(a, dtype=np.float32))
    fm = lambda w: np.asarray(w, dtype=np.float32).reshape(8, 128).T
    sh = {}
    sh["w_gu1"] = f(inp["ffn1_w_gu"][0])
    sh["w_gu2"] = f(inp["ffn2_w_gu"][0])
    sh["w_dn1"] = f(inp["ffn1_w_down"][0])
    sh["w_dn2"] = f(inp["ffn2_w_down"][0])
    sh["w_in"] = f(inp["w_in"][0])
    sh["w_bs"] = f(inp["w_branch_ssm"][0])
    sh["w_ba"] = f(inp["w_branch_attn"][0])
    sh["w_out"] = f(inp["w_out"][0])
    sh["nwf"] = f(np.stack([fm(inp["ffn1_norm_w"][0]), fm(inp["mix_norm_w"][0]), fm(inp["ssm_norm_w"][0]),
                            fm(inp["ffn2_norm_w"][0])], axis=1))
    sh["fnw"] = f(np.broadcast_to(np.asarray(inp["final_norm_w"], np.float32)[None, :], (128, D)))
    cwv = np.asarray(inp["conv_w"][0], np.float32)
    sh["cw"] = f(cwv.reshape(4, 12, 128).transpose(2, 1, 0))
    sh["cb"] = f(np.asarray(inp["conv_b"][0], np.float32).reshape(12, 128).T)
    hpv = np.stack([np.asarray(inp["dt_bias"][0], np.float32), np.asarray(inp["A_log"][0], np.float32),
                    np.asarray(inp["D_skip"][0], np.float32)], axis=0)
    sh["hp"] = f(np.broadcast_to(hpv[None], (128, 3, 16)))
    rb = np.asarray(inp["rel_bias"][0], np.float32)
    r = np.arange(128)[:, None]
    l = np.arange(640)[None, :]
    idx = np.clip(l - r, -128, 128) + 128
    kc = r // 64
    qc = l // 64
    valid = (qc >= kc) & (qc <= kc + 8)
    tab = rb[:, idx]
    tab = np.where(valid[None], tab, np.float32(NEG))
    sh["btab"] = f(tab)
    sh["cst"] = _consts()
    return sh


_NC_CACHE = {}


def kernel(**inputs):
    x = np.asarray(inputs["x"], dtype=np.float32)
    sh = _prep_shared(inputs)
    in_maps = []
    for c in range(8):
        b, h = c // 2, c % 2
        m = dict(sh)
        m["x_main"] = np.ascontiguousarray(x[b, h * NTOK:(h + 1) * NTOK])
        m["x_prev"] = np.ascontiguousarray(x[b, 0:NTOK])
        m["hmask"] = np.full((128, 1), float(h), dtype=np.float32)
        in_maps.append(m)
    if "nc" not in _NC_CACHE:
        _NC_CACHE["nc"] = build()
    res = run_bass_kernel_spmd(_NC_CACHE["nc"], in_maps, core_ids=list(range(8)))
    out = np.empty((4, 2 * NTOK, D), dtype=np.float32)
    for c in range(8):
        b, h = c // 2, c % 2
        out[b, h * NTOK:(h + 1) * NTOK] = np.asarray(res.results[c]["y"], dtype=np.float32)
    return out
```

```python
import numpy as np
from contextlib import ExitStack
import concourse.bass as bass
import concourse.mybir as mybir
from concourse.bass_utils import run_bass_kernel_spmd

F32 = mybir.dt.float32
BF16 = mybir.dt.bfloat16
AF = mybir.ActivationFunctionType
ALU = mybir.AluOpType

D = 1024
DFF = 2816
NFC = 22
T = 512
NTT = 4
NTOK = 4096
INP = 7696
EPS = 1e-6
NEG = -30000.0
C_Z, C_X, C_B, C_C, C_DT, C_Q, C_K, C_V, C_GS, C_GA = 0, 1024, 2048, 2304, 2560, 2576, 3600, 4624, 5648, 6672
NS = 5


class Buf:
    __slots__ = ("name", "lw", "rd")

    def __init__(self, name):
        self.name = name
        self.lw = None
        self.rd = {}


class Tn:
    def __init__(self, t, bufs):
        self.t = t
        self.b = bufs


class FW:
    def __init__(self, nc, stack, n_dma_sems=32):
        self.nc = nc
        self.engs = {"pe": nc.tensor, "act": nc.scalar, "dve": nc.vector, "pool": nc.gpsimd, "sp": nc.sync}
        self.sem = {}
        self.cnt = {}
        for k in self.engs:
            self.sem[k] = stack.enter_context(nc.semaphore("tick_" + k))
            self.cnt[k] = 0
        self.dsem = [stack.enter_context(nc.semaphore("dma_%d" % i)) for i in range(n_dma_sems)]
        self.dcnt = [0] * n_dma_sems
        self.dnext2 = {"sp": 0, "pool": 0}
        self.seen = {k: {} for k in self.engs}
        self.ninst = 0

    def _wait(self, eng, dep):
        kind, idx, val = dep
        key = (kind, idx)
        if kind == "e" and idx == eng and eng == "pe":
            return
        if self.seen[eng].get(key, 0) >= val:
            return
        s = self.sem[idx] if kind == "e" else self.dsem[idx]
        self.engs[eng].wait_ge(s, val)
        self.seen[eng][key] = val

    def _deps(self, eng, reads, writes):
        for b in reads:
            if b.lw is not None:
                self._wait(eng, b.lw)
        for b in writes:
            if b.lw is not None:
                self._wait(eng, b.lw)
            for d in b.rd.values():
                self._wait(eng, d)

    def _mark(self, dep, reads, writes):
        for b in writes:
            b.lw = dep
            b.rd = {}
        for b in reads:
            if b in writes:
                continue
            b.rd[(dep[0], dep[1])] = dep

    def op(self, eng, fn, reads=(), writes=()):
        self._deps(eng, reads, writes)
        ins = fn(self.engs[eng])
        self.cnt[eng] += 1
        ins.then_inc(self.sem[eng], 1)
        dep = ("e", eng, self.cnt[eng])
        self._mark(dep, reads, writes)
        self.ninst += 1
        return dep

    def dma(self, out, in_, reads=(), writes=(), q="sp"):
        half = len(self.dsem) // 2
        base = 0 if q == "sp" else half
        i = base + self.dnext2[q]
        self.dnext2[q] = (self.dnext2[q] + 1) % half
        if self.dcnt[i] > 0:
            self._wait(q, ("d", i, self.dcnt[i]))
        self._deps(q, reads, writes)
        ins = self.engs[q].dma_start(out=out, in_=in_)
        self.dcnt[i] += 16
        ins.then_inc(self.dsem[i], 16)
        dep = ("d", i, self.dcnt[i])
        self._mark(dep, reads, writes)
        self.ninst += 1
        return dep

    def barrier(self, engs=("pe", "act", "dve"), dmas=False):
        for e in engs:
            for f in self.engs:
                if self.cnt[f] > 0 and not (f == e and e == "pe"):
                    self._wait(e, ("e", f, self.cnt[f]))
            if dmas:
                for i in range(len(self.dsem)):
                    if self.dcnt[i] > 0:
                        self._wait(e, ("d", i, self.dcnt[i]))


def build(n_state=8, n_full=8, dbg=()):
    nc = bass.Bass("TRN2", target_bir_lowering=False)
    di = lambda n, s, dt=F32: nc.dram_tensor(n, s, dt, kind="ExternalInput").ap()
    x_prev = di("x_prev", [NTOK, D])
    x_main = di("x_main", [NTOK, D])
    w_gu = [di("w_gu1", [D, 2 * DFF]), di("w_gu2", [D, 2 * DFF])]
    w_dn = [di("w_dn1", [DFF, D]), di("w_dn2", [DFF, D])]
    w_in = di("w_in", [D, INP])
    w_bs = di("w_bs", [D, D])
    w_ba = di("w_ba", [D, D])
    w_out = di("w_out", [D, D])
    nwf_d = di("nwf", [128, 4, 8])
    fnw_d = di("fnw", [128, D])
    cw_d = di("cw", [128, 12, 4])
    cb_d = di("cb", [128, 12])
    hp_d = di("hp", [128, 3, 16])
    btab_d = di("btab", [16, 128, 640])
    hmask_d = di("hmask", [128, 1])
    cst_d = di("cst", [128, 4, 128])
    y_out = nc.dram_tensor("y", [NTOK, D], F32, kind="ExternalOutput").ap()
    dbg_out = {}
    for name, shape, dt in dbg:
        dbg_out[name] = nc.dram_tensor("dbg_" + name, list(shape), dt, kind="ExternalOutput").ap()

    st = ExitStack()
    fw = FW(nc, st)

    uid = [0]

    def sb(stack, name, shape, dt, nb=1):
        uid[0] += 1
        t = stack.enter_context(nc.sbuf_tensor("%s_%d" % (name, uid[0]), list(shape), dt))
        return Tn(t, [Buf("%s_%d" % (name, i)) for i in range(nb)])

    x_res = sb(st, "x_res", [128, NTT, D], F32, NTT)
    hT = sb(st, "hT", [128, 8, T], BF16, 1)
    ring = sb(st, "ring", [128, NS, 4096], BF16, NS)
    Hst = sb(st, "Hst", [128, 1024], F32)
    Hbf = sb(st, "Hbf", [128, 1024], BF16)
    EB = sb(st, "EB", [128, 16, 640], BF16)
    kT = sb(st, "kT", [128, 8, 2 * T], BF16)
    Vtok = sb(st, "Vtok", [128, 8, 1152], BF16)
    halo = sb(st, "halo", [128, 12, 3], BF16)
    cst = sb(st, "cst", [128, 4, 128], F32)
    identb = sb(st, "identb", [128, 128], BF16)
    NM8 = sb(st, "NM8", [128, 4, 128], BF16)
    nwf = sb(st, "nwf", [128, 4, 8], F32)
    fnw = sb(st, "fnw", [128, D], F32)
    cw = sb(st, "cw", [128, 12, 4], F32)
    cb = sb(st, "cb", [128, 12], F32)
    hp = sb(st, "hp", [128, 3, 16], F32)
    Aneg = sb(st, "Aneg", [128, 16], F32)
    hmask = sb(st, "hmask", [128, 1], F32)
    onesb = sb(st, "onesb", [128, 64], BF16)
    onesh = sb(st, "onesh", [128, 64], BF16)
    col1 = sb(st, "col1", [128, 1], F32)
    xnp = sb(st, "xnp", [128, 2, D], BF16, 2)
    ssp = sb(st, "ssp", [128, NTT], F32)
    rstdp = sb(st, "rstdp", [128, NTT], F32)

    psf_t = [st.enter_context(nc.psum_tensor("psf%d" % i, [128, 512], F32)) for i in range(6)]
    psf_b = [Buf("psf%d" % i) for i in range(6)]
    psb_t = [st.enter_context(nc.psum_tensor("psb%d" % i, [128, 1024], BF16)) for i in range(2)]
    psb_b = [Buf("psb%d" % i) for i in range(2)]
    pctr = {"f": 0, "b": 0, "r": 0}

    def psf():
        i = pctr["f"]
        pctr["f"] = (i + 1) % 4
        return psf_t[i], psf_b[i]

    def psfix(i):
        return psf_t[i], psf_b[i]

    def psb():
        i = pctr["b"]
        pctr["b"] = (i + 1) % 2
        return psb_t[i], psb_b[i]

    scratch = {}

    def wload(src2d, kc0, nkc, c0, ncols):
        i = pctr["r"]
        pctr["r"] = (i + 1) % NS
        slot2d = ring.t[:, i, 0:nkc * ncols]
        slot = slot2d.rearrange("p (k c) -> p k c", k=nkc)
        key = (src2d.tensor.name, kc0, nkc, c0, ncols)
        if key not in scratch:
            srcap = src2d.rearrange("(kc p) c -> p kc c", p=128)[:, kc0:kc0 + nkc, c0:c0 + ncols]
            fw.dma(slot, srcap, writes=[ring.b[i]], q="pool")
            d = nc.dram_tensor("ws%d" % len(scratch), [128, nkc * ncols], BF16, kind="Internal").ap()
            b = Buf("ws%d" % len(scratch))
            scratch[key] = (d, b)
            fw.dma(d, slot2d, reads=[ring.b[i]], writes=[b], q="sp")
        else:
            d, b = scratch[key]
            fw.dma(slot2d, d, reads=[b], writes=[ring.b[i]], q="sp")
        return slot, ring.b[i]

    def dump(name, ap, bufs):
        if name in dbg_out:
            fw.dma(dbg_out[name], ap, reads=bufs, q="sp")

    fw.dma(cst.t[:], cst_d, writes=cst.b)
    fw.dma(nwf.t[:], nwf_d, writes=nwf.b)
    fw.dma(fnw.t[:], fnw_d, writes=fnw.b)
    fw.dma(cw.t[:], cw_d, writes=cw.b)
    fw.dma(cb.t[:], cb_d, writes=cb.b)
    fw.dma(hp.t[:], hp_d, writes=hp.b)
    fw.dma(hmask.t[:], hmask_d, writes=hmask.b)
    fw.op("dve", lambda e: e.tensor_copy(identb.t[:], cst.t[:, 0, :]), reads=cst.b, writes=identb.b)
    for g in range(4):
        fw.op("dve", lambda e: e.tensor_copy(NM8.t[:, g, :], cst.t[:, 3, :]), reads=cst.b, writes=NM8.b)
    fw.op("dve", lambda e: e.memset(onesb.t[:], 1.0), writes=onesb.b)
    fw.op("dve", lambda e: e.memset(col1.t[:], 1.0), writes=col1.b)
    fw.op("dve", lambda e: e.memset(Hst.t[:], 0.0), writes=Hst.b)
    fw.op("dve", lambda e: e.memset(halo.t[:], 0.0), writes=halo.b)
    fw.op("dve", lambda e: e.memset(kT.t[:], 0.0), writes=kT.b)
    fw.op("dve", lambda e: e.memset(Vtok.t[:], 0.0), writes=Vtok.b)
    fw.op("dve", lambda e: e.memset(Vtok.t[:, 4:8, 0:64], 1.0), writes=Vtok.b)
    fw.op("dve", lambda e: e.memset(Vtok.t[:, 4:8, 1088:1152], 1.0), writes=Vtok.b)
    fw.op("dve", lambda e: e.tensor_copy(onesh.t[:], hmask.t[:, 0:1].to_broadcast([128, 64])), reads=hmask.b, writes=onesh.b)
    fw.op("act", lambda e: e.activation(Aneg.t[:], hp.t[:, 1, :], AF.Exp), reads=hp.b, writes=Aneg.b)
    fw.op("dve", lambda e: e.tensor_scalar(Aneg.t[:], Aneg.t[:], -1.0, None, ALU.mult), reads=Aneg.b, writes=Aneg.b)
    dgw = nc.dram_tensor("dgw", [128, 48, 128], BF16, kind="Internal").ap()
    with ExitStack() as phd:
        dg = sb(phd, "dg", [128, 48, 128], BF16)
        for m in range(12):
            for tap in range(4):
                fw.op("dve", lambda e: e.tensor_scalar(dg.t[:, m * 4 + tap, :], identb.t[:], cw.t[:, m, tap:tap + 1], None, ALU.mult),
                      reads=identb.b + cw.b, writes=dg.b)
        dgdep = fw.dma(dgw, dg.t[:], reads=dg.b)
        fw.barrier(("pe", "act", "dve", "sp"), dmas=True)

    for h in range(16):
        fw.dma(EB.t[:, h, :], btab_d[h], writes=EB.b, q="pool")

    identb_ap = identb.t[:]
    epsc = sb(st, "epsc", [128, 1], F32)
    fw.op("dve", lambda e: e.memset(epsc.t[:], EPS), writes=epsc.b)

    def rsqrt_eps(out_ap, in_ap, rb, wb):
        fw.op("act", lambda e: e.activation(out_ap, in_ap, AF.Ln, bias=epsc.t[:, 0:1]), reads=rb + epsc.b, writes=wb)
        fw.op("act", lambda e: e.activation(out_ap, out_ap, AF.Exp, scale=-0.5), reads=wb, writes=wb)

    def norm_prep():
        fw.op("dve", lambda e: e.memset(ssp.t[:], 0.0), writes=ssp.b)

    def norm_to_hT(ni, ph):
        xn, ss, rstd = xnp, ssp, rstdp
        junk = sb(ph, "njunk", [128, D], BF16)
        for tt in range(NTT):
            fw.op("act", lambda e: e.activation(junk.t[:], x_res.t[:, tt, :], AF.Square, scale=1.0 / 32.0,
                                                accum_out=ss.t[:, tt:tt + 1]),
                  reads=[x_res.b[tt]], writes=junk.b + ss.b)
        fw.op("act", lambda e: e.activation(rstd.t[:], ss.t[:], AF.Ln, bias=epsc.t[:, 0:1]), reads=ss.b + epsc.b, writes=rstd.b)
        fw.op("act", lambda e: e.activation(rstd.t[:], rstd.t[:], AF.Exp, scale=-0.5), reads=rstd.b, writes=rstd.b)
        for tt in range(NTT):
            k = tt % 2
            fw.op("act", lambda e: e.activation(xn.t[:, k, :], x_res.t[:, tt, :], AF.Copy, scale=rstd.t[:, tt:tt + 1]),
                  reads=[x_res.b[tt]] + rstd.b, writes=[xn.b[k]])
            pt, pb = psb()
            for j in range(8):
                fw.op("pe", lambda e: e.transpose(pt[:, j * 128:(j + 1) * 128], xn.t[:, k, j * 128:(j + 1) * 128], identb_ap),
                      reads=[xn.b[k]] + identb.b, writes=[pb])
            fw.op("dve", lambda e: e.tensor_tensor(hT.t[:, :, tt * 128:(tt + 1) * 128],
                                                   pt[:].rearrange("p (j c) -> p j c", j=8),
                                                   nwf.t[:, ni, :].unsqueeze(2).to_broadcast([128, 8, 128]), ALU.mult),
                  reads=[pb] + nwf.b, writes=hT.b)

    def proj_fm(src2d, c0, nchunks, nkc, rhs, rhs_bufs, evac, ncol=T, evac2=None, evac3=None):
        pend = []
        for u0 in range(0, nchunks, 4):
            nu = min(4, nchunks - u0)
            slot, sbuf_ = wload(src2d, 0, nkc, c0 + u0 * 128, nu * 128)
            for mo in range(nu):
                ps, pb = psf()
                for kc in range(nkc):
                    fw.op("pe", lambda e: e.matmul(ps[:, 0:ncol], slot[:, kc, mo * 128:(mo + 1) * 128], rhs(kc),
                                                   start=(kc == 0), stop=(kc == nkc - 1)),
                          reads=[sbuf_] + rhs_bufs, writes=[pb])
                evac(u0 + mo, ps, pb)
                pend.append(u0 + mo)
                if evac2 is not None and len(pend) >= 2:
                    evac2(pend[-2])
                if evac3 is not None and len(pend) >= 3:
                    evac3(pend[-3])
        if evac2 is not None and pend:
            evac2(pend[-1])
        if evac3 is not None:
            if len(pend) >= 2:
                evac3(pend[-2])
            if pend:
                evac3(pend[-1])

    def proj_tm(src2d, c0, ncols, nkc, lhs, lhs_bufs, evac):
        for u0 in range(0, ncols, 512):
            nc_ = min(512, ncols - u0)
            slot, sbuf_ = wload(src2d, 0, nkc, c0 + u0, nc_)
            for tt in range(NTT):
                ps, pb = psf()
                for kc in range(nkc):
                    fw.op("pe", lambda e: e.matmul(ps[:, 0:nc_], lhs(kc, tt), slot[:, kc, :],
                                                   start=(kc == 0), stop=(kc == nkc - 1)),
                          reads=[sbuf_] + lhs_bufs, writes=[pb])
                evac(u0 // 512, tt, ps, pb)

    def ffn(which, after=None):
        with ExitStack() as ph:
            actT = sb(ph, "actT", [128, NFC, T], BF16, 1)
            sg = sb(ph, "sg", [128, 2, T], F32, 2)
            norm_prep()
            wgu = w_gu[which]
            wd = w_dn[which]
            for f0 in range(0, NFC, 4):
                nf = min(4, NFC - f0)
                gs, gb = wload(wgu, 0, 8, f0 * 128, nf * 128)
                us, ub = wload(wgu, 0, 8, DFF + f0 * 128, nf * 128)
                for fo in range(nf):
                    f = f0 + fo
                    pg, pgb = psf()
                    pu, pub = psf()
                    for kc in range(8):
                        fw.op("pe", lambda e: e.matmul(pg[:, 0:T], gs[:, kc, fo * 128:(fo + 1) * 128], hT.t[:, kc, :],
                                                       start=(kc == 0), stop=(kc == 7)), reads=[gb] + hT.b, writes=[pgb])
                    for kc in range(8):
                        fw.op("pe", lambda e: e.matmul(pu[:, 0:T], us[:, kc, fo * 128:(fo + 1) * 128], hT.t[:, kc, :],
                                                       start=(kc == 0), stop=(kc == 7)), reads=[ub] + hT.b, writes=[pub])
                    k = f % 2
                    fw.op("act", lambda e: e.activation(sg.t[:, k, :], pg[:, 0:T], AF.Silu), reads=[pgb], writes=[sg.b[k]])
                    fw.op("dve", lambda e: e.tensor_tensor(actT.t[:, f, :], sg.t[:, k, :], pu[:, 0:T], ALU.mult),
                          reads=[sg.b[k], pub], writes=actT.b)
            groups = [(0, 8), (8, 8), (16, 6)]
            for dh in range(2):
                slots = [wload(wd, g0, gn, dh * 512, 512) for (g0, gn) in groups]
                for tt in range(NTT):
                    ps, pb = psf()
                    for gi, (g0, gn) in enumerate(groups):
                        sl, slb = slots[gi]
                        for fo in range(gn):
                            f = g0 + fo
                            fw.op("pe", lambda e: e.matmul(ps[:, 0:512], actT.t[:, f, tt * 128:(tt + 1) * 128], sl[:, fo, :],
                                                           start=(f == 0), stop=(f == NFC - 1)),
                                  reads=[slb] + actT.b, writes=[pb])
                    xs = x_res.t[:, tt, dh * 512:(dh + 1) * 512]
                    fw.op("dve", lambda e: e.scalar_tensor_tensor(xs, ps[:, 0:512], 0.5, xs, ALU.mult, ALU.add),
                          reads=[pb, x_res.b[tt]], writes=[x_res.b[tt]])
            dm = False
            if after is not None:
                dm = after(ph)
            fw.barrier(("pe", "act", "dve"), dmas=bool(dm))

    def mixer(full, need_kv, first_full, after=None):
        with ExitStack() as ph:
            xtok = sb(ph, "xtok", [128, NTT, D], BF16, 1)
            BT = sb(ph, "BT", [128, 2, T], BF16)
            CT = sb(ph, "CT", [128, 2, T], BF16)
            Btok = sb(ph, "Btok", [128, NTT, 256], BF16)
            dt = sb(ph, "dt", [128, NTT, 16], F32)
            if full:
                yssmT = sb(ph, "yssmT", [128, 8, T], BF16)
                yattT = sb(ph, "yattT", [128, 8, T], BF16)
            with ExitStack() as p1:
                pre = sb(p1, "pre", [128, 2, T + 4], BF16, 2)
                xfm = sb(p1, "xfm", [128, 2, T], BF16, 2)
                dgs = sb(p1, "dgs", [128, 48, 128], BF16)
                fw.barrier(("pool",))
                fw.dma(dgs.t[:], dgw, writes=dgs.b, q="pool")
                cps = {}

                def ev_xbc(m, ps, pb):
                    k = m % 2
                    fw.op("act", lambda e: e.copy(pre.t[:, k, 3:T + 3], ps[:, 0:T]), reads=[pb], writes=[pre.b[k]])
                    fw.op("dve", lambda e: e.tensor_copy(pre.t[:, k, 0:3], halo.t[:, m, :]), reads=halo.b, writes=[pre.b[k]])
                    fw.op("dve", lambda e: e.tensor_copy(halo.t[:, m, :], pre.t[:, k, T:T + 3]), reads=[pre.b[k]], writes=halo.b)

                def ev_xbc2(m):
                    k = m % 2
                    pc, pcb = psf()
                    for tap in range(4):
                        fw.op("pe", lambda e: e.matmul(pc[:, 0:T], dgs.t[:, m * 4 + tap, :], pre.t[:, k, tap:tap + T],
                                                       start=(tap == 0), stop=(tap == 3)), reads=dgs.b + [pre.b[k]], writes=[pcb])
                    if m < 8:
                        fw.op("act", lambda e: e.activation(xfm.t[:, k, :], pc[:, 0:T], AF.Silu, bias=cb.t[:, m:m + 1]), reads=[pcb] + cb.b, writes=[xfm.b[k]])
                    elif m < 10:
                        g = m - 8
                        fw.op("act", lambda e: e.activation(BT.t[:, g, :], pc[:, 0:T], AF.Silu, bias=cb.t[:, m:m + 1]), reads=[pcb] + cb.b, writes=BT.b)
                    else:
                        g = m - 10
                        fw.op("act", lambda e: e.activation(CT.t[:, g, :], pc[:, 0:T], AF.Silu, bias=cb.t[:, m:m + 1]), reads=[pcb] + cb.b, writes=CT.b)

                def ev_xbc3(m):
                    k = m % 2
                    if m < 8:
                        pt, ptb = psb()
                        for tt in range(NTT):
                            fw.op("pe", lambda e: e.transpose(pt[:, tt * 128:(tt + 1) * 128], xfm.t[:, k, tt * 128:(tt + 1) * 128], identb_ap),
                                  reads=[xfm.b[k]] + identb.b, writes=[ptb])
                        fw.op("dve", lambda e: e.tensor_copy(xtok.t[:, :, m * 128:(m + 1) * 128], pt[:, 0:512].rearrange("p (t c) -> p t c", t=NTT)),
                              reads=[ptb], writes=xtok.b)
                    elif m < 10:
                        g = m - 8
                        pt, ptb = psb()
                        for tt in range(NTT):
                            fw.op("pe", lambda e: e.transpose(pt[:, tt * 128:(tt + 1) * 128], BT.t[:, g, tt * 128:(tt + 1) * 128], identb_ap),
                                  reads=BT.b + identb.b, writes=[ptb])
                        fw.op("dve", lambda e: e.tensor_copy(Btok.t[:, :, g * 128:(g + 1) * 128], pt[:, 0:512].rearrange("p (t c) -> p t c", t=NTT)),
                              reads=[ptb], writes=Btok.b)

                def ev_dt(u, tt, ps, pb):
                    fw.op("dve", lambda e: e.tensor_tensor(dt.t[:, tt, :], ps[:, 0:16], hp.t[:, 0, :], ALU.add), reads=[pb] + hp.b, writes=dt.b)
                    fw.op("act", lambda e: e.activation(dt.t[:, tt, :], dt.t[:, tt, :], AF.Exp), reads=dt.b, writes=dt.b)
                    fw.op("act", lambda e: e.activation(dt.t[:, tt, :], dt.t[:, tt, :], AF.Ln, bias=col1.t[:, 0:1]), reads=dt.b + col1.b, writes=dt.b)

                proj_tm(w_in, C_DT, 16, 8, lambda kc, tt: hT.t[:, kc, tt * 128:(tt + 1) * 128], hT.b, ev_dt)
                proj_fm(w_in, C_X, 12, 8, lambda kc: hT.t[:, kc, :], hT.b, ev_xbc, evac2=ev_xbc2, evac3=ev_xbc3)

                fw.barrier()
            dump("xtok", xtok.t[:], xtok.b)
            dump("dt", dt.t[:], dt.b)

            def kv_unit(i):
                if i < 2:
                    slot, sbuf_ = wload(w_in, 0, 8, C_K + i * 512, 512)
                    for mo in range(4):
                        m = i * 4 + mo
                        ps, pb = psf()
                        for kc in range(8):
                            fw.op("pe", lambda e: e.matmul(ps[:, 0:T], slot[:, kc, mo * 128:(mo + 1) * 128], hT.t[:, kc, :],
                                                           start=(kc == 0), stop=(kc == 7)), reads=[sbuf_] + hT.b, writes=[pb])
                        fw.op("act", lambda e: e.copy(kT.t[:, m, T:2 * T], ps[:, 0:T]), reads=[pb], writes=kT.b)
                else:
                    u = i - 2
                    slot, sbuf_ = wload(w_in, 0, 8, C_V + u * 512, 512)
                    for tt in range(NTT):
                        ps, pb = psf()
                        for kc in range(8):
                            fw.op("pe", lambda e: e.matmul(ps[:, 0:512], hT.t[:, kc, tt * 128:(tt + 1) * 128], slot[:, kc, :],
                                                           start=(kc == 0), stop=(kc == 7)), reads=[sbuf_] + hT.b, writes=[pb])
                        fw.op("act", lambda e: e.copy(Vtok.t[:, 4 + tt, 64 + u * 512:64 + (u + 1) * 512], ps[:, 0:512]), reads=[pb], writes=Vtok.b)

            with ExitStack() as p2:
                NB = 2 if full else NTT
                a_t = sb(p2, "a_t", [128, NB, 16], F32, NB)
                wdt = sb(p2, "wdt", [128, NB, 16], F32, NB)
                decr = sb(p2, "decr", [128, NB, 16], F32, NB)
                xw = sb(p2, "xw", [128, NB, D], BF16, NB)
                if full:
                    sz = sb(p2, "sz", [128, 2, D], BF16, 2)
                    zslots = [wload(w_in, 0, 8, C_Z + u * 512, 512) for u in range(2)]
                    Y = sb(p2, "Y", [128, 2, 8, 128], F32, 2)
                    E = sb(p2, "E", [128, 2, 8, 128], BF16, 2)
                    Mm = sb(p2, "Mm", [128, 2, 8, 128], BF16, 2)
                    CBs = sb(p2, "CBs", [128, 2, 128], F32, 2)
                    t1 = sb(p2, "t1", [128, D], F32)
                    t2 = sb(p2, "t2", [128, 512], F32)
                    ynb = sb(p2, "ynb", [128, D], BF16)
                    ebias = sb(p2, "ebias", [128, 2, 16], F32, 2)
                    ecum = sb(p2, "ecum", [128, 2, 16], F32, 2)
                    gss = sb(p2, "gss", [128, 2], F32)
                    jk2 = sb(p2, "jk2", [128, 512], BF16)

                def ssd_pre(tt):
                    k = tt % NB
                    fw.op("dve", lambda e: e.tensor_tensor(a_t.t[:, k, :], dt.t[:, tt, :], Aneg.t[:], ALU.mult), reads=dt.b + Aneg.b, writes=[a_t.b[k]])
                    if full:
                        for u in range(2):
                            zs, zb = zslots[u]
                            pz, pzb = psf()
                            for kc in range(8):
                                fw.op("pe", lambda e: e.matmul(pz[:, 0:512], hT.t[:, kc, tt * 128:(tt + 1) * 128], zs[:, kc, :],
                                                               start=(kc == 0), stop=(kc == 7)), reads=[zb] + hT.b, writes=[pzb])
                            fw.op("act", lambda e: e.activation(sz.t[:, k, u * 512:(u + 1) * 512], pz[:, 0:512], AF.Silu), reads=[pzb], writes=[sz.b[k]])
                    pss, pssb = psf()
                    fw.op("pe", lambda e: e.matmul(pss[:, 0:16], cst.t[:, 2, :], a_t.t[:, k, :], start=True, stop=True), reads=cst.b + [a_t.b[k]], writes=[pssb])
                    fw.op("pe", lambda e: e.matmul(pss[:, 16:32], onesf.t[:], a_t.t[:, k, :], start=True, stop=True), reads=onesf.b + [a_t.b[k]], writes=[pssb])
                    if full:
                        fw.op("pe", lambda e: e.matmul(pss[:, 32:48], cst.t[:, 1, :], a_t.t[:, k, :], start=True, stop=True), reads=cst.b + [a_t.b[k]], writes=[pssb])
                    fw.op("act", lambda e: e.activation(wdt.t[:, k, :], pss[:, 0:16], AF.Exp), reads=[pssb], writes=[wdt.b[k]])
                    fw.op("dve", lambda e: e.tensor_tensor(wdt.t[:, k, :], wdt.t[:, k, :], dt.t[:, tt, :], ALU.mult), reads=[wdt.b[k]] + dt.b, writes=[wdt.b[k]])
                    fw.op("act", lambda e: e.activation(decr.t[:, k, :], pss[:, 16:32], AF.Exp), reads=[pssb], writes=[decr.b[k]])
                    if full:
                        fw.op("act", lambda e: e.activation(ebias.t[:, k, :], dt.t[:, tt, :], AF.Ln), reads=dt.b, writes=[ebias.b[k]])
                        fw.op("dve", lambda e: e.tensor_tensor(ebias.t[:, k, :], ebias.t[:, k, :], pss[:, 32:48], ALU.subtract), reads=[ebias.b[k], pssb], writes=[ebias.b[k]])
                        fw.op("act", lambda e: e.activation(ecum.t[:, k, :], pss[:, 32:48], AF.Exp), reads=[pssb], writes=[ecum.b[k]])
                    fw.op("dve", lambda e: e.tensor_tensor(xw.t[:, k, :].rearrange("p (h c) -> p h c", h=16),
                                                           xtok.t[:, tt, :].rearrange("p (h c) -> p h c", h=16),
                                                           wdt.t[:, k, :].unsqueeze(2).to_broadcast([128, 16, 64]), ALU.mult),
                          reads=xtok.b + [wdt.b[k]], writes=[xw.b[k]])

                def ssd_front(tt, g):
                    k = tt % 2
                    tsl = slice(tt * 128, (tt + 1) * 128)
                    fw.op("dve", lambda e: e.tensor_tensor(Y.t[:, g], cst.t[:, 1, :].unsqueeze(1).to_broadcast([128, 8, 128]),
                                                           a_t.t[:, k, g * 8:(g + 1) * 8].unsqueeze(2).to_broadcast([128, 8, 128]), ALU.mult),
                          reads=cst.b + [a_t.b[k]], writes=[Y.b[g]])
                    pc, pcb = psf()
                    fw.op("pe", lambda e: e.matmul(pc[:, 0:128], BT.t[:, g, tsl], CT.t[:, g, tsl], start=True, stop=True),
                          reads=BT.b + CT.b, writes=[pcb])
                    fw.op("act", lambda e: e.copy(CBs.t[:, g, :], pc[:, 0:128]), reads=[pcb], writes=[CBs.b[g]])

                def ssd_front_r(tt, g):
                    k = tt % 2
                    for q in range(2):
                        prt, prb = psf()
                        fw.op("pe", lambda e: e.matmul(prt[:, 0:512], onesf.t[:], Y.t[:, g, q * 4:(q + 1) * 4, :].rearrange("p h l -> p (h l)"),
                                                       start=True, stop=False), reads=onesf.b + [Y.b[g]], writes=[prb])
                        fw.op("pe", lambda e: e.matmul(prt[:, 0:512], identb_ap, NM8.t[:, 0:4, :].rearrange("p h l -> p (h l)"),
                                                       start=False, stop=True), reads=identb.b + NM8.b, writes=[prb])
                        for hh in range(4):
                            h = g * 8 + q * 4 + hh
                            fw.op("act", lambda e: e.activation(E.t[:, g, q * 4 + hh, :], prt[:, hh * 128:(hh + 1) * 128], AF.Exp,
                                                                bias=ebias.t[:, k, h:h + 1]), reads=[prb, ebias.b[k]], writes=[E.b[g]])

                def ssd_front_b(tt, g):
                    fw.op("dve", lambda e: e.tensor_tensor(Mm.t[:, g], E.t[:, g],
                                                           CBs.t[:, g, :].unsqueeze(1).to_broadcast([128, 8, 128]), ALU.mult),
                          reads=[E.b[g], CBs.b[g]], writes=[Mm.b[g]])

                def ssd_back(tt, g):
                    k = tt % 2
                    tsl = slice(tt * 128, (tt + 1) * 128)
                    gsl = slice(g * 512, (g + 1) * 512)
                    if g == 0:
                        fw.op("act", lambda e: e.copy(Hbf.t[:], Hst.t[:]), reads=Hst.b, writes=Hbf.b)
                    pyt, pyb = psfix(4)
                    for hh in range(8):
                        h = g * 8 + hh
                        fw.op("pe", lambda e: e.matmul(pyt[:, hh * 64:(hh + 1) * 64], Mm.t[:, g, hh, :], xtok.t[:, tt, h * 64:(h + 1) * 64],
                                                       start=True, stop=True), reads=[Mm.b[g]] + xtok.b, writes=[pyb])
                    po, pob = psfix(5)
                    fw.op("pe", lambda e: e.matmul(po[:, 0:512], CT.t[:, g, tsl], Hbf.t[:, gsl], start=True, stop=True),
                          reads=CT.b + Hbf.b, writes=[pob])

                def ssd_back_rest(tt, g):
                    k = tt % 2
                    tsl = slice(tt * 128, (tt + 1) * 128)
                    gsl = slice(g * 512, (g + 1) * 512)
                    pyt, pyb = psfix(4)
                    po, pob = psfix(5)
                    fw.op("dve", lambda e: e.tensor_tensor(t1.t[:, gsl].rearrange("p (h c) -> p h c", h=8),
                                                           po[:, 0:512].rearrange("p (h c) -> p h c", h=8),
                                                           ecum.t[:, k, g * 8:(g + 1) * 8].unsqueeze(2).to_broadcast([128, 8, 64]), ALU.mult),
                          reads=[pob, ecum.b[k]], writes=t1.b)
                    fw.op("dve", lambda e: e.tensor_tensor(t2.t[:].rearrange("p (h c) -> p h c", h=8),
                                                           xtok.t[:, tt, gsl].rearrange("p (h c) -> p h c", h=8),
                                                           hp.t[:, 2, g * 8:(g + 1) * 8].unsqueeze(2).to_broadcast([128, 8, 64]), ALU.mult),
                          reads=xtok.b + hp.b, writes=t2.b)
                    fw.op("dve", lambda e: e.tensor_tensor(t1.t[:, gsl], t1.t[:, gsl], t2.t[:], ALU.add), reads=t1.b + t2.b, writes=t1.b)
                    fw.op("dve", lambda e: e.tensor_tensor(t1.t[:, gsl], t1.t[:, gsl], pyt[:, 0:512], ALU.add), reads=t1.b + [pyb], writes=t1.b)
                    fw.op("dve", lambda e: e.tensor_tensor(t1.t[:, gsl], t1.t[:, gsl], sz.t[:, k, gsl], ALU.mult), reads=t1.b + [sz.b[k]], writes=t1.b)
                    fw.op("dve", lambda e: e.memset(gss.t[:, g:g + 1], 0.0), writes=gss.b)
                    fw.op("act", lambda e: e.activation(jk2.t[:], t1.t[:, gsl], AF.Square, scale=float(512 ** -0.5),
                                                        accum_out=gss.t[:, g:g + 1]), reads=t1.b, writes=jk2.b + gss.b)
                    rsqrt_eps(gss.t[:, g:g + 1], gss.t[:, g:g + 1], gss.b, gss.b)
                    fw.op("dve", lambda e: e.tensor_scalar(ynb.t[:, gsl], t1.t[:, gsl], gss.t[:, g:g + 1], None, ALU.mult),
                          reads=t1.b + gss.b, writes=ynb.b)
                    if g == 1:
                        if tt == 0:
                            dump("yssd", t1.t[:], t1.b)
                        pt, ptb = psb()
                        for j in range(8):
                            fw.op("pe", lambda e: e.transpose(pt[:, j * 128:(j + 1) * 128], ynb.t[:, j * 128:(j + 1) * 128], identb_ap),
                                  reads=ynb.b + identb.b, writes=[ptb])
                        fw.op("dve", lambda e: e.tensor_tensor(yssmT.t[:, :, tsl], pt[:].rearrange("p (j c) -> p j c", j=8),
                                                               nwf.t[:, 2, :].unsqueeze(2).to_broadcast([128, 8, 128]), ALU.mult),
                              reads=[ptb] + nwf.b, writes=yssmT.b)

                def ssd_post(tt):
                    k = tt % NB
                    for g in range(2):
                        pst_, pstb = psf()
                        fw.op("pe", lambda e: e.matmul(pst_[:, 0:512], Btok.t[:, tt, g * 128:(g + 1) * 128], xw.t[:, k, g * 512:(g + 1) * 512],
                                                       start=True, stop=True), reads=Btok.b + [xw.b[k]], writes=[pstb])
                        hs = Hst.t[:, g * 512:(g + 1) * 512]
                        fw.op("dve", lambda e: e.tensor_tensor(hs.rearrange("p (h c) -> p h c", h=8), hs.rearrange("p (h c) -> p h c", h=8),
                                                               decr.t[:, k, g * 8:(g + 1) * 8].unsqueeze(2).to_broadcast([128, 8, 64]), ALU.mult),
                              reads=Hst.b + [decr.b[k]], writes=Hst.b)
                        fw.op("dve", lambda e: e.tensor_tensor(hs, hs, pst_[:, 0:512], ALU.add), reads=Hst.b + [pstb], writes=Hst.b)

                if not full:
                    for tt in range(NTT):
                        ssd_pre(tt)
                    for tt in range(NTT):
                        ssd_post(tt)
                else:
                    steps = [(tt, g) for tt in range(NTT) for g in range(2)]
                    prev = None
                    for (tt, g) in steps:
                        if g == 0:
                            ssd_pre(tt)
                        ssd_front(tt, g)
                        if prev is not None:
                            ssd_back(*prev)
                        ssd_front_r(tt, g)
                        if prev is not None:
                            ssd_back_rest(*prev)
                            if prev[1] == 1:
                                ssd_post(prev[0])
                        ssd_front_b(tt, g)
                        prev = (tt, g)
                        if g == 1:
                            kv_unit(tt)
                    ssd_back(*prev)
                    ssd_back_rest(*prev)
                    ssd_post(prev[0])
                fw.barrier()
            dump("Hst", Hst.t[:], Hst.b)

            if need_kv:
                with ExitStack() as p3:
                    if not full:
                        for i in range(4):
                            kv_unit(i)
                    if full:
                        qT0 = sb(p3, "qT0", [128, 8, T], BF16)
                        qT1 = sb(p3, "qT1", [128, 8, T], BF16)
                        qTs = [qT0, qT1]
                        PT = sb(p3, "PT", [128, 4, 512], BF16, 4)
                        rden = sb(p3, "rden", [128, 512], F32)
                        dtmp = sb(p3, "dtmp", [128, 512], F32)
                        fw.op("dve", lambda e: e.memset(qT0.t[64:128, :, :], 0.0), writes=qT0.b)
                        fw.op("dve", lambda e: e.memset(qT1.t[0:64, :, :], 0.0), writes=qT1.b)

                        def ev_q(m, ps, pb):
                            fw.op("act", lambda e: e.activation(qT0.t[0:64, m, :], ps[0:64, 0:T], AF.Copy, scale=0.125), reads=[pb], writes=qT0.b)
                            fw.op("act", lambda e: e.activation(qT1.t[64:128, m, :], ps[64:128, 0:T], AF.Copy, scale=0.125), reads=[pb], writes=qT1.b)

                        proj_fm(w_in, C_Q, 8, 8, lambda kc: hT.t[:, kc, :], hT.b, ev_q)
                        vaug = sb(p3, "vaug", [128, 2, 8, 128], BF16, 2)

                        def prep_v(h):
                            hb = h % 2
                            if hb == 0:
                                vc, oc, osrc = 0, 64, 1088
                            else:
                                vc, oc, osrc = 64, 0, 0
                            fw.op("dve", lambda e: e.tensor_copy(vaug.t[:, hb, :, vc:vc + 64], Vtok.t[:, :, 64 + 64 * h:128 + 64 * h]),
                                  reads=Vtok.b, writes=[vaug.b[hb]])
                            fw.op("dve", lambda e: e.tensor_copy(vaug.t[:, hb, :, oc:oc + 64], Vtok.t[:, :, osrc:osrc + 64]),
                                  reads=Vtok.b, writes=[vaug.b[hb]])

                        def stageA(hpair, h2, j, k):
                            h = hpair * 2 + h2
                            qt_lo = max(0, j - 4)
                            qt_hi = min(3, j)
                            q0 = qt_lo * 128
                            nq = (qt_hi - qt_lo + 1) * 128
                            rel0 = (qt_lo * 2 + 8 - 2 * j) * 64
                            if j == 0 and h == 0:
                                prep_v(0)
                            if j == 3 and h < 15:
                                prep_v(h + 1)
                            sp_, spb = psf()
                            qh = qTs[h2]
                            fw.op("pe", lambda e: e.matmul(sp_[:, 0:nq], kT.t[:, hpair, j * 128:(j + 1) * 128], qh.t[:, hpair, q0:q0 + nq],
                                                           start=True, stop=False), reads=kT.b + qh.b, writes=[spb])
                            fw.op("pe", lambda e: e.matmul(sp_[:, 0:nq], identb_ap, EB.t[:, h, rel0:rel0 + nq],
                                                           start=False, stop=True), reads=identb.b + EB.b, writes=[spb])
                            fw.op("act", lambda e: e.activation(PT.t[:, k, 0:nq], sp_[:, 0:nq], AF.Exp), reads=[spb], writes=[PT.b[k]])

                        def stageB(hpair, h2, j, k):
                            h = hpair * 2 + h2
                            qt_lo = max(0, j - 4)
                            qt_hi = min(3, j)
                            q0 = qt_lo * 128
                            nq = (qt_hi - qt_lo + 1) * 128
                            pX, pXb = psfix(4 + h2)
                            segs = [(q0, nq)]
                            if 1 <= j <= 3:
                                segs = [(q0, nq - 128), (q0 + nq - 128, 128)]
                            for (sq0, snq) in segs:
                                fw.op("pe", lambda e: e.matmul(pX[:, sq0:sq0 + snq], vaug.t[:, h2, j, :], PT.t[:, k, sq0 - q0:sq0 - q0 + snq],
                                                               start=(j == 0), stop=(j == 7)), reads=[vaug.b[h2], PT.b[k]], writes=[pXb])
                            if h2 == 1 and j == 7:
                                pA, pAb = psfix(4)
                                pB, pBb = psfix(5)
                                fw.op("act", lambda e: e.copy(dtmp.t[0:64, :], pA[64:128, 0:512]), reads=[pAb], writes=dtmp.b)
                                fw.op("act", lambda e: e.copy(dtmp.t[64:128, :], pB[0:64, 0:512]), reads=[pBb], writes=dtmp.b)
                                fw.op("dve", lambda e: e.reciprocal(rden.t[:], dtmp.t[:]), reads=dtmp.b, writes=rden.b)
                                fw.op("dve", lambda e: e.tensor_tensor(yattT.t[0:64, hpair, :], pA[0:64, 0:512], rden.t[0:64, :], ALU.mult),
                                      reads=[pAb] + rden.b, writes=yattT.b)
                                fw.op("dve", lambda e: e.tensor_tensor(yattT.t[64:128, hpair, :], pB[64:128, 0:512], rden.t[64:128, :], ALU.mult),
                                      reads=[pBb] + rden.b, writes=yattT.b)

                        its = [(hpair, h2, j) for hpair in range(8) for h2 in range(2) for j in range(8)]
                        pend = []
                        for n_, (hpair, h2, j) in enumerate(its):
                            k = n_ % 4
                            stageA(hpair, h2, j, k)
                            pend.append((hpair, h2, j, k))
                            if len(pend) > 3:
                                stageB(*pend.pop(0))
                        while pend:
                            stageB(*pend.pop(0))
                    fw.barrier()
                fw.op("act", lambda e: e.copy(kT.t[:, :, 0:T], kT.t[:, :, T:2 * T]), reads=kT.b, writes=kT.b)
                fw.op("dve", lambda e: e.tensor_copy(Vtok.t[:, 0:4, :], Vtok.t[:, 4:8, :]), reads=Vtok.b, writes=Vtok.b)

            if full:
                dump("yssmT", yssmT.t[:], yssmT.b)
                dump("yattT", yattT.t[:], yattT.b)
                with ExitStack() as p4:
                    norm_prep()
                    mT = sb(p4, "mT", [128, 8, T], BF16)
                    s3 = sb(p4, "s3", [128, 2, T], F32, 2)
                    s4 = sb(p4, "s4", [128, 2, T], F32, 2)
                    for u0 in (0, 4):
                        sl1, b1 = wload(w_bs, 0, 8, u0 * 128, 512)
                        sl2, b2 = wload(w_ba, 0, 8, u0 * 128, 512)
                        sl3, b3 = wload(w_in, 0, 8, C_GS + u0 * 128, 512)
                        sl4, b4 = wload(w_in, 0, 8, C_GA + u0 * 128, 512)
                        for mo in range(4):
                            m = u0 + mo
                            k = m % 2
                            msl = slice(mo * 128, (mo + 1) * 128)
                            p1_, p1b = psf()
                            p2_, p2b = psf()
                            p3_, p3b = psf()
                            p4_, p4b = psf()
                            for kc in range(8):
                                fw.op("pe", lambda e: e.matmul(p3_[:, 0:T], sl3[:, kc, msl], hT.t[:, kc, :], start=(kc == 0), stop=(kc == 7)),
                                      reads=[b3] + hT.b, writes=[p3b])
                            for kc in range(8):
                                fw.op("pe", lambda e: e.matmul(p4_[:, 0:T], sl4[:, kc, msl], hT.t[:, kc, :], start=(kc == 0), stop=(kc == 7)),
                                      reads=[b4] + hT.b, writes=[p4b])
                            for kc in range(8):
                                fw.op("pe", lambda e: e.matmul(p1_[:, 0:T], sl1[:, kc, msl], yssmT.t[:, kc, :], start=(kc == 0), stop=(kc == 7)),
                                      reads=[b1] + yssmT.b, writes=[p1b])
                            for kc in range(8):
                                fw.op("pe", lambda e: e.matmul(p2_[:, 0:T], sl2[:, kc, msl], yattT.t[:, kc, :], start=(kc == 0), stop=(kc == 7)),
                                      reads=[b2] + yattT.b, writes=[p2b])
                            fw.op("act", lambda e: e.activation(s3.t[:, k, :], p3_[:, 0:T], AF.Sigmoid), reads=[p3b], writes=[s3.b[k]])
                            fw.op("act", lambda e: e.activation(s4.t[:, k, :], p4_[:, 0:T], AF.Sigmoid), reads=[p4b], writes=[s4.b[k]])
                            fw.op("dve", lambda e: e.tensor_tensor(s3.t[:, k, :], s3.t[:, k, :], p1_[:, 0:T], ALU.mult), reads=[s3.b[k], p1b], writes=[s3.b[k]])
                            fw.op("dve", lambda e: e.tensor_tensor(s4.t[:, k, :], s4.t[:, k, :], p2_[:, 0:T], ALU.mult), reads=[s4.b[k], p2b], writes=[s4.b[k]])
                            fw.op("dve", lambda e: e.tensor_tensor(mT.t[:, m, :], s3.t[:, k, :], s4.t[:, k, :], ALU.add), reads=[s3.b[k], s4.b[k]], writes=mT.b)

                    def ev_out(u, tt, ps, pb):
                        xs = x_res.t[:, tt, u * 512:(u + 1) * 512]
                        fw.op("dve", lambda e: e.tensor_tensor(xs, xs, ps[:, 0:512], ALU.add), reads=[pb, x_res.b[tt]], writes=[x_res.b[tt]])

                    proj_tm(w_out, 0, 1024, 8, lambda kc, tt: mT.t[:, kc, tt * 128:(tt + 1) * 128], mT.b, ev_out)
                    if after is not None:
                        after(p4)
                    fw.barrier()
            if not full:
                norm_prep()
                if after is not None:
                    after(ph)
                fw.barrier()

    onesf = sb(st, "onesf", [128, 128], F32)
    fw.op("dve", lambda e: e.memset(onesf.t[:], 1.0), writes=onesf.b)

    def tile_src(ti):
        full = ti >= n_state
        src = x_main if full else x_prev
        tok0 = ((ti - n_state) if full else (NTOK // T - n_state + ti)) * T
        return src, tok0

    def load_x(ti):
        if ti >= n_state + n_full:
            return
        src, tok0 = tile_src(ti)
        for tt in range(NTT):
            fw.dma(x_res.t[:, tt, :], src[tok0 + tt * 128: tok0 + (tt + 1) * 128, :], writes=[x_res.b[tt]], q="pool")

    def final_out(ph, tok0):
        yo = sb(ph, "yo", [128, 2, D], F32, 2)
        ss = ssp
        for tt in range(NTT):
            k = tt % 2
            fw.op("dve", lambda e: e.memset(ss.t[:, tt:tt + 1], 0.0), writes=ss.b)
            fw.op("act", lambda e: e.activation(xnp.t[:, k, :], x_res.t[:, tt, :], AF.Square, scale=1.0 / 32.0, accum_out=ss.t[:, tt:tt + 1]),
                  reads=[x_res.b[tt]], writes=[xnp.b[k]] + ss.b)
            rsqrt_eps(ss.t[:, tt:tt + 1], ss.t[:, tt:tt + 1], ss.b, ss.b)
            fw.op("dve", lambda e: e.scalar_tensor_tensor(yo.t[:, k, :], x_res.t[:, tt, :], ss.t[:, tt:tt + 1], fnw.t[:], ALU.mult, ALU.mult),
                  reads=[x_res.b[tt]] + ss.b + fnw.b, writes=[yo.b[k]])
            fw.dma(y_out[tok0 + tt * 128: tok0 + (tt + 1) * 128, :], yo.t[:, k, :], reads=[yo.b[k]], q="pool")

    load_x(0)
    norm_prep()
    with ExitStack() as ph0:
        norm_to_hT(0, ph0)
        fw.barrier()
    for ti in range(n_state + n_full):
        full = ti >= n_state
        src, tok0 = tile_src(ti)
        last_state = (ti == n_state - 1)

        def after_ffn1(ph, full=full, ti=ti):
            norm_to_hT(1, ph)
            if not full:
                load_x(ti + 1)
            return False

        ffn(0, after=after_ffn1)
        if full and ti == n_state:
            dump("x1", x_res.t[:], x_res.b)

        def after_mixer(ph, full=full, ti=ti, last_state=last_state):
            if last_state:
                hm = hmask.t[:, 0:1]
                fw.op("dve", lambda e: e.tensor_scalar(Hst.t[:], Hst.t[:], hm, None, ALU.mult), reads=Hst.b + hmask.b, writes=Hst.b)
                fw.op("dve", lambda e: e.tensor_scalar(halo.t[:], halo.t[:], hm, None, ALU.mult), reads=halo.b + hmask.b, writes=halo.b)
                fw.op("dve", lambda e: e.tensor_scalar(Vtok.t[:, 0:4, :], Vtok.t[:, 0:4, :], hm, None, ALU.mult), reads=Vtok.b + hmask.b, writes=Vtok.b)
            if full:
                if ti == n_state:
                    dump("x2", x_res.t[:], x_res.b)
                norm_to_hT(3, ph)
            else:
                norm_to_hT(0, ph)

        mixer(full, full or last_state, ti == n_state, after=after_mixer)
        if full:
            def after_ffn2(ph, tok0=tok0, ti=ti):
                final_out(ph, tok0)
                load_x(ti + 1)
                if ti + 1 < n_state + n_full:
                    norm_prep()
                    norm_to_hT(0, ph)
                return True

            ffn(1, after=after_ffn2)
    fw.barrier(("sp", "act", "pool"), dmas=True)
    st.close()
    return nc


def _consts():
    t = np.arange(128)
    ident = np.eye(128, dtype=np.float32)
    U = (t[:, None] <= t[None, :]).astype(np.float32)
    Ls = (t[:, None] > t[None, :]).astype(np.float32)
    nm = np.where(t[:, None] > t[None, :], NEG, 0.0).astype(np.float32)
    return np.ascontiguousarray(np.stack([ident, U, Ls, nm], axis=1))


def _prep_shared(inp):
    f = lambda a: np.ascontiguousarray(np.asarray(a, dtype=np.float32))
    fm = lambda w: np.asarray(w, dtype=np.float32).reshape(8, 128).T
    sh = {}
    sh["w_gu1"] = f(inp["ffn1_w_gu"][0])
    sh["w_gu2"] = f(inp["ffn2_w_gu"][0])
    sh["w_dn1"] = f(inp["ffn1_w_down"][0])
    sh["w_dn2"] = f(inp["ffn2_w_down"][0])
    sh["w_in"] = f(inp["w_in"][0])
    sh["w_bs"] = f(inp["w_branch_ssm"][0])
    sh["w_ba"] = f(inp["w_branch_attn"][0])
    sh["w_out"] = f(inp["w_out"][0])
    sh["nwf"] = f(np.stack([fm(inp["ffn1_norm_w"][0]), fm(inp["mix_norm_w"][0]), fm(inp["ssm_norm_w"][0]),
                            fm(inp["ffn2_norm_w"][0])], axis=1))
    sh["fnw"] = f(np.broadcast_to(np.asarray(inp["final_norm_w"], np.float32)[None, :], (128, D)))
    cwv = np.asarray(inp["conv_w"][0], np.float32)
    sh["cw"] = f(cwv.reshape(4, 12, 128).transpose(2, 1, 0))
    sh["cb"] = f(np.asarray(inp["conv_b"][0], np.float32).reshape(12, 128).T)
    hpv = np.stack([np.asarray(inp["dt_bias"][0], np.float32), np.asarray(inp["A_log"][0], np.float32),
                    np.asarray(inp["D_skip"][0], np.float32)], axis=0)
    sh["hp"] = f(np.broadcast_to(hpv[None], (128, 3, 16)))
    rb = np.asarray(inp["rel_bias"][0], np.float32)
    r = np.arange(128)[:, None]
    l = np.arange(640)[None, :]
    idx = np.clip(l - r, -128, 128) + 128
    kc = r // 64
    qc = l // 64
    valid = (qc >= kc) & (qc <= kc + 8)
    tab = rb[:, idx]
    tab = np.where(valid[None], tab, np.float32(NEG))
    sh["btab"] = f(tab)
    sh["cst"] = _consts()
    return sh


_NC_CACHE = {}


def kernel(**inputs):
    x = np.asarray(inputs["x"], dtype=np.float32)
    sh = _prep_shared(inputs)
    in_maps = []
    for c in range(8):
        b, h = c // 2, c % 2
        m = dict(sh)
        m["x_main"] = np.ascontiguousarray(x[b, h * NTOK:(h + 1) * NTOK])
        m["x_prev"] = np.ascontiguousarray(x[b, 0:NTOK])
        m["hmask"] = np.full((128, 1), float(h), dtype=np.float32)
        in_maps.append(m)
    if "nc" not in _NC_CACHE:
        _NC_CACHE["nc"] = build()
    res = run_bass_kernel_spmd(_NC_CACHE["nc"], in_maps, core_ids=list(range(8)))
    out = np.empty((4, 2 * NTOK, D), dtype=np.float32)
    for c in range(8):
        b, h = c // 2, c % 2
        out[b, h * NTOK:(h + 1) * NTOK] = np.asarray(res.results[c]["y"], dtype=np.float32)
    return out
```

```python
import numpy as np
from contextlib import ExitStack
import concourse.bass as bass
import concourse.mybir as mybir
from concourse.bass_utils import run_bass_kernel_spmd

F32 = mybir.dt.float32
BF16 = mybir.dt.bfloat16
AF = mybir.ActivationFunctionType
ALU = mybir.AluOpType

D = 1024
DFF = 2816
NFC = 22
T = 512
NTT = 4
NTOK = 4096
INP = 7696
EPS = 1e-6
NEG = -30000.0
C_Z, C_X, C_B, C_C, C_DT, C_Q, C_K, C_V, C_GS, C_GA = 0, 1024, 2048, 2304, 2560, 2576, 3600, 4624, 5648, 6672
NS = 5


class Buf:
    __slots__ = ("name", "lw", "rd")

    def __init__(self, name):
        self.name = name
        self.lw = None
        self.rd = {}


class Tn:
    def __init__(self, t, bufs):
        self.t = t
        self.b = bufs


class FW:
    def __init__(self, nc, stack, n_dma_sems=32):
        self.nc = nc
        self.engs = {"pe": nc.tensor, "act": nc.scalar, "dve": nc.vector, "pool": nc.gpsimd, "sp": nc.sync}
        self.sem = {}
        self.cnt = {}
        for k in self.engs:
            self.sem[k] = stack.enter_context(nc.semaphore("tick_" + k))
            self.cnt[k] = 0
        self.dsem = [stack.enter_context(nc.semaphore("dma_%d" % i)) for i in range(n_dma_sems)]
        self.dcnt = [0] * n_dma_sems
        self.dnext2 = {"sp": 0, "pool": 0}
        self.seen = {k: {} for k in self.engs}
        self.ninst = 0

    def _wait(self, eng, dep):
        kind, idx, val = dep
        key = (kind, idx)
        if kind == "e" and idx == eng and eng == "pe":
            return
        if self.seen[eng].get(key, 0) >= val:
            return
        s = self.sem[idx] if kind == "e" else self.dsem[idx]
        self.engs[eng].wait_ge(s, val)
        self.seen[eng][key] = val

    def _deps(self, eng, reads, writes):
        for b in reads:
            if b.lw is not None:
                self._wait(eng, b.lw)
        for b in writes:
            if b.lw is not None:
                self._wait(eng, b.lw)
            for d in b.rd.values():
                self._wait(eng, d)

    def _mark(self, dep, reads, writes):
        for b in writes:
            b.lw = dep
            b.rd = {}
        for b in reads:
            if b in writes:
                continue
            b.rd[(dep[0], dep[1])] = dep

    def op(self, eng, fn, reads=(), writes=()):
        self._deps(eng, reads, writes)
        ins = fn(self.engs[eng])
        self.cnt[eng] += 1
        ins.then_inc(self.sem[eng], 1)
        dep = ("e", eng, self.cnt[eng])
        self._mark(dep, reads, writes)
        self.ninst += 1
        return dep

    def dma(self, out, in_, reads=(), writes=(), q="sp"):
        half = len(self.dsem) // 2
        base = 0 if q == "sp" else half
        i = base + self.dnext2[q]
        self.dnext2[q] = (self.dnext2[q] + 1) % half
        if self.dcnt[i] > 0:
            self._wait(q, ("d", i, self.dcnt[i]))
        self._deps(q, reads, writes)
        ins = self.engs[q].dma_start(out=out, in_=in_)
        self.dcnt[i] += 16
        ins.then_inc(self.dsem[i], 16)
        dep = ("d", i, self.dcnt[i])
        self._mark(dep, reads, writes)
        self.ninst += 1
        return dep

    def barrier(self, engs=("pe", "act", "dve"), dmas=False):
        for e in engs:
            for f in self.engs:
                if self.cnt[f] > 0 and not (f == e and e == "pe"):
                    self._wait(e, ("e", f, self.cnt[f]))
            if dmas:
                for i in range(len(self.dsem)):
                    if self.dcnt[i] > 0:
                        self._wait(e, ("d", i, self.dcnt[i]))


def build(n_state=8, n_full=8, dbg=()):
    nc = bass.Bass("TRN2", target_bir_lowering=False)
    di = lambda n, s, dt=F32: nc.dram_tensor(n, s, dt, kind="ExternalInput").ap()
    x_prev = di("x_prev", [NTOK, D])
    x_main = di("x_main", [NTOK, D])
    w_gu = [di("w_gu1", [D, 2 * DFF]), di("w_gu2", [D, 2 * DFF])]
    w_dn = [di("w_dn1", [DFF, D]), di("w_dn2", [DFF, D])]
    w_in = di("w_in", [D, INP])
    w_bs = di("w_bs", [D, D])
    w_ba = di("w_ba", [D, D])
    w_out = di("w_out", [D, D])
    nwf_d = di("nwf", [128, 4, 8])
    fnw_d = di("fnw", [128, D])
    cw_d = di("cw", [128, 12, 4])
    cb_d = di("cb", [128, 12])
    hp_d = di("hp", [128, 3, 16])
    btab_d = di("btab", [16, 128, 640])
    hmask_d = di("hmask", [128, 1])
    cst_d = di("cst", [128, 4, 128])
    y_out = nc.dram_tensor("y", [NTOK, D], F32, kind="ExternalOutput").ap()
    dbg_out = {}
    for name, shape, dt in dbg:
        dbg_out[name] = nc.dram_tensor("dbg_" + name, list(shape), dt, kind="ExternalOutput").ap()

    st = ExitStack()
    fw = FW(nc, st)

    uid = [0]

    def sb(stack, name, shape, dt, nb=1):
        uid[0] += 1
        t = stack.enter_context(nc.sbuf_tensor("%s_%d" % (name, uid[0]), list(shape), dt))
        return Tn(t, [Buf("%s_%d" % (name, i)) for i in range(nb)])

    x_res = sb(st, "x_res", [128, NTT, D], F32, NTT)
    hT = sb(st, "hT", [128, 8, T], BF16, 1)
    ring = sb(st, "ring", [128, NS, 4096], BF16, NS)
    Hst = sb(st, "Hst", [128, 1024], F32)
    Hbf = sb(st, "Hbf", [128, 1024], BF16)
    EB = sb(st, "EB", [128, 16, 640], BF16)
    kT = sb(st, "kT", [128, 8, 2 * T], BF16)
    Vtok = sb(st, "Vtok", [128, 8, 1152], BF16)
    halo = sb(st, "halo", [128, 12, 3], BF16)
    cst = sb(st, "cst", [128, 4, 128], F32)
    identb = sb(st, "identb", [128, 128], BF16)
    NM8 = sb(st, "NM8", [128, 4, 128], BF16)
    nwf = sb(st, "nwf", [128, 4, 8], F32)
    fnw = sb(st, "fnw", [128, D], F32)
    cw = sb(st, "cw", [128, 12, 4], F32)
    cb = sb(st, "cb", [128, 12], F32)
    hp = sb(st, "hp", [128, 3, 16], F32)
    Aneg = sb(st, "Aneg", [128, 16], F32)
    hmask = sb(st, "hmask", [128, 1], F32)
    onesb = sb(st, "onesb", [128, 64], BF16)
    onesh = sb(st, "onesh", [128, 64], BF16)
    col1 = sb(st, "col1", [128, 1], F32)
    xnp = sb(st, "xnp", [128, 2, D], BF16, 2)
    ssp = sb(st, "ssp", [128, NTT], F32)
    rstdp = sb(st, "rstdp", [128, NTT], F32)

    psf_t = [st.enter_context(nc.psum_tensor("psf%d" % i, [128, 512], F32)) for i in range(6)]
    psf_b = [Buf("psf%d" % i) for i in range(6)]
    psb_t = [st.enter_context(nc.psum_tensor("psb%d" % i, [128, 1024], BF16)) for i in range(2)]
    psb_b = [Buf("psb%d" % i) for i in range(2)]
    pctr = {"f": 0, "b": 0, "r": 0}

    def psf():
        i = pctr["f"]
        pctr["f"] = (i + 1) % 4
        return psf_t[i], psf_b[i]

    def psfix(i):
        return psf_t[i], psf_b[i]

    def psb():
        i = pctr["b"]
        pctr["b"] = (i + 1) % 2
        return psb_t[i], psb_b[i]

    scratch = {}

    def wload(src2d, kc0, nkc, c0, ncols):
        i = pctr["r"]
        pctr["r"] = (i + 1) % NS
        slot2d = ring.t[:, i, 0:nkc * ncols]
        slot = slot2d.rearrange("p (k c) -> p k c", k=nkc)
        key = (src2d.tensor.name, kc0, nkc, c0, ncols)
        if key not in scratch:
            srcap = src2d.rearrange("(kc p) c -> p kc c", p=128)[:, kc0:kc0 + nkc, c0:c0 + ncols]
            fw.dma(slot, srcap, writes=[ring.b[i]], q="pool")
            d = nc.dram_tensor("ws%d" % len(scratch), [128, nkc * ncols], BF16, kind="Internal").ap()
            b = Buf("ws%d" % len(scratch))
            scratch[key] = (d, b)
            fw.dma(d, slot2d, reads=[ring.b[i]], writes=[b], q="sp")
        else:
            d, b = scratch[key]
            fw.dma(slot2d, d, reads=[b], writes=[ring.b[i]], q="sp")
        return slot, ring.b[i]

    def dump(name, ap, bufs):
        if name in dbg_out:
            fw.dma(dbg_out[name], ap, reads=bufs, q="sp")

    fw.dma(cst.t[:], cst_d, writes=cst.b)
    fw.dma(nwf.t[:], nwf_d, writes=nwf.b)
    fw.dma(fnw.t[:], fnw_d, writes=fnw.b)
    fw.dma(cw.t[:], cw_d, writes=cw.b)
    fw.dma(cb.t[:], cb_d, writes=cb.b)
    fw.dma(hp.t[:], hp_d, writes=hp.b)
    fw.dma(hmask.t[:], hmask_d, writes=hmask.b)
    fw.op("dve", lambda e: e.tensor_copy(identb.t[:], cst.t[:, 0, :]), reads=cst.b, writes=identb.b)
    for g in range(4):
        fw.op("dve", lambda e: e.tensor_copy(NM8.t[:, g, :], cst.t[:, 3, :]), reads=cst.b, writes=NM8.b)
    fw.op("dve", lambda e: e.memset(onesb.t[:], 1.0), writes=onesb.b)
    fw.op("dve", lambda e: e.memset(col1.t[:], 1.0), writes=col1.b)
    fw.op("dve", lambda e: e.memset(Hst.t[:], 0.0), writes=Hst.b)
    fw.op("dve", lambda e: e.memset(halo.t[:], 0.0), writes=halo.b)
    fw.op("dve", lambda e: e.memset(kT.t[:], 0.0), writes=kT.b)
    fw.op("dve", lambda e: e.memset(Vtok.t[:], 0.0), writes=Vtok.b)
    fw.op("dve", lambda e: e.memset(Vtok.t[:, 4:8, 0:64], 1.0), writes=Vtok.b)
    fw.op("dve", lambda e: e.memset(Vtok.t[:, 4:8, 1088:1152], 1.0), writes=Vtok.b)
    fw.op("dve", lambda e: e.tensor_copy(onesh.t[:], hmask.t[:, 0:1].to_broadcast([128, 64])), reads=hmask.b, writes=onesh.b)
    fw.op("act", lambda e: e.activation(Aneg.t[:], hp.t[:, 1, :], AF.Exp), reads=hp.b, writes=Aneg.b)
    fw.op("dve", lambda e: e.tensor_scalar(Aneg.t[:], Aneg.t[:], -1.0, None, ALU.mult), reads=Aneg.b, writes=Aneg.b)
    dgw = nc.dram_tensor("dgw", [128, 48, 128], BF16, kind="Internal").ap()
    with ExitStack() as phd:
        dg = sb(phd, "dg", [128, 48, 128], BF16)
        for m in range(12):
            for tap in range(4):
                fw.op("dve", lambda e: e.tensor_scalar(dg.t[:, m * 4 + tap, :], identb.t[:], cw.t[:, m, tap:tap + 1], None, ALU.mult),
                      reads=identb.b + cw.b, writes=dg.b)
        dgdep = fw.dma(dgw, dg.t[:], reads=dg.b)
        fw.barrier(("pe", "act", "dve", "sp"), dmas=True)

    for h in range(16):
        fw.dma(EB.t[:, h, :], btab_d[h], writes=EB.b, q="pool")

    identb_ap = identb.t[:]
    epsc = sb(st, "epsc", [128, 1], F32)
    fw.op("dve", lambda e: e.memset(epsc.t[:], EPS), writes=epsc.b)

    def rsqrt_eps(out_ap, in_ap, rb, wb):
        fw.op("act", lambda e: e.activation(out_ap, in_ap, AF.Ln, bias=epsc.t[:, 0:1]), reads=rb + epsc.b, writes=wb)
        fw.op("act", lambda e: e.activation(out_ap, out_ap, AF.Exp, scale=-0.5), reads=wb, writes=wb)

    def norm_prep():
        fw.op("dve", lambda e: e.memset(ssp.t[:], 0.0), writes=ssp.b)

    def norm_to_hT(ni, ph, src=None):
        xn, ss, rstd = xnp, ssp, rstdp
        if src is None:
            src = x_res
        junk = sb(ph, "njunk", [128, D], BF16)
        for tt in range(NTT):
            fw.op("act", lambda e: e.activation(junk.t[:], src.t[:, tt, :], AF.Square, scale=1.0 / 32.0,
                                                accum_out=ss.t[:, tt:tt + 1]),
                  reads=[src.b[tt]], writes=junk.b + ss.b)
        fw.op("act", lambda e: e.activation(rstd.t[:], ss.t[:], AF.Ln, bias=epsc.t[:, 0:1]), reads=ss.b + epsc.b, writes=rstd.b)
        fw.op("act", lambda e: e.activation(rstd.t[:], rstd.t[:], AF.Exp, scale=-0.5), reads=rstd.b, writes=rstd.b)
        for tt in range(NTT):
            k = tt % 2
            fw.op("act", lambda e: e.activation(xn.t[:, k, :], src.t[:, tt, :], AF.Copy, scale=rstd.t[:, tt:tt + 1]),
                  reads=[src.b[tt]] + rstd.b, writes=[xn.b[k]])
            pt, pb = psb()
            for j in range(8):
                fw.op("pe", lambda e: e.transpose(pt[:, j * 128:(j + 1) * 128], xn.t[:, k, j * 128:(j + 1) * 128], identb_ap),
                      reads=[xn.b[k]] + identb.b, writes=[pb])
            fw.op("dve", lambda e: e.tensor_tensor(hT.t[:, :, tt * 128:(tt + 1) * 128],
                                                   pt[:].rearrange("p (j c) -> p j c", j=8),
                                                   nwf.t[:, ni, :].unsqueeze(2).to_broadcast([128, 8, 128]), ALU.mult),
                  reads=[pb] + nwf.b, writes=hT.b)

    def proj_fm(src2d, c0, nchunks, nkc, rhs, rhs_bufs, evac, ncol=T, evac2=None, evac3=None):
        pend = []
        for u0 in range(0, nchunks, 4):
            nu = min(4, nchunks - u0)
            slot, sbuf_ = wload(src2d, 0, nkc, c0 + u0 * 128, nu * 128)
            for mo in range(nu):
                ps, pb = psf()
                for kc in range(nkc):
                    fw.op("pe", lambda e: e.matmul(ps[:, 0:ncol], slot[:, kc, mo * 128:(mo + 1) * 128], rhs(kc),
                                                   start=(kc == 0), stop=(kc == nkc - 1)),
                          reads=[sbuf_] + rhs_bufs, writes=[pb])
                evac(u0 + mo, ps, pb)
                pend.append(u0 + mo)
                if evac2 is not None and len(pend) >= 2:
                    evac2(pend[-2])
                if evac3 is not None and len(pend) >= 3:
                    evac3(pend[-3])
        if evac2 is not None and pend:
            evac2(pend[-1])
        if evac3 is not None:
            if len(pend) >= 2:
                evac3(pend[-2])
            if pend:
                evac3(pend[-1])

    def proj_tm(src2d, c0, ncols, nkc, lhs, lhs_bufs, evac):
        for u0 in range(0, ncols, 512):
            nc_ = min(512, ncols - u0)
            slot, sbuf_ = wload(src2d, 0, nkc, c0 + u0, nc_)
            for tt in range(NTT):
                ps, pb = psf()
                for kc in range(nkc):
                    fw.op("pe", lambda e: e.matmul(ps[:, 0:nc_], lhs(kc, tt), slot[:, kc, :],
                                                   start=(kc == 0), stop=(kc == nkc - 1)),
                          reads=[sbuf_] + lhs_bufs, writes=[pb])
                evac(u0 // 512, tt, ps, pb)

    def ffn(which, after=None, before=None):
        with ExitStack() as ph:
            actT = sb(ph, "actT", [128, NFC, T], BF16, 1)
            sg = sb(ph, "sg", [128, 2, T], F32, 2)
            norm_prep()
            if before is not None:
                before(ph)
            wgu = w_gu[which]
            wd = w_dn[which]
            for f0 in range(0, NFC, 4):
                nf = min(4, NFC - f0)
                gs, gb = wload(wgu, 0, 8, f0 * 128, nf * 128)
                us, ub = wload(wgu, 0, 8, DFF + f0 * 128, nf * 128)
                for fo in range(nf):
                    f = f0 + fo
                    pg, pgb = psf()
                    pu, pub = psf()
                    for kc in range(8):
                        fw.op("pe", lambda e: e.matmul(pg[:, 0:T], gs[:, kc, fo * 128:(fo + 1) * 128], hT.t[:, kc, :],
                                                       start=(kc == 0), stop=(kc == 7)), reads=[gb] + hT.b, writes=[pgb])
                    for kc in range(8):
                        fw.op("pe", lambda e: e.matmul(pu[:, 0:T], us[:, kc, fo * 128:(fo + 1) * 128], hT.t[:, kc, :],
                                                       start=(kc == 0), stop=(kc == 7)), reads=[ub] + hT.b, writes=[pub])
                    k = f % 2
                    fw.op("act", lambda e: e.activation(sg.t[:, k, :], pg[:, 0:T], AF.Silu), reads=[pgb], writes=[sg.b[k]])
                    fw.op("dve", lambda e: e.tensor_tensor(actT.t[:, f, :], sg.t[:, k, :], pu[:, 0:T], ALU.mult),
                          reads=[sg.b[k], pub], writes=actT.b)
            groups = [(0, 8), (8, 8), (16, 6)]
            for dh in range(2):
                slots = [wload(wd, g0, gn, dh * 512, 512) for (g0, gn) in groups]
                for tt in range(NTT):
                    ps, pb = psf()
                    for gi, (g0, gn) in enumerate(groups):
                        sl, slb = slots[gi]
                        for fo in range(gn):
                            f = g0 + fo
                            fw.op("pe", lambda e: e.matmul(ps[:, 0:512], actT.t[:, f, tt * 128:(tt + 1) * 128], sl[:, fo, :],
                                                           start=(f == 0), stop=(f == NFC - 1)),
                                  reads=[slb] + actT.b, writes=[pb])
                    xs = x_res.t[:, tt, dh * 512:(dh + 1) * 512]
                    fw.op("dve", lambda e: e.scalar_tensor_tensor(xs, ps[:, 0:512], 0.5, xs, ALU.mult, ALU.add),
                          reads=[pb, x_res.b[tt]], writes=[x_res.b[tt]])
            dm = False
            if after is not None:
                dm = after(ph)
            fw.barrier(("pe", "act", "dve"), dmas=bool(dm))

    def mixer(full, need_kv, first_full, after=None):
        with ExitStack() as ph:
            xtok = sb(ph, "xtok", [128, NTT, D], BF16, 1)
            BT = sb(ph, "BT", [128, 2, T], BF16)
            CT = sb(ph, "CT", [128, 2, T], BF16)
            Btok = sb(ph, "Btok", [128, NTT, 256], BF16)
            dt = sb(ph, "dt", [128, NTT, 16], F32)
            if full:
                yssmT = sb(ph, "yssmT", [128, 8, T], BF16)
                yattT = sb(ph, "yattT", [128, 8, T], BF16)
            with ExitStack() as p1:
                pre = sb(p1, "pre", [128, 2, T + 4], BF16, 2)
                xfm = sb(p1, "xfm", [128, 2, T], BF16, 2)
                dgs = sb(p1, "dgs", [128, 48, 128], BF16)
                fw.barrier(("pool",))
                fw.dma(dgs.t[:], dgw, writes=dgs.b, q="pool")
                cps = {}

                def ev_xbc(m, ps, pb):
                    k = m % 2
                    fw.op("act", lambda e: e.copy(pre.t[:, k, 3:T + 3], ps[:, 0:T]), reads=[pb], writes=[pre.b[k]])
                    fw.op("dve", lambda e: e.tensor_copy(pre.t[:, k, 0:3], halo.t[:, m, :]), reads=halo.b, writes=[pre.b[k]])
                    fw.op("dve", lambda e: e.tensor_copy(halo.t[:, m, :], pre.t[:, k, T:T + 3]), reads=[pre.b[k]], writes=halo.b)

                def ev_xbc2(m):
                    k = m % 2
                    pc, pcb = psf()
                    for tap in range(4):
                        fw.op("pe", lambda e: e.matmul(pc[:, 0:T], dgs.t[:, m * 4 + tap, :], pre.t[:, k, tap:tap + T],
                                                       start=(tap == 0), stop=(tap == 3)), reads=dgs.b + [pre.b[k]], writes=[pcb])
                    if m < 8:
                        fw.op("act", lambda e: e.activation(xfm.t[:, k, :], pc[:, 0:T], AF.Silu, bias=cb.t[:, m:m + 1]), reads=[pcb] + cb.b, writes=[xfm.b[k]])
                    elif m < 10:
                        g = m - 8
                        fw.op("act", lambda e: e.activation(BT.t[:, g, :], pc[:, 0:T], AF.Silu, bias=cb.t[:, m:m + 1]), reads=[pcb] + cb.b, writes=BT.b)
                    else:
                        g = m - 10
                        fw.op("act", lambda e: e.activation(CT.t[:, g, :], pc[:, 0:T], AF.Silu, bias=cb.t[:, m:m + 1]), reads=[pcb] + cb.b, writes=CT.b)

                def ev_xbc3(m):
                    k = m % 2
                    if m < 8:
                        pt, ptb = psb()
                        for tt in range(NTT):
                            fw.op("pe", lambda e: e.transpose(pt[:, tt * 128:(tt + 1) * 128], xfm.t[:, k, tt * 128:(tt + 1) * 128], identb_ap),
                                  reads=[xfm.b[k]] + identb.b, writes=[ptb])
                        fw.op("dve", lambda e: e.tensor_copy(xtok.t[:, :, m * 128:(m + 1) * 128], pt[:, 0:512].rearrange("p (t c) -> p t c", t=NTT)),
                              reads=[ptb], writes=xtok.b)
                    elif m < 10:
                        g = m - 8
                        pt, ptb = psb()
                        for tt in range(NTT):
                            fw.op("pe", lambda e: e.transpose(pt[:, tt * 128:(tt + 1) * 128], BT.t[:, g, tt * 128:(tt + 1) * 128], identb_ap),
                                  reads=BT.b + identb.b, writes=[ptb])
                        fw.op("dve", lambda e: e.tensor_copy(Btok.t[:, :, g * 128:(g + 1) * 128], pt[:, 0:512].rearrange("p (t c) -> p t c", t=NTT)),
                              reads=[ptb], writes=Btok.b)

                def ev_dt(u, tt, ps, pb):
                    fw.op("dve", lambda e: e.tensor_tensor(dt.t[:, tt, :], ps[:, 0:16], hp.t[:, 0, :], ALU.add), reads=[pb] + hp.b, writes=dt.b)
                    fw.op("act", lambda e: e.activation(dt.t[:, tt, :], dt.t[:, tt, :], AF.Exp), reads=dt.b, writes=dt.b)
                    fw.op("act", lambda e: e.activation(dt.t[:, tt, :], dt.t[:, tt, :], AF.Ln, bias=col1.t[:, 0:1]), reads=dt.b + col1.b, writes=dt.b)

                proj_tm(w_in, C_DT, 16, 8, lambda kc, tt: hT.t[:, kc, tt * 128:(tt + 1) * 128], hT.b, ev_dt)
                proj_fm(w_in, C_X, 12, 8, lambda kc: hT.t[:, kc, :], hT.b, ev_xbc, evac2=ev_xbc2, evac3=ev_xbc3)

                fw.barrier()
            dump("xtok", xtok.t[:], xtok.b)
            dump("dt", dt.t[:], dt.b)
            early = (not full) and (not need_kv) and (after is not None)
            if early:
                norm_prep()
                after(ph)

            def kv_unit(i):
                if i < 2:
                    slot, sbuf_ = wload(w_in, 0, 8, C_K + i * 512, 512)
                    for mo in range(4):
                        m = i * 4 + mo
                        ps, pb = psf()
                        for kc in range(8):
                            fw.op("pe", lambda e: e.matmul(ps[:, 0:T], slot[:, kc, mo * 128:(mo + 1) * 128], hT.t[:, kc, :],
                                                           start=(kc == 0), stop=(kc == 7)), reads=[sbuf_] + hT.b, writes=[pb])
                        fw.op("act", lambda e: e.copy(kT.t[:, m, T:2 * T], ps[:, 0:T]), reads=[pb], writes=kT.b)
                else:
                    u = i - 2
                    slot, sbuf_ = wload(w_in, 0, 8, C_V + u * 512, 512)
                    for tt in range(NTT):
                        ps, pb = psf()
                        for kc in range(8):
                            fw.op("pe", lambda e: e.matmul(ps[:, 0:512], hT.t[:, kc, tt * 128:(tt + 1) * 128], slot[:, kc, :],
                                                           start=(kc == 0), stop=(kc == 7)), reads=[sbuf_] + hT.b, writes=[pb])
                        fw.op("act", lambda e: e.copy(Vtok.t[:, 4 + tt, 64 + u * 512:64 + (u + 1) * 512], ps[:, 0:512]), reads=[pb], writes=Vtok.b)

            with ExitStack() as p2:
                NB = 2 if full else NTT
                a_t = sb(p2, "a_t", [128, NB, 16], F32, NB)
                wdt = sb(p2, "wdt", [128, NB, 16], F32, NB)
                decr = sb(p2, "decr", [128, NB, 16], F32, NB)
                xw = sb(p2, "xw", [128, NB, D], BF16, NB)
                if full:
                    sz = sb(p2, "sz", [128, 2, D], BF16, 2)
                    zslots = [wload(w_in, 0, 8, C_Z + u * 512, 512) for u in range(2)]
                    Y = sb(p2, "Y", [128, 2, 8, 128], F32, 2)
                    E = sb(p2, "E", [128, 2, 8, 128], BF16, 2)
                    Mm = sb(p2, "Mm", [128, 2, 8, 128], BF16, 2)
                    CBs = sb(p2, "CBs", [128, 2, 128], F32, 2)
                    t1 = sb(p2, "t1", [128, D], F32)
                    t2 = sb(p2, "t2", [128, 512], F32)
                    ynb = sb(p2, "ynb", [128, D], BF16)
                    ebias = sb(p2, "ebias", [128, 2, 16], F32, 2)
                    ecum = sb(p2, "ecum", [128, 2, 16], F32, 2)
                    gss = sb(p2, "gss", [128, 2], F32)
                    jk2 = sb(p2, "jk2", [128, 512], BF16)

                def ssd_pre(tt):
                    k = tt % NB
                    fw.op("dve", lambda e: e.tensor_tensor(a_t.t[:, k, :], dt.t[:, tt, :], Aneg.t[:], ALU.mult), reads=dt.b + Aneg.b, writes=[a_t.b[k]])
                    if full:
                        for u in range(2):
                            zs, zb = zslots[u]
                            pz, pzb = psf()
                            for kc in range(8):
                                fw.op("pe", lambda e: e.matmul(pz[:, 0:512], hT.t[:, kc, tt * 128:(tt + 1) * 128], zs[:, kc, :],
                                                               start=(kc == 0), stop=(kc == 7)), reads=[zb] + hT.b, writes=[pzb])
                            fw.op("act", lambda e: e.activation(sz.t[:, k, u * 512:(u + 1) * 512], pz[:, 0:512], AF.Silu), reads=[pzb], writes=[sz.b[k]])
                    pss, pssb = psf()
                    fw.op("pe", lambda e: e.matmul(pss[:, 0:16], cst.t[:, 2, :], a_t.t[:, k, :], start=True, stop=True), reads=cst.b + [a_t.b[k]], writes=[pssb])
                    fw.op("pe", lambda e: e.matmul(pss[:, 16:32], onesf.t[:], a_t.t[:, k, :], start=True, stop=True), reads=onesf.b + [a_t.b[k]], writes=[pssb])
                    if full:
                        fw.op("pe", lambda e: e.matmul(pss[:, 32:48], cst.t[:, 1, :], a_t.t[:, k, :], start=True, stop=True), reads=cst.b + [a_t.b[k]], writes=[pssb])
                    fw.op("act", lambda e: e.activation(wdt.t[:, k, :], pss[:, 0:16], AF.Exp), reads=[pssb], writes=[wdt.b[k]])
                    fw.op("dve", lambda e: e.tensor_tensor(wdt.t[:, k, :], wdt.t[:, k, :], dt.t[:, tt, :], ALU.mult), reads=[wdt.b[k]] + dt.b, writes=[wdt.b[k]])
                    fw.op("act", lambda e: e.activation(decr.t[:, k, :], pss[:, 16:32], AF.Exp), reads=[pssb], writes=[decr.b[k]])
                    if full:
                        fw.op("act", lambda e: e.activation(ebias.t[:, k, :], dt.t[:, tt, :], AF.Ln), reads=dt.b, writes=[ebias.b[k]])
                        fw.op("dve", lambda e: e.tensor_tensor(ebias.t[:, k, :], ebias.t[:, k, :], pss[:, 32:48], ALU.subtract), reads=[ebias.b[k], pssb], writes=[ebias.b[k]])
                        fw.op("act", lambda e: e.activation(ecum.t[:, k, :], pss[:, 32:48], AF.Exp), reads=[pssb], writes=[ecum.b[k]])
                    fw.op("dve", lambda e: e.tensor_tensor(xw.t[:, k, :].rearrange("p (h c) -> p h c", h=16),
                                                           xtok.t[:, tt, :].rearrange("p (h c) -> p h c", h=16),
                                                           wdt.t[:, k, :].unsqueeze(2).to_broadcast([128, 16, 64]), ALU.mult),
                          reads=xtok.b + [wdt.b[k]], writes=[xw.b[k]])

                def ssd_front(tt, g):
                    k = tt % 2
                    tsl = slice(tt * 128, (tt + 1) * 128)
                    fw.op("dve", lambda e: e.tensor_tensor(Y.t[:, g], cst.t[:, 1, :].unsqueeze(1).to_broadcast([128, 8, 128]),
                                                           a_t.t[:, k, g * 8:(g + 1) * 8].unsqueeze(2).to_broadcast([128, 8, 128]), ALU.mult),
                          reads=cst.b + [a_t.b[k]], writes=[Y.b[g]])
                    pc, pcb = psf()
                    fw.op("pe", lambda e: e.matmul(pc[:, 0:128], BT.t[:, g, tsl], CT.t[:, g, tsl], start=True, stop=True),
                          reads=BT.b + CT.b, writes=[pcb])
                    fw.op("act", lambda e: e.copy(CBs.t[:, g, :], pc[:, 0:128]), reads=[pcb], writes=[CBs.b[g]])

                def ssd_front_r(tt, g):
                    k = tt % 2
                    for q in range(2):
                        prt, prb = psf()
                        fw.op("pe", lambda e: e.matmul(prt[:, 0:512], onesf.t[:], Y.t[:, g, q * 4:(q + 1) * 4, :].rearrange("p h l -> p (h l)"),
                                                       start=True, stop=False), reads=onesf.b + [Y.b[g]], writes=[prb])
                        fw.op("pe", lambda e: e.matmul(prt[:, 0:512], identb_ap, NM8.t[:, 0:4, :].rearrange("p h l -> p (h l)"),
                                                       start=False, stop=True), reads=identb.b + NM8.b, writes=[prb])
                        for hh in range(4):
                            h = g * 8 + q * 4 + hh
                            fw.op("act", lambda e: e.activation(E.t[:, g, q * 4 + hh, :], prt[:, hh * 128:(hh + 1) * 128], AF.Exp,
                                                                bias=ebias.t[:, k, h:h + 1]), reads=[prb, ebias.b[k]], writes=[E.b[g]])

                def ssd_front_b(tt, g):
                    fw.op("dve", lambda e: e.tensor_tensor(Mm.t[:, g], E.t[:, g],
                                                           CBs.t[:, g, :].unsqueeze(1).to_broadcast([128, 8, 128]), ALU.mult),
                          reads=[E.b[g], CBs.b[g]], writes=[Mm.b[g]])

                def ssd_back(tt, g):
                    k = tt % 2
                    tsl = slice(tt * 128, (tt + 1) * 128)
                    gsl = slice(g * 512, (g + 1) * 512)
                    if g == 0:
                        fw.op("act", lambda e: e.copy(Hbf.t[:], Hst.t[:]), reads=Hst.b, writes=Hbf.b)
                    pyt, pyb = psfix(4)
                    for hh in range(8):
                        h = g * 8 + hh
                        fw.op("pe", lambda e: e.matmul(pyt[:, hh * 64:(hh + 1) * 64], Mm.t[:, g, hh, :], xtok.t[:, tt, h * 64:(h + 1) * 64],
                                                       start=True, stop=True), reads=[Mm.b[g]] + xtok.b, writes=[pyb])
                    po, pob = psfix(5)
                    fw.op("pe", lambda e: e.matmul(po[:, 0:512], CT.t[:, g, tsl], Hbf.t[:, gsl], start=True, stop=True),
                          reads=CT.b + Hbf.b, writes=[pob])

                def ssd_back_rest(tt, g):
                    k = tt % 2
                    tsl = slice(tt * 128, (tt + 1) * 128)
                    gsl = slice(g * 512, (g + 1) * 512)
                    pyt, pyb = psfix(4)
                    po, pob = psfix(5)
                    fw.op("dve", lambda e: e.tensor_tensor(t1.t[:, gsl].rearrange("p (h c) -> p h c", h=8),
                                                           po[:, 0:512].rearrange("p (h c) -> p h c", h=8),
                                                           ecum.t[:, k, g * 8:(g + 1) * 8].unsqueeze(2).to_broadcast([128, 8, 64]), ALU.mult),
                          reads=[pob, ecum.b[k]], writes=t1.b)
                    fw.op("dve", lambda e: e.tensor_tensor(t2.t[:].rearrange("p (h c) -> p h c", h=8),
                                                           xtok.t[:, tt, gsl].rearrange("p (h c) -> p h c", h=8),
                                                           hp.t[:, 2, g * 8:(g + 1) * 8].unsqueeze(2).to_broadcast([128, 8, 64]), ALU.mult),
                          reads=xtok.b + hp.b, writes=t2.b)
                    fw.op("dve", lambda e: e.tensor_tensor(t1.t[:, gsl], t1.t[:, gsl], t2.t[:], ALU.add), reads=t1.b + t2.b, writes=t1.b)
                    fw.op("dve", lambda e: e.tensor_tensor(t1.t[:, gsl], t1.t[:, gsl], pyt[:, 0:512], ALU.add), reads=t1.b + [pyb], writes=t1.b)
                    fw.op("dve", lambda e: e.tensor_tensor(t1.t[:, gsl], t1.t[:, gsl], sz.t[:, k, gsl], ALU.mult), reads=t1.b + [sz.b[k]], writes=t1.b)
                    fw.op("dve", lambda e: e.memset(gss.t[:, g:g + 1], 0.0), writes=gss.b)
                    fw.op("act", lambda e: e.activation(jk2.t[:], t1.t[:, gsl], AF.Square, scale=float(512 ** -0.5),
                                                        accum_out=gss.t[:, g:g + 1]), reads=t1.b, writes=jk2.b + gss.b)
                    rsqrt_eps(gss.t[:, g:g + 1], gss.t[:, g:g + 1], gss.b, gss.b)
                    fw.op("dve", lambda e: e.tensor_scalar(ynb.t[:, gsl], t1.t[:, gsl], gss.t[:, g:g + 1], None, ALU.mult),
                          reads=t1.b + gss.b, writes=ynb.b)
                    if g == 1:
                        if tt == 0:
                            dump("yssd", t1.t[:], t1.b)
                        pt, ptb = psb()
                        for j in range(8):
                            fw.op("pe", lambda e: e.transpose(pt[:, j * 128:(j + 1) * 128], ynb.t[:, j * 128:(j + 1) * 128], identb_ap),
                                  reads=ynb.b + identb.b, writes=[ptb])
                        fw.op("dve", lambda e: e.tensor_tensor(yssmT.t[:, :, tsl], pt[:].rearrange("p (j c) -> p j c", j=8),
                                                               nwf.t[:, 2, :].unsqueeze(2).to_broadcast([128, 8, 128]), ALU.mult),
                              reads=[ptb] + nwf.b, writes=yssmT.b)

                def ssd_post(tt):
                    k = tt % NB
                    for g in range(2):
                        pst_, pstb = psf()
                        fw.op("pe", lambda e: e.matmul(pst_[:, 0:512], Btok.t[:, tt, g * 128:(g + 1) * 128], xw.t[:, k, g * 512:(g + 1) * 512],
                                                       start=True, stop=True), reads=Btok.b + [xw.b[k]], writes=[pstb])
                        hs = Hst.t[:, g * 512:(g + 1) * 512]
                        fw.op("dve", lambda e: e.tensor_tensor(hs.rearrange("p (h c) -> p h c", h=8), hs.rearrange("p (h c) -> p h c", h=8),
                                                               decr.t[:, k, g * 8:(g + 1) * 8].unsqueeze(2).to_broadcast([128, 8, 64]), ALU.mult),
                              reads=Hst.b + [decr.b[k]], writes=Hst.b)
                        fw.op("dve", lambda e: e.tensor_tensor(hs, hs, pst_[:, 0:512], ALU.add), reads=Hst.b + [pstb], writes=Hst.b)

                if not full:
                    for tt in range(NTT):
                        ssd_pre(tt)
                    for tt in range(NTT):
                        ssd_post(tt)
                else:
                    steps = [(tt, g) for tt in range(NTT) for g in range(2)]
                    prev = None
                    for (tt, g) in steps:
                        if g == 0:
                            ssd_pre(tt)
                        ssd_front(tt, g)
                        if prev is not None:
                            ssd_back(*prev)
                        ssd_front_r(tt, g)
                        if prev is not None:
                            ssd_back_rest(*prev)
                            if prev[1] == 1:
                                ssd_post(prev[0])
                        ssd_front_b(tt, g)
                        prev = (tt, g)
                        if g == 1:
                            kv_unit(tt)
                    ssd_back(*prev)
                    ssd_back_rest(*prev)
                    ssd_post(prev[0])
                fw.barrier()
            dump("Hst", Hst.t[:], Hst.b)

            if need_kv:
                with ExitStack() as p3:
                    if not full:
                        for i in range(4):
                            kv_unit(i)
                    if full:
                        qT0 = sb(p3, "qT0", [128, 8, T], BF16)
                        qT1 = sb(p3, "qT1", [128, 8, T], BF16)
                        qTs = [qT0, qT1]
                        PT = sb(p3, "PT", [128, 4, 512], BF16, 4)
                        rden = sb(p3, "rden", [128, 512], F32)
                        dtmp = sb(p3, "dtmp", [128, 512], F32)
                        fw.op("dve", lambda e: e.memset(qT0.t[64:128, :, :], 0.0), writes=qT0.b)
                        fw.op("dve", lambda e: e.memset(qT1.t[0:64, :, :], 0.0), writes=qT1.b)

                        def ev_q(m, ps, pb):
                            fw.op("act", lambda e: e.activation(qT0.t[0:64, m, :], ps[0:64, 0:T], AF.Copy, scale=0.125), reads=[pb], writes=qT0.b)
                            fw.op("act", lambda e: e.activation(qT1.t[64:128, m, :], ps[64:128, 0:T], AF.Copy, scale=0.125), reads=[pb], writes=qT1.b)

                        proj_fm(w_in, C_Q, 8, 8, lambda kc: hT.t[:, kc, :], hT.b, ev_q)
                        vaug = sb(p3, "vaug", [128, 2, 8, 128], BF16, 2)

                        def prep_v(h):
                            hb = h % 2
                            if hb == 0:
                                vc, oc, osrc = 0, 64, 1088
                            else:
                                vc, oc, osrc = 64, 0, 0
                            fw.op("dve", lambda e: e.tensor_copy(vaug.t[:, hb, :, vc:vc + 64], Vtok.t[:, :, 64 + 64 * h:128 + 64 * h]),
                                  reads=Vtok.b, writes=[vaug.b[hb]])
                            fw.op("dve", lambda e: e.tensor_copy(vaug.t[:, hb, :, oc:oc + 64], Vtok.t[:, :, osrc:osrc + 64]),
                                  reads=Vtok.b, writes=[vaug.b[hb]])

                        def stageA(hpair, h2, j, k):
                            h = hpair * 2 + h2
                            qt_lo = max(0, j - 4)
                            qt_hi = min(3, j)
                            q0 = qt_lo * 128
                            nq = (qt_hi - qt_lo + 1) * 128
                            rel0 = (qt_lo * 2 + 8 - 2 * j) * 64
                            if j == 0 and h == 0:
                                prep_v(0)
                            if j == 3 and h < 15:
                                prep_v(h + 1)
                            sp_, spb = psf()
                            qh = qTs[h2]
                            fw.op("pe", lambda e: e.matmul(sp_[:, 0:nq], kT.t[:, hpair, j * 128:(j + 1) * 128], qh.t[:, hpair, q0:q0 + nq],
                                                           start=True, stop=False), reads=kT.b + qh.b, writes=[spb])
                            fw.op("pe", lambda e: e.matmul(sp_[:, 0:nq], identb_ap, EB.t[:, h, rel0:rel0 + nq],
                                                           start=False, stop=True), reads=identb.b + EB.b, writes=[spb])
                            fw.op("act", lambda e: e.activation(PT.t[:, k, 0:nq], sp_[:, 0:nq], AF.Exp), reads=[spb], writes=[PT.b[k]])

                        def stageB(hpair, h2, j, k):
                            h = hpair * 2 + h2
                            qt_lo = max(0, j - 4)
                            qt_hi = min(3, j)
                            q0 = qt_lo * 128
                            nq = (qt_hi - qt_lo + 1) * 128
                            pX, pXb = psfix(4 + h2)
                            segs = [(q0, nq)]
                            if 1 <= j <= 3:
                                segs = [(q0, nq - 128), (q0 + nq - 128, 128)]
                            for (sq0, snq) in segs:
                                fw.op("pe", lambda e: e.matmul(pX[:, sq0:sq0 + snq], vaug.t[:, h2, j, :], PT.t[:, k, sq0 - q0:sq0 - q0 + snq],
                                                               start=(j == 0), stop=(j == 7)), reads=[vaug.b[h2], PT.b[k]], writes=[pXb])
                            if h2 == 1 and j == 7:
                                pA, pAb = psfix(4)
                                pB, pBb = psfix(5)
                                fw.op("act", lambda e: e.copy(dtmp.t[0:64, :], pA[64:128, 0:512]), reads=[pAb], writes=dtmp.b)
                                fw.op("act", lambda e: e.copy(dtmp.t[64:128, :], pB[0:64, 0:512]), reads=[pBb], writes=dtmp.b)
                                fw.op("dve", lambda e: e.reciprocal(rden.t[:], dtmp.t[:]), reads=dtmp.b, writes=rden.b)
                                fw.op("dve", lambda e: e.tensor_tensor(yattT.t[0:64, hpair, :], pA[0:64, 0:512], rden.t[0:64, :], ALU.mult),
                                      reads=[pAb] + rden.b, writes=yattT.b)
                                fw.op("dve", lambda e: e.tensor_tensor(yattT.t[64:128, hpair, :], pB[64:128, 0:512], rden.t[64:128, :], ALU.mult),
                                      reads=[pBb] + rden.b, writes=yattT.b)

                        its = [(hpair, h2, j) for hpair in range(8) for h2 in range(2) for j in range(8)]
                        pend = []
                        for n_, (hpair, h2, j) in enumerate(its):
                            k = n_ % 4
                            stageA(hpair, h2, j, k)
                            pend.append((hpair, h2, j, k))
                            if len(pend) > 3:
                                stageB(*pend.pop(0))
                        while pend:
                            stageB(*pend.pop(0))
                    fw.barrier()
                fw.op("act", lambda e: e.copy(kT.t[:, :, 0:T], kT.t[:, :, T:2 * T]), reads=kT.b, writes=kT.b)
                fw.op("dve", lambda e: e.tensor_copy(Vtok.t[:, 0:4, :], Vtok.t[:, 4:8, :]), reads=Vtok.b, writes=Vtok.b)

            if full:
                dump("yssmT", yssmT.t[:], yssmT.b)
                dump("yattT", yattT.t[:], yattT.b)
                with ExitStack() as p4:
                    norm_prep()
                    mT = sb(p4, "mT", [128, 8, T], BF16)
                    s3 = sb(p4, "s3", [128, 2, T], F32, 2)
                    s4 = sb(p4, "s4", [128, 2, T], F32, 2)
                    for u0 in (0, 4):
                        sl1, b1 = wload(w_bs, 0, 8, u0 * 128, 512)
                        sl2, b2 = wload(w_ba, 0, 8, u0 * 128, 512)
                        sl3, b3 = wload(w_in, 0, 8, C_GS + u0 * 128, 512)
                        sl4, b4 = wload(w_in, 0, 8, C_GA + u0 * 128, 512)
                        for mo in range(4):
                            m = u0 + mo
                            k = m % 2
                            msl = slice(mo * 128, (mo + 1) * 128)
                            p1_, p1b = psf()
                            p2_, p2b = psf()
                            p3_, p3b = psf()
                            p4_, p4b = psf()
                            for kc in range(8):
                                fw.op("pe", lambda e: e.matmul(p3_[:, 0:T], sl3[:, kc, msl], hT.t[:, kc, :], start=(kc == 0), stop=(kc == 7)),
                                      reads=[b3] + hT.b, writes=[p3b])
                            for kc in range(8):
                                fw.op("pe", lambda e: e.matmul(p4_[:, 0:T], sl4[:, kc, msl], hT.t[:, kc, :], start=(kc == 0), stop=(kc == 7)),
                                      reads=[b4] + hT.b, writes=[p4b])
                            for kc in range(8):
                                fw.op("pe", lambda e: e.matmul(p1_[:, 0:T], sl1[:, kc, msl], yssmT.t[:, kc, :], start=(kc == 0), stop=(kc == 7)),
                                      reads=[b1] + yssmT.b, writes=[p1b])
                            for kc in range(8):
                                fw.op("pe", lambda e: e.matmul(p2_[:, 0:T], sl2[:, kc, msl], yattT.t[:, kc, :], start=(kc == 0), stop=(kc == 7)),
                                      reads=[b2] + yattT.b, writes=[p2b])
                            fw.op("act", lambda e: e.activation(s3.t[:, k, :], p3_[:, 0:T], AF.Sigmoid), reads=[p3b], writes=[s3.b[k]])
                            fw.op("act", lambda e: e.activation(s4.t[:, k, :], p4_[:, 0:T], AF.Sigmoid), reads=[p4b], writes=[s4.b[k]])
                            fw.op("dve", lambda e: e.tensor_tensor(s3.t[:, k, :], s3.t[:, k, :], p1_[:, 0:T], ALU.mult), reads=[s3.b[k], p1b], writes=[s3.b[k]])
                            fw.op("dve", lambda e: e.tensor_tensor(s4.t[:, k, :], s4.t[:, k, :], p2_[:, 0:T], ALU.mult), reads=[s4.b[k], p2b], writes=[s4.b[k]])
                            fw.op("dve", lambda e: e.tensor_tensor(mT.t[:, m, :], s3.t[:, k, :], s4.t[:, k, :], ALU.add), reads=[s3.b[k], s4.b[k]], writes=mT.b)

                    def ev_out(u, tt, ps, pb):
                        xs = x_res.t[:, tt, u * 512:(u + 1) * 512]
                        fw.op("dve", lambda e: e.tensor_tensor(xs, xs, ps[:, 0:512], ALU.add), reads=[pb, x_res.b[tt]], writes=[x_res.b[tt]])

                    proj_tm(w_out, 0, 1024, 8, lambda kc, tt: mT.t[:, kc, tt * 128:(tt + 1) * 128], mT.b, ev_out)
                    if after is not None:
                        after(p4)
                    fw.barrier()
            if not full:
                if not early:
                    norm_prep()
                    if after is not None:
                        after(ph)
                fw.barrier()

    onesf = sb(st, "onesf", [128, 128], F32)
    fw.op("dve", lambda e: e.memset(onesf.t[:], 1.0), writes=onesf.b)

    def tile_src(ti):
        full = ti >= n_state
        src = x_main if full else x_prev
        tok0 = ((ti - n_state) if full else (NTOK // T - n_state + ti)) * T
        return src, tok0

    def load_x(ti):
        if ti >= n_state + n_full:
            return
        src, tok0 = tile_src(ti)
        for tt in range(NTT):
            fw.dma(x_res.t[:, tt, :], src[tok0 + tt * 128: tok0 + (tt + 1) * 128, :], writes=[x_res.b[tt]], q="pool")

    def final_out(ph, tok0):
        yo = sb(ph, "yo", [128, 2, D], F32, 2)
        ss = ssp
        for tt in range(NTT):
            k = tt % 2
            fw.op("dve", lambda e: e.memset(ss.t[:, tt:tt + 1], 0.0), writes=ss.b)
            fw.op("act", lambda e: e.activation(xnp.t[:, k, :], x_res.t[:, tt, :], AF.Square, scale=1.0 / 32.0, accum_out=ss.t[:, tt:tt + 1]),
                  reads=[x_res.b[tt]], writes=[xnp.b[k]] + ss.b)
            rsqrt_eps(ss.t[:, tt:tt + 1], ss.t[:, tt:tt + 1], ss.b, ss.b)
            fw.op("dve", lambda e: e.scalar_tensor_tensor(yo.t[:, k, :], x_res.t[:, tt, :], ss.t[:, tt:tt + 1], fnw.t[:], ALU.mult, ALU.mult),
                  reads=[x_res.b[tt]] + ss.b + fnw.b, writes=[yo.b[k]])
            fw.dma(y_out[tok0 + tt * 128: tok0 + (tt + 1) * 128, :], yo.t[:, k, :], reads=[yo.b[k]], q="pool")

    load_x(0)
    norm_prep()
    with ExitStack() as ph0:
        norm_to_hT(0, ph0)
        fw.barrier()
    for ti in range(n_state + n_full):
        full = ti >= n_state
        src, tok0 = tile_src(ti)
        last_state = (ti == n_state - 1)

        def after_ffn1(ph, full=full, ti=ti):
            norm_to_hT(1, ph)
            if not full:
                load_x(ti + 1)
            return False

        ffn(0, after=after_ffn1)
        if full and ti == n_state:
            dump("x1", x_res.t[:], x_res.b)

        def after_mixer(ph, full=full, ti=ti, last_state=last_state):
            if last_state:
                hm = hmask.t[:, 0:1]
                fw.op("dve", lambda e: e.tensor_scalar(Hst.t[:], Hst.t[:], hm, None, ALU.mult), reads=Hst.b + hmask.b, writes=Hst.b)
                fw.op("dve", lambda e: e.tensor_scalar(halo.t[:], halo.t[:], hm, None, ALU.mult), reads=halo.b + hmask.b, writes=halo.b)
                fw.op("dve", lambda e: e.tensor_scalar(Vtok.t[:, 0:4, :], Vtok.t[:, 0:4, :], hm, None, ALU.mult), reads=Vtok.b + hmask.b, writes=Vtok.b)
            if full:
                if ti == n_state:
                    dump("x2", x_res.t[:], x_res.b)
                norm_to_hT(3, ph)
            else:
                norm_to_hT(0, ph)

        mixer(full, full or last_state, ti == n_state, after=after_mixer)
        if full:
            nxt = {}

            def before_ffn2(ph, ti=ti, nxt=nxt):
                if ti + 1 < n_state + n_full:
                    xq = sb(ph, "x_nxt", [128, NTT, D], F32, NTT)
                    nxt["x"] = xq
                    fw.barrier(("pool",))
                    src2, tok2 = tile_src(ti + 1)
                    for tt in range(NTT):
                        fw.dma(xq.t[:, tt, :], src2[tok2 + tt * 128: tok2 + (tt + 1) * 128, :], writes=[xq.b[tt]], q="pool")

            def after_ffn2(ph, tok0=tok0, ti=ti, nxt=nxt):
                if "x" in nxt:
                    norm_to_hT(0, ph, src=nxt["x"])
                final_out(ph, tok0)
                if "x" in nxt:
                    xq = nxt["x"]
                    for tt in range(NTT):
                        eng = "dve" if tt % 2 == 0 else "act"
                        if eng == "dve":
                            fw.op("dve", lambda e: e.tensor_copy(x_res.t[:, tt, :], xq.t[:, tt, :]), reads=[xq.b[tt]], writes=[x_res.b[tt]])
                        else:
                            fw.op("act", lambda e: e.copy(x_res.t[:, tt, :], xq.t[:, tt, :]), reads=[xq.b[tt]], writes=[x_res.b[tt]])
                return True

            ffn(1, after=after_ffn2, before=before_ffn2)
    fw.barrier(("sp", "act", "pool"), dmas=True)
    st.close()
    return nc


def _consts():
    t = np.arange(128)
    ident = np.eye(128, dtype=np.float32)
    U = (t[:, None] <= t[None, :]).astype(np.float32)
    Ls = (t[:, None] > t[None, :]).astype(np.float32)
    nm = np.where(t[:, None] > t[None, :], NEG, 0.0).astype(np.float32)
    return np.ascontiguousarray(np.stack([ident, U, Ls, nm], axis=1))


def _prep_shared(inp):
    f = lambda a: np.ascontiguousarray(np.asarray(a, dtype=np.float32))
    fm = lambda w: np.asarray(w, dtype=np.float32).reshape(8, 128).T
    sh = {}
    sh["w_gu1"] = f(inp["ffn1_w_gu"][0])
    sh["w_gu2"] = f(inp["ffn2_w_gu"][0])
    sh["w_dn1"] = f(inp["ffn1_w_down"][0])
    sh["w_dn2"] = f(inp["ffn2_w_down"][0])
    sh["w_in"] = f(inp["w_in"][0])
    sh["w_bs"] = f(inp["w_branch_ssm"][0])
    sh["w_ba"] = f(inp["w_branch_attn"][0])
    sh["w_out"] = f(inp["w_out"][0])
    sh["nwf"] = f(np.stack([fm(inp["ffn1_norm_w"][0]), fm(inp["mix_norm_w"][0]), fm(inp["ssm_norm_w"][0]),
                            fm(inp["ffn2_norm_w"][0])], axis=1))
    sh["fnw"] = f(np.broadcast_to(np.asarray(inp["final_norm_w"], np.float32)[None, :], (128, D)))
    cwv = np.asarray(inp["conv_w"][0], np.float32)
    sh["cw"] = f(cwv.reshape(4, 12, 128).transpose(2, 1, 0))
    sh["cb"] = f(np.asarray(inp["conv_b"][0], np.float32).reshape(12, 128).T)
    hpv = np.stack([np.asarray(inp["dt_bias"][0], np.float32), np.asarray(inp["A_log"][0], np.float32),
                    np.asarray(inp["D_skip"][0], np.float32)], axis=0)
    sh["hp"] = f(np.broadcast_to(hpv[None], (128, 3, 16)))
    rb = np.asarray(inp["rel_bias"][0], np.float32)
    r = np.arange(128)[:, None]
    l = np.arange(640)[None, :]
    idx = np.clip(l - r, -128, 128) + 128
    kc = r // 64
    qc = l // 64
    valid = (qc >= kc) & (qc <= kc + 8)
    tab = rb[:, idx]
    tab = np.where(valid[None], tab, np.float32(NEG))
    sh["btab"] = f(tab)
    sh["cst"] = _consts()
    return sh


_NC_CACHE = {}


def kernel(**inputs):
    x = np.asarray(inputs["x"], dtype=np.float32)
    sh = _prep_shared(inputs)
    in_maps = []
    for c in range(8):
        b, h = c // 2, c % 2
        m = dict(sh)
        m["x_main"] = np.ascontiguousarray(x[b, h * NTOK:(h + 1) * NTOK])
        m["x_prev"] = np.ascontiguousarray(x[b, 0:NTOK])
        m["hmask"] = np.full((128, 1), float(h), dtype=np.float32)
        in_maps.append(m)
    if "nc" not in _NC_CACHE:
        _NC_CACHE["nc"] = build()
    res = run_bass_kernel_spmd(_NC_CACHE["nc"], in_maps, core_ids=list(range(8)))
    out = np.empty((4, 2 * NTOK, D), dtype=np.float32)
    for c in range(8):
        b, h = c // 2, c % 2
        out[b, h * NTOK:(h + 1) * NTOK] = np.asarray(res.results[c]["y"], dtype=np.float32)
    return out
```

```python
import numpy as np
from contextlib import ExitStack
import concourse.bass as bass
import concourse.mybir as mybir
from concourse.bass_utils import run_bass_kernel_spmd

F32 = mybir.dt.float32
BF16 = mybir.dt.bfloat16
AF = mybir.ActivationFunctionType
ALU = mybir.AluOpType

D = 1024
DFF = 2816
NFC = 22
T = 512
NTT = 4
NTOK = 4096
INP = 7696
EPS = 1e-6
NEG = -30000.0
C_Z, C_X, C_B, C_C, C_DT, C_Q, C_K, C_V, C_GS, C_GA = 0, 1024, 2048, 2304, 2560, 2576, 3600, 4624, 5648, 6672
NS = 5


class Buf:
    __slots__ = ("name", "lw", "rd")

    def __init__(self, name):
        self.name = name
        self.lw = None
        self.rd = {}


class Tn:
    def __init__(self, t, bufs):
        self.t = t
        self.b = bufs


class FW:
    def __init__(self, nc, stack, n_dma_sems=32):
        self.nc = nc
        self.engs = {"pe": nc.tensor, "act": nc.scalar, "dve": nc.vector, "pool": nc.gpsimd, "sp": nc.sync}
        self.sem = {}
        self.cnt = {}
        for k in self.engs:
            self.sem[k] = stack.enter_context(nc.semaphore("tick_" + k))
            self.cnt[k] = 0
        self.dsem = [stack.enter_context(nc.semaphore("dma_%d" % i)) for i in range(n_dma_sems)]
        self.dcnt = [0] * n_dma_sems
        self.dnext2 = {"sp": 0, "pool": 0}
        self.seen = {k: {} for k in self.engs}
        self.ninst = 0

    def _wait(self, eng, dep):
        kind, idx, val = dep
        key = (kind, idx)
        if kind == "e" and idx == eng and eng == "pe":
            return
        if self.seen[eng].get(key, 0) >= val:
            return
        s = self.sem[idx] if kind == "e" else self.dsem[idx]
        self.engs[eng].wait_ge(s, val)
        self.seen[eng][key] = val

    def _deps(self, eng, reads, writes):
        for b in reads:
            if b.lw is not None:
                self._wait(eng, b.lw)
        for b in writes:
            if b.lw is not None:
                self._wait(eng, b.lw)
            for d in b.rd.values():
                self._wait(eng, d)

    def _mark(self, dep, reads, writes):
        for b in writes:
            b.lw = dep
            b.rd = {}
        for b in reads:
            if b in writes:
                continue
            b.rd[(dep[0], dep[1])] = dep

    def op(self, eng, fn, reads=(), writes=()):
        self._deps(eng, reads, writes)
        ins = fn(self.engs[eng])
        self.cnt[eng] += 1
        ins.then_inc(self.sem[eng], 1)
        dep = ("e", eng, self.cnt[eng])
        self._mark(dep, reads, writes)
        self.ninst += 1
        return dep

    def dma(self, out, in_, reads=(), writes=(), q="sp"):
        half = len(self.dsem) // 2
        base = 0 if q == "sp" else half
        i = base + self.dnext2[q]
        self.dnext2[q] = (self.dnext2[q] + 1) % half
        if self.dcnt[i] > 0:
            self._wait(q, ("d", i, self.dcnt[i]))
        self._deps(q, reads, writes)
        ins = self.engs[q].dma_start(out=out, in_=in_)
        self.dcnt[i] += 16
        ins.then_inc(self.dsem[i], 16)
        dep = ("d", i, self.dcnt[i])
        self._mark(dep, reads, writes)
        self.ninst += 1
        return dep

    def barrier(self, engs=("pe", "act", "dve"), dmas=False):
        for e in engs:
            for f in self.engs:
                if self.cnt[f] > 0 and not (f == e and e == "pe"):
                    self._wait(e, ("e", f, self.cnt[f]))
            if dmas:
                for i in range(len(self.dsem)):
                    if self.dcnt[i] > 0:
                        self._wait(e, ("d", i, self.dcnt[i]))


def build(n_state=8, n_full=8, dbg=()):
    nc = bass.Bass("TRN2", target_bir_lowering=False)
    di = lambda n, s, dt=F32: nc.dram_tensor(n, s, dt, kind="ExternalInput").ap()
    x_prev = di("x_prev", [NTOK, D])
    x_main = di("x_main", [NTOK, D])
    w_gu = [di("w_gu1", [D, 2 * DFF]), di("w_gu2", [D, 2 * DFF])]
    w_dn = [di("w_dn1", [DFF, D]), di("w_dn2", [DFF, D])]
    w_in = di("w_in", [D, INP])
    w_bs = di("w_bs", [D, D])
    w_ba = di("w_ba", [D, D])
    w_out = di("w_out", [D, D])
    nwf_d = di("nwf", [128, 4, 8])
    fnw_d = di("fnw", [128, D])
    cw_d = di("cw", [128, 12, 4])
    cb_d = di("cb", [128, 12])
    hp_d = di("hp", [128, 3, 16])
    btab_d = di("btab", [16, 128, 640])
    hmask_d = di("hmask", [128, 1])
    cst_d = di("cst", [128, 4, 128])
    y_out = nc.dram_tensor("y", [NTOK, D], F32, kind="ExternalOutput").ap()
    dbg_out = {}
    for name, shape, dt in dbg:
        dbg_out[name] = nc.dram_tensor("dbg_" + name, list(shape), dt, kind="ExternalOutput").ap()

    st = ExitStack()
    fw = FW(nc, st)

    uid = [0]

    def sb(stack, name, shape, dt, nb=1):
        uid[0] += 1
        t = stack.enter_context(nc.sbuf_tensor("%s_%d" % (name, uid[0]), list(shape), dt))
        return Tn(t, [Buf("%s_%d" % (name, i)) for i in range(nb)])

    x_res = sb(st, "x_res", [128, NTT, D], F32, NTT)
    hT = sb(st, "hT", [128, 8, T], BF16, 1)
    ring = sb(st, "ring", [128, NS, 4096], BF16, NS)
    Hst = sb(st, "Hst", [128, 1024], F32)
    Hbf = sb(st, "Hbf", [128, 1024], BF16)
    EB = sb(st, "EB", [128, 16, 640], BF16)
    kT = sb(st, "kT", [128, 8, 2 * T], BF16)
    Vtok = sb(st, "Vtok", [128, 8, 1152], BF16)
    halo = sb(st, "halo", [128, 12, 3], BF16)
    cst = sb(st, "cst", [128, 4, 128], F32)
    identb = sb(st, "identb", [128, 128], BF16)
    NM8 = sb(st, "NM8", [128, 4, 128], BF16)
    nwf = sb(st, "nwf", [128, 4, 8], F32)
    fnw = sb(st, "fnw", [128, D], F32)
    cw = sb(st, "cw", [128, 12, 4], F32)
    cb = sb(st, "cb", [128, 12], F32)
    hp = sb(st, "hp", [128, 3, 16], F32)
    Aneg = sb(st, "Aneg", [128, 16], F32)
    hmask = sb(st, "hmask", [128, 1], F32)
    onesb = sb(st, "onesb", [128, 64], BF16)
    onesh = sb(st, "onesh", [128, 64], BF16)
    col1 = sb(st, "col1", [128, 1], F32)
    xnp = sb(st, "xnp", [128, 2, D], BF16, 2)
    ssp = sb(st, "ssp", [128, NTT], F32)
    rstdp = sb(st, "rstdp", [128, NTT], F32)

    psf_t = [st.enter_context(nc.psum_tensor("psf%d" % i, [128, 512], F32)) for i in range(6)]
    psf_b = [Buf("psf%d" % i) for i in range(6)]
    psb_t = [st.enter_context(nc.psum_tensor("psb%d" % i, [128, 1024], BF16)) for i in range(2)]
    psb_b = [Buf("psb%d" % i) for i in range(2)]
    pctr = {"f": 0, "b": 0, "r": 0}

    def psf():
        i = pctr["f"]
        pctr["f"] = (i + 1) % 4
        return psf_t[i], psf_b[i]

    def psfix(i):
        return psf_t[i], psf_b[i]

    psb_f32 = [psb_t[i][:].bitcast(F32) for i in range(2)]

    def acc_bank(hpair, h2):
        if hpair % 2 == 0:
            return psf_t[4 + h2], psf_b[4 + h2]
        return psb_f32[h2], psb_b[h2]

    def psb():
        i = pctr["b"]
        pctr["b"] = (i + 1) % 2
        return psb_t[i], psb_b[i]

    scratch = {}

    def wload(src2d, kc0, nkc, c0, ncols):
        i = pctr["r"]
        pctr["r"] = (i + 1) % NS
        slot2d = ring.t[:, i, 0:nkc * ncols]
        slot = slot2d.rearrange("p (k c) -> p k c", k=nkc)
        key = (src2d.tensor.name, kc0, nkc, c0, ncols)
        if key not in scratch:
            srcap = src2d.rearrange("(kc p) c -> p kc c", p=128)[:, kc0:kc0 + nkc, c0:c0 + ncols]
            fw.dma(slot, srcap, writes=[ring.b[i]], q="pool")
            d = nc.dram_tensor("ws%d" % len(scratch), [128, nkc * ncols], BF16, kind="Internal").ap()
            b = Buf("ws%d" % len(scratch))
            scratch[key] = (d, b)
            fw.dma(d, slot2d, reads=[ring.b[i]], writes=[b], q="sp")
        else:
            d, b = scratch[key]
            fw.dma(slot2d, d, reads=[b], writes=[ring.b[i]], q="sp")
        return slot, ring.b[i]

    def dump(name, ap, bufs):
        if name in dbg_out:
            fw.dma(dbg_out[name], ap, reads=bufs, q="sp")

    fw.dma(cst.t[:], cst_d, writes=cst.b)
    fw.dma(nwf.t[:], nwf_d, writes=nwf.b)
    fw.dma(fnw.t[:], fnw_d, writes=fnw.b)
    fw.dma(cw.t[:], cw_d, writes=cw.b)
    fw.dma(cb.t[:], cb_d, writes=cb.b)
    fw.dma(hp.t[:], hp_d, writes=hp.b)
    fw.dma(hmask.t[:], hmask_d, writes=hmask.b)
    fw.op("dve", lambda e: e.tensor_copy(identb.t[:], cst.t[:, 0, :]), reads=cst.b, writes=identb.b)
    for g in range(4):
        fw.op("dve", lambda e: e.tensor_copy(NM8.t[:, g, :], cst.t[:, 3, :]), reads=cst.b, writes=NM8.b)
    fw.op("dve", lambda e: e.memset(onesb.t[:], 1.0), writes=onesb.b)
    fw.op("dve", lambda e: e.memset(col1.t[:], 1.0), writes=col1.b)
    fw.op("dve", lambda e: e.memset(Hst.t[:], 0.0), writes=Hst.b)
    fw.op("dve", lambda e: e.memset(halo.t[:], 0.0), writes=halo.b)
    fw.op("dve", lambda e: e.memset(kT.t[:], 0.0), writes=kT.b)
    fw.op("dve", lambda e: e.memset(Vtok.t[:], 0.0), writes=Vtok.b)
    fw.op("dve", lambda e: e.memset(Vtok.t[:, 4:8, 0:64], 1.0), writes=Vtok.b)
    fw.op("dve", lambda e: e.memset(Vtok.t[:, 4:8, 1088:1152], 1.0), writes=Vtok.b)
    fw.op("dve", lambda e: e.tensor_copy(onesh.t[:], hmask.t[:, 0:1].to_broadcast([128, 64])), reads=hmask.b, writes=onesh.b)
    fw.op("act", lambda e: e.activation(Aneg.t[:], hp.t[:, 1, :], AF.Exp), reads=hp.b, writes=Aneg.b)
    fw.op("dve", lambda e: e.tensor_scalar(Aneg.t[:], Aneg.t[:], -1.0, None, ALU.mult), reads=Aneg.b, writes=Aneg.b)
    dgw = nc.dram_tensor("dgw", [128, 48, 128], BF16, kind="Internal").ap()
    with ExitStack() as phd:
        dg = sb(phd, "dg", [128, 48, 128], BF16)
        for m in range(12):
            for tap in range(4):
                fw.op("dve", lambda e: e.tensor_scalar(dg.t[:, m * 4 + tap, :], identb.t[:], cw.t[:, m, tap:tap + 1], None, ALU.mult),
                      reads=identb.b + cw.b, writes=dg.b)
        dgdep = fw.dma(dgw, dg.t[:], reads=dg.b)
        fw.barrier(("pe", "act", "dve", "sp"), dmas=True)

    for h in range(16):
        fw.dma(EB.t[:, h, :], btab_d[h], writes=EB.b, q="pool")

    identb_ap = identb.t[:]
    epsc = sb(st, "epsc", [128, 1], F32)
    fw.op("dve", lambda e: e.memset(epsc.t[:], EPS), writes=epsc.b)

    def rsqrt_eps(out_ap, in_ap, rb, wb):
        fw.op("act", lambda e: e.activation(out_ap, in_ap, AF.Ln, bias=epsc.t[:, 0:1]), reads=rb + epsc.b, writes=wb)
        fw.op("act", lambda e: e.activation(out_ap, out_ap, AF.Exp, scale=-0.5), reads=wb, writes=wb)

    def norm_prep():
        fw.op("dve", lambda e: e.memset(ssp.t[:], 0.0), writes=ssp.b)

    def norm_to_hT(ni, ph, src=None):
        xn, ss, rstd = xnp, ssp, rstdp
        if src is None:
            src = x_res
        junk = sb(ph, "njunk", [128, D], BF16)
        for tt in range(NTT):
            fw.op("act", lambda e: e.activation(junk.t[:], src.t[:, tt, :], AF.Square, scale=1.0 / 32.0,
                                                accum_out=ss.t[:, tt:tt + 1]),
                  reads=[src.b[tt]], writes=junk.b + ss.b)
        fw.op("act", lambda e: e.activation(rstd.t[:], ss.t[:], AF.Ln, bias=epsc.t[:, 0:1]), reads=ss.b + epsc.b, writes=rstd.b)
        fw.op("act", lambda e: e.activation(rstd.t[:], rstd.t[:], AF.Exp, scale=-0.5), reads=rstd.b, writes=rstd.b)
        for tt in range(NTT):
            k = tt % 2
            fw.op("act", lambda e: e.activation(xn.t[:, k, :], src.t[:, tt, :], AF.Copy, scale=rstd.t[:, tt:tt + 1]),
                  reads=[src.b[tt]] + rstd.b, writes=[xn.b[k]])
            pt, pb = psb()
            for j in range(8):
                fw.op("pe", lambda e: e.transpose(pt[:, j * 128:(j + 1) * 128], xn.t[:, k, j * 128:(j + 1) * 128], identb_ap),
                      reads=[xn.b[k]] + identb.b, writes=[pb])
            fw.op("dve", lambda e: e.tensor_tensor(hT.t[:, :, tt * 128:(tt + 1) * 128],
                                                   pt[:].rearrange("p (j c) -> p j c", j=8),
                                                   nwf.t[:, ni, :].unsqueeze(2).to_broadcast([128, 8, 128]), ALU.mult),
                  reads=[pb] + nwf.b, writes=hT.b)

    def proj_fm(src2d, c0, nchunks, nkc, rhs, rhs_bufs, evac, ncol=T, evac2=None, evac3=None):
        pend = []
        for u0 in range(0, nchunks, 4):
            nu = min(4, nchunks - u0)
            slot, sbuf_ = wload(src2d, 0, nkc, c0 + u0 * 128, nu * 128)
            for mo in range(nu):
                ps, pb = psf()
                for kc in range(nkc):
                    fw.op("pe", lambda e: e.matmul(ps[:, 0:ncol], slot[:, kc, mo * 128:(mo + 1) * 128], rhs(kc),
                                                   start=(kc == 0), stop=(kc == nkc - 1)),
                          reads=[sbuf_] + rhs_bufs, writes=[pb])
                evac(u0 + mo, ps, pb)
                pend.append(u0 + mo)
                if evac2 is not None and len(pend) >= 2:
                    evac2(pend[-2])
                if evac3 is not None and len(pend) >= 3:
                    evac3(pend[-3])
        if evac2 is not None and pend:
            evac2(pend[-1])
        if evac3 is not None:
            if len(pend) >= 2:
                evac3(pend[-2])
            if pend:
                evac3(pend[-1])

    def proj_tm(src2d, c0, ncols, nkc, lhs, lhs_bufs, evac):
        for u0 in range(0, ncols, 512):
            nc_ = min(512, ncols - u0)
            slot, sbuf_ = wload(src2d, 0, nkc, c0 + u0, nc_)
            for tt in range(NTT):
                ps, pb = psf()
                for kc in range(nkc):
                    fw.op("pe", lambda e: e.matmul(ps[:, 0:nc_], lhs(kc, tt), slot[:, kc, :],
                                                   start=(kc == 0), stop=(kc == nkc - 1)),
                          reads=[sbuf_] + lhs_bufs, writes=[pb])
                evac(u0 // 512, tt, ps, pb)

    def ffn(which, after=None, before=None):
        with ExitStack() as ph:
            actT = sb(ph, "actT", [128, NFC, T], BF16, 1)
            sg = sb(ph, "sg", [128, 2, T], F32, 2)
            norm_prep()
            if before is not None:
                before(ph)
            wgu = w_gu[which]
            wd = w_dn[which]
            for f0 in range(0, NFC, 4):
                nf = min(4, NFC - f0)
                gs, gb = wload(wgu, 0, 8, f0 * 128, nf * 128)
                us, ub = wload(wgu, 0, 8, DFF + f0 * 128, nf * 128)
                for fo in range(nf):
                    f = f0 + fo
                    pg, pgb = psf()
                    pu, pub = psf()
                    for kc in range(8):
                        fw.op("pe", lambda e: e.matmul(pg[:, 0:T], gs[:, kc, fo * 128:(fo + 1) * 128], hT.t[:, kc, :],
                                                       start=(kc == 0), stop=(kc == 7)), reads=[gb] + hT.b, writes=[pgb])
                    for kc in range(8):
                        fw.op("pe", lambda e: e.matmul(pu[:, 0:T], us[:, kc, fo * 128:(fo + 1) * 128], hT.t[:, kc, :],
                                                       start=(kc == 0), stop=(kc == 7)), reads=[ub] + hT.b, writes=[pub])
                    k = f % 2
                    fw.op("act", lambda e: e.activation(sg.t[:, k, :], pg[:, 0:T], AF.Silu), reads=[pgb], writes=[sg.b[k]])
                    fw.op("dve", lambda e: e.tensor_tensor(actT.t[:, f, :], sg.t[:, k, :], pu[:, 0:T], ALU.mult),
                          reads=[sg.b[k], pub], writes=actT.b)
            groups = [(0, 8), (8, 8), (16, 6)]
            for dh in range(2):
                slots = [wload(wd, g0, gn, dh * 512, 512) for (g0, gn) in groups]
                for tt in range(NTT):
                    ps, pb = psf()
                    for gi, (g0, gn) in enumerate(groups):
                        sl, slb = slots[gi]
                        for fo in range(gn):
                            f = g0 + fo
                            fw.op("pe", lambda e: e.matmul(ps[:, 0:512], actT.t[:, f, tt * 128:(tt + 1) * 128], sl[:, fo, :],
                                                           start=(f == 0), stop=(f == NFC - 1)),
                                  reads=[slb] + actT.b, writes=[pb])
                    xs = x_res.t[:, tt, dh * 512:(dh + 1) * 512]
                    fw.op("dve", lambda e: e.scalar_tensor_tensor(xs, ps[:, 0:512], 0.5, xs, ALU.mult, ALU.add),
                          reads=[pb, x_res.b[tt]], writes=[x_res.b[tt]])
            dm = False
            if after is not None:
                dm = after(ph)
            fw.barrier(("pe", "act", "dve"), dmas=bool(dm))

    def mixer(full, need_kv, first_full, after=None):
        with ExitStack() as ph:
            xtok = sb(ph, "xtok", [128, NTT, D], BF16, 1)
            BT = sb(ph, "BT", [128, 2, T], BF16)
            CT = sb(ph, "CT", [128, 2, T], BF16)
            Btok = sb(ph, "Btok", [128, NTT, 256], BF16)
            dt = sb(ph, "dt", [128, NTT, 16], F32)
            if full:
                yssmT = sb(ph, "yssmT", [128, 8, T], BF16)
                yattT = sb(ph, "yattT", [128, 8, T], BF16)
            with ExitStack() as p1:
                pre = sb(p1, "pre", [128, 2, T + 4], BF16, 2)
                xfm = sb(p1, "xfm", [128, 2, T], BF16, 2)
                dgs = sb(p1, "dgs", [128, 48, 128], BF16)
                fw.barrier(("pool",))
                fw.dma(dgs.t[:], dgw, writes=dgs.b, q="pool")
                cps = {}

                def ev_xbc(m, ps, pb):
                    k = m % 2
                    fw.op("act", lambda e: e.copy(pre.t[:, k, 3:T + 3], ps[:, 0:T]), reads=[pb], writes=[pre.b[k]])
                    fw.op("dve", lambda e: e.tensor_copy(pre.t[:, k, 0:3], halo.t[:, m, :]), reads=halo.b, writes=[pre.b[k]])
                    fw.op("dve", lambda e: e.tensor_copy(halo.t[:, m, :], pre.t[:, k, T:T + 3]), reads=[pre.b[k]], writes=halo.b)

                def ev_xbc2(m):
                    k = m % 2
                    pc, pcb = psf()
                    for tap in range(4):
                        fw.op("pe", lambda e: e.matmul(pc[:, 0:T], dgs.t[:, m * 4 + tap, :], pre.t[:, k, tap:tap + T],
                                                       start=(tap == 0), stop=(tap == 3)), reads=dgs.b + [pre.b[k]], writes=[pcb])
                    if m < 8:
                        fw.op("act", lambda e: e.activation(xfm.t[:, k, :], pc[:, 0:T], AF.Silu, bias=cb.t[:, m:m + 1]), reads=[pcb] + cb.b, writes=[xfm.b[k]])
                    elif m < 10:
                        g = m - 8
                        fw.op("act", lambda e: e.activation(BT.t[:, g, :], pc[:, 0:T], AF.Silu, bias=cb.t[:, m:m + 1]), reads=[pcb] + cb.b, writes=BT.b)
                    else:
                        g = m - 10
                        fw.op("act", lambda e: e.activation(CT.t[:, g, :], pc[:, 0:T], AF.Silu, bias=cb.t[:, m:m + 1]), reads=[pcb] + cb.b, writes=CT.b)

                def ev_xbc3(m):
                    k = m % 2
                    if m < 8:
                        pt, ptb = psb()
                        for tt in range(NTT):
                            fw.op("pe", lambda e: e.transpose(pt[:, tt * 128:(tt + 1) * 128], xfm.t[:, k, tt * 128:(tt + 1) * 128], identb_ap),
                                  reads=[xfm.b[k]] + identb.b, writes=[ptb])
                        fw.op("dve", lambda e: e.tensor_copy(xtok.t[:, :, m * 128:(m + 1) * 128], pt[:, 0:512].rearrange("p (t c) -> p t c", t=NTT)),
                              reads=[ptb], writes=xtok.b)
                    elif m < 10:
                        g = m - 8
                        pt, ptb = psb()
                        for tt in range(NTT):
                            fw.op("pe", lambda e: e.transpose(pt[:, tt * 128:(tt + 1) * 128], BT.t[:, g, tt * 128:(tt + 1) * 128], identb_ap),
                                  reads=BT.b + identb.b, writes=[ptb])
                        fw.op("dve", lambda e: e.tensor_copy(Btok.t[:, :, g * 128:(g + 1) * 128], pt[:, 0:512].rearrange("p (t c) -> p t c", t=NTT)),
                              reads=[ptb], writes=Btok.b)

                def ev_dt(u, tt, ps, pb):
                    fw.op("dve", lambda e: e.tensor_tensor(dt.t[:, tt, :], ps[:, 0:16], hp.t[:, 0, :], ALU.add), reads=[pb] + hp.b, writes=dt.b)
                    fw.op("act", lambda e: e.activation(dt.t[:, tt, :], dt.t[:, tt, :], AF.Exp), reads=dt.b, writes=dt.b)
                    fw.op("act", lambda e: e.activation(dt.t[:, tt, :], dt.t[:, tt, :], AF.Ln, bias=col1.t[:, 0:1]), reads=dt.b + col1.b, writes=dt.b)

                proj_tm(w_in, C_DT, 16, 8, lambda kc, tt: hT.t[:, kc, tt * 128:(tt + 1) * 128], hT.b, ev_dt)
                proj_fm(w_in, C_X, 12, 8, lambda kc: hT.t[:, kc, :], hT.b, ev_xbc, evac2=ev_xbc2, evac3=ev_xbc3)

                fw.barrier()
            dump("xtok", xtok.t[:], xtok.b)
            dump("dt", dt.t[:], dt.b)
            early = (not full) and (not need_kv) and (after is not None)
            if early:
                norm_prep()
                after(ph)

            def kv_unit(i):
                if i < 2:
                    slot, sbuf_ = wload(w_in, 0, 8, C_K + i * 512, 512)
                    for mo in range(4):
                        m = i * 4 + mo
                        ps, pb = psf()
                        for kc in range(8):
                            fw.op("pe", lambda e: e.matmul(ps[:, 0:T], slot[:, kc, mo * 128:(mo + 1) * 128], hT.t[:, kc, :],
                                                           start=(kc == 0), stop=(kc == 7)), reads=[sbuf_] + hT.b, writes=[pb])
                        fw.op("act", lambda e: e.copy(kT.t[:, m, T:2 * T], ps[:, 0:T]), reads=[pb], writes=kT.b)
                else:
                    u = i - 2
                    slot, sbuf_ = wload(w_in, 0, 8, C_V + u * 512, 512)
                    for tt in range(NTT):
                        ps, pb = psf()
                        for kc in range(8):
                            fw.op("pe", lambda e: e.matmul(ps[:, 0:512], hT.t[:, kc, tt * 128:(tt + 1) * 128], slot[:, kc, :],
                                                           start=(kc == 0), stop=(kc == 7)), reads=[sbuf_] + hT.b, writes=[pb])
                        fw.op("act", lambda e: e.copy(Vtok.t[:, 4 + tt, 64 + u * 512:64 + (u + 1) * 512], ps[:, 0:512]), reads=[pb], writes=Vtok.b)

            with ExitStack() as p2:
                NB = 2 if full else NTT
                a_t = sb(p2, "a_t", [128, NB, 16], F32, NB)
                wdt = sb(p2, "wdt", [128, NB, 16], F32, NB)
                decr = sb(p2, "decr", [128, NB, 16], F32, NB)
                xw = sb(p2, "xw", [128, NB, D], BF16, NB)
                if full:
                    sz = sb(p2, "sz", [128, 2, D], BF16, 2)
                    zslots = [wload(w_in, 0, 8, C_Z + u * 512, 512) for u in range(2)]
                    Y = sb(p2, "Y", [128, 2, 8, 128], F32, 2)
                    E = sb(p2, "E", [128, 2, 8, 128], BF16, 2)
                    Mm = sb(p2, "Mm", [128, 2, 8, 128], BF16, 2)
                    CBs = sb(p2, "CBs", [128, 2, 128], F32, 2)
                    t1 = sb(p2, "t1", [128, D], F32)
                    t2 = sb(p2, "t2", [128, 512], F32)
                    ynb = sb(p2, "ynb", [128, D], BF16)
                    ebias = sb(p2, "ebias", [128, 2, 16], F32, 2)
                    ecum = sb(p2, "ecum", [128, 2, 16], F32, 2)
                    gss = sb(p2, "gss", [128, 2], F32)
                    jk2 = sb(p2, "jk2", [128, 512], BF16)

                def ssd_pre(tt):
                    k = tt % NB
                    fw.op("dve", lambda e: e.tensor_tensor(a_t.t[:, k, :], dt.t[:, tt, :], Aneg.t[:], ALU.mult), reads=dt.b + Aneg.b, writes=[a_t.b[k]])
                    if full:
                        for u in range(2):
                            zs, zb = zslots[u]
                            pz, pzb = psf()
                            for kc in range(8):
                                fw.op("pe", lambda e: e.matmul(pz[:, 0:512], hT.t[:, kc, tt * 128:(tt + 1) * 128], zs[:, kc, :],
                                                               start=(kc == 0), stop=(kc == 7)), reads=[zb] + hT.b, writes=[pzb])
                            fw.op("act", lambda e: e.activation(sz.t[:, k, u * 512:(u + 1) * 512], pz[:, 0:512], AF.Silu), reads=[pzb], writes=[sz.b[k]])
                    pss, pssb = psf()
                    fw.op("pe", lambda e: e.matmul(pss[:, 0:16], cst.t[:, 2, :], a_t.t[:, k, :], start=True, stop=True), reads=cst.b + [a_t.b[k]], writes=[pssb])
                    fw.op("pe", lambda e: e.matmul(pss[:, 16:32], onesf.t[:], a_t.t[:, k, :], start=True, stop=True), reads=onesf.b + [a_t.b[k]], writes=[pssb])
                    if full:
                        fw.op("pe", lambda e: e.matmul(pss[:, 32:48], cst.t[:, 1, :], a_t.t[:, k, :], start=True, stop=True), reads=cst.b + [a_t.b[k]], writes=[pssb])
                    fw.op("act", lambda e: e.activation(wdt.t[:, k, :], pss[:, 0:16], AF.Exp), reads=[pssb], writes=[wdt.b[k]])
                    fw.op("dve", lambda e: e.tensor_tensor(wdt.t[:, k, :], wdt.t[:, k, :], dt.t[:, tt, :], ALU.mult), reads=[wdt.b[k]] + dt.b, writes=[wdt.b[k]])
                    fw.op("act", lambda e: e.activation(decr.t[:, k, :], pss[:, 16:32], AF.Exp), reads=[pssb], writes=[decr.b[k]])
                    if full:
                        fw.op("act", lambda e: e.activation(ebias.t[:, k, :], dt.t[:, tt, :], AF.Ln), reads=dt.b, writes=[ebias.b[k]])
                        fw.op("dve", lambda e: e.tensor_tensor(ebias.t[:, k, :], ebias.t[:, k, :], pss[:, 32:48], ALU.subtract), reads=[ebias.b[k], pssb], writes=[ebias.b[k]])
                        fw.op("act", lambda e: e.activation(ecum.t[:, k, :], pss[:, 32:48], AF.Exp), reads=[pssb], writes=[ecum.b[k]])
                    fw.op("dve", lambda e: e.tensor_tensor(xw.t[:, k, :].rearrange("p (h c) -> p h c", h=16),
                                                           xtok.t[:, tt, :].rearrange("p (h c) -> p h c", h=16),
                                                           wdt.t[:, k, :].unsqueeze(2).to_broadcast([128, 16, 64]), ALU.mult),
                          reads=xtok.b + [wdt.b[k]], writes=[xw.b[k]])

                def ssd_front(tt, g):
                    k = tt % 2
                    tsl = slice(tt * 128, (tt + 1) * 128)
                    fw.op("dve", lambda e: e.tensor_tensor(Y.t[:, g], cst.t[:, 1, :].unsqueeze(1).to_broadcast([128, 8, 128]),
                                                           a_t.t[:, k, g * 8:(g + 1) * 8].unsqueeze(2).to_broadcast([128, 8, 128]), ALU.mult),
                          reads=cst.b + [a_t.b[k]], writes=[Y.b[g]])
                    pc, pcb = psf()
                    fw.op("pe", lambda e: e.matmul(pc[:, 0:128], BT.t[:, g, tsl], CT.t[:, g, tsl], start=True, stop=True),
                          reads=BT.b + CT.b, writes=[pcb])
                    fw.op("act", lambda e: e.copy(CBs.t[:, g, :], pc[:, 0:128]), reads=[pcb], writes=[CBs.b[g]])

                def ssd_front_r(tt, g):
                    k = tt % 2
                    for q in range(2):
                        prt, prb = psf()
                        fw.op("pe", lambda e: e.matmul(prt[:, 0:512], onesf.t[:], Y.t[:, g, q * 4:(q + 1) * 4, :].rearrange("p h l -> p (h l)"),
                                                       start=True, stop=False), reads=onesf.b + [Y.b[g]], writes=[prb])
                        fw.op("pe", lambda e: e.matmul(prt[:, 0:512], identb_ap, NM8.t[:, 0:4, :].rearrange("p h l -> p (h l)"),
                                                       start=False, stop=True), reads=identb.b + NM8.b, writes=[prb])
                        for hh in range(4):
                            h = g * 8 + q * 4 + hh
                            fw.op("act", lambda e: e.activation(E.t[:, g, q * 4 + hh, :], prt[:, hh * 128:(hh + 1) * 128], AF.Exp,
                                                                bias=ebias.t[:, k, h:h + 1]), reads=[prb, ebias.b[k]], writes=[E.b[g]])

                def ssd_front_b(tt, g):
                    fw.op("dve", lambda e: e.tensor_tensor(Mm.t[:, g], E.t[:, g],
                                                           CBs.t[:, g, :].unsqueeze(1).to_broadcast([128, 8, 128]), ALU.mult),
                          reads=[E.b[g], CBs.b[g]], writes=[Mm.b[g]])

                def ssd_back(tt, g):
                    k = tt % 2
                    tsl = slice(tt * 128, (tt + 1) * 128)
                    gsl = slice(g * 512, (g + 1) * 512)
                    if g == 0:
                        fw.op("act", lambda e: e.copy(Hbf.t[:], Hst.t[:]), reads=Hst.b, writes=Hbf.b)
                    pyt, pyb = psfix(4)
                    for hh in range(8):
                        h = g * 8 + hh
                        fw.op("pe", lambda e: e.matmul(pyt[:, hh * 64:(hh + 1) * 64], Mm.t[:, g, hh, :], xtok.t[:, tt, h * 64:(h + 1) * 64],
                                                       start=True, stop=True), reads=[Mm.b[g]] + xtok.b, writes=[pyb])
                    po, pob = psfix(5)
                    fw.op("pe", lambda e: e.matmul(po[:, 0:512], CT.t[:, g, tsl], Hbf.t[:, gsl], start=True, stop=True),
                          reads=CT.b + Hbf.b, writes=[pob])

                def ssd_back_rest(tt, g):
                    k = tt % 2
                    tsl = slice(tt * 128, (tt + 1) * 128)
                    gsl = slice(g * 512, (g + 1) * 512)
                    pyt, pyb = psfix(4)
                    po, pob = psfix(5)
                    fw.op("dve", lambda e: e.tensor_tensor(t1.t[:, gsl].rearrange("p (h c) -> p h c", h=8),
                                                           po[:, 0:512].rearrange("p (h c) -> p h c", h=8),
                                                           ecum.t[:, k, g * 8:(g + 1) * 8].unsqueeze(2).to_broadcast([128, 8, 64]), ALU.mult),
                          reads=[pob, ecum.b[k]], writes=t1.b)
                    fw.op("dve", lambda e: e.tensor_tensor(t2.t[:].rearrange("p (h c) -> p h c", h=8),
                                                           xtok.t[:, tt, gsl].rearrange("p (h c) -> p h c", h=8),
                                                           hp.t[:, 2, g * 8:(g + 1) * 8].unsqueeze(2).to_broadcast([128, 8, 64]), ALU.mult),
                          reads=xtok.b + hp.b, writes=t2.b)
                    fw.op("dve", lambda e: e.tensor_tensor(t1.t[:, gsl], t1.t[:, gsl], t2.t[:], ALU.add), reads=t1.b + t2.b, writes=t1.b)
                    fw.op("dve", lambda e: e.tensor_tensor(t1.t[:, gsl], t1.t[:, gsl], pyt[:, 0:512], ALU.add), reads=t1.b + [pyb], writes=t1.b)
                    fw.op("dve", lambda e: e.tensor_tensor(t1.t[:, gsl], t1.t[:, gsl], sz.t[:, k, gsl], ALU.mult), reads=t1.b + [sz.b[k]], writes=t1.b)
                    fw.op("dve", lambda e: e.memset(gss.t[:, g:g + 1], 0.0), writes=gss.b)
                    fw.op("act", lambda e: e.activation(jk2.t[:], t1.t[:, gsl], AF.Square, scale=float(512 ** -0.5),
                                                        accum_out=gss.t[:, g:g + 1]), reads=t1.b, writes=jk2.b + gss.b)
                    rsqrt_eps(gss.t[:, g:g + 1], gss.t[:, g:g + 1], gss.b, gss.b)
                    fw.op("dve", lambda e: e.tensor_scalar(ynb.t[:, gsl], t1.t[:, gsl], gss.t[:, g:g + 1], None, ALU.mult),
                          reads=t1.b + gss.b, writes=ynb.b)
                    if g == 1:
                        if tt == 0:
                            dump("yssd", t1.t[:], t1.b)
                        pt, ptb = psb()
                        for j in range(8):
                            fw.op("pe", lambda e: e.transpose(pt[:, j * 128:(j + 1) * 128], ynb.t[:, j * 128:(j + 1) * 128], identb_ap),
                                  reads=ynb.b + identb.b, writes=[ptb])
                        fw.op("dve", lambda e: e.tensor_tensor(yssmT.t[:, :, tsl], pt[:].rearrange("p (j c) -> p j c", j=8),
                                                               nwf.t[:, 2, :].unsqueeze(2).to_broadcast([128, 8, 128]), ALU.mult),
                              reads=[ptb] + nwf.b, writes=yssmT.b)

                def ssd_post(tt):
                    k = tt % NB
                    for g in range(2):
                        pst_, pstb = psf()
                        fw.op("pe", lambda e: e.matmul(pst_[:, 0:512], Btok.t[:, tt, g * 128:(g + 1) * 128], xw.t[:, k, g * 512:(g + 1) * 512],
                                                       start=True, stop=True), reads=Btok.b + [xw.b[k]], writes=[pstb])
                        hs = Hst.t[:, g * 512:(g + 1) * 512]
                        fw.op("dve", lambda e: e.tensor_tensor(hs.rearrange("p (h c) -> p h c", h=8), hs.rearrange("p (h c) -> p h c", h=8),
                                                               decr.t[:, k, g * 8:(g + 1) * 8].unsqueeze(2).to_broadcast([128, 8, 64]), ALU.mult),
                              reads=Hst.b + [decr.b[k]], writes=Hst.b)
                        fw.op("dve", lambda e: e.tensor_tensor(hs, hs, pst_[:, 0:512], ALU.add), reads=Hst.b + [pstb], writes=Hst.b)

                if not full:
                    for tt in range(NTT):
                        ssd_pre(tt)
                    for tt in range(NTT):
                        ssd_post(tt)
                else:
                    steps = [(tt, g) for tt in range(NTT) for g in range(2)]
                    prev = None
                    for (tt, g) in steps:
                        if g == 0:
                            ssd_pre(tt)
                        ssd_front(tt, g)
                        if prev is not None:
                            ssd_back(*prev)
                        ssd_front_r(tt, g)
                        if prev is not None:
                            ssd_back_rest(*prev)
                            if prev[1] == 1:
                                ssd_post(prev[0])
                        ssd_front_b(tt, g)
                        prev = (tt, g)
                        if g == 1:
                            kv_unit(tt)
                    ssd_back(*prev)
                    ssd_back_rest(*prev)
                    ssd_post(prev[0])
                fw.barrier()
            dump("Hst", Hst.t[:], Hst.b)

            if need_kv:
                with ExitStack() as p3:
                    if not full:
                        for i in range(4):
                            kv_unit(i)
                    if full:
                        qT0 = sb(p3, "qT0", [128, 8, T], BF16)
                        qT1 = sb(p3, "qT1", [128, 8, T], BF16)
                        qTs = [qT0, qT1]
                        PT = sb(p3, "PT", [128, 4, 512], BF16, 4)
                        rden = sb(p3, "rden", [128, 512], F32)
                        dtmp = sb(p3, "dtmp", [128, 512], F32)
                        fw.op("dve", lambda e: e.memset(qT0.t[64:128, :, :], 0.0), writes=qT0.b)
                        fw.op("dve", lambda e: e.memset(qT1.t[0:64, :, :], 0.0), writes=qT1.b)

                        def ev_q(m, ps, pb):
                            fw.op("act", lambda e: e.activation(qT0.t[0:64, m, :], ps[0:64, 0:T], AF.Copy, scale=0.125), reads=[pb], writes=qT0.b)
                            fw.op("act", lambda e: e.activation(qT1.t[64:128, m, :], ps[64:128, 0:T], AF.Copy, scale=0.125), reads=[pb], writes=qT1.b)

                        proj_fm(w_in, C_Q, 8, 8, lambda kc: hT.t[:, kc, :], hT.b, ev_q)
                        vaug = sb(p3, "vaug", [128, 2, 8, 128], BF16, 2)

                        def prep_v(h):
                            hb = h % 2
                            if hb == 0:
                                vc, oc, osrc = 0, 64, 1088
                            else:
                                vc, oc, osrc = 64, 0, 0
                            fw.op("dve", lambda e: e.tensor_copy(vaug.t[:, hb, :, vc:vc + 64], Vtok.t[:, :, 64 + 64 * h:128 + 64 * h]),
                                  reads=Vtok.b, writes=[vaug.b[hb]])
                            fw.op("dve", lambda e: e.tensor_copy(vaug.t[:, hb, :, oc:oc + 64], Vtok.t[:, :, osrc:osrc + 64]),
                                  reads=Vtok.b, writes=[vaug.b[hb]])

                        def stageA(hpair, h2, j, k):
                            h = hpair * 2 + h2
                            qt_lo = max(0, j - 4)
                            qt_hi = min(3, j)
                            q0 = qt_lo * 128
                            nq = (qt_hi - qt_lo + 1) * 128
                            rel0 = (qt_lo * 2 + 8 - 2 * j) * 64
                            if j == 0 and h == 0:
                                prep_v(0)
                            if j == 3 and h < 15:
                                prep_v(h + 1)
                            sp_, spb = psf()
                            qh = qTs[h2]
                            fw.op("pe", lambda e: e.matmul(sp_[:, 0:nq], kT.t[:, hpair, j * 128:(j + 1) * 128], qh.t[:, hpair, q0:q0 + nq],
                                                           start=True, stop=False), reads=kT.b + qh.b, writes=[spb])
                            fw.op("pe", lambda e: e.matmul(sp_[:, 0:nq], identb_ap, EB.t[:, h, rel0:rel0 + nq],
                                                           start=False, stop=True), reads=identb.b + EB.b, writes=[spb])
                            fw.op("act", lambda e: e.activation(PT.t[:, k, 0:nq], sp_[:, 0:nq], AF.Exp), reads=[spb], writes=[PT.b[k]])

                        def stageB(hpair, h2, j, k):
                            h = hpair * 2 + h2
                            qt_lo = max(0, j - 4)
                            qt_hi = min(3, j)
                            q0 = qt_lo * 128
                            nq = (qt_hi - qt_lo + 1) * 128
                            pX, pXb = acc_bank(hpair, h2)
                            segs = [(q0, nq)]
                            if 1 <= j <= 3:
                                segs = [(q0, nq - 128), (q0 + nq - 128, 128)]
                            for (sq0, snq) in segs:
                                fw.op("pe", lambda e: e.matmul(pX[:, sq0:sq0 + snq], vaug.t[:, h2, j, :], PT.t[:, k, sq0 - q0:sq0 - q0 + snq],
                                                               start=(j == 0), stop=(j == 7)), reads=[vaug.b[h2], PT.b[k]], writes=[pXb])
                            if h2 == 1 and j == 7:
                                pA, pAb = acc_bank(hpair, 0)
                                pB, pBb = acc_bank(hpair, 1)
                                fw.op("act", lambda e: e.copy(dtmp.t[0:64, :], pA[64:128, 0:512]), reads=[pAb], writes=dtmp.b)
                                fw.op("act", lambda e: e.copy(dtmp.t[64:128, :], pB[0:64, 0:512]), reads=[pBb], writes=dtmp.b)
                                fw.op("dve", lambda e: e.reciprocal(rden.t[:], dtmp.t[:]), reads=dtmp.b, writes=rden.b)
                                fw.op("dve", lambda e: e.tensor_tensor(yattT.t[0:64, hpair, :], pA[0:64, 0:512], rden.t[0:64, :], ALU.mult),
                                      reads=[pAb] + rden.b, writes=yattT.b)
                                fw.op("dve", lambda e: e.tensor_tensor(yattT.t[64:128, hpair, :], pB[64:128, 0:512], rden.t[64:128, :], ALU.mult),
                                      reads=[pBb] + rden.b, writes=yattT.b)

                        its = [(hpair, h2, j) for hpair in range(8) for h2 in range(2) for j in range(8)]
                        pend = []
                        for n_, (hpair, h2, j) in enumerate(its):
                            k = n_ % 4
                            stageA(hpair, h2, j, k)
                            pend.append((hpair, h2, j, k))
                            if len(pend) > 3:
                                stageB(*pend.pop(0))
                        while pend:
                            stageB(*pend.pop(0))
                    fw.barrier()
                fw.op("act", lambda e: e.copy(kT.t[:, :, 0:T], kT.t[:, :, T:2 * T]), reads=kT.b, writes=kT.b)
                fw.op("dve", lambda e: e.tensor_copy(Vtok.t[:, 0:4, :], Vtok.t[:, 4:8, :]), reads=Vtok.b, writes=Vtok.b)

            if full:
                dump("yssmT", yssmT.t[:], yssmT.b)
                dump("yattT", yattT.t[:], yattT.b)
                with ExitStack() as p4:
                    norm_prep()
                    mT = sb(p4, "mT", [128, 8, T], BF16)
                    s3 = sb(p4, "s3", [128, 2, T], F32, 2)
                    s4 = sb(p4, "s4", [128, 2, T], F32, 2)
                    for u0 in (0, 4):
                        sl1, b1 = wload(w_bs, 0, 8, u0 * 128, 512)
                        sl2, b2 = wload(w_ba, 0, 8, u0 * 128, 512)
                        sl3, b3 = wload(w_in, 0, 8, C_GS + u0 * 128, 512)
                        sl4, b4 = wload(w_in, 0, 8, C_GA + u0 * 128, 512)
                        for mo in range(4):
                            m = u0 + mo
                            k = m % 2
                            msl = slice(mo * 128, (mo + 1) * 128)
                            p1_, p1b = psf()
                            p2_, p2b = psf()
                            p3_, p3b = psf()
                            p4_, p4b = psf()
                            for kc in range(8):
                                fw.op("pe", lambda e: e.matmul(p3_[:, 0:T], sl3[:, kc, msl], hT.t[:, kc, :], start=(kc == 0), stop=(kc == 7)),
                                      reads=[b3] + hT.b, writes=[p3b])
                            for kc in range(8):
                                fw.op("pe", lambda e: e.matmul(p4_[:, 0:T], sl4[:, kc, msl], hT.t[:, kc, :], start=(kc == 0), stop=(kc == 7)),
                                      reads=[b4] + hT.b, writes=[p4b])
                            for kc in range(8):
                                fw.op("pe", lambda e: e.matmul(p1_[:, 0:T], sl1[:, kc, msl], yssmT.t[:, kc, :], start=(kc == 0), stop=(kc == 7)),
                                      reads=[b1] + yssmT.b, writes=[p1b])
                            for kc in range(8):
                                fw.op("pe", lambda e: e.matmul(p2_[:, 0:T], sl2[:, kc, msl], yattT.t[:, kc, :], start=(kc == 0), stop=(kc == 7)),
                                      reads=[b2] + yattT.b, writes=[p2b])
                            fw.op("act", lambda e: e.activation(s3.t[:, k, :], p3_[:, 0:T], AF.Sigmoid), reads=[p3b], writes=[s3.b[k]])
                            fw.op("act", lambda e: e.activation(s4.t[:, k, :], p4_[:, 0:T], AF.Sigmoid), reads=[p4b], writes=[s4.b[k]])
                            fw.op("dve", lambda e: e.tensor_tensor(s3.t[:, k, :], s3.t[:, k, :], p1_[:, 0:T], ALU.mult), reads=[s3.b[k], p1b], writes=[s3.b[k]])
                            fw.op("dve", lambda e: e.tensor_tensor(s4.t[:, k, :], s4.t[:, k, :], p2_[:, 0:T], ALU.mult), reads=[s4.b[k], p2b], writes=[s4.b[k]])
                            fw.op("dve", lambda e: e.tensor_tensor(mT.t[:, m, :], s3.t[:, k, :], s4.t[:, k, :], ALU.add), reads=[s3.b[k], s4.b[k]], writes=mT.b)

                    def ev_out(u, tt, ps, pb):
                        xs = x_res.t[:, tt, u * 512:(u + 1) * 512]
                        fw.op("dve", lambda e: e.tensor_tensor(xs, xs, ps[:, 0:512], ALU.add), reads=[pb, x_res.b[tt]], writes=[x_res.b[tt]])

                    proj_tm(w_out, 0, 1024, 8, lambda kc, tt: mT.t[:, kc, tt * 128:(tt + 1) * 128], mT.b, ev_out)
                    if after is not None:
                        after(p4)
                    fw.barrier()
            if not full:
                if not early:
                    norm_prep()
                    if after is not None:
                        after(ph)
                fw.barrier()

    onesf = sb(st, "onesf", [128, 128], F32)
    fw.op("dve", lambda e: e.memset(onesf.t[:], 1.0), writes=onesf.b)

    def tile_src(ti):
        full = ti >= n_state
        src = x_main if full else x_prev
        tok0 = ((ti - n_state) if full else (NTOK // T - n_state + ti)) * T
        return src, tok0

    def load_x(ti):
        if ti >= n_state + n_full:
            return
        src, tok0 = tile_src(ti)
        for tt in range(NTT):
            fw.dma(x_res.t[:, tt, :], src[tok0 + tt * 128: tok0 + (tt + 1) * 128, :], writes=[x_res.b[tt]], q="pool")

    def final_out(ph, tok0):
        yo = sb(ph, "yo", [128, 2, D], F32, 2)
        ss = ssp
        for tt in range(NTT):
            k = tt % 2
            fw.op("dve", lambda e: e.memset(ss.t[:, tt:tt + 1], 0.0), writes=ss.b)
            fw.op("act", lambda e: e.activation(xnp.t[:, k, :], x_res.t[:, tt, :], AF.Square, scale=1.0 / 32.0, accum_out=ss.t[:, tt:tt + 1]),
                  reads=[x_res.b[tt]], writes=[xnp.b[k]] + ss.b)
            rsqrt_eps(ss.t[:, tt:tt + 1], ss.t[:, tt:tt + 1], ss.b, ss.b)
            fw.op("dve", lambda e: e.scalar_tensor_tensor(yo.t[:, k, :], x_res.t[:, tt, :], ss.t[:, tt:tt + 1], fnw.t[:], ALU.mult, ALU.mult),
                  reads=[x_res.b[tt]] + ss.b + fnw.b, writes=[yo.b[k]])
            fw.dma(y_out[tok0 + tt * 128: tok0 + (tt + 1) * 128, :], yo.t[:, k, :], reads=[yo.b[k]], q="pool")

    load_x(0)
    norm_prep()
    with ExitStack() as ph0:
        norm_to_hT(0, ph0)
        fw.barrier()
    for ti in range(n_state + n_full):
        full = ti >= n_state
        src, tok0 = tile_src(ti)
        last_state = (ti == n_state - 1)

        def after_ffn1(ph, full=full, ti=ti):
            norm_to_hT(1, ph)
            if not full:
                load_x(ti + 1)
            return False

        ffn(0, after=after_ffn1)
        if full and ti == n_state:
            dump("x1", x_res.t[:], x_res.b)

        def after_mixer(ph, full=full, ti=ti, last_state=last_state):
            if last_state:
                hm = hmask.t[:, 0:1]
                fw.op("dve", lambda e: e.tensor_scalar(Hst.t[:], Hst.t[:], hm, None, ALU.mult), reads=Hst.b + hmask.b, writes=Hst.b)
                fw.op("dve", lambda e: e.tensor_scalar(halo.t[:], halo.t[:], hm, None, ALU.mult), reads=halo.b + hmask.b, writes=halo.b)
                fw.op("dve", lambda e: e.tensor_scalar(Vtok.t[:, 0:4, :], Vtok.t[:, 0:4, :], hm, None, ALU.mult), reads=Vtok.b + hmask.b, writes=Vtok.b)
            if full:
                if ti == n_state:
                    dump("x2", x_res.t[:], x_res.b)
                norm_to_hT(3, ph)
            else:
                norm_to_hT(0, ph)

        mixer(full, full or last_state, ti == n_state, after=after_mixer)
        if full:
            nxt = {}

            def before_ffn2(ph, ti=ti, nxt=nxt):
                if ti + 1 < n_state + n_full:
                    xq = sb(ph, "x_nxt", [128, NTT, D], F32, NTT)
                    nxt["x"] = xq
                    fw.barrier(("pool",))
                    src2, tok2 = tile_src(ti + 1)
                    for tt in range(NTT):
                        fw.dma(xq.t[:, tt, :], src2[tok2 + tt * 128: tok2 + (tt + 1) * 128, :], writes=[xq.b[tt]], q="pool")

            def after_ffn2(ph, tok0=tok0, ti=ti, nxt=nxt):
                if "x" in nxt:
                    norm_to_hT(0, ph, src=nxt["x"])
                final_out(ph, tok0)
                if "x" in nxt:
                    xq = nxt["x"]
                    for tt in range(NTT):
                        eng = "dve" if tt % 2 == 0 else "act"
                        if eng == "dve":
                            fw.op("dve", lambda e: e.tensor_copy(x_res.t[:, tt, :], xq.t[:, tt, :]), reads=[xq.b[tt]], writes=[x_res.b[tt]])
                        else:
                            fw.op("act", lambda e: e.copy(x_res.t[:, tt, :], xq.t[:, tt, :]), reads=[xq.b[tt]], writes=[x_res.b[tt]])
                return True

            ffn(1, after=after_ffn2, before=before_ffn2)
    fw.barrier(("sp", "act", "pool"), dmas=True)
    st.close()
    return nc


def _consts():
    t = np.arange(128)
    ident = np.eye(128, dtype=np.float32)
    U = (t[:, None] <= t[None, :]).astype(np.float32)
    Ls = (t[:, None] > t[None, :]).astype(np.float32)
    nm = np.where(t[:, None] > t[None, :], NEG, 0.0).astype(np.float32)
    return np.ascontiguousarray(np.stack([ident, U, Ls, nm], axis=1))


def _prep_shared(inp):
    f = lambda a: np.ascontiguousarray(np.asarray(a, dtype=np.float32))
    fm = lambda w: np.asarray(w, dtype=np.float32).reshape(8, 128).T
    sh = {}
    sh["w_gu1"] = f(inp["ffn1_w_gu"][0])
    sh["w_gu2"] = f(inp["ffn2_w_gu"][0])
    sh["w_dn1"] = f(inp["ffn1_w_down"][0])
    sh["w_dn2"] = f(inp["ffn2_w_down"][0])
    sh["w_in"] = f(inp["w_in"][0])
    sh["w_bs"] = f(inp["w_branch_ssm"][0])
    sh["w_ba"] = f(inp["w_branch_attn"][0])
    sh["w_out"] = f(inp["w_out"][0])
    sh["nwf"] = f(np.stack([fm(inp["ffn1_norm_w"][0]), fm(inp["mix_norm_w"][0]), fm(inp["ssm_norm_w"][0]),
                            fm(inp["ffn2_norm_w"][0])], axis=1))
    sh["fnw"] = f(np.broadcast_to(np.asarray(inp["final_norm_w"], np.float32)[None, :], (128, D)))
    cwv = np.asarray(inp["conv_w"][0], np.float32)
    sh["cw"] = f(cwv.reshape(4, 12, 128).transpose(2, 1, 0))
    sh["cb"] = f(np.asarray(inp["conv_b"][0], np.float32).reshape(12, 128).T)
    hpv = np.stack([np.asarray(inp["dt_bias"][0], np.float32), np.asarray(inp["A_log"][0], np.float32),
                    np.asarray(inp["D_skip"][0], np.float32)], axis=0)
    sh["hp"] = f(np.broadcast_to(hpv[None], (128, 3, 16)))
    rb = np.asarray(inp["rel_bias"][0], np.float32)
    r = np.arange(128)[:, None]
    l = np.arange(640)[None, :]
    idx = np.clip(l - r, -128, 128) + 128
    kc = r // 64
    qc = l // 64
    valid = (qc >= kc) & (qc <= kc + 8)
    tab = rb[:, idx]
    tab = np.where(valid[None], tab, np.float32(NEG))
    sh["btab"] = f(tab)
    sh["cst"] = _consts()
    return sh


_NC_CACHE = {}


def kernel(**inputs):
    x = np.asarray(inputs["x"], dtype=np.float32)
    sh = _prep_shared(inputs)
    in_maps = []
    for c in range(8):
        b, h = c // 2, c % 2
        m = dict(sh)
        m["x_main"] = np.ascontiguousarray(x[b, h * NTOK:(h + 1) * NTOK])
        m["x_prev"] = np.ascontiguousarray(x[b, 0:NTOK])
        m["hmask"] = np.full((128, 1), float(h), dtype=np.float32)
        in_maps.append(m)
    if "nc" not in _NC_CACHE:
        _NC_CACHE["nc"] = build()
    res = run_bass_kernel_spmd(_NC_CACHE["nc"], in_maps, core_ids=list(range(8)))
    out = np.empty((4, 2 * NTOK, D), dtype=np.float32)
    for c in range(8):
        b, h = c // 2, c % 2
        out[b, h * NTOK:(h + 1) * NTOK] = np.asarray(res.results[c]["y"], dtype=np.float32)
    return out
```

```python
import numpy as np
from contextlib import ExitStack
import concourse.bass as bass
import concourse.mybir as mybir
from concourse.bass_utils import run_bass_kernel_spmd

F32 = mybir.dt.float32
BF16 = mybir.dt.bfloat16
AF = mybir.ActivationFunctionType
ALU = mybir.AluOpType

D = 1024
DFF = 2816
NFC = 22
T = 512
NTT = 4
NTOK = 4096
INP = 7696
EPS = 1e-6
NEG = -30000.0
C_Z, C_X, C_B, C_C, C_DT, C_Q, C_K, C_V, C_GS, C_GA = 0, 1024, 2048, 2304, 2560, 2576, 3600, 4624, 5648, 6672
NS = 5


class Buf:
    __slots__ = ("name", "lw", "rd")

    def __init__(self, name):
        self.name = name
        self.lw = None
        self.rd = {}


class Tn:
    def __init__(self, t, bufs):
        self.t = t
        self.b = bufs


class FW:
    def __init__(self, nc, stack, n_dma_sems=32):
        self.nc = nc
        self.engs = {"pe": nc.tensor, "act": nc.scalar, "dve": nc.vector, "pool": nc.gpsimd, "sp": nc.sync}
        self.sem = {}
        self.cnt = {}
        for k in self.engs:
            self.sem[k] = stack.enter_context(nc.semaphore("tick_" + k))
            self.cnt[k] = 0
        self.dsem = [stack.enter_context(nc.semaphore("dma_%d" % i)) for i in range(n_dma_sems)]
        self.dcnt = [0] * n_dma_sems
        self.dnext2 = {"sp": 0, "pool": 0}
        self.seen = {k: {} for k in self.engs}
        self.ninst = 0

    def _wait(self, eng, dep):
        kind, idx, val = dep
        key = (kind, idx)
        if kind == "e" and idx == eng and eng == "pe":
            return
        if self.seen[eng].get(key, 0) >= val:
            return
        s = self.sem[idx] if kind == "e" else self.dsem[idx]
        self.engs[eng].wait_ge(s, val)
        self.seen[eng][key] = val

    def _deps(self, eng, reads, writes):
        for b in reads:
            if b.lw is not None:
                self._wait(eng, b.lw)
        for b in writes:
            if b.lw is not None:
                self._wait(eng, b.lw)
            for d in b.rd.values():
                self._wait(eng, d)

    def _mark(self, dep, reads, writes):
        for b in writes:
            b.lw = dep
            b.rd = {}
        for b in reads:
            if b in writes:
                continue
            b.rd[(dep[0], dep[1])] = dep

    def op(self, eng, fn, reads=(), writes=()):
        self._deps(eng, reads, writes)
        ins = fn(self.engs[eng])
        self.cnt[eng] += 1
        ins.then_inc(self.sem[eng], 1)
        dep = ("e", eng, self.cnt[eng])
        self._mark(dep, reads, writes)
        self.ninst += 1
        return dep

    def dma(self, out, in_, reads=(), writes=(), q="sp"):
        half = len(self.dsem) // 2
        base = 0 if q == "sp" else half
        i = base + self.dnext2[q]
        self.dnext2[q] = (self.dnext2[q] + 1) % half
        if self.dcnt[i] > 0:
            self._wait(q, ("d", i, self.dcnt[i]))
        self._deps(q, reads, writes)
        ins = self.engs[q].dma_start(out=out, in_=in_)
        self.dcnt[i] += 16
        ins.then_inc(self.dsem[i], 16)
        dep = ("d", i, self.dcnt[i])
        self._mark(dep, reads, writes)
        self.ninst += 1
        return dep

    def barrier(self, engs=("act", "dve"), dmas=False):
        for e in engs:
            for f in self.engs:
                if self.cnt[f] > 0 and not (f == e and e == "pe"):
                    self._wait(e, ("e", f, self.cnt[f]))
            if dmas:
                for i in range(len(self.dsem)):
                    if self.dcnt[i] > 0:
                        self._wait(e, ("d", i, self.dcnt[i]))


def build(n_state=8, n_full=8, dbg=()):
    nc = bass.Bass("TRN2", target_bir_lowering=False)
    di = lambda n, s, dt=F32: nc.dram_tensor(n, s, dt, kind="ExternalInput").ap()
    x_prev = di("x_prev", [NTOK, D])
    x_main = di("x_main", [NTOK, D])
    w_gu = [di("w_gu1", [D, 2 * DFF]), di("w_gu2", [D, 2 * DFF])]
    w_dn = [di("w_dn1", [DFF, D]), di("w_dn2", [DFF, D])]
    w_in = di("w_in", [D, INP])
    w_bs = di("w_bs", [D, D])
    w_ba = di("w_ba", [D, D])
    w_out = di("w_out", [D, D])
    nwf_d = di("nwf", [128, 4, 8])
    fnw_d = di("fnw", [128, D])
    cw_d = di("cw", [128, 12, 4])
    cb_d = di("cb", [128, 12])
    hp_d = di("hp", [128, 3, 16])
    btab_d = di("btab", [16, 128, 640])
    hmask_d = di("hmask", [128, 1])
    cst_d = di("cst", [128, 4, 128])
    y_out = nc.dram_tensor("y", [NTOK, D], F32, kind="ExternalOutput").ap()
    dbg_out = {}
    for name, shape, dt in dbg:
        dbg_out[name] = nc.dram_tensor("dbg_" + name, list(shape), dt, kind="ExternalOutput").ap()

    st = ExitStack()
    fw = FW(nc, st)

    uid = [0]

    def sb(stack, name, shape, dt, nb=1):
        uid[0] += 1
        t = stack.enter_context(nc.sbuf_tensor("%s_%d" % (name, uid[0]), list(shape), dt))
        return Tn(t, [Buf("%s_%d" % (name, i)) for i in range(nb)])

    x_res = sb(st, "x_res", [128, NTT, D], F32, NTT)
    hT = sb(st, "hT", [128, 8, T], BF16, 1)
    ring = sb(st, "ring", [128, NS, 4096], BF16, NS)
    Hst = sb(st, "Hst", [128, 1024], F32)
    Hbf = sb(st, "Hbf", [128, 1024], BF16)
    EB = sb(st, "EB", [128, 16, 640], BF16)
    kT = sb(st, "kT", [128, 8, 2 * T], BF16)
    Vtok = sb(st, "Vtok", [128, 8, 1152], BF16)
    halo = sb(st, "halo", [128, 12, 3], BF16)
    cst = sb(st, "cst", [128, 4, 128], F32)
    identb = sb(st, "identb", [128, 128], BF16)
    NM8 = sb(st, "NM8", [128, 4, 128], BF16)
    nwf = sb(st, "nwf", [128, 4, 8], F32)
    fnw = sb(st, "fnw", [128, D], F32)
    cw = sb(st, "cw", [128, 12, 4], F32)
    cb = sb(st, "cb", [128, 12], F32)
    hp = sb(st, "hp", [128, 3, 16], F32)
    Aneg = sb(st, "Aneg", [128, 16], F32)
    hmask = sb(st, "hmask", [128, 1], F32)
    onesb = sb(st, "onesb", [128, 64], BF16)
    onesh = sb(st, "onesh", [128, 64], BF16)
    col1 = sb(st, "col1", [128, 1], F32)
    xnp = sb(st, "xnp", [128, 2, D], BF16, 2)
    ssp = sb(st, "ssp", [128, NTT], F32)
    rstdp = sb(st, "rstdp", [128, NTT], F32)

    psf_t = [st.enter_context(nc.psum_tensor("psf%d" % i, [128, 512], F32)) for i in range(6)]
    psf_b = [Buf("psf%d" % i) for i in range(6)]
    psb_t = [st.enter_context(nc.psum_tensor("psb%d" % i, [128, 1024], BF16)) for i in range(2)]
    psb_b = [Buf("psb%d" % i) for i in range(2)]
    pctr = {"f": 0, "b": 0, "r": 0}

    def psf():
        i = pctr["f"]
        pctr["f"] = (i + 1) % 4
        return psf_t[i], psf_b[i]

    def psfix(i):
        return psf_t[i], psf_b[i]

    psb_f32 = [psb_t[i][:].bitcast(F32) for i in range(2)]

    def acc_bank(hpair, h2):
        if hpair % 2 == 0:
            return psf_t[4 + h2], psf_b[4 + h2]
        return psb_f32[h2], psb_b[h2]

    def psb():
        i = pctr["b"]
        pctr["b"] = (i + 1) % 2
        return psb_t[i], psb_b[i]

    scratch = {}

    def wload(src2d, kc0, nkc, c0, ncols):
        i = pctr["r"]
        pctr["r"] = (i + 1) % NS
        slot2d = ring.t[:, i, 0:nkc * ncols]
        slot = slot2d.rearrange("p (k c) -> p k c", k=nkc)
        key = (src2d.tensor.name, kc0, nkc, c0, ncols)
        if key not in scratch:
            srcap = src2d.rearrange("(kc p) c -> p kc c", p=128)[:, kc0:kc0 + nkc, c0:c0 + ncols]
            fw.dma(slot, srcap, writes=[ring.b[i]], q="pool")
            d = nc.dram_tensor("ws%d" % len(scratch), [128, nkc * ncols], BF16, kind="Internal").ap()
            b = Buf("ws%d" % len(scratch))
            scratch[key] = (d, b)
            fw.dma(d, slot2d, reads=[ring.b[i]], writes=[b], q="sp")
        else:
            d, b = scratch[key]
            fw.dma(slot2d, d, reads=[b], writes=[ring.b[i]], q="sp")
        return slot, ring.b[i]

    def dump(name, ap, bufs):
        if name in dbg_out:
            fw.dma(dbg_out[name], ap, reads=bufs, q="sp")

    fw.dma(cst.t[:], cst_d, writes=cst.b)
    fw.dma(nwf.t[:], nwf_d, writes=nwf.b)
    fw.dma(fnw.t[:], fnw_d, writes=fnw.b)
    fw.dma(cw.t[:], cw_d, writes=cw.b)
    fw.dma(cb.t[:], cb_d, writes=cb.b)
    fw.dma(hp.t[:], hp_d, writes=hp.b)
    fw.dma(hmask.t[:], hmask_d, writes=hmask.b)
    fw.op("dve", lambda e: e.tensor_copy(identb.t[:], cst.t[:, 0, :]), reads=cst.b, writes=identb.b)
    for g in range(4):
        fw.op("dve", lambda e: e.tensor_copy(NM8.t[:, g, :], cst.t[:, 3, :]), reads=cst.b, writes=NM8.b)
    fw.op("dve", lambda e: e.memset(onesb.t[:], 1.0), writes=onesb.b)
    fw.op("dve", lambda e: e.memset(col1.t[:], 1.0), writes=col1.b)
    fw.op("dve", lambda e: e.memset(Hst.t[:], 0.0), writes=Hst.b)
    fw.op("dve", lambda e: e.memset(halo.t[:], 0.0), writes=halo.b)
    fw.op("dve", lambda e: e.memset(kT.t[:], 0.0), writes=kT.b)
    fw.op("dve", lambda e: e.memset(Vtok.t[:], 0.0), writes=Vtok.b)
    fw.op("dve", lambda e: e.memset(Vtok.t[:, 4:8, 0:64], 1.0), writes=Vtok.b)
    fw.op("dve", lambda e: e.memset(Vtok.t[:, 4:8, 1088:1152], 1.0), writes=Vtok.b)
    fw.op("dve", lambda e: e.tensor_copy(onesh.t[:], hmask.t[:, 0:1].to_broadcast([128, 64])), reads=hmask.b, writes=onesh.b)
    fw.op("act", lambda e: e.activation(Aneg.t[:], hp.t[:, 1, :], AF.Exp), reads=hp.b, writes=Aneg.b)
    fw.op("dve", lambda e: e.tensor_scalar(Aneg.t[:], Aneg.t[:], -1.0, None, ALU.mult), reads=Aneg.b, writes=Aneg.b)
    dgw = nc.dram_tensor("dgw", [128, 48, 128], BF16, kind="Internal").ap()
    with ExitStack() as phd:
        dg = sb(phd, "dg", [128, 48, 128], BF16)
        for m in range(12):
            for tap in range(4):
                fw.op("dve", lambda e: e.tensor_scalar(dg.t[:, m * 4 + tap, :], identb.t[:], cw.t[:, m, tap:tap + 1], None, ALU.mult),
                      reads=identb.b + cw.b, writes=dg.b)
        dgdep = fw.dma(dgw, dg.t[:], reads=dg.b)
        fw.barrier(("pe", "act", "dve", "sp"), dmas=True)

    for h in range(16):
        fw.dma(EB.t[:, h, :], btab_d[h], writes=EB.b, q="pool")

    identb_ap = identb.t[:]
    epsc = sb(st, "epsc", [128, 1], F32)
    fw.op("dve", lambda e: e.memset(epsc.t[:], EPS), writes=epsc.b)

    def rsqrt_eps(out_ap, in_ap, rb, wb):
        fw.op("act", lambda e: e.activation(out_ap, in_ap, AF.Ln, bias=epsc.t[:, 0:1]), reads=rb + epsc.b, writes=wb)
        fw.op("act", lambda e: e.activation(out_ap, out_ap, AF.Exp, scale=-0.5), reads=wb, writes=wb)

    def norm_prep():
        fw.op("dve", lambda e: e.memset(ssp.t[:], 0.0), writes=ssp.b)

    def norm_to_hT(ni, ph, src=None):
        xn, ss, rstd = xnp, ssp, rstdp
        if src is None:
            src = x_res
        junk = sb(ph, "njunk", [128, D], BF16)
        for tt in range(NTT):
            fw.op("act", lambda e: e.activation(junk.t[:], src.t[:, tt, :], AF.Square, scale=1.0 / 32.0,
                                                accum_out=ss.t[:, tt:tt + 1]),
                  reads=[src.b[tt]], writes=junk.b + ss.b)
        fw.op("act", lambda e: e.activation(rstd.t[:], ss.t[:], AF.Ln, bias=epsc.t[:, 0:1]), reads=ss.b + epsc.b, writes=rstd.b)
        fw.op("act", lambda e: e.activation(rstd.t[:], rstd.t[:], AF.Exp, scale=-0.5), reads=rstd.b, writes=rstd.b)
        for tt in range(NTT):
            k = tt % 2
            fw.op("act", lambda e: e.activation(xn.t[:, k, :], src.t[:, tt, :], AF.Copy, scale=rstd.t[:, tt:tt + 1]),
                  reads=[src.b[tt]] + rstd.b, writes=[xn.b[k]])
            pt, pb = psb()
            for j in range(8):
                fw.op("pe", lambda e: e.transpose(pt[:, j * 128:(j + 1) * 128], xn.t[:, k, j * 128:(j + 1) * 128], identb_ap),
                      reads=[xn.b[k]] + identb.b, writes=[pb])
            fw.op("dve", lambda e: e.tensor_tensor(hT.t[:, :, tt * 128:(tt + 1) * 128],
                                                   pt[:].rearrange("p (j c) -> p j c", j=8),
                                                   nwf.t[:, ni, :].unsqueeze(2).to_broadcast([128, 8, 128]), ALU.mult),
                  reads=[pb] + nwf.b, writes=hT.b)

    def proj_fm(src2d, c0, nchunks, nkc, rhs, rhs_bufs, evac, ncol=T, evac2=None, evac3=None):
        pend = []
        for u0 in range(0, nchunks, 4):
            nu = min(4, nchunks - u0)
            slot, sbuf_ = wload(src2d, 0, nkc, c0 + u0 * 128, nu * 128)
            for mo in range(nu):
                ps, pb = psf()
                for kc in range(nkc):
                    fw.op("pe", lambda e: e.matmul(ps[:, 0:ncol], slot[:, kc, mo * 128:(mo + 1) * 128], rhs(kc),
                                                   start=(kc == 0), stop=(kc == nkc - 1)),
                          reads=[sbuf_] + rhs_bufs, writes=[pb])
                evac(u0 + mo, ps, pb)
                pend.append(u0 + mo)
                if evac2 is not None and len(pend) >= 2:
                    evac2(pend[-2])
                if evac3 is not None and len(pend) >= 3:
                    evac3(pend[-3])
        if evac2 is not None and pend:
            evac2(pend[-1])
        if evac3 is not None:
            if len(pend) >= 2:
                evac3(pend[-2])
            if pend:
                evac3(pend[-1])

    def proj_tm(src2d, c0, ncols, nkc, lhs, lhs_bufs, evac):
        for u0 in range(0, ncols, 512):
            nc_ = min(512, ncols - u0)
            slot, sbuf_ = wload(src2d, 0, nkc, c0 + u0, nc_)
            for tt in range(NTT):
                ps, pb = psf()
                for kc in range(nkc):
                    fw.op("pe", lambda e: e.matmul(ps[:, 0:nc_], lhs(kc, tt), slot[:, kc, :],
                                                   start=(kc == 0), stop=(kc == nkc - 1)),
                          reads=[sbuf_] + lhs_bufs, writes=[pb])
                evac(u0 // 512, tt, ps, pb)

    def ffn(which, after=None, before=None):
        with ExitStack() as ph:
            actT = sb(ph, "actT", [128, NFC, T], BF16, 1)
            sg = sb(ph, "sg", [128, 2, T], F32, 2)
            norm_prep()
            if before is not None:
                before(ph)
            wgu = w_gu[which]
            wd = w_dn[which]
            for f0 in range(0, NFC, 4):
                nf = min(4, NFC - f0)
                gs, gb = wload(wgu, 0, 8, f0 * 128, nf * 128)
                us, ub = wload(wgu, 0, 8, DFF + f0 * 128, nf * 128)
                for fo in range(nf):
                    f = f0 + fo
                    pg, pgb = psf()
                    pu, pub = psf()
                    for kc in range(8):
                        fw.op("pe", lambda e: e.matmul(pg[:, 0:T], gs[:, kc, fo * 128:(fo + 1) * 128], hT.t[:, kc, :],
                                                       start=(kc == 0), stop=(kc == 7)), reads=[gb] + hT.b, writes=[pgb])
                    for kc in range(8):
                        fw.op("pe", lambda e: e.matmul(pu[:, 0:T], us[:, kc, fo * 128:(fo + 1) * 128], hT.t[:, kc, :],
                                                       start=(kc == 0), stop=(kc == 7)), reads=[ub] + hT.b, writes=[pub])
                    k = f % 2
                    fw.op("act", lambda e: e.activation(sg.t[:, k, :], pg[:, 0:T], AF.Silu), reads=[pgb], writes=[sg.b[k]])
                    fw.op("dve", lambda e: e.tensor_tensor(actT.t[:, f, :], sg.t[:, k, :], pu[:, 0:T], ALU.mult),
                          reads=[sg.b[k], pub], writes=actT.b)
            groups = [(0, 8), (8, 8), (16, 6)]
            for dh in range(2):
                slots = [wload(wd, g0, gn, dh * 512, 512) for (g0, gn) in groups]
                for tt in range(NTT):
                    ps, pb = psf()
                    for gi, (g0, gn) in enumerate(groups):
                        sl, slb = slots[gi]
                        for fo in range(gn):
                            f = g0 + fo
                            fw.op("pe", lambda e: e.matmul(ps[:, 0:512], actT.t[:, f, tt * 128:(tt + 1) * 128], sl[:, fo, :],
                                                           start=(f == 0), stop=(f == NFC - 1)),
                                  reads=[slb] + actT.b, writes=[pb])
                    xs = x_res.t[:, tt, dh * 512:(dh + 1) * 512]
                    fw.op("dve", lambda e: e.scalar_tensor_tensor(xs, ps[:, 0:512], 0.5, xs, ALU.mult, ALU.add),
                          reads=[pb, x_res.b[tt]], writes=[x_res.b[tt]])
            dm = False
            if after is not None:
                dm = after(ph)
            fw.barrier(("act", "dve"), dmas=bool(dm))

    def mixer(full, need_kv, first_full, after=None):
        with ExitStack() as ph:
            xtok = sb(ph, "xtok", [128, NTT, D], BF16, 1)
            BT = sb(ph, "BT", [128, 2, T], BF16)
            CT = sb(ph, "CT", [128, 2, T], BF16)
            Btok = sb(ph, "Btok", [128, NTT, 256], BF16)
            dt = sb(ph, "dt", [128, NTT, 16], F32)
            if full:
                yssmT = sb(ph, "yssmT", [128, 8, T], BF16)
                yattT = sb(ph, "yattT", [128, 8, T], BF16)
            with ExitStack() as p1:
                pre = sb(p1, "pre", [128, 2, T + 4], BF16, 2)
                xfm = sb(p1, "xfm", [128, 2, T], BF16, 2)
                dgs = sb(p1, "dgs", [128, 48, 128], BF16)
                fw.barrier(("pool",))
                fw.dma(dgs.t[:], dgw, writes=dgs.b, q="pool")
                cps = {}

                def ev_xbc(m, ps, pb):
                    k = m % 2
                    fw.op("act", lambda e: e.copy(pre.t[:, k, 3:T + 3], ps[:, 0:T]), reads=[pb], writes=[pre.b[k]])
                    fw.op("dve", lambda e: e.tensor_copy(pre.t[:, k, 0:3], halo.t[:, m, :]), reads=halo.b, writes=[pre.b[k]])
                    fw.op("dve", lambda e: e.tensor_copy(halo.t[:, m, :], pre.t[:, k, T:T + 3]), reads=[pre.b[k]], writes=halo.b)

                def ev_xbc2(m):
                    k = m % 2
                    pc, pcb = psf()
                    for tap in range(4):
                        fw.op("pe", lambda e: e.matmul(pc[:, 0:T], dgs.t[:, m * 4 + tap, :], pre.t[:, k, tap:tap + T],
                                                       start=(tap == 0), stop=(tap == 3)), reads=dgs.b + [pre.b[k]], writes=[pcb])
                    if m < 8:
                        fw.op("act", lambda e: e.activation(xfm.t[:, k, :], pc[:, 0:T], AF.Silu, bias=cb.t[:, m:m + 1]), reads=[pcb] + cb.b, writes=[xfm.b[k]])
                    elif m < 10:
                        g = m - 8
                        fw.op("act", lambda e: e.activation(BT.t[:, g, :], pc[:, 0:T], AF.Silu, bias=cb.t[:, m:m + 1]), reads=[pcb] + cb.b, writes=BT.b)
                    else:
                        g = m - 10
                        fw.op("act", lambda e: e.activation(CT.t[:, g, :], pc[:, 0:T], AF.Silu, bias=cb.t[:, m:m + 1]), reads=[pcb] + cb.b, writes=CT.b)

                def ev_xbc3(m):
                    k = m % 2
                    if m < 8:
                        pt, ptb = psb()
                        for tt in range(NTT):
                            fw.op("pe", lambda e: e.transpose(pt[:, tt * 128:(tt + 1) * 128], xfm.t[:, k, tt * 128:(tt + 1) * 128], identb_ap),
                                  reads=[xfm.b[k]] + identb.b, writes=[ptb])
                        fw.op("dve", lambda e: e.tensor_copy(xtok.t[:, :, m * 128:(m + 1) * 128], pt[:, 0:512].rearrange("p (t c) -> p t c", t=NTT)),
                              reads=[ptb], writes=xtok.b)
                    elif m < 10:
                        g = m - 8
                        pt, ptb = psb()
                        for tt in range(NTT):
                            fw.op("pe", lambda e: e.transpose(pt[:, tt * 128:(tt + 1) * 128], BT.t[:, g, tt * 128:(tt + 1) * 128], identb_ap),
                                  reads=BT.b + identb.b, writes=[ptb])
                        fw.op("dve", lambda e: e.tensor_copy(Btok.t[:, :, g * 128:(g + 1) * 128], pt[:, 0:512].rearrange("p (t c) -> p t c", t=NTT)),
                              reads=[ptb], writes=Btok.b)

                def ev_dt(u, tt, ps, pb):
                    fw.op("dve", lambda e: e.tensor_tensor(dt.t[:, tt, :], ps[:, 0:16], hp.t[:, 0, :], ALU.add), reads=[pb] + hp.b, writes=dt.b)
                    fw.op("act", lambda e: e.activation(dt.t[:, tt, :], dt.t[:, tt, :], AF.Exp), reads=dt.b, writes=dt.b)
                    fw.op("act", lambda e: e.activation(dt.t[:, tt, :], dt.t[:, tt, :], AF.Ln, bias=col1.t[:, 0:1]), reads=dt.b + col1.b, writes=dt.b)

                proj_tm(w_in, C_DT, 16, 8, lambda kc, tt: hT.t[:, kc, tt * 128:(tt + 1) * 128], hT.b, ev_dt)
                proj_fm(w_in, C_X, 12, 8, lambda kc: hT.t[:, kc, :], hT.b, ev_xbc, evac2=ev_xbc2, evac3=ev_xbc3)

                fw.barrier()
            dump("xtok", xtok.t[:], xtok.b)
            dump("dt", dt.t[:], dt.b)
            early = (not full) and (not need_kv) and (after is not None)
            if early:
                norm_prep()
                after(ph)

            def kv_unit(i):
                if i < 2:
                    slot, sbuf_ = wload(w_in, 0, 8, C_K + i * 512, 512)
                    for mo in range(4):
                        m = i * 4 + mo
                        ps, pb = psf()
                        for kc in range(8):
                            fw.op("pe", lambda e: e.matmul(ps[:, 0:T], slot[:, kc, mo * 128:(mo + 1) * 128], hT.t[:, kc, :],
                                                           start=(kc == 0), stop=(kc == 7)), reads=[sbuf_] + hT.b, writes=[pb])
                        fw.op("act", lambda e: e.copy(kT.t[:, m, T:2 * T], ps[:, 0:T]), reads=[pb], writes=kT.b)
                else:
                    u = i - 2
                    slot, sbuf_ = wload(w_in, 0, 8, C_V + u * 512, 512)
                    for tt in range(NTT):
                        ps, pb = psf()
                        for kc in range(8):
                            fw.op("pe", lambda e: e.matmul(ps[:, 0:512], hT.t[:, kc, tt * 128:(tt + 1) * 128], slot[:, kc, :],
                                                           start=(kc == 0), stop=(kc == 7)), reads=[sbuf_] + hT.b, writes=[pb])
                        fw.op("act", lambda e: e.copy(Vtok.t[:, 4 + tt, 64 + u * 512:64 + (u + 1) * 512], ps[:, 0:512]), reads=[pb], writes=Vtok.b)

            with ExitStack() as p2:
                NB = 2 if full else NTT
                a_t = sb(p2, "a_t", [128, NB, 16], F32, NB)
                wdt = sb(p2, "wdt", [128, NB, 16], F32, NB)
                decr = sb(p2, "decr", [128, NB, 16], F32, NB)
                xw = sb(p2, "xw", [128, NB, D], BF16, NB)
                if full:
                    sz = sb(p2, "sz", [128, 2, D], BF16, 2)
                    zslots = [wload(w_in, 0, 8, C_Z + u * 512, 512) for u in range(2)]
                    Y = sb(p2, "Y", [128, 2, 8, 128], F32, 2)
                    E = sb(p2, "E", [128, 2, 8, 128], BF16, 2)
                    Mm = sb(p2, "Mm", [128, 2, 8, 128], BF16, 2)
                    CBs = sb(p2, "CBs", [128, 2, 128], F32, 2)
                    t1 = sb(p2, "t1", [128, D], F32)
                    t2 = sb(p2, "t2", [128, 512], F32)
                    ynb = sb(p2, "ynb", [128, D], BF16)
                    ebias = sb(p2, "ebias", [128, 2, 16], F32, 2)
                    ecum = sb(p2, "ecum", [128, 2, 16], F32, 2)
                    gss = sb(p2, "gss", [128, 2], F32)
                    jk2 = sb(p2, "jk2", [128, 512], BF16)

                def ssd_pre(tt):
                    k = tt % NB
                    fw.op("dve", lambda e: e.tensor_tensor(a_t.t[:, k, :], dt.t[:, tt, :], Aneg.t[:], ALU.mult), reads=dt.b + Aneg.b, writes=[a_t.b[k]])
                    if full:
                        for u in range(2):
                            zs, zb = zslots[u]
                            pz, pzb = psf()
                            for kc in range(8):
                                fw.op("pe", lambda e: e.matmul(pz[:, 0:512], hT.t[:, kc, tt * 128:(tt + 1) * 128], zs[:, kc, :],
                                                               start=(kc == 0), stop=(kc == 7)), reads=[zb] + hT.b, writes=[pzb])
                            fw.op("act", lambda e: e.activation(sz.t[:, k, u * 512:(u + 1) * 512], pz[:, 0:512], AF.Silu), reads=[pzb], writes=[sz.b[k]])
                    pss, pssb = psf()
                    fw.op("pe", lambda e: e.matmul(pss[:, 0:16], cst.t[:, 2, :], a_t.t[:, k, :], start=True, stop=True), reads=cst.b + [a_t.b[k]], writes=[pssb])
                    fw.op("pe", lambda e: e.matmul(pss[:, 16:32], onesf.t[:], a_t.t[:, k, :], start=True, stop=True), reads=onesf.b + [a_t.b[k]], writes=[pssb])
                    if full:
                        fw.op("pe", lambda e: e.matmul(pss[:, 32:48], cst.t[:, 1, :], a_t.t[:, k, :], start=True, stop=True), reads=cst.b + [a_t.b[k]], writes=[pssb])
                    fw.op("act", lambda e: e.activation(wdt.t[:, k, :], pss[:, 0:16], AF.Exp), reads=[pssb], writes=[wdt.b[k]])
                    fw.op("dve", lambda e: e.tensor_tensor(wdt.t[:, k, :], wdt.t[:, k, :], dt.t[:, tt, :], ALU.mult), reads=[wdt.b[k]] + dt.b, writes=[wdt.b[k]])
                    fw.op("act", lambda e: e.activation(decr.t[:, k, :], pss[:, 16:32], AF.Exp), reads=[pssb], writes=[decr.b[k]])
                    if full:
                        fw.op("act", lambda e: e.activation(ebias.t[:, k, :], dt.t[:, tt, :], AF.Ln), reads=dt.b, writes=[ebias.b[k]])
                        fw.op("dve", lambda e: e.tensor_tensor(ebias.t[:, k, :], ebias.t[:, k, :], pss[:, 32:48], ALU.subtract), reads=[ebias.b[k], pssb], writes=[ebias.b[k]])
                        fw.op("act", lambda e: e.activation(ecum.t[:, k, :], pss[:, 32:48], AF.Exp), reads=[pssb], writes=[ecum.b[k]])
                    fw.op("dve", lambda e: e.tensor_tensor(xw.t[:, k, :].rearrange("p (h c) -> p h c", h=16),
                                                           xtok.t[:, tt, :].rearrange("p (h c) -> p h c", h=16),
                                                           wdt.t[:, k, :].unsqueeze(2).to_broadcast([128, 16, 64]), ALU.mult),
                          reads=xtok.b + [wdt.b[k]], writes=[xw.b[k]])

                def ssd_front(tt, g):
                    k = tt % 2
                    tsl = slice(tt * 128, (tt + 1) * 128)
                    fw.op("dve", lambda e: e.tensor_tensor(Y.t[:, g], cst.t[:, 1, :].unsqueeze(1).to_broadcast([128, 8, 128]),
                                                           a_t.t[:, k, g * 8:(g + 1) * 8].unsqueeze(2).to_broadcast([128, 8, 128]), ALU.mult),
                          reads=cst.b + [a_t.b[k]], writes=[Y.b[g]])
                    pc, pcb = psf()
                    fw.op("pe", lambda e: e.matmul(pc[:, 0:128], BT.t[:, g, tsl], CT.t[:, g, tsl], start=True, stop=True),
                          reads=BT.b + CT.b, writes=[pcb])
                    fw.op("act", lambda e: e.copy(CBs.t[:, g, :], pc[:, 0:128]), reads=[pcb], writes=[CBs.b[g]])

                def ssd_front_r(tt, g):
                    k = tt % 2
                    for q in range(2):
                        prt, prb = psf()
                        fw.op("pe", lambda e: e.matmul(prt[:, 0:512], onesf.t[:], Y.t[:, g, q * 4:(q + 1) * 4, :].rearrange("p h l -> p (h l)"),
                                                       start=True, stop=False), reads=onesf.b + [Y.b[g]], writes=[prb])
                        fw.op("pe", lambda e: e.matmul(prt[:, 0:512], identb_ap, NM8.t[:, 0:4, :].rearrange("p h l -> p (h l)"),
                                                       start=False, stop=True), reads=identb.b + NM8.b, writes=[prb])
                        for hh in range(4):
                            h = g * 8 + q * 4 + hh
                            fw.op("act", lambda e: e.activation(E.t[:, g, q * 4 + hh, :], prt[:, hh * 128:(hh + 1) * 128], AF.Exp,
                                                                bias=ebias.t[:, k, h:h + 1]), reads=[prb, ebias.b[k]], writes=[E.b[g]])

                def ssd_front_b(tt, g):
                    fw.op("dve", lambda e: e.tensor_tensor(Mm.t[:, g], E.t[:, g],
                                                           CBs.t[:, g, :].unsqueeze(1).to_broadcast([128, 8, 128]), ALU.mult),
                          reads=[E.b[g], CBs.b[g]], writes=[Mm.b[g]])

                def ssd_back(tt, g):
                    k = tt % 2
                    tsl = slice(tt * 128, (tt + 1) * 128)
                    gsl = slice(g * 512, (g + 1) * 512)
                    if g == 0:
                        fw.op("act", lambda e: e.copy(Hbf.t[:], Hst.t[:]), reads=Hst.b, writes=Hbf.b)
                    pyt, pyb = psfix(4)
                    for hh in range(8):
                        h = g * 8 + hh
                        fw.op("pe", lambda e: e.matmul(pyt[:, hh * 64:(hh + 1) * 64], Mm.t[:, g, hh, :], xtok.t[:, tt, h * 64:(h + 1) * 64],
                                                       start=True, stop=True), reads=[Mm.b[g]] + xtok.b, writes=[pyb])
                    po, pob = psfix(5)
                    fw.op("pe", lambda e: e.matmul(po[:, 0:512], CT.t[:, g, tsl], Hbf.t[:, gsl], start=True, stop=True),
                          reads=CT.b + Hbf.b, writes=[pob])

                def ssd_back_rest(tt, g):
                    k = tt % 2
                    tsl = slice(tt * 128, (tt + 1) * 128)
                    gsl = slice(g * 512, (g + 1) * 512)
                    pyt, pyb = psfix(4)
                    po, pob = psfix(5)
                    fw.op("dve", lambda e: e.tensor_tensor(t1.t[:, gsl].rearrange("p (h c) -> p h c", h=8),
                                                           po[:, 0:512].rearrange("p (h c) -> p h c", h=8),
                                                           ecum.t[:, k, g * 8:(g + 1) * 8].unsqueeze(2).to_broadcast([128, 8, 64]), ALU.mult),
                          reads=[pob, ecum.b[k]], writes=t1.b)
                    fw.op("dve", lambda e: e.tensor_tensor(t2.t[:].rearrange("p (h c) -> p h c", h=8),
                                                           xtok.t[:, tt, gsl].rearrange("p (h c) -> p h c", h=8),
                                                           hp.t[:, 2, g * 8:(g + 1) * 8].unsqueeze(2).to_broadcast([128, 8, 64]), ALU.mult),
                          reads=xtok.b + hp.b, writes=t2.b)
                    fw.op("dve", lambda e: e.tensor_tensor(t1.t[:, gsl], t1.t[:, gsl], t2.t[:], ALU.add), reads=t1.b + t2.b, writes=t1.b)
                    fw.op("dve", lambda e: e.tensor_tensor(t1.t[:, gsl], t1.t[:, gsl], pyt[:, 0:512], ALU.add), reads=t1.b + [pyb], writes=t1.b)
                    fw.op("dve", lambda e: e.tensor_tensor(t1.t[:, gsl], t1.t[:, gsl], sz.t[:, k, gsl], ALU.mult), reads=t1.b + [sz.b[k]], writes=t1.b)
                    fw.op("dve", lambda e: e.memset(gss.t[:, g:g + 1], 0.0), writes=gss.b)
                    fw.op("act", lambda e: e.activation(jk2.t[:], t1.t[:, gsl], AF.Square, scale=float(512 ** -0.5),
                                                        accum_out=gss.t[:, g:g + 1]), reads=t1.b, writes=jk2.b + gss.b)
                    rsqrt_eps(gss.t[:, g:g + 1], gss.t[:, g:g + 1], gss.b, gss.b)
                    fw.op("dve", lambda e: e.tensor_scalar(ynb.t[:, gsl], t1.t[:, gsl], gss.t[:, g:g + 1], None, ALU.mult),
                          reads=t1.b + gss.b, writes=ynb.b)
                    if g == 1:
                        if tt == 0:
                            dump("yssd", t1.t[:], t1.b)
                        pt, ptb = psb()
                        for j in range(8):
                            fw.op("pe", lambda e: e.transpose(pt[:, j * 128:(j + 1) * 128], ynb.t[:, j * 128:(j + 1) * 128], identb_ap),
                                  reads=ynb.b + identb.b, writes=[ptb])
                        fw.op("dve", lambda e: e.tensor_tensor(yssmT.t[:, :, tsl], pt[:].rearrange("p (j c) -> p j c", j=8),
                                                               nwf.t[:, 2, :].unsqueeze(2).to_broadcast([128, 8, 128]), ALU.mult),
                              reads=[ptb] + nwf.b, writes=yssmT.b)

                def ssd_post(tt):
                    k = tt % NB
                    for g in range(2):
                        pst_, pstb = psf()
                        fw.op("pe", lambda e: e.matmul(pst_[:, 0:512], Btok.t[:, tt, g * 128:(g + 1) * 128], xw.t[:, k, g * 512:(g + 1) * 512],
                                                       start=True, stop=True), reads=Btok.b + [xw.b[k]], writes=[pstb])
                        hs = Hst.t[:, g * 512:(g + 1) * 512]
                        fw.op("dve", lambda e: e.tensor_tensor(hs.rearrange("p (h c) -> p h c", h=8), hs.rearrange("p (h c) -> p h c", h=8),
                                                               decr.t[:, k, g * 8:(g + 1) * 8].unsqueeze(2).to_broadcast([128, 8, 64]), ALU.mult),
                              reads=Hst.b + [decr.b[k]], writes=Hst.b)
                        fw.op("dve", lambda e: e.tensor_tensor(hs, hs, pst_[:, 0:512], ALU.add), reads=Hst.b + [pstb], writes=Hst.b)

                if not full:
                    for tt in range(NTT):
                        ssd_pre(tt)
                    for tt in range(NTT):
                        ssd_post(tt)
                else:
                    steps = [(tt, g) for tt in range(NTT) for g in range(2)]
                    prev = None
                    for (tt, g) in steps:
                        if g == 0:
                            ssd_pre(tt)
                        ssd_front(tt, g)
                        if prev is not None:
                            ssd_back(*prev)
                        ssd_front_r(tt, g)
                        if prev is not None:
                            ssd_back_rest(*prev)
                            if prev[1] == 1:
                                ssd_post(prev[0])
                        ssd_front_b(tt, g)
                        prev = (tt, g)
                        if g == 1:
                            kv_unit(tt)
                    ssd_back(*prev)
                    ssd_back_rest(*prev)
                    ssd_post(prev[0])
                fw.barrier()
            dump("Hst", Hst.t[:], Hst.b)

            if need_kv:
                with ExitStack() as p3:
                    if not full:
                        for i in range(4):
                            kv_unit(i)
                    if full:
                        qT0 = sb(p3, "qT0", [128, 8, T], BF16)
                        qT1 = sb(p3, "qT1", [128, 8, T], BF16)
                        qTs = [qT0, qT1]
                        PT = sb(p3, "PT", [128, 4, 512], BF16, 4)
                        rden = sb(p3, "rden", [128, 512], F32)
                        dtmp = sb(p3, "dtmp", [128, 512], F32)
                        fw.op("dve", lambda e: e.memset(qT0.t[64:128, :, :], 0.0), writes=qT0.b)
                        fw.op("dve", lambda e: e.memset(qT1.t[0:64, :, :], 0.0), writes=qT1.b)

                        def ev_q(m, ps, pb):
                            fw.op("act", lambda e: e.activation(qT0.t[0:64, m, :], ps[0:64, 0:T], AF.Copy, scale=0.125), reads=[pb], writes=qT0.b)
                            fw.op("act", lambda e: e.activation(qT1.t[64:128, m, :], ps[64:128, 0:T], AF.Copy, scale=0.125), reads=[pb], writes=qT1.b)

                        proj_fm(w_in, C_Q, 8, 8, lambda kc: hT.t[:, kc, :], hT.b, ev_q)
                        vaug = sb(p3, "vaug", [128, 2, 8, 128], BF16, 2)

                        def prep_v(h):
                            hb = h % 2
                            if hb == 0:
                                vc, oc, osrc = 0, 64, 1088
                            else:
                                vc, oc, osrc = 64, 0, 0
                            fw.op("dve", lambda e: e.tensor_copy(vaug.t[:, hb, :, vc:vc + 64], Vtok.t[:, :, 64 + 64 * h:128 + 64 * h]),
                                  reads=Vtok.b, writes=[vaug.b[hb]])
                            fw.op("dve", lambda e: e.tensor_copy(vaug.t[:, hb, :, oc:oc + 64], Vtok.t[:, :, osrc:osrc + 64]),
                                  reads=Vtok.b, writes=[vaug.b[hb]])

                        def stageA(hpair, h2, j, k):
                            h = hpair * 2 + h2
                            qt_lo = max(0, j - 4)
                            qt_hi = min(3, j)
                            q0 = qt_lo * 128
                            nq = (qt_hi - qt_lo + 1) * 128
                            rel0 = (qt_lo * 2 + 8 - 2 * j) * 64
                            if j == 0 and h == 0:
                                prep_v(0)
                            if j == 3 and h < 15:
                                prep_v(h + 1)
                            sp_, spb = psf()
                            qh = qTs[h2]
                            fw.op("pe", lambda e: e.matmul(sp_[:, 0:nq], kT.t[:, hpair, j * 128:(j + 1) * 128], qh.t[:, hpair, q0:q0 + nq],
                                                           start=True, stop=False), reads=kT.b + qh.b, writes=[spb])
                            fw.op("pe", lambda e: e.matmul(sp_[:, 0:nq], identb_ap, EB.t[:, h, rel0:rel0 + nq],
                                                           start=False, stop=True), reads=identb.b + EB.b, writes=[spb])
                            fw.op("act", lambda e: e.activation(PT.t[:, k, 0:nq], sp_[:, 0:nq], AF.Exp), reads=[spb], writes=[PT.b[k]])

                        def stageB(hpair, h2, j, k):
                            h = hpair * 2 + h2
                            qt_lo = max(0, j - 4)
                            qt_hi = min(3, j)
                            q0 = qt_lo * 128
                            nq = (qt_hi - qt_lo + 1) * 128
                            pX, pXb = acc_bank(hpair, h2)
                            segs = [(q0, nq)]
                            if 1 <= j <= 3:
                                segs = [(q0, nq - 128), (q0 + nq - 128, 128)]
                            for (sq0, snq) in segs:
                                fw.op("pe", lambda e: e.matmul(pX[:, sq0:sq0 + snq], vaug.t[:, h2, j, :], PT.t[:, k, sq0 - q0:sq0 - q0 + snq],
                                                               start=(j == 0), stop=(j == 7)), reads=[vaug.b[h2], PT.b[k]], writes=[pXb])
                            if h2 == 1 and j == 7:
                                pA, pAb = acc_bank(hpair, 0)
                                pB, pBb = acc_bank(hpair, 1)
                                fw.op("act", lambda e: e.copy(dtmp.t[0:64, :], pA[64:128, 0:512]), reads=[pAb], writes=dtmp.b)
                                fw.op("act", lambda e: e.copy(dtmp.t[64:128, :], pB[0:64, 0:512]), reads=[pBb], writes=dtmp.b)
                                fw.op("dve", lambda e: e.reciprocal(rden.t[:], dtmp.t[:]), reads=dtmp.b, writes=rden.b)
                                fw.op("dve", lambda e: e.tensor_tensor(yattT.t[0:64, hpair, :], pA[0:64, 0:512], rden.t[0:64, :], ALU.mult),
                                      reads=[pAb] + rden.b, writes=yattT.b)
                                fw.op("dve", lambda e: e.tensor_tensor(yattT.t[64:128, hpair, :], pB[64:128, 0:512], rden.t[64:128, :], ALU.mult),
                                      reads=[pBb] + rden.b, writes=yattT.b)

                        its = [(hpair, h2, j) for hpair in range(8) for h2 in range(2) for j in range(8)]
                        pend = []
                        for n_, (hpair, h2, j) in enumerate(its):
                            k = n_ % 4
                            stageA(hpair, h2, j, k)
                            pend.append((hpair, h2, j, k))
                            if len(pend) > 3:
                                stageB(*pend.pop(0))
                        while pend:
                            stageB(*pend.pop(0))
                    fw.barrier()
                fw.op("act", lambda e: e.copy(kT.t[:, :, 0:T], kT.t[:, :, T:2 * T]), reads=kT.b, writes=kT.b)
                fw.op("dve", lambda e: e.tensor_copy(Vtok.t[:, 0:4, :], Vtok.t[:, 4:8, :]), reads=Vtok.b, writes=Vtok.b)

            if full:
                dump("yssmT", yssmT.t[:], yssmT.b)
                dump("yattT", yattT.t[:], yattT.b)
                with ExitStack() as p4:
                    norm_prep()
                    mT = sb(p4, "mT", [128, 8, T], BF16)
                    s3 = sb(p4, "s3", [128, 2, T], F32, 2)
                    s4 = sb(p4, "s4", [128, 2, T], F32, 2)
                    for u0 in (0, 4):
                        sl1, b1 = wload(w_bs, 0, 8, u0 * 128, 512)
                        sl2, b2 = wload(w_ba, 0, 8, u0 * 128, 512)
                        sl3, b3 = wload(w_in, 0, 8, C_GS + u0 * 128, 512)
                        sl4, b4 = wload(w_in, 0, 8, C_GA + u0 * 128, 512)
                        for mo in range(4):
                            m = u0 + mo
                            k = m % 2
                            msl = slice(mo * 128, (mo + 1) * 128)
                            p1_, p1b = psf()
                            p2_, p2b = psf()
                            p3_, p3b = psf()
                            p4_, p4b = psf()
                            for kc in range(8):
                                fw.op("pe", lambda e: e.matmul(p3_[:, 0:T], sl3[:, kc, msl], hT.t[:, kc, :], start=(kc == 0), stop=(kc == 7)),
                                      reads=[b3] + hT.b, writes=[p3b])
                            for kc in range(8):
                                fw.op("pe", lambda e: e.matmul(p4_[:, 0:T], sl4[:, kc, msl], hT.t[:, kc, :], start=(kc == 0), stop=(kc == 7)),
                                      reads=[b4] + hT.b, writes=[p4b])
                            for kc in range(8):
                                fw.op("pe", lambda e: e.matmul(p1_[:, 0:T], sl1[:, kc, msl], yssmT.t[:, kc, :], start=(kc == 0), stop=(kc == 7)),
                                      reads=[b1] + yssmT.b, writes=[p1b])
                            for kc in range(8):
                                fw.op("pe", lambda e: e.matmul(p2_[:, 0:T], sl2[:, kc, msl], yattT.t[:, kc, :], start=(kc == 0), stop=(kc == 7)),
                                      reads=[b2] + yattT.b, writes=[p2b])
                            fw.op("act", lambda e: e.activation(s3.t[:, k, :], p3_[:, 0:T], AF.Sigmoid), reads=[p3b], writes=[s3.b[k]])
                            fw.op("act", lambda e: e.activation(s4.t[:, k, :], p4_[:, 0:T], AF.Sigmoid), reads=[p4b], writes=[s4.b[k]])
                            fw.op("dve", lambda e: e.tensor_tensor(s3.t[:, k, :], s3.t[:, k, :], p1_[:, 0:T], ALU.mult), reads=[s3.b[k], p1b], writes=[s3.b[k]])
                            fw.op("dve", lambda e: e.tensor_tensor(s4.t[:, k, :], s4.t[:, k, :], p2_[:, 0:T], ALU.mult), reads=[s4.b[k], p2b], writes=[s4.b[k]])
                            fw.op("dve", lambda e: e.tensor_tensor(mT.t[:, m, :], s3.t[:, k, :], s4.t[:, k, :], ALU.add), reads=[s3.b[k], s4.b[k]], writes=mT.b)

                    def ev_out(u, tt, ps, pb):
                        xs = x_res.t[:, tt, u * 512:(u + 1) * 512]
                        fw.op("dve", lambda e: e.tensor_tensor(xs, xs, ps[:, 0:512], ALU.add), reads=[pb, x_res.b[tt]], writes=[x_res.b[tt]])

                    proj_tm(w_out, 0, 1024, 8, lambda kc, tt: mT.t[:, kc, tt * 128:(tt + 1) * 128], mT.b, ev_out)
                    if after is not None:
                        after(p4)
                    fw.barrier()
            if not full:
                if not early:
                    norm_prep()
                    if after is not None:
                        after(ph)
                fw.barrier()

    onesf = sb(st, "onesf", [128, 128], F32)
    fw.op("dve", lambda e: e.memset(onesf.t[:], 1.0), writes=onesf.b)

    def tile_src(ti):
        full = ti >= n_state
        src = x_main if full else x_prev
        tok0 = ((ti - n_state) if full else (NTOK // T - n_state + ti)) * T
        return src, tok0

    def load_x(ti):
        if ti >= n_state + n_full:
            return
        src, tok0 = tile_src(ti)
        for tt in range(NTT):
            fw.dma(x_res.t[:, tt, :], src[tok0 + tt * 128: tok0 + (tt + 1) * 128, :], writes=[x_res.b[tt]], q="pool")

    def final_out(ph, tok0):
        yo = sb(ph, "yo", [128, 2, D], F32, 2)
        ss = ssp
        for tt in range(NTT):
            k = tt % 2
            fw.op("dve", lambda e: e.memset(ss.t[:, tt:tt + 1], 0.0), writes=ss.b)
            fw.op("act", lambda e: e.activation(xnp.t[:, k, :], x_res.t[:, tt, :], AF.Square, scale=1.0 / 32.0, accum_out=ss.t[:, tt:tt + 1]),
                  reads=[x_res.b[tt]], writes=[xnp.b[k]] + ss.b)
            rsqrt_eps(ss.t[:, tt:tt + 1], ss.t[:, tt:tt + 1], ss.b, ss.b)
            fw.op("dve", lambda e: e.scalar_tensor_tensor(yo.t[:, k, :], x_res.t[:, tt, :], ss.t[:, tt:tt + 1], fnw.t[:], ALU.mult, ALU.mult),
                  reads=[x_res.b[tt]] + ss.b + fnw.b, writes=[yo.b[k]])
            fw.dma(y_out[tok0 + tt * 128: tok0 + (tt + 1) * 128, :], yo.t[:, k, :], reads=[yo.b[k]], q="pool")

    load_x(0)
    norm_prep()
    with ExitStack() as ph0:
        norm_to_hT(0, ph0)
        fw.barrier()
    for ti in range(n_state + n_full):
        full = ti >= n_state
        src, tok0 = tile_src(ti)
        last_state = (ti == n_state - 1)

        def after_ffn1(ph, full=full, ti=ti):
            norm_to_hT(1, ph)
            if not full:
                load_x(ti + 1)
            return False

        ffn(0, after=after_ffn1)
        if full and ti == n_state:
            dump("x1", x_res.t[:], x_res.b)

        def after_mixer(ph, full=full, ti=ti, last_state=last_state):
            if last_state:
                hm = hmask.t[:, 0:1]
                fw.op("dve", lambda e: e.tensor_scalar(Hst.t[:], Hst.t[:], hm, None, ALU.mult), reads=Hst.b + hmask.b, writes=Hst.b)
                fw.op("dve", lambda e: e.tensor_scalar(halo.t[:], halo.t[:], hm, None, ALU.mult), reads=halo.b + hmask.b, writes=halo.b)
                fw.op("dve", lambda e: e.tensor_scalar(Vtok.t[:, 0:4, :], Vtok.t[:, 0:4, :], hm, None, ALU.mult), reads=Vtok.b + hmask.b, writes=Vtok.b)
            if full:
                if ti == n_state:
                    dump("x2", x_res.t[:], x_res.b)
                norm_to_hT(3, ph)
            else:
                norm_to_hT(0, ph)

        mixer(full, full or last_state, ti == n_state, after=after_mixer)
        if full:
            nxt = {}

            def before_ffn2(ph, ti=ti, nxt=nxt):
                if ti + 1 < n_state + n_full:
                    xq = sb(ph, "x_nxt", [128, NTT, D], F32, NTT)
                    nxt["x"] = xq
                    fw.barrier(("pool",))
                    src2, tok2 = tile_src(ti + 1)
                    for tt in range(NTT):
                        fw.dma(xq.t[:, tt, :], src2[tok2 + tt * 128: tok2 + (tt + 1) * 128, :], writes=[xq.b[tt]], q="pool")

            def after_ffn2(ph, tok0=tok0, ti=ti, nxt=nxt):
                if "x" in nxt:
                    norm_to_hT(0, ph, src=nxt["x"])
                final_out(ph, tok0)
                if "x" in nxt:
                    xq = nxt["x"]
                    for tt in range(NTT):
                        eng = "dve" if tt % 2 == 0 else "act"
                        if eng == "dve":
                            fw.op("dve", lambda e: e.tensor_copy(x_res.t[:, tt, :], xq.t[:, tt, :]), reads=[xq.b[tt]], writes=[x_res.b[tt]])
                        else:
                            fw.op("act", lambda e: e.copy(x_res.t[:, tt, :], xq.t[:, tt, :]), reads=[xq.b[tt]], writes=[x_res.b[tt]])
                return True

            ffn(1, after=after_ffn2, before=before_ffn2)
    fw.barrier(("sp", "act", "pool"), dmas=True)
    st.close()
    return nc


def _consts():
    t = np.arange(128)
    ident = np.eye(128, dtype=np.float32)
    U = (t[:, None] <= t[None, :]).astype(np.float32)
    Ls = (t[:, None] > t[None, :]).astype(np.float32)
    nm = np.where(t[:, None] > t[None, :], NEG, 0.0).astype(np.float32)
    return np.ascontiguousarray(np.stack([ident, U, Ls, nm], axis=1))


def _prep_shared(inp):
    f = lambda a: np.ascontiguousarray(np.asarray(a, dtype=np.float32))
    fm = lambda w: np.asarray(w, dtype=np.float32).reshape(8, 128).T
    sh = {}
    sh["w_gu1"] = f(inp["ffn1_w_gu"][0])
    sh["w_gu2"] = f(inp["ffn2_w_gu"][0])
    sh["w_dn1"] = f(inp["ffn1_w_down"][0])
    sh["w_dn2"] = f(inp["ffn2_w_down"][0])
    sh["w_in"] = f(inp["w_in"][0])
    sh["w_bs"] = f(inp["w_branch_ssm"][0])
    sh["w_ba"] = f(inp["w_branch_attn"][0])
    sh["w_out"] = f(inp["w_out"][0])
    sh["nwf"] = f(np.stack([fm(inp["ffn1_norm_w"][0]), fm(inp["mix_norm_w"][0]), fm(inp["ssm_norm_w"][0]),
                            fm(inp["ffn2_norm_w"][0])], axis=1))
    sh["fnw"] = f(np.broadcast_to(np.asarray(inp["final_norm_w"], np.float32)[None, :], (128, D)))
    cwv = np.asarray(inp["conv_w"][0], np.float32)
    sh["cw"] = f(cwv.reshape(4, 12, 128).transpose(2, 1, 0))
    sh["cb"] = f(np.asarray(inp["conv_b"][0], np.float32).reshape(12, 128).T)
    hpv = np.stack([np.asarray(inp["dt_bias"][0], np.float32), np.asarray(inp["A_log"][0], np.float32),
                    np.asarray(inp["D_skip"][0], np.float32)], axis=0)
    sh["hp"] = f(np.broadcast_to(hpv[None], (128, 3, 16)))
    rb = np.asarray(inp["rel_bias"][0], np.float32)
    r = np.arange(128)[:, None]
    l = np.arange(640)[None, :]
    idx = np.clip(l - r, -128, 128) + 128
    kc = r // 64
    qc = l // 64
    valid = (qc >= kc) & (qc <= kc + 8)
    tab = rb[:, idx]
    tab = np.where(valid[None], tab, np.float32(NEG))
    sh["btab"] = f(tab)
    sh["cst"] = _consts()
    return sh


_NC_CACHE = {}


def kernel(**inputs):
    x = np.asarray(inputs["x"], dtype=np.float32)
    sh = _prep_shared(inputs)
    in_maps = []
    for c in range(8):
        b, h = c // 2, c % 2
        m = dict(sh)
        m["x_main"] = np.ascontiguousarray(x[b, h * NTOK:(h + 1) * NTOK])
        m["x_prev"] = np.ascontiguousarray(x[b, 0:NTOK])
        m["hmask"] = np.full((128, 1), float(h), dtype=np.float32)
        in_maps.append(m)
    if "nc" not in _NC_CACHE:
        _NC_CACHE["nc"] = build()
    res = run_bass_kernel_spmd(_NC_CACHE["nc"], in_maps, core_ids=list(range(8)))
    out = np.empty((4, 2 * NTOK, D), dtype=np.float32)
    for c in range(8):
        b, h = c // 2, c % 2
        out[b, h * NTOK:(h + 1) * NTOK] = np.asarray(res.results[c]["y"], dtype=np.float32)
    return out
```
